# Optimizing a Trainium2 kernel written in Bass

```python
import math
import jax, jax.numpy as jnp
from jax import lax
import numpy as np

D_MODEL = 1024
BATCH = 2
SEQ = 8192
DEPTH = 2

HEAD_DIM = 64
N_HEADS_MOBA = 6
N_HEADS_SSD = 6
N_HEADS_DIL = 6
N_HEADS_DELTA = 6
W_MOBA = N_HEADS_MOBA * HEAD_DIM
W_SSD = N_HEADS_SSD * HEAD_DIM
W_DIL = N_HEADS_DIL * HEAD_DIM
W_DELTA = N_HEADS_DELTA * HEAD_DIM
D_MIX = W_MOBA + W_SSD + W_DIL + W_DELTA

MOBA_BLOCK = 256
MOBA_TOPK = 3
MOBA_Q_CHUNK = 64

SSD_STATE = 128
SSD_GROUPS = 2
SSD_CHUNK = 128
SSD_XBC = W_SSD + 2 * SSD_GROUPS * SSD_STATE

CONV_WIDTH = 4

DIL_PAIRS = ((128, 1), (512, 4), (2048, 16))
DIL_Q_CHUNK = 128

DELTA_CHUNK = 64
DELTA_QKV = 3 * W_DELTA

LN_EPS = 1e-5
RMS_EPS = 1e-6
DEEPNORM_ALPHA = (2.0 * DEPTH) ** 0.25
DEEPNORM_BETA = (8.0 * DEPTH) ** -0.25

IN_SPLIT_SIZES = (
    W_MOBA, W_MOBA, W_MOBA, W_MOBA,
    SSD_XBC, W_SSD, N_HEADS_SSD,
    W_DIL, W_DIL, W_DIL, W_DIL,
    DELTA_QKV, W_DELTA, N_HEADS_DELTA, N_HEADS_DELTA,
)
IN_COLS = int(sum(IN_SPLIT_SIZES))
IN_SPLIT_POINTS = tuple(int(i) for i in np.cumsum(IN_SPLIT_SIZES)[:-1])

kernel_name = "hymba_style_moba_ssd_dilated_gdn_deepnorm"

NEG_INF = -jnp.inf


def alibi_slopes():
    n = N_HEADS_DIL + N_HEADS_MOBA
    s = 2.0 ** (-8.0 * (np.arange(n) + 1) / n)
    s = jnp.asarray(s, dtype=jnp.float32)
    return s[:N_HEADS_DIL], s[N_HEADS_DIL:]


def layer_norm(x, g, b):
    xf = x.astype(jnp.float32)
    mu = xf.mean(-1, keepdims=True)
    var = jnp.square(xf - mu).mean(-1, keepdims=True)
    return ((xf - mu) * lax.rsqrt(var + LN_EPS) * g + b).astype(x.dtype)


def rms_norm(x, w):
    xf = x.astype(jnp.float32)
    return (xf * lax.rsqrt(jnp.mean(xf * xf, -1, keepdims=True) + RMS_EPS) * w).astype(x.dtype)


def l2_normalize(x):
    xf = x.astype(jnp.float32)
    return xf * lax.rsqrt(jnp.sum(xf * xf, -1, keepdims=True) + RMS_EPS)


def causal_dw_conv(x, w, b):
    K, C = w.shape
    y = lax.conv_general_dilated(
        x, w[:, None, :].astype(x.dtype), window_strides=(1,), padding=[(K - 1, 0)],
        dimension_numbers=("NWC", "WIO", "NWC"), feature_group_count=C)
    return y + b.astype(x.dtype)


def moba_attention(q, k, v, slopes):
    Bsz, S, H, Dh = q.shape
    L = MOBA_BLOCK
    nblk = -(-S // L)
    pad = nblk * L - S
    scale = Dh ** -0.5
    kp = jnp.pad(k, ((0, 0), (0, pad), (0, 0), (0, 0)))
    vp = jnp.pad(v, ((0, 0), (0, pad), (0, 0), (0, 0)))
    kb = kp.reshape(Bsz, nblk, L, H, Dh).transpose(0, 3, 1, 2, 4)
    vb = vp.reshape(Bsz, nblk, L, H, Dh).transpose(0, 3, 1, 2, 4)
    kmean = kb.astype(jnp.float32).mean(axis=3)
    qt = q.transpose(0, 2, 1, 3)

    gate = jnp.einsum("bhsd,bhnd->bhsn", qt.astype(jnp.float32), kmean)
    qblk = jnp.arange(S) // L
    past = jnp.arange(nblk)[None, :] < qblk[:, None]
    gate = jnp.where(past[None, None], gate, NEG_INF)
    k_sel = min(MOBA_TOPK, nblk)
    _, sel = lax.top_k(gate, k_sel)
    valid = sel < qblk[None, None, :, None]

    bi = jnp.arange(Bsz)[:, None, None, None]
    hi = jnp.arange(H)[None, :, None, None]
    Qc = MOBA_Q_CHUNK
    sl = slopes[None, :, None, None]

    def chunk(c):
        t0 = c * Qc
        tq = t0 + jnp.arange(Qc)
        qc = lax.dynamic_slice_in_dim(qt, t0, Qc, axis=2)
        selc = lax.dynamic_slice_in_dim(sel, t0, Qc, axis=2)
        validc = lax.dynamic_slice_in_dim(valid, t0, Qc, axis=2)
        kg = kb[bi, hi, selc]
        vg = vb[bi, hi, selc]
        pos_g = selc[..., None] * L + jnp.arange(L)
        dist_g = (tq[None, None, :, None, None] - pos_g).astype(jnp.float32)
        s_g = jnp.einsum("bhqd,bhqjld->bhqjl", qc, kg).astype(jnp.float32) * scale
        s_g = s_g - sl[..., None] * dist_g
        s_g = jnp.where(validc[..., None], s_g, NEG_INF).reshape(Bsz, H, Qc, k_sel * L)
        own = t0 // L
        ko = lax.dynamic_index_in_dim(kb, own, axis=2, keepdims=False)
        vo = lax.dynamic_index_in_dim(vb, own, axis=2, keepdims=False)
        pos_o = own * L + jnp.arange(L)
        dist_o = (tq[:, None] - pos_o[None, :]).astype(jnp.float32)
        s_o = jnp.einsum("bhqd,bhld->bhql", qc, ko).astype(jnp.float32) * scale
        s_o = s_o - sl * dist_o[None, None]
        s_o = jnp.where((pos_o[None, :] <= tq[:, None])[None, None], s_o, NEG_INF)
        p = jax.nn.softmax(jnp.concatenate([s_g, s_o], axis=-1), axis=-1).astype(v.dtype)
        p_g = p[..., :k_sel * L].reshape(Bsz, H, Qc, k_sel, L)
        p_o = p[..., k_sel * L:]
        return (jnp.einsum("bhqjl,bhqjld->bhqd", p_g, vg)
                + jnp.einsum("bhql,bhld->bhqd", p_o, vo))

    outs = lax.map(chunk, jnp.arange(S // Qc))
    return outs.transpose(1, 0, 3, 2, 4).reshape(Bsz, S, H, Dh)


def ssd_chunked(X, a, Bh, Ch):
    Bsz, S, H, P = X.shape
    N = Bh.shape[-1]
    L = SSD_CHUNK
    nc = S // L
    X = X.reshape(Bsz, nc, L, H, P)
    Bc = Bh.reshape(Bsz, nc, L, H, N)
    Cc = Ch.reshape(Bsz, nc, L, H, N)
    a = a.reshape(Bsz, nc, L, H).transpose(0, 3, 1, 2)
    a_cum = jnp.cumsum(a, axis=-1)
    causal = jnp.tril(jnp.ones((L, L), dtype=bool))
    Lmat = jnp.exp(jnp.where(causal, a_cum[..., :, None] - a_cum[..., None, :], NEG_INF))
    scores = jnp.einsum("bclhn,bcshn->bhcls", Cc, Bc)
    y_diag = jnp.einsum("bhcls,bcshp->bclhp", scores * Lmat, X)
    decay_states = jnp.exp(a_cum[..., -1:] - a_cum)
    states = jnp.einsum("bclhn,bhcl,bclhp->bchpn", Bc, decay_states, X)
    chunk_decay = jnp.exp(a_cum[..., -1])

    def step(h, inp):
        st, dec = inp
        return h * dec[..., None, None] + st, h

    h0 = jnp.zeros((Bsz, H, P, N), dtype=X.dtype)
    _, h_prev = lax.scan(step, h0, (jnp.moveaxis(states, 1, 0), jnp.moveaxis(chunk_decay, 2, 0)))
    h_prev = jnp.moveaxis(h_prev, 0, 1)
    y_off = jnp.einsum("bclhn,bchpn,bhcl->bclhp", Cc, h_prev, jnp.exp(a_cum))
    return (y_diag + y_off).reshape(Bsz, S, H, P)


def ssd_mixer(xbc, dt_raw, dt_bias, A_log, D_skip):
    Bsz, S, _ = xbc.shape
    xbc = xbc.astype(jnp.float32)
    xs, Bm, Cm = jnp.split(xbc, [W_SSD, W_SSD + SSD_GROUPS * SSD_STATE], axis=-1)
    xs = xs.reshape(Bsz, S, N_HEADS_SSD, HEAD_DIM)
    hpg = N_HEADS_SSD // SSD_GROUPS
    Bh = jnp.repeat(Bm.reshape(Bsz, S, SSD_GROUPS, SSD_STATE), hpg, axis=2)
    Ch = jnp.repeat(Cm.reshape(Bsz, S, SSD_GROUPS, SSD_STATE), hpg, axis=2)
    dt = jax.nn.softplus(dt_raw.astype(jnp.float32) + dt_bias)
    A = -jnp.exp(A_log.astype(jnp.float32))
    y = ssd_chunked(xs * dt[..., None], dt * A, Bh, Ch)
    y = y + D_skip.astype(jnp.float32)[:, None] * xs
    return y.reshape(Bsz, S, W_SSD)


def dilated_attention(q, k, v, slopes):
    Bsz, S, H, Dh = q.shape
    scale = Dh ** -0.5
    Qc = DIL_Q_CHUNK
    sl = slopes[None, :, None, None]

    def chunk(c):
        t0 = c * Qc
        tq = t0 + jnp.arange(Qc)
        qc = lax.dynamic_slice_in_dim(q, t0, Qc, axis=1)
        outs, lses = [], []
        for (w, d) in DIL_PAIRS:
            nk = w // d + 1
            dist = d * jnp.arange(nk)
            idx = tq[:, None] - dist[None, :]
            ok = idx >= 0
            idxc = jnp.maximum(idx, 0)
            kg = jnp.take(k, idxc, axis=1)
            vg = jnp.take(v, idxc, axis=1)
            s = jnp.einsum("bqhd,bqnhd->bhqn", qc, kg).astype(jnp.float32) * scale
            s = s - sl * dist.astype(jnp.float32)[None, None, None, :]
            s = jnp.where(ok[None, None], s, NEG_INF)
            m = jnp.max(s, axis=-1, keepdims=True)
            e = jnp.exp(s - m)
            den = jnp.sum(e, axis=-1, keepdims=True)
            outs.append(jnp.einsum("bhqn,bqnhd->bhqd", (e / den).astype(v.dtype), vg).astype(jnp.float32))
            lses.append(m + jnp.log(den))
        wts = jax.nn.softmax(jnp.concatenate(lses, axis=-1), axis=-1)
        out = sum(wts[..., g:g + 1] * outs[g] for g in range(len(DIL_PAIRS)))
        return out.astype(v.dtype)

    outs = lax.map(chunk, jnp.arange(S // Qc))
    return outs.transpose(1, 0, 3, 2, 4).reshape(Bsz, S, H, Dh)


def gated_delta_chunked(q, k, v, beta, g):
    Bsz, S, H, Dk = q.shape
    Dv = v.shape[-1]
    C = DELTA_CHUNK
    nc = S // C

    def to_chunks(t):
        return t.reshape(Bsz, nc, C, H, -1).transpose(0, 3, 1, 2, 4)

    q, k, v = to_chunks(q), to_chunks(k), to_chunks(v)
    beta = beta.reshape(Bsz, nc, C, H).transpose(0, 3, 1, 2)
    g_cum = jnp.cumsum(g.reshape(Bsz, nc, C, H).transpose(0, 3, 1, 2), axis=-1)
    incl = jnp.tril(jnp.ones((C, C), dtype=bool))
    strict = jnp.tril(jnp.ones((C, C), dtype=bool), -1)
    decay = jnp.exp(jnp.where(incl, g_cum[..., :, None] - g_cum[..., None, :], NEG_INF))
    kb = k * beta[..., None]
    Lm = jnp.where(strict, jnp.einsum("bhcid,bhcjd->bhcij", kb, k) * decay, 0.0)
    eye = jnp.eye(C, dtype=q.dtype)
    T = lax.linalg.triangular_solve(eye + Lm, jnp.broadcast_to(eye, Lm.shape),
                                    left_side=True, lower=True, unit_diagonal=True)
    u = T @ (v * beta[..., None])
    w = T @ (kb * jnp.exp(g_cum)[..., None])
    attn = jnp.where(incl, jnp.einsum("bhcid,bhcjd->bhcij", q, k) * decay, 0.0)

    def step(St, inp):
        qc, kc, uc, wc, gc, ac = inp
        v_new = uc - wc @ St
        o = (qc * jnp.exp(gc)[..., None]) @ St + ac @ v_new
        glast = gc[..., -1]
        St = St * jnp.exp(glast)[..., None, None] + jnp.einsum(
            "bhcd,bhce->bhde", kc * jnp.exp(glast[..., None] - gc)[..., None], v_new)
        return St, o

    xs = tuple(jnp.moveaxis(t, 2, 0) for t in (q, k, u, w, g_cum, attn))
    S0 = jnp.zeros((Bsz, H, Dk, Dv), dtype=q.dtype)
    _, o = lax.scan(step, S0, xs)
    return o.transpose(1, 0, 3, 2, 4).reshape(Bsz, S, H, Dv)


def hybrid_layer(x, w_in, w_out, ssm_conv_w, ssm_conv_b, ssm_dt_bias, ssm_A_log, ssm_D,
                 ssm_norm_w, dn_conv_w, dn_conv_b, dn_dt_bias, dn_A_log, dn_norm_w, ln_g, ln_b):
    Bsz, S, _ = x.shape
    slopes_dil, slopes_moba = alibi_slopes()
    proj = x @ w_in
    (a_q, a_k, a_v, a_g, s_xbc, s_z, s_dt, c_q, c_k, c_v, c_g,
     d_qkv, d_g, d_b, d_a) = jnp.split(proj, IN_SPLIT_POINTS, axis=-1)

    def heads(t):
        return t.reshape(Bsz, S, -1, HEAD_DIM)

    y_a = moba_attention(heads(a_q), heads(a_k), heads(a_v), slopes_moba).reshape(Bsz, S, W_MOBA)
    y_a = y_a * jax.nn.silu(a_g)

    xbc = jax.nn.silu(causal_dw_conv(s_xbc, ssm_conv_w, ssm_conv_b))
    y_b = ssd_mixer(xbc, s_dt, ssm_dt_bias, ssm_A_log, ssm_D)
    y_b = rms_norm(y_b * jax.nn.silu(s_z.astype(jnp.float32)), ssm_norm_w).astype(x.dtype)

    y_c = dilated_attention(heads(c_q), heads(c_k), heads(c_v), slopes_dil).reshape(Bsz, S, W_DIL)
    y_c = y_c * jax.nn.silu(c_g)

    qkv = jax.nn.silu(causal_dw_conv(d_qkv, dn_conv_w, dn_conv_b))
    dq, dk, dv = jnp.split(qkv, [W_DELTA, 2 * W_DELTA], axis=-1)
    dq = l2_normalize(heads(dq)) * (HEAD_DIM ** -0.5)
    dk = l2_normalize(heads(dk))
    dv = heads(dv).astype(jnp.float32)
    d_beta = jax.nn.sigmoid(d_b.astype(jnp.float32))
    d_logdecay = -jnp.exp(dn_A_log.astype(jnp.float32)) * jax.nn.softplus(
        d_a.astype(jnp.float32) + dn_dt_bias)
    o_d = gated_delta_chunked(dq, dk, dv, d_beta, d_logdecay)
    y_d = rms_norm(o_d, dn_norm_w).reshape(Bsz, S, W_DELTA).astype(x.dtype) * jax.nn.silu(d_g)

    mix = jnp.concatenate([y_a, y_b, y_c, y_d], axis=-1) @ w_out
    return layer_norm(DEEPNORM_ALPHA * x + mix, ln_g, ln_b)


def setup_inputs(seed: int = 0) -> dict:
    key = jax.random.key(seed)
    ks = jax.random.split(key, 20)
    f32 = jnp.float32

    def inv_softplus_dt(k, shape):
        dt = jnp.exp(jax.random.uniform(k, shape, f32, math.log(1e-3), math.log(1e-1)))
        return dt + jnp.log(-jnp.expm1(-dt))

    return {
        "x": jax.random.normal(ks[0], (BATCH, SEQ, D_MODEL), f32),
        "w_in": jax.random.normal(ks[1], (DEPTH, D_MODEL, IN_COLS), f32) * D_MODEL ** -0.5,
        "w_out": jax.random.normal(ks[2], (DEPTH, D_MIX, D_MODEL), f32) * (D_MIX ** -0.5) * DEEPNORM_BETA,
        "ssm_conv_w": jax.random.normal(ks[3], (DEPTH, CONV_WIDTH, SSD_XBC), f32) * CONV_WIDTH ** -0.5,
        "ssm_conv_b": jax.random.normal(ks[4], (DEPTH, SSD_XBC), f32) * 0.01,
        "ssm_dt_bias": inv_softplus_dt(ks[5], (DEPTH, N_HEADS_SSD)),
        "ssm_A_log": jnp.log(jax.random.uniform(ks[6], (DEPTH, N_HEADS_SSD), f32, 1.0, 16.0)),
        "ssm_D": 1.0 + 0.01 * jax.random.normal(ks[7], (DEPTH, N_HEADS_SSD), f32),
        "ssm_norm_w": 1.0 + 0.01 * jax.random.normal(ks[8], (DEPTH, W_SSD), f32),
        "dn_conv_w": jax.random.normal(ks[9], (DEPTH, CONV_WIDTH, DELTA_QKV), f32) * CONV_WIDTH ** -0.5,
        "dn_conv_b": jax.random.normal(ks[10], (DEPTH, DELTA_QKV), f32) * 0.01,
        "dn_dt_bias": inv_softplus_dt(ks[11], (DEPTH, N_HEADS_DELTA)),
        "dn_A_log": jnp.log(jax.random.uniform(ks[12], (DEPTH, N_HEADS_DELTA), f32, 1.0, 16.0)),
        "dn_norm_w": 1.0 + 0.01 * jax.random.normal(ks[13], (DEPTH, HEAD_DIM), f32),
        "ln_g": 1.0 + 0.01 * jax.random.normal(ks[14], (DEPTH, D_MODEL), f32),
        "ln_b": 0.01 * jax.random.normal(ks[15], (DEPTH, D_MODEL), f32),
    }


def reference(x, w_in, w_out, ssm_conv_w, ssm_conv_b, ssm_dt_bias, ssm_A_log, ssm_D,
              ssm_norm_w, dn_conv_w, dn_conv_b, dn_dt_bias, dn_A_log, dn_norm_w, ln_g, ln_b):
    for l in range(DEPTH):
        x = hybrid_layer(x, w_in[l], w_out[l], ssm_conv_w[l], ssm_conv_b[l], ssm_dt_bias[l],
                         ssm_A_log[l], ssm_D[l], ssm_norm_w[l], dn_conv_w[l], dn_conv_b[l],
                         dn_dt_bias[l], dn_A_log[l], dn_norm_w[l], ln_g[l], ln_b[l])
    return x
```

```python
import contextlib
import numpy as np
import concourse.bass as bass
import concourse.mybir as mybir

F32 = mybir.dt.float32
BF16 = mybir.dt.bfloat16
I32 = mybir.dt.int32
U8 = mybir.dt.uint8
AF = mybir.ActivationFunctionType
ALU = mybir.AluOpType
AX = mybir.AxisListType


class Buf:
    __slots__ = ("name", "t", "writer", "readers", "excl")

    def __init__(self, name, t, excl=False):
        self.name = name
        self.t = t
        self.excl = excl
        self.writer = None
        self.readers = []

    def __getitem__(self, idx):
        return self.t[idx]


class Prog:
    ENG = ("pe", "dve", "act", "pool", "sp")
    NDMA = 6

    def __init__(self, nc, same_engine_sync=True):
        self.nc = nc
        self.stack = contextlib.ExitStack()
        self.ops = {e: [] for e in self.ENG}
        self.cnt = {e: 0 for e in self.ENG}
        self.sems = {}
        for e in self.ENG:
            self.sems[e] = self.stack.enter_context(nc.semaphore("s_" + e))
        self.dq = {}
        for q in ("sp", "pool", "act"):
            self.dq[q] = {"n": 0, "sems": []}
            for i in range(self.NDMA):
                s = self.stack.enter_context(nc.semaphore(f"d_{q}{i}"))
                self.sems[f"d_{q}{i}"] = s
                self.dq[q]["sems"].append(f"d_{q}{i}")
        self.waited = {e: {} for e in self.ENG}
        self.same = same_engine_sync
        self.nbuf = 0

    def sbuf(self, name, shape, dt):
        t = self.stack.enter_context(self.nc.sbuf_tensor(name, list(shape), dt))
        return Buf(name, t)

    def psum(self, name, shape, dt=F32):
        t = self.stack.enter_context(self.nc.psum_tensor(name, list(shape), dt))
        return Buf(name, t, excl=True)

    def dram(self, name, shape, dt, kind):
        t = self.nc.dram_tensor(name, list(shape), dt, kind=kind)
        return Buf(name, t.ap())

    def alias(self, name, t):
        return Buf(name, t)

    def _need(self, eng, deps):
        w = self.waited[eng]
        for (k, v) in deps:
            if k == eng and (eng == "pe" or not self.same):
                continue
            if w.get(k, 0) >= v:
                continue
            w[k] = v
            sem = self.sems[k]
            self.ops[eng].append(lambda e, sem=sem, v=v: e.wait_ge(sem, v))

    def _deps(self, reads, writes, eng=None):
        deps = []
        for b in reads:
            if b.writer is not None:
                deps.append(b.writer)
            if b.excl:
                deps.extend(r for r in b.readers if r[0] != eng)
        for b in writes:
            if b.writer is not None:
                deps.append(b.writer)
            deps.extend(b.readers)
        return deps

    def _mark(self, reads, writes, tag):
        for b in reads:
            b.readers.append(tag)
            if len(b.readers) > 64:
                m = {}
                for (k, v) in b.readers:
                    if m.get(k, 0) < v:
                        m[k] = v
                b.readers = list(m.items())
        for b in writes:
            b.writer = tag
            b.readers = []

    def op(self, eng, fn, reads=(), writes=()):
        self._need(eng, self._deps(reads, writes, eng))
        self.cnt[eng] += 1
        v = self.cnt[eng]
        sem = self.sems[eng]
        self.ops[eng].append(lambda e, fn=fn, sem=sem: fn(e).then_inc(sem, 1))
        self._mark(reads, writes, (eng, v))

    def dma(self, q, out_ap, in_ap, reads=(), writes=(), **kw):
        dq = self.dq[q]
        n = dq["n"]
        dq["n"] += 1
        key = dq["sems"][n % self.NDMA]
        val = 16 * (n // self.NDMA + 1)
        deps = self._deps(reads, writes)
        if val > 16:
            deps.append((key, val - 16))
        self._need(q, deps)
        sem = self.sems[key]
        self.ops[q].append(lambda e, sem=sem, o=out_ap, i=in_ap, kw=kw: e.dma_start(out=o, in_=i, **kw).then_inc(sem, 16))
        self._mark(reads, writes, (key, val))

    def finish(self, final_bufs=()):
        deps = []
        for q, dq in self.dq.items():
            n = dq["n"]
            for i in range(min(n, self.NDMA)):
                cnt_i = (n - 1 - i) // self.NDMA + 1
                deps.append((dq["sems"][i], 16 * cnt_i))
        for e in self.ENG:
            if e != "sp" and self.cnt[e] > 0:
                deps.append((e, self.cnt[e]))
        self._need("sp", deps)
        nc = self.nc
        with nc.Block() as block:
            @block.tensor
            def _(e):
                for f in self.ops["pe"]:
                    f(e)

            @block.vector
            def _(e):
                for f in self.ops["dve"]:
                    f(e)

            @block.scalar
            def _(e):
                for f in self.ops["act"]:
                    f(e)

            @block.gpsimd
            def _(e):
                for f in self.ops["pool"]:
                    f(e)

            @block.sync
            def _(e):
                for f in self.ops["sp"]:
                    f(e)
        self.stack.close()
        return nc


D_MODEL = 1024
IN_COLS = 5906


def emit_proj(P, xT_d, w_d, out_d, T, C=IN_COLS, pfx="pj"):
    KC = D_MODEL // 128
    wbf = P.sbuf(pfx + "_wbf", [128, KC, C], BF16)
    xbf = P.sbuf(pfx + "_xbf", [128, KC, T], BF16)
    HW = (C + 1) // 2
    wst = [P.sbuf(pfx + f"_wst{i}", [128, HW], F32) for i in range(2)]
    xst = [P.sbuf(pfx + f"_xst{i}", [128, T], F32) for i in range(2)]
    n = 0
    for k in range(KC):
        b = xst[k % 2]
        P.dma("sp", b[:, :], xT_d[k * 128:(k + 1) * 128, :], reads=[xT_d], writes=[b])
        P.op("pool", lambda e, b=b, k=k: e.tensor_copy(out=xbf[:, k, :], in_=b[:, :]), reads=[b], writes=[xbf])
    for k in range(KC):
        for h in range(2):
            c0, c1 = h * HW, min(C, (h + 1) * HW)
            b = wst[n % 2]
            P.dma("sp" if n % 2 == 0 else "act", b[:, :c1 - c0], w_d[k * 128:(k + 1) * 128, c0:c1], reads=[w_d], writes=[b])
            if n % 2 == 0:
                P.op("dve", lambda e, b=b, k=k, c0=c0, c1=c1: e.tensor_copy(out=wbf[:, k, c0:c1], in_=b[:, :c1 - c0]), reads=[b], writes=[wbf])
            else:
                P.op("pool", lambda e, b=b, k=k, c0=c0, c1=c1: e.tensor_copy(out=wbf[:, k, c0:c1], in_=b[:, :c1 - c0]), reads=[b], writes=[wbf])
            n += 1
    ps = [P.psum(pfx + f"_ps{i}", [128, 512]) for i in range(4)]
    ost = [P.sbuf(pfx + f"_ost{i}", [128, HW], F32) for i in range(2)]
    chunks = [(c, min(C, c + 512)) for c in range(0, C, 512)]
    ci = 0
    hi = 0
    for tt in range(T // 128):
        for h in range(2):
            ob = ost[hi % 2]
            hi += 1
            h0, h1 = h * HW, min(C, (h + 1) * HW)
            for c0 in range(h0, h1, 512):
                c1 = min(h1, c0 + 512)
                p = ps[ci % 4]
                for k in range(KC):
                    P.op("pe", lambda e, p=p, k=k, tt=tt, c0=c0, c1=c1: e.matmul(
                        p[:, :c1 - c0], lhsT=xbf[:, k, tt * 128:(tt + 1) * 128], rhs=wbf[:, k, c0:c1],
                        start=(k == 0), stop=(k == KC - 1)), reads=[xbf, wbf], writes=[p])
                if ci % 2 == 0:
                    P.op("dve", lambda e, p=p, ob=ob, c0=c0, c1=c1, h0=h0: e.tensor_copy(
                        out=ob[:, c0 - h0:c1 - h0], in_=p[:, :c1 - c0]), reads=[p], writes=[ob])
                else:
                    P.op("act", lambda e, p=p, ob=ob, c0=c0, c1=c1, h0=h0: e.copy(
                        out=ob[:, c0 - h0:c1 - h0], in_=p[:, :c1 - c0]), reads=[p], writes=[ob])
                ci += 1
            P.dma("pool", out_d[tt * 128:(tt + 1) * 128, h0:h1], ob[:, :h1 - h0], reads=[ob], writes=[out_d])


def build_proj(T):
    nc = bass.Bass("TRN2", target_bir_lowering=False)
    P = Prog(nc)
    xT = P.dram("xT", [D_MODEL, T], F32, "ExternalInput")
    w = P.dram("w", [D_MODEL, IN_COLS], F32, "ExternalInput")
    o = P.dram("proj", [T, IN_COLS], F32, "ExternalOutput")
    emit_proj(P, xT, w, o, T)
    return P.finish()


SEQ = 8192
NEGM = 30000.0


def emit_moba(P, qT_d, kT_d, v_d, gT_d, ab_d, bi_d, cb_d, idb_d, idf_d, y_d, NH, S=SEQ, pfx="mb"):
    NT = S // 128
    NQ = S // 512
    NB = S // 256
    qf = P.sbuf(pfx + "_qf", [64, S], F32)
    kf = P.sbuf(pfx + "_kf", [64, S], F32)
    qa = P.sbuf(pfx + "_qa", [96, S], BF16)
    ka = P.sbuf(pfx + "_ka", [96, S], BF16)
    vf = P.sbuf(pfx + "_vf", [128, NT, 64], F32)
    va = P.sbuf(pfx + "_va", [128, NT, 128], BF16)
    cb = P.sbuf(pfx + "_cb", [128, 4, 512], BF16)
    idb = P.sbuf(pfx + "_idb", [128, 128], BF16)
    idf = P.sbuf(pfx + "_idf", [128, 128], F32)
    ab = P.sbuf(pfx + "_ab", [128, 64], F32)
    km = P.sbuf(pfx + "_km", [64, NB], F32)
    gsb = P.sbuf(pfx + "_gsb", [128, 32], F32)
    mx8 = P.sbuf(pfx + "_mx8", [128, 8], F32)
    mb = P.sbuf(pfx + "_mbs", [128, 32], F32)
    pts = [P.sbuf(pfx + f"_pt{i}", [128, 512], BF16) for i in range(3)]
    gch = [P.sbuf(pfx + f"_gch{i}", [64, 512], F32) for i in range(2)]
    rden = [P.sbuf(pfx + f"_rden{i}", [64, 512], F32) for i in range(2)]
    yo = [P.sbuf(pfx + f"_yo{i}", [64, 512], F32) for i in range(2)]
    acc = [P.psum(pfx + f"_acc{i}", [128, 512]) for i in range(2)]
    sps = [P.psum(pfx + f"_sps{i}", [128, 512]) for i in range(3)]
    gps = [P.psum(pfx + f"_gps{i}", [128, 32]) for i in range(2)]
    tps = [P.psum(pfx + f"_tps{i}", [32, 128]) for i in range(1)]

    P.dma("sp", cb[:, :, :], cb_d.t.rearrange("k p q -> p k q"), reads=[cb_d], writes=[cb])
    P.dma("sp", idb[:, :], idb_d[:, :], reads=[idb_d], writes=[idb])
    P.dma("sp", idf[:, :], idf_d[:, :], reads=[idf_d], writes=[idf])
    P.dma("sp", ka[64:96, :], bi_d[:, :], reads=[bi_d], writes=[ka])
    P.op("pool", lambda e: e.memset(va[:, :, 64:128], 1.0), writes=[va])
    si = 0
    for h in range(NH):
        P.dma("sp", qf[:, :], qT_d[h], reads=[qT_d], writes=[qf])
        P.dma("act", kf[:, :], kT_d[h], reads=[kT_d], writes=[kf])
        P.dma("pool", vf[:, :, :], v_d.t[h].rearrange("(t p) d -> p t d", p=128), reads=[v_d], writes=[vf])
        P.dma("sp", ab[:, :], ab_d[h], reads=[ab_d], writes=[ab])
        P.op("act", lambda e: e.mul(qa[0:64, :], qf[:, :], 0.125), reads=[qf], writes=[qa])
        P.op("pool", lambda e: e.tensor_copy(out=ka[0:64, :], in_=kf[:, :]), reads=[kf], writes=[ka])
        P.op("pool", lambda e: e.tensor_copy(out=va[:, :, 0:64], in_=vf[:, :, :]), reads=[vf], writes=[va])
        P.op("dve", lambda e: e.tensor_reduce(out=km[:, :], in_=kf.t[:, :].rearrange("p (n l) -> p n l", l=256),
                                              axis=AX.X, op=ALU.add), reads=[kf], writes=[km])
        P.op("dve", lambda e: e.tensor_scalar(out=km[:, :], in0=km[:, :], scalar1=1.0 / 256.0, scalar2=None, op0=ALU.mult),
             reads=[km], writes=[km])
        for t in range(NT):
            qb = t // 2
            gp = gps[t % 2]
            P.op("pe", lambda e, gp=gp, t=t: e.matmul(gp[:, :], lhsT=qf[:, t * 128:(t + 1) * 128], rhs=km[:, :],
                                                     start=True, stop=True), reads=[qf, km], writes=[gp])
            P.op("dve", lambda e: e.memset(gsb[:, :], -1e30), writes=[gsb])
            if qb > 0:
                P.op("dve", lambda e, gp=gp, qb=qb: e.tensor_copy(out=gsb[:, 0:qb], in_=gp[:, 0:qb]), reads=[gp], writes=[gsb])
            P.op("dve", lambda e: e.max(out=mx8[:, :], in_=gsb[:, :]), reads=[gsb], writes=[mx8])
            P.op("dve", lambda e: e.tensor_scalar(out=mb[:, :], in0=gsb[:, :], scalar1=mx8[:, 2:3], scalar2=NEGM,
                                                  op0=ALU.is_ge, op1=ALU.mult), reads=[gsb, mx8], writes=[mb])
            P.op("dve", lambda e, qb=qb: e.memset(mb[:, qb:qb + 1], NEGM), writes=[mb])
            if qb + 1 < 32:
                P.op("dve", lambda e, qb=qb: e.memset(mb[:, qb + 1:32], 0.0), writes=[mb])
            P.op("dve", lambda e: e.tensor_scalar(out=mb[:, :], in0=mb[:, :], scalar1=-NEGM, scalar2=None, op0=ALU.add),
                 reads=[mb], writes=[mb])
            tp = tps[0]
            P.op("pe", lambda e, tp=tp: e.transpose(out=tp[:, :], in_=mb[:, :], identity=idf[:, :]), reads=[mb, idf], writes=[tp])
            P.op("act", lambda e, tp=tp, t=t: e.copy(out=qa[64:96, t * 128:(t + 1) * 128], in_=tp[:, :]), reads=[tp], writes=[qa])
        for qt in range(NQ):
            ac = acc[qt % 2]
            nk = 4 * qt + 4
            for kt in range(nk):
                sp_ = sps[si % 3]
                pt = pts[si % 3]
                si += 1
                diag = kt >= 4 * qt
                P.op("pe", lambda e, sp_=sp_, kt=kt, qt=qt, diag=diag: e.matmul(
                    sp_[:, :], lhsT=ka[:, kt * 128:(kt + 1) * 128], rhs=qa[:, qt * 512:(qt + 1) * 512],
                    start=True, stop=not diag), reads=[ka, qa], writes=[sp_])
                if diag:
                    P.op("pe", lambda e, sp_=sp_, r=kt - 4 * qt: e.matmul(
                        sp_[:, :], lhsT=idb[:, :], rhs=cb[:, r, :], start=False, stop=True), reads=[idb, cb], writes=[sp_])
                ri = kt - 4 * qt + 60
                P.op("act", lambda e, sp_=sp_, pt=pt, ri=ri: e.activation(out=pt[:, :], in_=sp_[:, :], func=AF.Exp,
                                                                          bias=ab[:, ri:ri + 1], scale=1.0),
                     reads=[sp_, ab], writes=[pt])
                P.op("pe", lambda e, ac=ac, pt=pt, kt=kt, nk=nk: e.matmul(
                    ac[:, :], lhsT=va[:, kt, :], rhs=pt[:, :], start=(kt == 0), stop=(kt == nk - 1)),
                    reads=[va, pt], writes=[ac])
            g = gch[qt % 2]
            rd = rden[qt % 2]
            y = yo[qt % 2]
            P.dma("sp", g[:, :], gT_d[h][:, qt * 512:(qt + 1) * 512], reads=[gT_d], writes=[g])
            P.op("act", lambda e, g=g: e.activation(out=g[:, :], in_=g[:, :], func=AF.Silu), reads=[g], writes=[g])
            P.op("dve", lambda e, rd=rd, ac=ac: e.reciprocal(out=rd[:, :], in_=ac[64:128, :]), reads=[ac], writes=[rd])
            P.op("dve", lambda e, y=y, ac=ac, rd=rd: e.tensor_tensor(out=y[:, :], in0=ac[0:64, :], in1=rd[:, :], op=ALU.mult),
                 reads=[ac, rd], writes=[y])
            P.op("pool", lambda e, y=y, g=g: e.tensor_tensor(out=y[:, :], in0=y[:, :], in1=g[:, :], op=ALU.mult),
                 reads=[y, g], writes=[y])
            P.dma("pool", y_d[h][:, qt * 512:(qt + 1) * 512], y[:, :], reads=[y], writes=[y_d])


def moba_consts(S=SEQ):
    import ml_dtypes
    bf = ml_dtypes.bfloat16
    bi = np.zeros((32, S), np.float32)
    for j in range(S // 256):
        bi[j, j * 256:(j + 1) * 256] = 1.0
    cbm = np.zeros((4, 128, 512), np.float32)
    for r in range(4):
        for p in range(128):
            tk = r * 128 + p
            for blk in range(2):
                if tk // 256 == blk:
                    q = np.arange(blk * 256, (blk + 1) * 256)
                    cbm[r, p, q] = np.where(tk > q, -NEGM, 0.0)
    n = 12
    s = 2.0 ** (-8.0 * (np.arange(n) + 1) / n)
    slopes_moba = s[6:]
    ab = np.zeros((6, 128, 64), np.float32)
    for h in range(6):
        for ri in range(64):
            rel = ri - 60
            ab[h, :, ri] = slopes_moba[h] * (128.0 * rel + np.arange(128))
    return dict(bi=bi.astype(bf), cb=cbm.astype(bf), idb=np.eye(128, dtype=np.float32).astype(bf),
                idf=np.eye(128, dtype=np.float32), ab=ab)


def build_moba(NH, S=SEQ):
    nc = bass.Bass("TRN2", target_bir_lowering=False)
    P = Prog(nc)
    qT = P.dram("qT", [NH, 64, S], F32, "ExternalInput")
    kT = P.dram("kT", [NH, 64, S], F32, "ExternalInput")
    v = P.dram("v", [NH, S, 64], F32, "ExternalInput")
    gT = P.dram("gT", [NH, 64, S], F32, "ExternalInput")
    ab = P.dram("ab", [NH, 128, 64], F32, "ExternalInput")
    bi = P.dram("bi", [32, S], BF16, "ExternalInput")
    cb = P.dram("cb", [4, 128, 512], BF16, "ExternalInput")
    idb = P.dram("idb", [128, 128], BF16, "ExternalInput")
    idf = P.dram("idf", [128, 128], F32, "ExternalInput")
    y = P.dram("yT", [NH, 64, S], F32, "ExternalOutput")
    emit_moba(P, qT, kT, v, gT, ab, bi, cb, idb, idf, y, NH, S)
    return P.finish()


DIL_D = (1, 4, 16)
DIL_GROUPS = (0, 1, 2)
DBG = None


def emit_dil(P, qT_d, kT_d, vp_d, gT_d, bias_d, y_d, NH, S=SEQ, pfx="dl"):
    NT = S // 128
    qf = P.sbuf(pfx + "_qf", [64, S], F32)
    kf = qf
    qa = P.sbuf(pfx + "_qa", [64, S], BF16)
    ka = P.sbuf(pfx + "_ka", [64, S], BF16)
    vf = P.sbuf(pfx + "_vf", [128, NT, 64], F32)
    va = [P.sbuf(pfx + f"_va{g}", [128, NT, 128], BF16) for g in range(3)]
    bs = P.sbuf(pfx + "_bias", [128, 6, 512], F32)
    num = P.sbuf(pfx + "_num", [128, S], F32)
    tmp = [P.sbuf(pfx + f"_tmp{i}", [128, 512], F32) for i in range(2)]
    pts = [P.sbuf(pfx + f"_pt{i}", [128, 512], BF16) for i in range(4)]
    gch = [P.sbuf(pfx + f"_gch{i}", [64, 512], F32) for i in range(2)]
    rden = [P.sbuf(pfx + f"_rden{i}", [64, 512], F32) for i in range(2)]
    yo = [P.sbuf(pfx + f"_yo{i}", [64, 512], F32) for i in range(2)]
    sps = [P.psum(pfx + f"_sps{i}", [128, 512]) for i in range(3)]
    ops_ = [P.psum(pfx + f"_ops{i}", [128, 512]) for i in range(2)]
    for g in range(3):
        P.op("pool", lambda e, g=g: e.memset(va[g][:, :, 64:128], 1.0), writes=[va[g]])
    si = 0
    oi = 0
    for h in range(NH):
        P.dma("sp", qf[:, :], qT_d[h], reads=[qT_d], writes=[qf])
        P.op("act", lambda e: e.mul(qa[:, :], qf[:, :], 0.125), reads=[qf], writes=[qa])
        P.dma("sp", kf[:, :], kT_d[h], reads=[kT_d], writes=[kf])
        P.op("pool", lambda e: e.tensor_copy(out=ka[:, :], in_=kf[:, :]), reads=[kf], writes=[ka])
        P.dma("sp", bs[:, :, :], bias_d.t[h].rearrange("g o p q -> p (g o) q"), reads=[bias_d], writes=[bs])
        for g in range(3):
            P.dma("pool", vf[:, :, :], vp_d.t[h, g].rearrange("(t p) d -> p t d", p=128), reads=[vp_d], writes=[vf])
            P.op("pool", lambda e, g=g: e.tensor_copy(out=va[g][:, :, 0:64], in_=vf[:, :, :]), reads=[vf], writes=[va[g]])
        for g, d in enumerate(DIL_D):
            if g not in DIL_GROUPS:
                continue
            NTd = NT // d
            qv = qa.t[:, :].rearrange("p (u d) -> p u d", d=d)
            kv = ka.t[:, :].rearrange("p (u d) -> p u d", d=d)
            nv = num.t[:, :].rearrange("p (u d) -> p u d", d=d)
            for r in range(d):
                for jb in range(NTd // 4):
                    op_ = ops_[oi % 2]
                    oi += 1
                    ptl = {}
                    for o in (1, 0):
                        i0 = 1 if (o == 1 and jb == 0) else 0
                        sp_ = sps[si % 3]
                        tm = tmp[si % 2]
                        pt = pts[si % 4]
                        ptl[o] = pt
                        si += 1
                        for i in range(i0, 4):
                            jq = jb * 4 + i
                            jk = jq - o
                            P.op("pe", lambda e, sp_=sp_, i=i, jq=jq, jk=jk, r=r, kv=kv, qv=qv: e.matmul(
                                sp_[:, i * 128:(i + 1) * 128], lhsT=kv[:, jk * 128:(jk + 1) * 128, r],
                                rhs=qv[:, jq * 128:(jq + 1) * 128, r], start=True, stop=True,
                                skip_group_check=True), reads=[ka, qa], writes=[sp_])
                        c0 = i0 * 128
                        P.op("dve", lambda e, sp_=sp_, tm=tm, c0=c0, go=g * 2 + o: e.tensor_tensor(
                            out=tm[:, c0:512], in0=sp_[:, c0:512], in1=bs[:, go, c0:512], op=ALU.add),
                            reads=[sp_, bs], writes=[tm])
                        if DBG is not None and h == 0 and jb == 0 and r == 0 and o == 0 and g == DIL_GROUPS[0]:
                            P.dma("sp", DBG[:, :], tm[:, :], reads=[tm], writes=[DBG])
                        P.op("act", lambda e, tm=tm, pt=pt, c0=c0: e.activation(out=pt[:, c0:512], in_=tm[:, c0:512], func=AF.Exp),
                             reads=[tm], writes=[pt])
                    for i in range(4):
                        os_ = (0,) if (jb == 0 and i == 0) else (1, 0)
                        for o in os_:
                            jk = jb * 4 + i - o
                            P.op("pe", lambda e, op_=op_, pt=ptl[o], i=i, tl=r * NTd + jk, g=g, fp=(o == os_[0]), last=(o == 0): e.matmul(
                                op_[:, i * 128:(i + 1) * 128], lhsT=va[g][:, tl, :], rhs=pt[:, i * 128:(i + 1) * 128],
                                start=fp, stop=last, skip_group_check=True), reads=[va[g], pt], writes=[op_])
                    u0 = jb * 512
                    if g == DIL_GROUPS[0]:
                        P.op("dve", lambda e, op_=op_, u0=u0, r=r, nv=nv: e.tensor_copy(out=nv[:, u0:u0 + 512, r], in_=op_[:, :]),
                             reads=[op_], writes=[num])
                    else:
                        P.op("dve", lambda e, op_=op_, u0=u0, r=r, nv=nv: e.tensor_tensor(
                            out=nv[:, u0:u0 + 512, r], in0=op_[:, :], in1=nv[:, u0:u0 + 512, r], op=ALU.add),
                            reads=[op_, num], writes=[num])
        for qt in range(S // 512):
            g_ = gch[qt % 2]
            rd = rden[qt % 2]
            y = yo[qt % 2]
            cs = slice(qt * 512, (qt + 1) * 512)
            P.dma("sp", g_[:, :], gT_d[h][:, cs], reads=[gT_d], writes=[g_])
            P.op("act", lambda e, g_=g_: e.activation(out=g_[:, :], in_=g_[:, :], func=AF.Silu), reads=[g_], writes=[g_])
            P.op("dve", lambda e, rd=rd, cs=cs: e.reciprocal(out=rd[:, :], in_=num[64:128, cs]), reads=[num], writes=[rd])
            P.op("dve", lambda e, y=y, rd=rd, cs=cs: e.tensor_tensor(out=y[:, :], in0=num[0:64, cs], in1=rd[:, :], op=ALU.mult),
                 reads=[num, rd], writes=[y])
            P.op("pool", lambda e, y=y, g_=g_: e.tensor_tensor(out=y[:, :], in0=y[:, :], in1=g_[:, :], op=ALU.mult),
                 reads=[y, g_], writes=[y])
            P.dma("pool", y_d[h][:, cs], y[:, :], reads=[y], writes=[y_d])


def dil_consts():
    n = 12
    s = 2.0 ** (-8.0 * (np.arange(n) + 1) / n)
    slopes = s[:6]
    bias = np.zeros((6, 3, 2, 128, 512), np.float32)
    p = np.arange(128)[:, None]
    x = np.arange(128)[None, :]
    for h in range(6):
        for g, d in enumerate(DIL_D):
            for o in range(2):
                nn = (x - p) + 128 * o
                b = np.where((nn >= 0) & (nn <= 128), -slopes[h] * d * nn, -NEGM).astype(np.float32)
                bias[h, g, o] = np.tile(b, (1, 4))
    return bias


def dil_perm(S=SEQ):
    out = []
    for d in DIL_D:
        u = np.arange(S // d)
        out.append(np.concatenate([u * d + r for r in range(d)]))
    return out


def build_dil(NH, S=SEQ):
    nc = bass.Bass("TRN2", target_bir_lowering=False)
    P = Prog(nc)
    qT = P.dram("qT", [NH, 64, S], F32, "ExternalInput")
    kT = P.dram("kT", [NH, 64, S], F32, "ExternalInput")
    vp = P.dram("vp", [NH, 3, S, 64], F32, "ExternalInput")
    gT = P.dram("gT", [NH, 64, S], F32, "ExternalInput")
    bias = P.dram("bias", [NH, 3, 2, 128, 512], F32, "ExternalInput")
    y = P.dram("yT", [NH, 64, S], F32, "ExternalOutput")
    emit_dil(P, qT, kT, vp, gT, bias, y, NH, S)
    return P.finish()


def emit_conv_silu(P, src_d, w_sb, C, S, zp, acc, outs, q="sp"):
    P.dma(q, zp[0:C, 3:S + 3], src_d, reads=[], writes=[zp])
    H = S // 2
    for hh in range(2):
        a0, a1 = hh * H, (hh + 1) * H
        P.op("dve", lambda e, a0=a0, a1=a1: e.tensor_scalar(out=acc[0:C, a0:a1], in0=zp[0:C, 3 + a0:3 + a1], scalar1=w_sb[0:C, 3:4],
                                                          scalar2=w_sb[0:C, 4:5], op0=ALU.mult, op1=ALU.add), reads=[zp, w_sb], writes=[acc])
        for k in range(3):
            P.op("dve", lambda e, k=k, a0=a0, a1=a1: e.scalar_tensor_tensor(out=acc[0:C, a0:a1], in0=zp[0:C, k + a0:k + a1], scalar=w_sb[0:C, k:k + 1],
                                                                          in1=acc[0:C, a0:a1], op0=ALU.mult, op1=ALU.add),
                 reads=[zp, w_sb, acc], writes=[acc])
    for (ob, oap) in outs:
        P.op("act", lambda e, oap=oap: e.activation(out=oap, in_=acc[0:C, :], func=AF.Silu), reads=[acc], writes=[ob])


def emit_softplus(P, eng_dve, out_b, out_ap, x_b, x_ap, t1_b, t1_ap, shape_p):
    P.op("act", lambda e: e.activation(out=t1_ap, in_=x_ap, func=AF.Abs), reads=[x_b], writes=[t1_b])
    P.op("act", lambda e: e.activation(out=t1_ap, in_=t1_ap, func=AF.Exp, scale=-1.0), reads=[t1_b], writes=[t1_b])
    P.op("act", lambda e: e.activation(out=t1_ap, in_=t1_ap, func=AF.Ln, bias=1.0, scale=1.0), reads=[t1_b], writes=[t1_b])
    P.op("dve", lambda e: e.scalar_tensor_tensor(out=out_ap, in0=x_ap, scalar=0.0, in1=t1_ap, op0=ALU.max, op1=ALU.add),
         reads=[x_b, t1_b], writes=[out_b])


SSD_NCH = 10 ** 9
SSD_STOP = 99


def emit_ssd(P, xpre_d, bpre_d, cpre_d, cwx_d, cwb_d, cwc_d, dtc_d, sc_d, tri_d, ones_d, idf_d, idb_d, mneg_d, y_d, NH, S=SEQ, pfx="sd"):
    NT = S // 128
    zp = P.sbuf(pfx + "_zp", [128, S + 3], F32)
    acc = P.sbuf(pfx + "_acc", [128, S], F32)
    xsT = P.sbuf(pfx + "_xsT", [64, S], F32)
    BT = P.sbuf(pfx + "_BT", [128, S], BF16)
    CT = P.sbuf(pfx + "_CT", [128, S], BF16)
    yac = P.sbuf(pfx + "_yac", [64, S], F32)
    cwx = P.sbuf(pfx + "_cwx", [64, 5], F32)
    cwb = P.sbuf(pfx + "_cwb", [128, 5], F32)
    cwc = P.sbuf(pfx + "_cwc", [128, 5], F32)
    sc = P.sbuf(pfx + "_sc", [128, 3], F32)
    tri = P.sbuf(pfx + "_tri", [128, 128], F32)
    ones = P.sbuf(pfx + "_ones", [128, 128], F32)
    idf = P.sbuf(pfx + "_idf", [128, 128], F32)
    idb = P.sbuf(pfx + "_idb", [128, 128], BF16)
    mneg = P.sbuf(pfx + "_mneg", [128, 128], BF16)
    dtr = P.sbuf(pfx + "_dtr", [128, NT], F32)
    t1 = P.sbuf(pfx + "_t1", [128, NT], F32)
    dt = P.sbuf(pfx + "_dt", [128, NT], F32)
    a_ = P.sbuf(pfx + "_a", [128, NT], F32)
    acum = P.sbuf(pfx + "_acum", [128, NT], F32)
    nacum = P.sbuf(pfx + "_nacum", [128, NT], F32)
    dB = P.sbuf(pfx + "_dB", [128, NT], F32)
    dlast = P.sbuf(pfx + "_dlast", [128, NT], F32)
    Aneg = P.sbuf(pfx + "_Aneg", [128, 1], F32)
    dg = [P.sbuf(pfx + f"_dg{i}", [128, 128], F32) for i in range(2)]
    EB = [P.sbuf(pfx + f"_EB{i}", [128, 128], F32) for i in range(2)]
    LmT = [P.sbuf(pfx + f"_LmT{i}", [128, 128], F32) for i in range(2)]
    SLT = [P.sbuf(pfx + f"_SLT{i}", [128, 128], BF16) for i in range(2)]
    CgT = [P.sbuf(pfx + f"_CgT{i}", [128, 128], BF16) for i in range(2)]
    X = [P.sbuf(pfx + f"_X{i}", [128, 64], BF16) for i in range(2)]
    Bd = [P.sbuf(pfx + f"_Bd{i}", [128, 128], BF16) for i in range(2)]
    hf = P.sbuf(pfx + "_hf", [128, 64], F32)
    hb = [P.sbuf(pfx + f"_hb{i}", [128, 64], BF16) for i in range(2)]
    p_gb = [P.psum(pfx + f"_pgb{i}", [128, 128]) for i in range(2)]
    p_sc = [P.psum(pfx + f"_psc{i}", [128, 128]) for i in range(1)]
    p_tx = [P.psum(pfx + f"_ptx{i}", [128, 64]) for i in range(1)]
    p_tb = [P.psum(pfx + f"_ptb{i}", [128, 128], BF16) for i in range(1)]
    p_y = [P.psum(pfx + f"_py{i}", [64, 128]) for i in range(2)]
    p_h = [P.psum(pfx + f"_ph{i}", [128, 64]) for i in range(1)]
    p_misc = p_gb[0]

    for (sb, d_) in ((tri, tri_d), (ones, ones_d), (idf, idf_d), (idb, idb_d), (mneg, mneg_d)):
        P.dma("sp", sb[:, :], d_[:, :], reads=[d_], writes=[sb])
    P.op("pool", lambda e: e.memset(zp[:, 0:3], 0.0), writes=[zp])
    for h in range(NH):
        P.dma("sp", cwx[:, :], cwx_d[h], reads=[cwx_d], writes=[cwx])
        P.dma("sp", cwb[:, :], cwb_d[h], reads=[cwb_d], writes=[cwb])
        P.dma("sp", cwc[:, :], cwc_d[h], reads=[cwc_d], writes=[cwc])
        P.dma("sp", sc[:, :], sc_d[h], reads=[sc_d], writes=[sc])
        P.dma("sp", dtr[:, :], dtc_d[h], reads=[dtc_d], writes=[dtr])
        emit_conv_silu(P, xpre_d[h], cwx, 64, S, zp, acc, [(xsT, xsT[:, :])])
        emit_conv_silu(P, bpre_d[h], cwb, 128, S, zp, acc, [(BT, BT[:, :])])
        emit_conv_silu(P, cpre_d[h], cwc, 128, S, zp, acc, [(CT, CT[:, :])])
        if SSD_STOP <= 1:
            P.dma("pool", y_d[h], xsT[:, :], reads=[xsT], writes=[y_d])
            continue
        P.op("dve", lambda e: e.tensor_scalar(out=dtr[:, :], in0=dtr[:, :], scalar1=sc[:, 0:1], scalar2=None, op0=ALU.add), reads=[dtr, sc], writes=[dtr])
        emit_softplus(P, "dve", dt, dt[:, :], dtr, dtr[:, :], t1, t1[:, :], 128)
        P.op("act", lambda e: e.activation(out=Aneg[:, :], in_=sc[:, 1:2], func=AF.Exp), reads=[sc], writes=[Aneg])
        P.op("dve", lambda e: e.tensor_scalar(out=Aneg[:, :], in0=Aneg[:, :], scalar1=-1.0, scalar2=None, op0=ALU.mult), reads=[Aneg], writes=[Aneg])
        P.op("dve", lambda e: e.tensor_scalar(out=a_[:, :], in0=dt[:, :], scalar1=Aneg[:, 0:1], scalar2=None, op0=ALU.mult), reads=[dt, Aneg], writes=[a_])
        if SSD_STOP <= 2:
            P.dma("pool", y_d[h][:, 0:NT], a_[0:64, :], reads=[a_], writes=[y_d])
            continue
        P.op("pe", lambda e: e.matmul(p_misc[:, 0:NT], lhsT=tri[:, :], rhs=a_[:, :], start=True, stop=True), reads=[tri, a_], writes=[p_misc])
        P.op("dve", lambda e: e.tensor_copy(out=acum[:, :], in_=p_misc[:, 0:NT]), reads=[p_misc], writes=[acum])
        P.op("dve", lambda e: e.tensor_scalar(out=nacum[:, :], in0=acum[:, :], scalar1=-1.0, scalar2=None, op0=ALU.mult), reads=[acum], writes=[nacum])
        if SSD_STOP == 30:
            P.dma("pool", y_d[h][:, 2 * NT:3 * NT], acum[0:64, :], reads=[acum], writes=[y_d])
            continue
        P.op("pe", lambda e: e.matmul(p_misc[:, 0:NT], lhsT=ones[:, :], rhs=a_[:, :], start=True, stop=True), reads=[ones, a_], writes=[p_misc])
        P.op("act", lambda e: e.activation(out=dlast[:, :], in_=p_misc[:, 0:NT], func=AF.Exp), reads=[p_misc], writes=[dlast])
        if SSD_STOP == 31:
            P.dma("pool", y_d[h][:, 2 * NT:3 * NT], dlast[0:64, :], reads=[dlast], writes=[y_d])
            continue
        P.op("dve", lambda e: e.tensor_tensor(out=dB[:, :], in0=p_misc[:, 0:NT], in1=acum[:, :], op=ALU.subtract), reads=[p_misc, acum], writes=[dB])
        P.op("act", lambda e: e.activation(out=dB[:, :], in_=dB[:, :], func=AF.Exp), reads=[dB], writes=[dB])
        if SSD_STOP <= 3 or SSD_STOP == 32:
            P.dma("pool", y_d[h][:, 0:NT], dB[0:64, :], reads=[dB], writes=[y_d])
            P.dma("pool", y_d[h][:, NT:2 * NT], dlast[0:64, :], reads=[dlast], writes=[y_d])
            P.dma("pool", y_d[h][:, 2 * NT:3 * NT], acum[0:64, :], reads=[acum], writes=[y_d])
            continue
        P.op("dve", lambda e: e.memset(hf[:, :], 0.0), writes=[hf])
        P.op("pool", lambda e: e.memset(hb[0][:, :], 0.0), writes=[hb[0]])
        for c in range(min(NT, SSD_NCH)):
            cs = slice(c * 128, (c + 1) * 128)
            i2 = c % 2
            gb = p_gb[i2]
            P.op("pool", lambda e, c=c, i2=i2: e.tensor_scalar(out=dg[i2][:, :], in0=idf[:, :], scalar1=acum[:, c:c + 1], scalar2=None, op0=ALU.mult),
                 reads=[idf, acum], writes=[dg[i2]])
            P.op("pe", lambda e, gb=gb, i2=i2: e.matmul(gb[:, :], lhsT=ones[:, :], rhs=dg[i2][:, :], start=True, stop=True), reads=[ones, dg[i2]], writes=[gb])
            P.op("act", lambda e, gb=gb, i2=i2: e.activation(out=EB[i2][:, :], in_=gb[:, :], func=AF.Exp), reads=[gb], writes=[EB[i2]])
            P.op("pe", lambda e, gb=gb: e.matmul(gb[:, :], lhsT=idb[:, :], rhs=mneg[:, :], start=False, stop=True), reads=[idb, mneg], writes=[gb])
            P.op("act", lambda e, gb=gb, i2=i2, c=c: e.activation(out=LmT[i2][:, :], in_=gb[:, :], func=AF.Exp, bias=nacum[:, c:c + 1], scale=1.0),
                 reads=[gb, nacum], writes=[LmT[i2]])
            ps = p_sc[0]
            P.op("pe", lambda e, ps=ps, cs=cs: e.matmul(ps[:, :], lhsT=BT[:, cs], rhs=CT[:, cs], start=True, stop=True), reads=[BT, CT], writes=[ps])
            P.op("dve", lambda e, ps=ps, i2=i2: e.tensor_tensor(out=SLT[i2][:, :], in0=ps[:, :], in1=LmT[i2][:, :], op=ALU.mult), reads=[ps, LmT[i2]], writes=[SLT[i2]])
            P.op("pool", lambda e, i2=i2, cs=cs: e.tensor_tensor(out=CgT[i2][:, :], in0=CT[:, cs], in1=EB[i2][:, :], op=ALU.mult), reads=[CT, EB[i2]], writes=[CgT[i2]])
            px = p_tx[0]
            P.op("pe", lambda e, px=px, cs=cs: e.transpose(out=px[:, :], in_=xsT[:, cs], identity=idf[0:64, 0:64]), reads=[xsT, idf], writes=[px])
            P.op("dve", lambda e, px=px, i2=i2, c=c: e.tensor_scalar(out=X[i2][:, :], in0=px[:, :], scalar1=dt[:, c:c + 1], scalar2=None, op0=ALU.mult),
                 reads=[px, dt], writes=[X[i2]])
            pb = p_tb[0]
            P.op("pe", lambda e, pb=pb, cs=cs: e.transpose(out=pb[:, :], in_=BT[:, cs], identity=idb[:, :]), reads=[BT, idb], writes=[pb])
            P.op("dve", lambda e, pb=pb, i2=i2, c=c: e.tensor_scalar(out=Bd[i2][:, :], in0=pb[:, :], scalar1=dB[:, c:c + 1], scalar2=None, op0=ALU.mult),
                 reads=[pb, dB], writes=[Bd[i2]])
            py = p_y[i2]
            hcur = hb[c % 2]
            hnxt = hb[(c + 1) % 2]
            P.op("pe", lambda e, py=py, i2=i2: e.matmul(py[:, :], lhsT=X[i2][:, :], rhs=SLT[i2][:, :], start=True, stop=False), reads=[X[i2], SLT[i2]], writes=[py])
            P.op("pe", lambda e, py=py, i2=i2, hcur=hcur: e.matmul(py[:, :], lhsT=hcur[:, :], rhs=CgT[i2][:, :], start=False, stop=True), reads=[hcur, CgT[i2]], writes=[py])
            P.op("dve", lambda e, py=py, cs=cs: e.scalar_tensor_tensor(out=yac[:, cs], in0=xsT[:, cs], scalar=sc[0:64, 2:3], in1=py[:, :], op0=ALU.mult, op1=ALU.add),
                 reads=[xsT, sc, py], writes=[yac])
            ph = p_h[0]
            P.op("pe", lambda e, ph=ph, i2=i2: e.matmul(ph[:, :], lhsT=Bd[i2][:, :], rhs=X[i2][:, :], start=True, stop=True), reads=[Bd[i2], X[i2]], writes=[ph])
            P.op("dve", lambda e, ph=ph, c=c: e.scalar_tensor_tensor(out=hf[:, :], in0=hf[:, :], scalar=dlast[:, c:c + 1], in1=ph[:, :], op0=ALU.mult, op1=ALU.add),
                 reads=[hf, dlast, ph], writes=[hf])
            P.op("act", lambda e, hnxt=hnxt: e.copy(out=hnxt[:, :], in_=hf[:, :]), reads=[hf], writes=[hnxt])
        P.dma("pool", y_d[h], yac[:, :], reads=[yac], writes=[y_d])
        if NT % 2 == 1 or True:
            P.op("pool", lambda e: e.memset(hb[0][:, :], 0.0), writes=[hb[0]])


def rec_consts():
    import ml_dtypes
    bf = ml_dtypes.bfloat16
    s1 = np.arange(128)[:, None]
    s2 = np.arange(128)[None, :]
    tri = (s1 <= s2).astype(np.float32)
    mneg = np.where(s2 < s1, -NEGM, 0.0).astype(np.float32)
    mpos = np.where(s2 >= s1, NEGM, 0.0).astype(np.float32)
    return dict(tri=tri, ones=np.ones((128, 128), np.float32), idf=np.eye(128, dtype=np.float32),
                idb=np.eye(128, dtype=np.float32).astype(bf), mneg=mneg.astype(bf), mpos=mpos.astype(bf))


def build_ssd(NH, S=SEQ):
    nc = bass.Bass("TRN2", target_bir_lowering=False)
    P = Prog(nc)
    NT = S // 128
    xpre = P.dram("xpre", [NH, 64, S], F32, "ExternalInput")
    bpre = P.dram("bpre", [NH, 128, S], F32, "ExternalInput")
    cpre = P.dram("cpre", [NH, 128, S], F32, "ExternalInput")
    cwx = P.dram("cwx", [NH, 64, 5], F32, "ExternalInput")
    cwb = P.dram("cwb", [NH, 128, 5], F32, "ExternalInput")
    cwc = P.dram("cwc", [NH, 128, 5], F32, "ExternalInput")
    dtc = P.dram("dtc", [NH, 128, NT], F32, "ExternalInput")
    sc = P.dram("sc", [NH, 128, 3], F32, "ExternalInput")
    tri = P.dram("tri", [128, 128], F32, "ExternalInput")
    ones = P.dram("ones", [128, 128], F32, "ExternalInput")
    idf = P.dram("idf", [128, 128], F32, "ExternalInput")
    idb = P.dram("idb", [128, 128], BF16, "ExternalInput")
    mneg = P.dram("mneg", [128, 128], BF16, "ExternalInput")
    y = P.dram("yT", [NH, 64, S], F32, "ExternalOutput")
    emit_ssd(P, xpre, bpre, cpre, cwx, cwb, cwc, dtc, sc, tri, ones, idf, idb, mneg, y, NH, S)
    return P.finish()


GDN_NCH = 10 ** 9


def emit_gdn(P, qpre_d, kpre_d, vpre_d, cwq_d, cwk_d, cwv_d, bcol_d, acol_d, sc_d, gate_d, nw_d,
             tri_d, ones_d, idf_d, idb_d, mneg_d, mpos_d, y_d, NH, S=SEQ, pfx="gd"):
    NT = S // 128
    zp = P.sbuf(pfx + "_zp", [128, S + 3], F32)
    acc = P.sbuf(pfx + "_acc", [128, S], F32)
    qn = P.sbuf(pfx + "_qn", [64, S], BF16)
    kn = P.sbuf(pfx + "_kn", [64, S], BF16)
    vT = P.sbuf(pfx + "_vT", [64, S], BF16)
    gt = P.sbuf(pfx + "_gt", [128, NT, 64], F32)
    yall = P.sbuf(pfx + "_yall", [128, NT, 64], F32)
    cw = [P.sbuf(pfx + f"_cw{i}", [64, 5], F32) for i in range(3)]
    sc = P.sbuf(pfx + "_sc", [128, 2], F32)
    nw = P.sbuf(pfx + "_nw", [128, 64], F32)
    tri = P.sbuf(pfx + "_tri", [128, 128], F32)
    ones = P.sbuf(pfx + "_ones", [128, 128], F32)
    idf = P.sbuf(pfx + "_idf", [128, 128], F32)
    idb = P.sbuf(pfx + "_idb", [128, 128], BF16)
    mneg = P.sbuf(pfx + "_mneg", [128, 128], BF16)
    mpos = P.sbuf(pfx + "_mpos", [128, 128], BF16)
    col = {n: P.sbuf(pfx + "_c_" + n, [128, NT], F32) for n in
           ("b", "a", "t1", "beta", "nbeta", "g", "gcum", "ngcum", "egc", "bege", "dk", "dlast")}
    Aneg = P.sbuf(pfx + "_Aneg", [128, 1], F32)
    rt = [P.sbuf(pfx + f"_rt{i}", [64, 512], F32) for i in range(2)]
    eps_t = P.sbuf(pfx + "_eps", [128, 1], F32)
    P.op("dve", lambda e: e.memset(eps_t[:, :], 1e-6), writes=[eps_t])

    def f32t(n, k=2):
        return [P.sbuf(pfx + f"_{n}{i}", [128, 128], F32) for i in range(k)]

    def bft(n, shape, k=2):
        return [P.sbuf(pfx + f"_{n}{i}", shape, BF16) for i in range(k)]
    dg = f32t("dg")
    EB = f32t("EB")
    dec = f32t("dec")
    decT = f32t("decT")
    Y = f32t("Y", 2)
    YT = f32t("YT", 2)
    PT = f32t("PT", 2)
    TTb = bft("TTb", [128, 128])
    attnT = bft("attnT", [128, 128])
    qgT = bft("qgT", [64, 128])
    kbg = bft("kbg", [128, 64])
    kd = bft("kd", [128, 64])
    vb = bft("vb", [128, 64])
    wT = bft("wT", [64, 128])
    u_sb = [P.sbuf(pfx + f"_u{i}", [128, 64], F32) for i in range(2)]
    vnew = bft("vnew", [128, 64])
    Sf = P.sbuf(pfx + "_Sf", [64, 64], F32)
    Sb = bft("Sb", [64, 64])
    junk = P.sbuf(pfx + "_junk", [128, 64], F32)
    ssq = [P.sbuf(pfx + f"_ssq{i}", [128, 1], F32) for i in range(2)]
    nwg = [P.sbuf(pfx + f"_nwg{i}", [128, 64], F32) for i in range(2)]
    pg = [P.psum(pfx + f"_pg{i}", [128, 128]) for i in range(2)]
    pa = [P.psum(pfx + f"_pa{i}", [128, 512]) for i in range(3)]
    pb = [P.psum(pfx + f"_pb{i}", [128, 64], BF16) for i in range(1)]
    pc = [P.psum(pfx + f"_pc{i}", [128, 128]) for i in range(2)]
    pai = [0]

    def nxt_pa():
        pai[0] += 1
        return pa[pai[0] % 3]
    pci = [0]

    def nxt_pc():
        pci[0] += 1
        return pc[pci[0] % 2]

    for (sb, d_) in ((tri, tri_d), (ones, ones_d), (idf, idf_d), (idb, idb_d), (mneg, mneg_d), (mpos, mpos_d)):
        P.dma("sp", sb[:, :], d_[:, :], reads=[d_], writes=[sb])
    P.op("pool", lambda e: e.memset(zp[:, 0:3], 0.0), writes=[zp])
    for h in range(NH):
        for i, d_ in enumerate((cwq_d, cwk_d, cwv_d)):
            P.dma("sp", cw[i][:, :], d_[h], reads=[d_], writes=[cw[i]])
        P.dma("sp", sc[:, :], sc_d[h], reads=[sc_d], writes=[sc])
        P.dma("sp", nw[:, :], nw_d[h], reads=[nw_d], writes=[nw])
        P.dma("sp", col["b"][:, :], bcol_d[h], reads=[bcol_d], writes=[col["b"]])
        P.dma("sp", col["a"][:, :], acol_d[h], reads=[acol_d], writes=[col["a"]])
        P.dma("act", gt[:, :, :], gate_d.t[h].rearrange("(t p) d -> p t d", p=128), reads=[gate_d], writes=[gt])
        P.op("act", lambda e: e.activation(out=gt[:, :, :], in_=gt[:, :, :], func=AF.Silu), reads=[gt], writes=[gt])
        for which, (src_d, outb) in enumerate(((qpre_d, qn), (kpre_d, kn))):
            emit_conv_silu(P, src_d[h], cw[which], 64, S, zp, acc, [(acc, acc[0:64, :])])
            P.op("act", lambda e: e.activation(out=zp[0:64, 3:S + 3], in_=acc[0:64, :], func=AF.Square), reads=[acc], writes=[zp])
            for j in range(S // 512):
                cs = slice(j * 512, (j + 1) * 512)
                ps = nxt_pa()
                r_ = rt[j % 2]
                P.op("pe", lambda e, ps=ps, j=j: e.matmul(ps[0:64, :], lhsT=ones[0:64, 0:64], rhs=zp[0:64, 3 + j * 512:3 + (j + 1) * 512],
                                                         start=True, stop=True), reads=[ones, zp], writes=[ps])
                P.op("act", lambda e, ps=ps, r_=r_: e.activation(out=r_[:, :], in_=ps[0:64, :], func=AF.Sqrt, bias=eps_t[0:64, 0:1], scale=1.0),
                     reads=[ps, eps_t], writes=[r_])
                P.op("dve", lambda e, r_=r_: e.reciprocal(out=r_[:, :], in_=r_[:, :]), reads=[r_], writes=[r_])
                P.op("dve", lambda e, r_=r_, cs=cs, outb=outb, sc_=(0.125 if which == 0 else 1.0): e.scalar_tensor_tensor(
                    out=outb[:, cs], in0=acc[0:64, cs], scalar=sc_, in1=r_[:, :], op0=ALU.mult, op1=ALU.mult),
                    reads=[acc, r_], writes=[outb])
        emit_conv_silu(P, vpre_d[h], cw[2], 64, S, zp, acc, [(vT, vT[:, :])])
        c_ = col
        P.op("act", lambda e: e.activation(out=c_["beta"][:, :], in_=c_["b"][:, :], func=AF.Sigmoid), reads=[c_["b"]], writes=[c_["beta"]])
        P.op("dve", lambda e: e.tensor_scalar(out=c_["nbeta"][:, :], in0=c_["beta"][:, :], scalar1=-1.0, scalar2=None, op0=ALU.mult),
             reads=[c_["beta"]], writes=[c_["nbeta"]])
        P.op("dve", lambda e: e.tensor_scalar(out=c_["a"][:, :], in0=c_["a"][:, :], scalar1=sc[:, 0:1], scalar2=None, op0=ALU.add),
             reads=[c_["a"], sc], writes=[c_["a"]])
        emit_softplus(P, "dve", c_["g"], c_["g"][:, :], c_["a"], c_["a"][:, :], c_["t1"], c_["t1"][:, :], 128)
        P.op("act", lambda e: e.activation(out=Aneg[:, :], in_=sc[:, 1:2], func=AF.Exp), reads=[sc], writes=[Aneg])
        P.op("dve", lambda e: e.tensor_scalar(out=Aneg[:, :], in0=Aneg[:, :], scalar1=-1.0, scalar2=None, op0=ALU.mult), reads=[Aneg], writes=[Aneg])
        P.op("dve", lambda e: e.tensor_scalar(out=c_["g"][:, :], in0=c_["g"][:, :], scalar1=Aneg[:, 0:1], scalar2=None, op0=ALU.mult),
             reads=[c_["g"], Aneg], writes=[c_["g"]])
        pm = pg[0]
        P.op("pe", lambda e: e.matmul(pm[:, 0:NT], lhsT=tri[:, :], rhs=c_["g"][:, :], start=True, stop=True), reads=[tri, c_["g"]], writes=[pm])
        P.op("dve", lambda e: e.tensor_copy(out=c_["gcum"][:, :], in_=pm[:, 0:NT]), reads=[pm], writes=[c_["gcum"]])
        P.op("dve", lambda e: e.tensor_scalar(out=c_["ngcum"][:, :], in0=c_["gcum"][:, :], scalar1=-1.0, scalar2=None, op0=ALU.mult),
             reads=[c_["gcum"]], writes=[c_["ngcum"]])
        P.op("act", lambda e: e.activation(out=c_["egc"][:, :], in_=c_["gcum"][:, :], func=AF.Exp), reads=[c_["gcum"]], writes=[c_["egc"]])
        P.op("dve", lambda e: e.tensor_tensor(out=c_["bege"][:, :], in0=c_["egc"][:, :], in1=c_["beta"][:, :], op=ALU.mult),
             reads=[c_["egc"], c_["beta"]], writes=[c_["bege"]])
        P.op("pe", lambda e: e.matmul(pm[:, 0:NT], lhsT=ones[:, :], rhs=c_["g"][:, :], start=True, stop=True), reads=[ones, c_["g"]], writes=[pm])
        P.op("act", lambda e: e.activation(out=c_["dlast"][:, :], in_=pm[:, 0:NT], func=AF.Exp), reads=[pm], writes=[c_["dlast"]])
        P.op("dve", lambda e: e.tensor_tensor(out=c_["dk"][:, :], in0=pm[:, 0:NT], in1=c_["gcum"][:, :], op=ALU.subtract),
             reads=[pm, c_["gcum"]], writes=[c_["dk"]])
        P.op("act", lambda e: e.activation(out=c_["dk"][:, :], in_=c_["dk"][:, :], func=AF.Exp), reads=[c_["dk"]], writes=[c_["dk"]])
        P.op("dve", lambda e: e.memset(Sf[:, :], 0.0), writes=[Sf])
        P.op("pool", lambda e: e.memset(Sb[0][:, :], 0.0), writes=[Sb[0]])
        for c in range(min(NT, GDN_NCH)):
            cs = slice(c * 128, (c + 1) * 128)
            i2 = c % 2
            g1, g2 = pg[0], pg[1]
            P.op("pool", lambda e, c=c, i2=i2: e.tensor_scalar(out=dg[i2][:, :], in0=idf[:, :], scalar1=c_["gcum"][:, c:c + 1], scalar2=None, op0=ALU.mult),
                 reads=[idf, c_["gcum"]], writes=[dg[i2]])
            P.op("pe", lambda e, i2=i2: e.matmul(g1[:, :], lhsT=ones[:, :], rhs=dg[i2][:, :], start=True, stop=True), reads=[ones, dg[i2]], writes=[g1])
            P.op("pe", lambda e, i2=i2: e.matmul(g2[:, :], lhsT=ones[:, :], rhs=dg[i2][:, :], start=True, stop=True), reads=[ones, dg[i2]], writes=[g2])
            P.op("act", lambda e, i2=i2: e.activation(out=EB[i2][0:64, :], in_=g1[0:64, :], func=AF.Exp), reads=[g1], writes=[EB[i2]])
            P.op("pe", lambda e: e.matmul(g1[:, :], lhsT=idb[:, :], rhs=mpos[:, :], start=False, stop=True), reads=[idb, mpos], writes=[g1])
            P.op("pe", lambda e: e.matmul(g2[:, :], lhsT=idb[:, :], rhs=mneg[:, :], start=False, stop=True), reads=[idb, mneg], writes=[g2])
            P.op("act", lambda e, i2=i2, c=c: e.activation(out=dec[i2][:, :], in_=g1[:, :], func=AF.Exp, bias=c_["gcum"][:, c:c + 1], scale=-1.0),
                 reads=[g1, c_["gcum"]], writes=[dec[i2]])
            P.op("act", lambda e, i2=i2, c=c: e.activation(out=decT[i2][:, :], in_=g2[:, :], func=AF.Exp, bias=c_["ngcum"][:, c:c + 1], scale=1.0),
                 reads=[g2, c_["ngcum"]], writes=[decT[i2]])
            P.op("pool", lambda e, i2=i2, cs=cs: e.tensor_tensor(out=qgT[i2][:, :], in0=qn[:, cs], in1=EB[i2][0:64, :], op=ALU.mult),
                 reads=[qn, EB[i2]], writes=[qgT[i2]])
            pk = nxt_pa()
            P.op("pe", lambda e, pk=pk, cs=cs: e.matmul(pk[:, 0:128], lhsT=kn[:, cs], rhs=kn[:, cs], start=True, stop=True), reads=[kn], writes=[pk])
            Yc, YTc, PTc = Y[0], YT[0], PT[0]
            P.op("dve", lambda e, pk=pk, i2=i2, c=c, Yc=Yc: e.scalar_tensor_tensor(out=Yc[:, :], in0=pk[:, 0:128], scalar=c_["nbeta"][:, c:c + 1],
                                                                                 in1=dec[i2][:, :], op0=ALU.mult, op1=ALU.mult),
                 reads=[pk, c_["nbeta"], dec[i2]], writes=[Yc])
            pt_ = nxt_pa()
            P.op("pe", lambda e, pt_=pt_, Yc=Yc: e.transpose(out=pt_[:, 0:128], in_=Yc[:, :], identity=idf[:, :]), reads=[Yc, idf], writes=[pt_])
            P.op("act", lambda e, pt_=pt_, YTc=YTc: e.copy(out=YTc[:, :], in_=pt_[:, 0:128]), reads=[pt_], writes=[YTc])
            P.op("dve", lambda e, pt_=pt_, PTc=PTc: e.tensor_tensor(out=PTc[:, :], in0=pt_[:, 0:128], in1=idf[:, :], op=ALU.add), reads=[pt_, idf], writes=[PTc])
            pq = nxt_pa()
            P.op("pe", lambda e, pq=pq, cs=cs: e.matmul(pq[:, 0:128], lhsT=kn[:, cs], rhs=qn[:, cs], start=True, stop=True), reads=[kn, qn], writes=[pq])
            P.op("dve", lambda e, pq=pq, i2=i2: e.tensor_tensor(out=attnT[i2][:, :], in0=pq[:, 0:128], in1=decT[i2][:, :], op=ALU.mult),
                 reads=[pq, decT[i2]], writes=[attnT[i2]])
            cur = 0
            for lev in range(6):
                nx = 1 - cur
                p1 = nxt_pa()
                P.op("pe", lambda e, p1=p1, cur=cur: e.matmul(p1[:, 0:128], lhsT=YT[cur][:, :], rhs=Y[cur][:, :], start=True, stop=True),
                     reads=[YT[cur], Y[cur]], writes=[p1])
                P.op("act", lambda e, p1=p1, nx=nx: e.copy(out=Y[nx][:, :], in_=p1[:, 0:128]), reads=[p1], writes=[Y[nx]])
                if lev < 5:
                    p2 = nxt_pa()
                    P.op("pe", lambda e, p2=p2, cur=cur: e.matmul(p2[:, 0:128], lhsT=Y[cur][:, :], rhs=YT[cur][:, :], start=True, stop=True),
                         reads=[YT[cur], Y[cur]], writes=[p2])
                    P.op("dve", lambda e, p2=p2, nx=nx: e.tensor_copy(out=YT[nx][:, :], in_=p2[:, 0:128]), reads=[p2], writes=[YT[nx]])
                p3 = nxt_pa()
                P.op("pe", lambda e, p3=p3, nx=nx, cur=cur: e.matmul(p3[:, 0:128], lhsT=Y[nx][:, :], rhs=PT[cur][:, :], start=True, stop=True),
                     reads=[Y[nx], PT[cur]], writes=[p3])
                P.op("dve", lambda e, p3=p3, nx=nx, cur=cur: e.tensor_tensor(out=PT[nx][:, :], in0=p3[:, 0:128], in1=PT[cur][:, :], op=ALU.add),
                     reads=[p3, PT[cur]], writes=[PT[nx]])
                cur = nx
            P.op("act", lambda e, i2=i2, cur=cur: e.copy(out=TTb[i2][:, :], in_=PT[cur][:, :]), reads=[PT[cur]], writes=[TTb[i2]])
            pkt = pb[0]
            P.op("pe", lambda e, cs=cs: e.transpose(out=pkt[:, :], in_=kn[:, cs], identity=idb[0:64, 0:64]), reads=[kn, idb], writes=[pkt])
            P.op("dve", lambda e, i2=i2, c=c: e.tensor_scalar(out=kbg[i2][:, :], in0=pkt[:, :], scalar1=c_["bege"][:, c:c + 1], scalar2=None, op0=ALU.mult),
                 reads=[pkt, c_["bege"]], writes=[kbg[i2]])
            P.op("dve", lambda e, i2=i2, c=c: e.tensor_scalar(out=kd[i2][:, :], in0=pkt[:, :], scalar1=c_["dk"][:, c:c + 1], scalar2=None, op0=ALU.mult),
                 reads=[pkt, c_["dk"]], writes=[kd[i2]])
            P.op("pe", lambda e, cs=cs: e.transpose(out=pkt[:, :], in_=vT[:, cs], identity=idb[0:64, 0:64]), reads=[vT, idb], writes=[pkt])
            P.op("dve", lambda e, i2=i2, c=c: e.tensor_scalar(out=vb[i2][:, :], in0=pkt[:, :], scalar1=c_["beta"][:, c:c + 1], scalar2=None, op0=ALU.mult),
                 reads=[pkt, c_["beta"]], writes=[vb[i2]])
            pu = nxt_pc()
            P.op("pe", lambda e, pu=pu, i2=i2: e.matmul(pu[:, 0:64], lhsT=TTb[i2][:, :], rhs=vb[i2][:, :], start=True, stop=True), reads=[TTb[i2], vb[i2]], writes=[pu])
            P.op("act", lambda e, pu=pu, i2=i2: e.copy(out=u_sb[i2][:, :], in_=pu[:, 0:64]), reads=[pu], writes=[u_sb[i2]])
            pw = nxt_pc()
            P.op("pe", lambda e, pw=pw, i2=i2: e.matmul(pw[0:64, :], lhsT=kbg[i2][:, :], rhs=TTb[i2][:, :], start=True, stop=True), reads=[kbg[i2], TTb[i2]], writes=[pw])
            P.op("act", lambda e, pw=pw, i2=i2: e.copy(out=wT[i2][:, :], in_=pw[0:64, :]), reads=[pw], writes=[wT[i2]])
            Scur, Snxt = Sb[c % 2], Sb[(c + 1) % 2]
            pv = nxt_pc()
            P.op("pe", lambda e, pv=pv, i2=i2, Scur=Scur: e.matmul(pv[:, 0:64], lhsT=wT[i2][:, :], rhs=Scur[:, :], start=True, stop=True), reads=[wT[i2], Scur], writes=[pv])
            P.op("dve", lambda e, pv=pv, i2=i2: e.tensor_tensor(out=vnew[i2][:, :], in0=u_sb[i2][:, :], in1=pv[:, 0:64], op=ALU.subtract),
                 reads=[u_sb[i2], pv], writes=[vnew[i2]])
            po = nxt_pc()
            P.op("pe", lambda e, po=po, i2=i2, Scur=Scur: e.matmul(po[:, 0:64], lhsT=qgT[i2][:, :], rhs=Scur[:, :], start=True, stop=False), reads=[qgT[i2], Scur], writes=[po])
            P.op("pe", lambda e, po=po, i2=i2: e.matmul(po[:, 0:64], lhsT=attnT[i2][:, :], rhs=vnew[i2][:, :], start=False, stop=True), reads=[attnT[i2], vnew[i2]], writes=[po])
            pS = nxt_pc()
            P.op("pe", lambda e, pS=pS, i2=i2: e.matmul(pS[0:64, 0:64], lhsT=kd[i2][:, :], rhs=vnew[i2][:, :], start=True, stop=True), reads=[kd[i2], vnew[i2]], writes=[pS])
            P.op("dve", lambda e, pS=pS, c=c: e.scalar_tensor_tensor(out=Sf[:, :], in0=Sf[:, :], scalar=c_["dlast"][0:64, c:c + 1], in1=pS[0:64, 0:64],
                                                                    op0=ALU.mult, op1=ALU.add), reads=[Sf, c_["dlast"], pS], writes=[Sf])
            P.op("act", lambda e, Snxt=Snxt: e.copy(out=Snxt[:, :], in_=Sf[:, :]), reads=[Sf], writes=[Snxt])
            P.op("act", lambda e, po=po, i2=i2: e.activation(out=junk[:, :], in_=po[:, 0:64], func=AF.Square, accum_out=ssq[i2][:, 0:1]),
                 reads=[po], writes=[junk, ssq[i2]])
            P.op("act", lambda e, i2=i2: e.activation(out=ssq[i2][:, :], in_=ssq[i2][:, :], func=AF.Sqrt, bias=eps_t[:, 0:1], scale=1.0 / 64.0),
                 reads=[ssq[i2], eps_t], writes=[ssq[i2]])
            P.op("dve", lambda e, i2=i2: e.reciprocal(out=ssq[i2][:, :], in_=ssq[i2][:, :]), reads=[ssq[i2]], writes=[ssq[i2]])
            P.op("pool", lambda e, i2=i2, c=c: e.tensor_tensor(out=nwg[i2][:, :], in0=gt[:, c, :], in1=nw[:, :], op=ALU.mult), reads=[gt, nw], writes=[nwg[i2]])
            P.op("dve", lambda e, po=po, i2=i2, c=c: e.scalar_tensor_tensor(out=yall[:, c, :], in0=po[:, 0:64], scalar=ssq[i2][:, 0:1], in1=nwg[i2][:, :],
                                                                           op0=ALU.mult, op1=ALU.mult), reads=[po, ssq[i2], nwg[i2]], writes=[yall])
        P.dma("pool", y_d.t[h].rearrange("(t p) d -> p t d", p=128), yall[:, :, :], reads=[yall], writes=[y_d])
        P.op("pool", lambda e: e.memset(Sb[0][:, :], 0.0), writes=[Sb[0]])
        if min(NT, GDN_NCH) % 2 == 1:
            P.op("pool", lambda e: e.memset(Sb[1][:, :], 0.0), writes=[Sb[1]])


def build_gdn(NH, S=SEQ):
    nc = bass.Bass("TRN2", target_bir_lowering=False)
    P = Prog(nc)
    NT = S // 128
    d = {}
    for n, shp in (("qpre", [NH, 64, S]), ("kpre", [NH, 64, S]), ("vpre", [NH, 64, S]), ("cwq", [NH, 64, 5]), ("cwk", [NH, 64, 5]),
                   ("cwv", [NH, 64, 5]), ("bcol", [NH, 128, NT]), ("acol", [NH, 128, NT]), ("sc", [NH, 128, 2]), ("gate", [NH, S, 64]),
                   ("nw", [NH, 128, 64]), ("tri", [128, 128]), ("ones", [128, 128]), ("idf", [128, 128])):
        d[n] = P.dram(n, shp, F32, "ExternalInput")
    for n in ("idb", "mneg", "mpos"):
        d[n] = P.dram(n, [128, 128], BF16, "ExternalInput")
    y = P.dram("y", [NH, S, 64], F32, "ExternalOutput")
    emit_gdn(P, d["qpre"], d["kpre"], d["vpre"], d["cwq"], d["cwk"], d["cwv"], d["bcol"], d["acol"], d["sc"], d["gate"], d["nw"],
             d["tri"], d["ones"], d["idf"], d["idb"], d["mneg"], d["mpos"], y, NH, S)
    return P.finish()


D_MIX = 1536
ALPHA = (2.0 * 2) ** 0.25


def emit_outln(P, yaT_d, ybT_d, ycT_d, ydT_d, zT_d, snw_d, x_d, w_d, lng_d, lnb_d, ones_d, xo_d, T, pfx="ol"):
    FC = D_MIX // 128
    wbf = P.sbuf(pfx + "_wbf", [128, FC, D_MODEL], BF16)
    mix = P.sbuf(pfx + "_mix", [128, FC, T], BF16)
    wst = [P.sbuf(pfx + f"_wst{i}", [128, D_MODEL], F32) for i in range(2)]
    mst = [P.sbuf(pfx + f"_mst{i}", [128, T], F32) for i in range(2)]
    zst = [P.sbuf(pfx + f"_zst{i}", [128, T], F32) for i in range(2)]
    gz = P.sbuf(pfx + "_gz", [128, 3, T], F32)
    sq = P.sbuf(pfx + "_sq", [128, T], F32)
    rs = P.sbuf(pfx + "_rs", [128, T], F32)
    snw = P.sbuf(pfx + "_snw", [128, 3], F32)
    lng = P.sbuf(pfx + "_lng", [128, D_MODEL], F32)
    lnb = P.sbuf(pfx + "_lnb", [128, D_MODEL], F32)
    ones = P.sbuf(pfx + "_ones", [128, 128], F32)
    eps6 = P.sbuf(pfx + "_eps6", [128, 1], F32)
    eps5 = P.sbuf(pfx + "_eps5", [128, 1], F32)
    xt = [P.sbuf(pfx + f"_xt{i}", [128, D_MODEL], F32) for i in range(2)]
    zt = [P.sbuf(pfx + f"_zt{i}", [128, D_MODEL], F32) for i in range(2)]
    st = [P.sbuf(pfx + f"_st{i}", [128, 2, 6], F32) for i in range(2)]
    mv = [P.sbuf(pfx + f"_mv{i}", [128, 2], F32) for i in range(2)]
    pss = [P.psum(pfx + f"_pss{i}", [128, 512]) for i in range(2)]
    po = [P.psum(pfx + f"_po{i}", [128, 512]) for i in range(4)]
    P.op("dve", lambda e: e.memset(eps6[:, :], 1e-6), writes=[eps6])
    P.op("dve", lambda e: e.memset(eps5[:, :], 1e-5), writes=[eps5])
    P.dma("sp", snw[:, :], snw_d[:, :], reads=[snw_d], writes=[snw])
    P.dma("sp", lng[:, :], lng_d[:, :], reads=[lng_d], writes=[lng])
    P.dma("sp", lnb[:, :], lnb_d[:, :], reads=[lnb_d], writes=[lnb])
    P.dma("sp", ones[:, :], ones_d[:, :], reads=[ones_d], writes=[ones])
    for k in range(FC):
        b = wst[k % 2]
        P.dma("act", b[:, :], w_d[k * 128:(k + 1) * 128, :], reads=[w_d], writes=[b])
        P.op("pool", lambda e, b=b, k=k: e.tensor_copy(out=wbf[:, k, :], in_=b[:, :]), reads=[b], writes=[wbf])
    n = 0
    for (src, f0) in ((yaT_d, 0), (ycT_d, 6), (ydT_d, 9)):
        for k in range(3):
            b = mst[n % 2]
            n += 1
            P.dma("sp", b[:, :], src[k * 128:(k + 1) * 128, :], reads=[src], writes=[b])
            P.op("dve", lambda e, b=b, fk=f0 + k: e.tensor_copy(out=mix[:, fk, :], in_=b[:, :]), reads=[b], writes=[mix])
    for k in range(3):
        b = mst[n % 2]
        zb = zst[k % 2]
        n += 1
        P.dma("sp", b[:, :], ybT_d[k * 128:(k + 1) * 128, :], reads=[ybT_d], writes=[b])
        P.dma("sp", zb[:, :], zT_d[k * 128:(k + 1) * 128, :], reads=[zT_d], writes=[zb])
        P.op("act", lambda e, zb=zb: e.activation(out=zb[:, :], in_=zb[:, :], func=AF.Silu), reads=[zb], writes=[zb])
        P.op("dve", lambda e, b=b, zb=zb, k=k: e.tensor_tensor(out=gz[:, k, :], in0=b[:, :], in1=zb[:, :], op=ALU.mult), reads=[b, zb], writes=[gz])
    for j in range(T // 512):
        cs = slice(j * 512, (j + 1) * 512)
        ps = pss[j % 2]
        for k in range(3):
            P.op("act", lambda e, k=k, cs=cs: e.activation(out=sq[:, cs], in_=gz[:, k, cs], func=AF.Square), reads=[gz], writes=[sq])
            P.op("pe", lambda e, ps=ps, k=k, cs=cs: e.matmul(ps[:, :], lhsT=ones[:, :], rhs=sq[:, cs], start=(k == 0), stop=(k == 2)),
                 reads=[ones, sq], writes=[ps])
        P.op("act", lambda e, ps=ps, cs=cs: e.activation(out=rs[:, cs], in_=ps[:, :], func=AF.Sqrt, bias=eps6[:, 0:1], scale=1.0 / 384.0),
             reads=[ps, eps6], writes=[rs])
        P.op("dve", lambda e, cs=cs: e.reciprocal(out=rs[:, cs], in_=rs[:, cs]), reads=[rs], writes=[rs])
        for k in range(3):
            P.op("dve", lambda e, k=k, cs=cs: e.scalar_tensor_tensor(out=mix[:, 3 + k, cs], in0=gz[:, k, cs], scalar=snw[:, k:k + 1], in1=rs[:, cs],
                                                                    op0=ALU.mult, op1=ALU.mult), reads=[gz, snw, rs], writes=[mix])
    for tt in range(T // 128):
        x_ = xt[tt % 2]
        z_ = zt[tt % 2]
        s_ = st[tt % 2]
        m_ = mv[tt % 2]
        P.dma("sp", x_[:, :], x_d[tt * 128:(tt + 1) * 128, :], reads=[x_d], writes=[x_])
        for hh in range(2):
            p = po[(tt * 2 + hh) % 4]
            for k in range(FC):
                P.op("pe", lambda e, p=p, k=k, tt=tt, hh=hh: e.matmul(p[:, :], lhsT=mix[:, k, tt * 128:(tt + 1) * 128],
                                                                      rhs=wbf[:, k, hh * 512:(hh + 1) * 512], start=(k == 0), stop=(k == FC - 1)),
                     reads=[mix, wbf], writes=[p])
            P.op("dve", lambda e, p=p, hh=hh, x_=x_, z_=z_: e.scalar_tensor_tensor(out=z_[:, hh * 512:(hh + 1) * 512], in0=x_[:, hh * 512:(hh + 1) * 512],
                                                                                 scalar=ALPHA, in1=p[:, :], op0=ALU.mult, op1=ALU.add),
                 reads=[x_, p], writes=[z_])
            P.op("dve", lambda e, hh=hh, z_=z_, s_=s_: e.bn_stats(out=s_[:, hh, :], in_=z_[:, hh * 512:(hh + 1) * 512]), reads=[z_], writes=[s_])
        P.op("dve", lambda e, s_=s_, m_=m_: e.bn_aggr(out=m_[:, :], in_=s_[:, :, :]), reads=[s_], writes=[m_])
        P.op("act", lambda e, m_=m_: e.activation(out=m_[:, 1:2], in_=m_[:, 1:2], func=AF.Sqrt, bias=eps5[:, 0:1], scale=1.0), reads=[m_, eps5], writes=[m_])
        P.op("dve", lambda e, m_=m_: e.reciprocal(out=m_[:, 1:2], in_=m_[:, 1:2]), reads=[m_], writes=[m_])
        P.op("dve", lambda e, z_=z_, m_=m_: e.tensor_scalar(out=z_[:, :], in0=z_[:, :], scalar1=m_[:, 0:1], scalar2=m_[:, 1:2], op0=ALU.subtract, op1=ALU.mult),
             reads=[z_, m_], writes=[z_])
        P.op("pool", lambda e, z_=z_: e.tensor_tensor(out=z_[:, :], in0=z_[:, :], in1=lng[:, :], op=ALU.mult), reads=[z_, lng], writes=[z_])
        P.op("pool", lambda e, z_=z_: e.tensor_tensor(out=z_[:, :], in0=z_[:, :], in1=lnb[:, :], op=ALU.add), reads=[z_, lnb], writes=[z_])
        P.dma("pool", xo_d[tt * 128:(tt + 1) * 128, :], z_[:, :], reads=[z_], writes=[xo_d])


def build_outln(T):
    nc = bass.Bass("TRN2", target_bir_lowering=False)
    P = Prog(nc)
    d = {}
    for n in ("yaT", "ybT", "ycT", "ydT", "zT"):
        d[n] = P.dram(n, [384, T], F32, "ExternalInput")
    d["snw"] = P.dram("snw", [128, 3], F32, "ExternalInput")
    d["x"] = P.dram("x", [T, D_MODEL], F32, "ExternalInput")
    d["w"] = P.dram("w", [D_MIX, D_MODEL], F32, "ExternalInput")
    d["lng"] = P.dram("lng", [128, D_MODEL], F32, "ExternalInput")
    d["lnb"] = P.dram("lnb", [128, D_MODEL], F32, "ExternalInput")
    d["ones"] = P.dram("ones", [128, 128], F32, "ExternalInput")
    xo = P.dram("xo", [T, D_MODEL], F32, "ExternalOutput")
    emit_outln(P, d["yaT"], d["ybT"], d["ycT"], d["ydT"], d["zT"], d["snw"], d["x"], d["w"], d["lng"], d["lnb"], d["ones"], xo, T)
    return P.finish()


from concourse.bass_utils import run_bass_kernel_spmd

BATCH = 2
DEPTH = 2
OFF_A = 0
OFF_B = 1536
OFF_C = 1536 + 1286
OFF_D = 1536 + 1286 + 1536
_NC_CACHE = {}


def _get_nc(name, fn):
    if name not in _NC_CACHE:
        _NC_CACHE[name] = fn()
    return _NC_CACHE[name]


def _cT(a):
    return np.ascontiguousarray(a.T)


def _mixer_cores():
    return [(b, (2 * hp, 2 * hp + 1)) for b in range(BATCH) for hp in range(3)]


def _run(nc, in_maps):
    res = run_bass_kernel_spmd(nc, in_maps, core_ids=list(range(len(in_maps))))
    return res.results


def _layer(x, l, p):
    S = SEQ
    NT = S // 128
    xf = x.reshape(BATCH * S, D_MODEL)
    T = BATCH * S // 8
    ncp = _get_nc("proj", lambda: build_proj(T))
    res = _run(ncp, [{"xT": _cT(xf[c * T:(c + 1) * T]), "w": p["w_in"][l]} for c in range(8)])
    proj = np.concatenate([r["proj"] for r in res], axis=0).reshape(BATCH, S, IN_COLS)
    cores = _mixer_cores()

    def hT(b, hs, off):
        return np.ascontiguousarray(np.stack([proj[b, :, off + h * 64: off + (h + 1) * 64].T for h in hs]))

    def hN(b, hs, off):
        return np.ascontiguousarray(np.stack([proj[b, :, off + h * 64: off + (h + 1) * 64] for h in hs]))

    def colL(b, off):
        return np.ascontiguousarray(proj[b, :, off].reshape(NT, 128).T)
    CA = moba_consts()
    nca = _get_nc("moba", lambda: build_moba(2))
    res = _run(nca, [dict(qT=hT(b, hs, OFF_A), kT=hT(b, hs, OFF_A + 384), v=hN(b, hs, OFF_A + 768), gT=hT(b, hs, OFF_A + 1152),
                          ab=np.ascontiguousarray(CA["ab"][list(hs)]), bi=CA["bi"], cb=CA["cb"], idb=CA["idb"], idf=CA["idf"])
                     for (b, hs) in cores])
    yaT = np.zeros((BATCH, 384, S), np.float32)
    for (b, hs), r in zip(cores, res):
        for j, h in enumerate(hs):
            yaT[b, h * 64:(h + 1) * 64] = r["yT"][j]
    bias = dil_consts()
    perm = dil_perm()
    ncc = _get_nc("dil", lambda: build_dil(2))
    res = _run(ncc, [dict(qT=hT(b, hs, OFF_C), kT=hT(b, hs, OFF_C + 384),
                          vp=np.ascontiguousarray(np.stack([np.stack([proj[b, :, OFF_C + 768 + h * 64: OFF_C + 768 + (h + 1) * 64][pp] for pp in perm])
                                                            for h in hs])),
                          gT=hT(b, hs, OFF_C + 1152), bias=np.ascontiguousarray(bias[list(hs)])) for (b, hs) in cores])
    ycT = np.zeros((BATCH, 384, S), np.float32)
    for (b, hs), r in zip(cores, res):
        for j, h in enumerate(hs):
            ycT[b, h * 64:(h + 1) * 64] = r["yT"][j]
    CR = rec_consts()
    cw5 = np.concatenate([p["ssm_conv_w"][l].T, p["ssm_conv_b"][l][:, None]], axis=1).astype(np.float32)
    ncb = _get_nc("ssd", lambda: build_ssd(2))
    maps = []
    for (b, hs) in cores:
        m = {k: CR[k] for k in ("tri", "ones", "idf", "idb", "mneg")}
        m["xpre"] = hT(b, hs, OFF_B)
        m["bpre"] = np.ascontiguousarray(np.stack([proj[b, :, OFF_B + 384 + (h // 3) * 128: OFF_B + 384 + (h // 3 + 1) * 128].T for h in hs]))
        m["cpre"] = np.ascontiguousarray(np.stack([proj[b, :, OFF_B + 640 + (h // 3) * 128: OFF_B + 640 + (h // 3 + 1) * 128].T for h in hs]))
        m["cwx"] = np.ascontiguousarray(np.stack([cw5[h * 64:(h + 1) * 64] for h in hs]))
        m["cwb"] = np.ascontiguousarray(np.stack([cw5[384 + (h // 3) * 128: 384 + (h // 3 + 1) * 128] for h in hs]))
        m["cwc"] = np.ascontiguousarray(np.stack([cw5[640 + (h // 3) * 128: 640 + (h // 3 + 1) * 128] for h in hs]))
        m["dtc"] = np.ascontiguousarray(np.stack([colL(b, OFF_B + 896 + 384 + h) for h in hs]))
        m["sc"] = np.ascontiguousarray(np.stack([np.tile(np.stack([p["ssm_dt_bias"][l, h], p["ssm_A_log"][l, h], p["ssm_D"][l, h]])[None].astype(np.float32),
                                                         (128, 1)) for h in hs]))
        maps.append(m)
    res = _run(ncb, maps)
    ybT = np.zeros((BATCH, 384, S), np.float32)
    for (b, hs), r in zip(cores, res):
        for j, h in enumerate(hs):
            ybT[b, h * 64:(h + 1) * 64] = r["yT"][j]
    cd5 = np.concatenate([p["dn_conv_w"][l].T, p["dn_conv_b"][l][:, None]], axis=1).astype(np.float32)
    ncd = _get_nc("gdn", lambda: build_gdn(2))
    maps = []
    for (b, hs) in cores:
        m = dict(CR)
        for nm, off in (("q", 0), ("k", 384), ("v", 768)):
            m[nm + "pre"] = hT(b, hs, OFF_D + off)
            m["cw" + nm] = np.ascontiguousarray(np.stack([cd5[off + h * 64: off + (h + 1) * 64] for h in hs]))
        m["gate"] = hN(b, hs, OFF_D + 1152)
        m["bcol"] = np.ascontiguousarray(np.stack([colL(b, OFF_D + 1536 + h) for h in hs]))
        m["acol"] = np.ascontiguousarray(np.stack([colL(b, OFF_D + 1542 + h) for h in hs]))
        m["sc"] = np.ascontiguousarray(np.stack([np.tile(np.stack([p["dn_dt_bias"][l, h], p["dn_A_log"][l, h]])[None].astype(np.float32), (128, 1))
                                                 for h in hs]))
        m["nw"] = np.ascontiguousarray(np.stack([np.tile(p["dn_norm_w"][l][None].astype(np.float32), (128, 1)) for h in hs]))
        maps.append(m)
    res = _run(ncd, maps)
    ydT = np.zeros((BATCH, 384, S), np.float32)
    for (b, hs), r in zip(cores, res):
        for j, h in enumerate(hs):
            ydT[b, h * 64:(h + 1) * 64] = r["y"][j].T
    nco = _get_nc("outln", lambda: build_outln(T))
    maps = []
    ones = np.ones((128, 128), np.float32)
    lng = np.ascontiguousarray(np.tile(p["ln_g"][l][None], (128, 1)).astype(np.float32))
    lnb = np.ascontiguousarray(np.tile(p["ln_b"][l][None], (128, 1)).astype(np.float32))
    snw = np.ascontiguousarray(p["ssm_norm_w"][l].reshape(3, 128).T.astype(np.float32))
    for c in range(8):
        b = (c * T) // S
        ts = slice((c * T) % S, (c * T) % S + T)
        maps.append(dict(yaT=np.ascontiguousarray(yaT[b, :, ts]), ybT=np.ascontiguousarray(ybT[b, :, ts]), ycT=np.ascontiguousarray(ycT[b, :, ts]),
                         ydT=np.ascontiguousarray(ydT[b, :, ts]), zT=_cT(proj[b, ts, OFF_B + 896: OFF_B + 896 + 384]), snw=snw,
                         x=np.ascontiguousarray(xf[c * T:(c + 1) * T]), w=p["w_out"][l], lng=lng, lnb=lnb, ones=ones))
    res = _run(nco, maps)
    return np.concatenate([r["xo"] for r in res], axis=0).reshape(BATCH, S, D_MODEL)


def kernel(**inputs):
    p = {k: np.asarray(v, dtype=np.float32) for k, v in inputs.items()}
    x = p["x"]
    for l in range(DEPTH):
        x = _layer(x, l, p)
    return x.astype(np.float32)
```

```python
import contextlib
import numpy as np
import concourse.bass as bass
import concourse.mybir as mybir

F32 = mybir.dt.float32
BF16 = mybir.dt.bfloat16
I32 = mybir.dt.int32
U8 = mybir.dt.uint8
AF = mybir.ActivationFunctionType
ALU = mybir.AluOpType
AX = mybir.AxisListType


class Buf:
    __slots__ = ("name", "t", "writer", "readers", "excl")

    def __init__(self, name, t, excl=False):
        self.name = name
        self.t = t
        self.excl = excl
        self.writer = None
        self.readers = []

    def __getitem__(self, idx):
        return self.t[idx]


class Src:
    def __init__(self, buf, fn):
        self.buf = buf
        self.fn = fn

    def __getitem__(self, k):
        return self.fn(k)


class Prog:
    ENG = ("pe", "dve", "act", "pool", "sp")
    NDMA = 6

    def __init__(self, nc, same_engine_sync=True):
        self.nc = nc
        self.stack = contextlib.ExitStack()
        self.ops = {e: [] for e in self.ENG}
        self.cnt = {e: 0 for e in self.ENG}
        self.sems = {}
        for e in self.ENG:
            self.sems[e] = self.stack.enter_context(nc.semaphore("s_" + e))
        self.dq = {}
        for q in ("sp", "pool", "act"):
            self.dq[q] = {"n": 0, "sems": []}
            for i in range(self.NDMA):
                s = self.stack.enter_context(nc.semaphore(f"d_{q}{i}"))
                self.sems[f"d_{q}{i}"] = s
                self.dq[q]["sems"].append(f"d_{q}{i}")
        self.waited = {e: {} for e in self.ENG}
        self.same = same_engine_sync
        self.nbuf = 0
        self.ARENA_BYTES = 206 * 1024
        self.arena = self.stack.enter_context(nc.sbuf_tensor("arena", [128, self.ARENA_BYTES // 4], F32))
        self.arena_off = 0
        self.banks = [self.stack.enter_context(nc.psum_tensor(f"bank{i}", [128, 512], F32)) for i in range(8)]
        self.banks_used = 0

    @staticmethod
    def _view(base, shape, dt, off_bytes):
        esz = mybir.dt.size(dt)
        n = 1
        for d in shape[1:]:
            n *= d
        t = base if dt == F32 else base.bitcast(dt)
        o = off_bytes // esz
        ap = t[0:shape[0], o:o + n]
        if len(shape) == 3:
            ap = ap.rearrange("p (a b) -> p a b", b=shape[2])
        elif len(shape) == 4:
            ap = ap.rearrange("p (a b c) -> p a b c", b=shape[2], c=shape[3])
        return ap, n * esz

    def sbuf(self, name, shape, dt):
        ap, nb = self._view(self.arena, list(shape), dt, self.arena_off)
        self.arena_off += (nb + 63) // 64 * 64
        assert self.arena_off <= self.ARENA_BYTES, f"SBUF arena overflow at {name}: {self.arena_off}"
        return Buf(name, ap)

    def psum(self, name, shape, dt=F32):
        assert self.banks_used < 8, "out of PSUM banks at " + name
        ap, nb = self._view(self.banks[self.banks_used], list(shape), dt, 0)
        assert nb <= 2048
        self.banks_used += 1
        return Buf(name, ap, excl=True)

    @contextlib.contextmanager
    def scope(self):
        sv = (self.arena_off, self.banks_used)
        try:
            yield
        finally:
            self.barrier()
            self.arena_off, self.banks_used = sv

    def _all_deps(self):
        deps = []
        for q, dq in self.dq.items():
            n = dq["n"]
            for i in range(min(n, self.NDMA)):
                cnt_i = (n - 1 - i) // self.NDMA + 1
                deps.append((dq["sems"][i], 16 * cnt_i))
        for e in self.ENG:
            if self.cnt[e] > 0:
                deps.append((e, self.cnt[e]))
        return deps

    def barrier(self):
        deps = self._all_deps()
        for e in self.ENG:
            self._need(e, [d for d in deps if d[0] != e])

    def dram(self, name, shape, dt, kind=None):
        if kind is None:
            t = self.nc.dram_tensor(name, list(shape), dt)
        else:
            t = self.nc.dram_tensor(name, list(shape), dt, kind=kind)
        return Buf(name, t.ap())

    def alias(self, name, t):
        return Buf(name, t)

    def _need(self, eng, deps):
        w = self.waited[eng]
        for (k, v) in deps:
            if k == eng and (eng == "pe" or not self.same):
                continue
            if w.get(k, 0) >= v:
                continue
            w[k] = v
            sem = self.sems[k]
            self.ops[eng].append(lambda e, sem=sem, v=v: e.wait_ge(sem, v))

    def _deps(self, reads, writes, eng=None):
        deps = []
        for b in reads:
            if b.writer is not None:
                deps.append(b.writer)
            if b.excl:
                deps.extend(r for r in b.readers if r[0] != eng)
        for b in writes:
            if b.writer is not None:
                deps.append(b.writer)
            deps.extend(b.readers)
        return deps

    def _mark(self, reads, writes, tag):
        for b in reads:
            b.readers.append(tag)
            if len(b.readers) > 64:
                m = {}
                for (k, v) in b.readers:
                    if m.get(k, 0) < v:
                        m[k] = v
                b.readers = list(m.items())
        for b in writes:
            b.writer = tag
            b.readers = []

    def op(self, eng, fn, reads=(), writes=()):
        self._need(eng, self._deps(reads, writes, eng))
        self.cnt[eng] += 1
        v = self.cnt[eng]
        sem = self.sems[eng]
        self.ops[eng].append(lambda e, fn=fn, sem=sem: fn(e).then_inc(sem, 1))
        self._mark(reads, writes, (eng, v))

    def dma(self, q, out_ap, in_ap, reads=(), writes=(), **kw):
        dq = self.dq[q]
        n = dq["n"]
        dq["n"] += 1
        key = dq["sems"][n % self.NDMA]
        val = 16 * (n // self.NDMA + 1)
        deps = self._deps(reads, writes)
        if val > 16:
            deps.append((key, val - 16))
        self._need(q, deps)
        sem = self.sems[key]
        self.ops[q].append(lambda e, sem=sem, o=out_ap, i=in_ap, kw=kw: e.dma_start(out=o, in_=i, **kw).then_inc(sem, 16))
        self._mark(reads, writes, (key, val))

    def coll(self, kind, alu, groups, in_b, out_b):
        q = "pool"
        if "cc" not in self.sems:
            self.sems["cc"] = self.stack.enter_context(self.nc.semaphore("s_cc"))
            self.ncc = 0
        self.ncc += 1
        val = self.ncc
        deps = self._deps([in_b], [out_b])
        if val > 1:
            deps.append(("cc", val - 1))
        self._need(q, deps)
        sem = self.sems["cc"]
        self.ops[q].append(lambda e, sem=sem: e.collective_compute(kind, alu, replica_groups=groups, ins=[in_b.t.opt()],
                                                                   outs=[out_b.t.opt()]).then_inc(sem, 1))
        self._mark([in_b], [out_b], ("cc", val))

    def finish(self, final_bufs=()):
        deps = []
        for q, dq in self.dq.items():
            n = dq["n"]
            for i in range(min(n, self.NDMA)):
                cnt_i = (n - 1 - i) // self.NDMA + 1
                deps.append((dq["sems"][i], 16 * cnt_i))
        for e in self.ENG:
            if e != "sp" and self.cnt[e] > 0:
                deps.append((e, self.cnt[e]))
        self._need("sp", deps)
        nc = self.nc
        with nc.Block() as block:
            @block.tensor
            def _(e):
                for f in self.ops["pe"]:
                    f(e)

            @block.vector
            def _(e):
                for f in self.ops["dve"]:
                    f(e)

            @block.scalar
            def _(e):
                for f in self.ops["act"]:
                    f(e)

            @block.gpsimd
            def _(e):
                for f in self.ops["pool"]:
                    f(e)

            @block.sync
            def _(e):
                for f in self.ops["sp"]:
                    f(e)
        self.stack.close()
        return nc


D_MODEL = 1024
IN_COLS = 5906
DEPTH = 2


SEQ = 8192
NEGM = 30000.0


def emit_moba(P, qT_d, kT_d, v_d, gT_d, ab_d, bi_d, cb_d, idb_d, idf_d, y_d, NH, S=SEQ, pfx="mb"):
    NT = S // 128
    NQ = S // 512
    NB = S // 256
    qf = P.sbuf(pfx + "_qf", [64, S], F32)
    kf = P.sbuf(pfx + "_kf", [64, S], F32)
    qa = P.sbuf(pfx + "_qa", [96, S], BF16)
    ka = P.sbuf(pfx + "_ka", [96, S], BF16)
    vf = P.sbuf(pfx + "_vf", [128, NT, 64], F32)
    va = P.sbuf(pfx + "_va", [128, NT, 128], BF16)
    cb = P.sbuf(pfx + "_cb", [128, 4, 512], BF16)
    idb = P.sbuf(pfx + "_idb", [128, 128], BF16)
    idf = P.sbuf(pfx + "_idf", [128, 128], F32)
    ab = P.sbuf(pfx + "_ab", [128, 64], F32)
    km = P.sbuf(pfx + "_km", [64, NB], F32)
    gsb = P.sbuf(pfx + "_gsb", [128, 32], F32)
    mx8 = P.sbuf(pfx + "_mx8", [128, 8], F32)
    mb = P.sbuf(pfx + "_mbs", [128, 32], F32)
    pts = [P.sbuf(pfx + f"_pt{i}", [128, 512], BF16) for i in range(3)]
    gch = [P.sbuf(pfx + f"_gch{i}", [64, 512], F32) for i in range(2)]
    rden = [P.sbuf(pfx + f"_rden{i}", [64, 512], F32) for i in range(2)]
    yo = [P.sbuf(pfx + f"_yo{i}", [64, 512], F32) for i in range(2)]
    acc = [P.psum(pfx + f"_acc{i}", [128, 512]) for i in range(2)]
    sps = [P.psum(pfx + f"_sps{i}", [128, 512]) for i in range(3)]
    gps = [P.psum(pfx + f"_gps{i}", [128, 32]) for i in range(2)]
    tps = [P.psum(pfx + f"_tps{i}", [32, 128]) for i in range(1)]

    P.dma("sp", cb[:, :, :], cb_d.t.rearrange("k p q -> p k q"), reads=[cb_d], writes=[cb])
    P.dma("sp", idb[:, :], idb_d[:, :], reads=[idb_d], writes=[idb])
    P.dma("sp", idf[:, :], idf_d[:, :], reads=[idf_d], writes=[idf])
    P.dma("sp", ka[64:96, :], bi_d[:, :], reads=[bi_d], writes=[ka])
    P.op("pool", lambda e: e.memset(va[:, :, 64:128], 1.0), writes=[va])
    si = 0
    for h in range(NH):
        P.dma("sp", qf[:, :], qT_d[h], reads=[qT_d.buf], writes=[qf])
        P.dma("act", kf[:, :], kT_d[h], reads=[kT_d.buf], writes=[kf])
        P.dma("pool", vf[:, :, :], v_d[h], reads=[v_d.buf], writes=[vf])
        P.dma("sp", ab[:, :], ab_d[h], reads=[ab_d.buf], writes=[ab])
        P.op("act", lambda e: e.mul(qa[0:64, :], qf[:, :], 0.125), reads=[qf], writes=[qa])
        P.op("pool", lambda e: e.tensor_copy(out=ka[0:64, :], in_=kf[:, :]), reads=[kf], writes=[ka])
        P.op("pool", lambda e: e.tensor_copy(out=va[:, :, 0:64], in_=vf[:, :, :]), reads=[vf], writes=[va])
        P.op("dve", lambda e: e.tensor_reduce(out=km[:, :], in_=kf.t[:, :].rearrange("p (n l) -> p n l", l=256),
                                              axis=AX.X, op=ALU.add), reads=[kf], writes=[km])
        P.op("dve", lambda e: e.tensor_scalar(out=km[:, :], in0=km[:, :], scalar1=1.0 / 256.0, scalar2=None, op0=ALU.mult),
             reads=[km], writes=[km])
        for t in range(NT):
            qb = t // 2
            gp = gps[t % 2]
            P.op("pe", lambda e, gp=gp, t=t: e.matmul(gp[:, :], lhsT=qf[:, t * 128:(t + 1) * 128], rhs=km[:, :],
                                                     start=True, stop=True), reads=[qf, km], writes=[gp])
            P.op("dve", lambda e: e.memset(gsb[:, :], -1e30), writes=[gsb])
            if qb > 0:
                P.op("dve", lambda e, gp=gp, qb=qb: e.tensor_copy(out=gsb[:, 0:qb], in_=gp[:, 0:qb]), reads=[gp], writes=[gsb])
            P.op("dve", lambda e: e.max(out=mx8[:, :], in_=gsb[:, :]), reads=[gsb], writes=[mx8])
            P.op("dve", lambda e: e.tensor_scalar(out=mb[:, :], in0=gsb[:, :], scalar1=mx8[:, 2:3], scalar2=NEGM,
                                                  op0=ALU.is_ge, op1=ALU.mult), reads=[gsb, mx8], writes=[mb])
            P.op("dve", lambda e, qb=qb: e.memset(mb[:, qb:qb + 1], NEGM), writes=[mb])
            if qb + 1 < 32:
                P.op("dve", lambda e, qb=qb: e.memset(mb[:, qb + 1:32], 0.0), writes=[mb])
            P.op("dve", lambda e: e.tensor_scalar(out=mb[:, :], in0=mb[:, :], scalar1=-NEGM, scalar2=None, op0=ALU.add),
                 reads=[mb], writes=[mb])
            tp = tps[0]
            P.op("pe", lambda e, tp=tp: e.transpose(out=tp[:, :], in_=mb[:, :], identity=idf[:, :]), reads=[mb, idf], writes=[tp])
            P.op("act", lambda e, tp=tp, t=t: e.copy(out=qa[64:96, t * 128:(t + 1) * 128], in_=tp[:, :]), reads=[tp], writes=[qa])
        for qt in range(NQ):
            ac = acc[qt % 2]
            nk = 4 * qt + 4
            for kt in range(nk):
                sp_ = sps[si % 3]
                pt = pts[si % 3]
                si += 1
                diag = kt >= 4 * qt
                P.op("pe", lambda e, sp_=sp_, kt=kt, qt=qt, diag=diag: e.matmul(
                    sp_[:, :], lhsT=ka[:, kt * 128:(kt + 1) * 128], rhs=qa[:, qt * 512:(qt + 1) * 512],
                    start=True, stop=not diag), reads=[ka, qa], writes=[sp_])
                if diag:
                    P.op("pe", lambda e, sp_=sp_, r=kt - 4 * qt: e.matmul(
                        sp_[:, :], lhsT=idb[:, :], rhs=cb[:, r, :], start=False, stop=True), reads=[idb, cb], writes=[sp_])
                ri = kt - 4 * qt + 60
                P.op("act", lambda e, sp_=sp_, pt=pt, ri=ri: e.activation(out=pt[:, :], in_=sp_[:, :], func=AF.Exp,
                                                                          bias=ab[:, ri:ri + 1], scale=1.0),
                     reads=[sp_, ab], writes=[pt])
                P.op("pe", lambda e, ac=ac, pt=pt, kt=kt, nk=nk: e.matmul(
                    ac[:, :], lhsT=va[:, kt, :], rhs=pt[:, :], start=(kt == 0), stop=(kt == nk - 1)),
                    reads=[va, pt], writes=[ac])
            g = gch[qt % 2]
            rd = rden[qt % 2]
            y = yo[qt % 2]
            P.dma("sp", g[:, :], gT_d[h][:, qt * 512:(qt + 1) * 512], reads=[gT_d.buf], writes=[g])
            P.op("act", lambda e, g=g: e.activation(out=g[:, :], in_=g[:, :], func=AF.Silu), reads=[g], writes=[g])
            P.op("dve", lambda e, rd=rd, ac=ac: e.reciprocal(out=rd[:, :], in_=ac[64:128, :]), reads=[ac], writes=[rd])
            P.op("dve", lambda e, y=y, ac=ac, rd=rd: e.tensor_tensor(out=y[:, :], in0=ac[0:64, :], in1=rd[:, :], op=ALU.mult),
                 reads=[ac, rd], writes=[y])
            P.op("pool", lambda e, y=y, g=g: e.tensor_tensor(out=y[:, :], in0=y[:, :], in1=g[:, :], op=ALU.mult),
                 reads=[y, g], writes=[y])
            P.dma("pool", y_d[h][:, qt * 512:(qt + 1) * 512], y[:, :], reads=[y], writes=[y_d.buf])


def moba_consts(S=SEQ):
    import ml_dtypes
    bf = ml_dtypes.bfloat16
    bi = np.zeros((32, S), np.float32)
    for j in range(S // 256):
        bi[j, j * 256:(j + 1) * 256] = 1.0
    cbm = np.zeros((4, 128, 512), np.float32)
    for r in range(4):
        for p in range(128):
            tk = r * 128 + p
            for blk in range(2):
                if tk // 256 == blk:
                    q = np.arange(blk * 256, (blk + 1) * 256)
                    cbm[r, p, q] = np.where(tk > q, -NEGM, 0.0)
    n = 12
    s = 2.0 ** (-8.0 * (np.arange(n) + 1) / n)
    slopes_moba = s[6:]
    ab = np.zeros((6, 128, 64), np.float32)
    for h in range(6):
        for ri in range(64):
            rel = ri - 60
            ab[h, :, ri] = slopes_moba[h] * (128.0 * rel + np.arange(128))
    return dict(bi=bi.astype(bf), cb=cbm.astype(bf), idb=np.eye(128, dtype=np.float32).astype(bf),
                idf=np.eye(128, dtype=np.float32), ab=ab)


DIL_D = (1, 4, 16)
DIL_GROUPS = (0, 1, 2)
DBG = None


def emit_dil(P, qT_d, kT_d, vp_d, gT_d, bias_d, y_d, NH, S=SEQ, pfx="dl"):
    NT = S // 128
    qf = P.sbuf(pfx + "_qf", [64, S], F32)
    kf = qf
    qa = P.sbuf(pfx + "_qa", [64, S], BF16)
    ka = P.sbuf(pfx + "_ka", [64, S], BF16)
    vf = P.sbuf(pfx + "_vf", [128, NT, 64], F32)
    va = [P.sbuf(pfx + f"_va{g}", [128, NT, 128], BF16) for g in range(3)]
    bs = P.sbuf(pfx + "_bias", [128, 6, 512], F32)
    num = P.sbuf(pfx + "_num", [128, S], F32)
    tmp = [P.sbuf(pfx + f"_tmp{i}", [128, 512], F32) for i in range(2)]
    pts = [P.sbuf(pfx + f"_pt{i}", [128, 512], BF16) for i in range(4)]
    gch = [P.sbuf(pfx + f"_gch{i}", [64, 512], F32) for i in range(2)]
    rden = [P.sbuf(pfx + f"_rden{i}", [64, 512], F32) for i in range(2)]
    yo = [P.sbuf(pfx + f"_yo{i}", [64, 512], F32) for i in range(2)]
    sps = [P.psum(pfx + f"_sps{i}", [128, 512]) for i in range(3)]
    ops_ = [P.psum(pfx + f"_ops{i}", [128, 512]) for i in range(2)]
    for g in range(3):
        P.op("pool", lambda e, g=g: e.memset(va[g][:, :, 64:128], 1.0), writes=[va[g]])
    si = 0
    oi = 0
    for h in range(NH):
        P.dma("sp", qf[:, :], qT_d[h], reads=[qT_d.buf], writes=[qf])
        P.op("act", lambda e: e.mul(qa[:, :], qf[:, :], 0.125), reads=[qf], writes=[qa])
        P.dma("sp", kf[:, :], kT_d[h], reads=[kT_d.buf], writes=[kf])
        P.op("pool", lambda e: e.tensor_copy(out=ka[:, :], in_=kf[:, :]), reads=[kf], writes=[ka])
        P.dma("sp", bs[:, :, :], bias_d[h], reads=[bias_d.buf], writes=[bs])
        for g in range(3):
            P.dma("pool", vf.t.rearrange("p (r j) c -> p r j c", r=DIL_D[g]), vp_d[(h, g)], reads=[vp_d.buf], writes=[vf])
            P.op("pool", lambda e, g=g: e.tensor_copy(out=va[g][:, :, 0:64], in_=vf[:, :, :]), reads=[vf], writes=[va[g]])
        for g, d in enumerate(DIL_D):
            if g not in DIL_GROUPS:
                continue
            NTd = NT // d
            qv = qa.t[:, :].rearrange("p (u d) -> p u d", d=d)
            kv = ka.t[:, :].rearrange("p (u d) -> p u d", d=d)
            nv = num.t[:, :].rearrange("p (u d) -> p u d", d=d)
            for r in range(d):
                for jb in range(NTd // 4):
                    op_ = ops_[oi % 2]
                    oi += 1
                    ptl = {}
                    for o in (1, 0):
                        i0 = 1 if (o == 1 and jb == 0) else 0
                        sp_ = sps[si % 3]
                        tm = tmp[si % 2]
                        pt = pts[si % 4]
                        ptl[o] = pt
                        si += 1
                        for i in range(i0, 4):
                            jq = jb * 4 + i
                            jk = jq - o
                            P.op("pe", lambda e, sp_=sp_, i=i, jq=jq, jk=jk, r=r, kv=kv, qv=qv: e.matmul(
                                sp_[:, i * 128:(i + 1) * 128], lhsT=kv[:, jk * 128:(jk + 1) * 128, r],
                                rhs=qv[:, jq * 128:(jq + 1) * 128, r], start=True, stop=True,
                                skip_group_check=True), reads=[ka, qa], writes=[sp_])
                        c0 = i0 * 128
                        P.op("dve", lambda e, sp_=sp_, tm=tm, c0=c0, go=g * 2 + o: e.tensor_tensor(
                            out=tm[:, c0:512], in0=sp_[:, c0:512], in1=bs[:, go, c0:512], op=ALU.add),
                            reads=[sp_, bs], writes=[tm])
                        if DBG is not None and h == 0 and jb == 0 and r == 0 and o == 0 and g == DIL_GROUPS[0]:
                            P.dma("sp", DBG[:, :], tm[:, :], reads=[tm], writes=[DBG])
                        P.op("act", lambda e, tm=tm, pt=pt, c0=c0: e.activation(out=pt[:, c0:512], in_=tm[:, c0:512], func=AF.Exp),
                             reads=[tm], writes=[pt])
                    for i in range(4):
                        os_ = (0,) if (jb == 0 and i == 0) else (1, 0)
                        for o in os_:
                            jk = jb * 4 + i - o
                            P.op("pe", lambda e, op_=op_, pt=ptl[o], i=i, tl=r * NTd + jk, g=g, fp=(o == os_[0]), last=(o == 0): e.matmul(
                                op_[:, i * 128:(i + 1) * 128], lhsT=va[g][:, tl, :], rhs=pt[:, i * 128:(i + 1) * 128],
                                start=fp, stop=last, skip_group_check=True), reads=[va[g], pt], writes=[op_])
                    u0 = jb * 512
                    if g == DIL_GROUPS[0]:
                        P.op("dve", lambda e, op_=op_, u0=u0, r=r, nv=nv: e.tensor_copy(out=nv[:, u0:u0 + 512, r], in_=op_[:, :]),
                             reads=[op_], writes=[num])
                    else:
                        P.op("dve", lambda e, op_=op_, u0=u0, r=r, nv=nv: e.tensor_tensor(
                            out=nv[:, u0:u0 + 512, r], in0=op_[:, :], in1=nv[:, u0:u0 + 512, r], op=ALU.add),
                            reads=[op_, num], writes=[num])
        for qt in range(S // 512):
            g_ = gch[qt % 2]
            rd = rden[qt % 2]
            y = yo[qt % 2]
            cs = slice(qt * 512, (qt + 1) * 512)
            P.dma("sp", g_[:, :], gT_d[h][:, cs], reads=[gT_d.buf], writes=[g_])
            P.op("act", lambda e, g_=g_: e.activation(out=g_[:, :], in_=g_[:, :], func=AF.Silu), reads=[g_], writes=[g_])
            P.op("dve", lambda e, rd=rd, cs=cs: e.reciprocal(out=rd[:, :], in_=num[64:128, cs]), reads=[num], writes=[rd])
            P.op("dve", lambda e, y=y, rd=rd, cs=cs: e.tensor_tensor(out=y[:, :], in0=num[0:64, cs], in1=rd[:, :], op=ALU.mult),
                 reads=[num, rd], writes=[y])
            P.op("pool", lambda e, y=y, g_=g_: e.tensor_tensor(out=y[:, :], in0=y[:, :], in1=g_[:, :], op=ALU.mult),
                 reads=[y, g_], writes=[y])
            P.dma("pool", y_d[h][:, cs], y[:, :], reads=[y], writes=[y_d.buf])


def dil_consts():
    n = 12
    s = 2.0 ** (-8.0 * (np.arange(n) + 1) / n)
    slopes = s[:6]
    bias = np.zeros((6, 3, 2, 128, 512), np.float32)
    p = np.arange(128)[:, None]
    x = np.arange(128)[None, :]
    for h in range(6):
        for g, d in enumerate(DIL_D):
            for o in range(2):
                nn = (x - p) + 128 * o
                b = np.where((nn >= 0) & (nn <= 128), -slopes[h] * d * nn, -NEGM).astype(np.float32)
                bias[h, g, o] = np.tile(b, (1, 4))
    return bias


def dil_perm(S=SEQ):
    out = []
    for d in DIL_D:
        u = np.arange(S // d)
        out.append(np.concatenate([u * d + r for r in range(d)]))
    return out


def emit_conv_silu(P, src_d, w_sb, C, S, zp, acc, outs, q="sp", src_buf=None):
    P.dma(q, zp[0:C, 3:S + 3], src_d, reads=([src_buf] if src_buf is not None else []), writes=[zp])
    H = S // 2
    for hh in range(2):
        a0, a1 = hh * H, (hh + 1) * H
        P.op("dve", lambda e, a0=a0, a1=a1: e.tensor_scalar(out=acc[0:C, a0:a1], in0=zp[0:C, 3 + a0:3 + a1], scalar1=w_sb[0:C, 3:4],
                                                          scalar2=w_sb[0:C, 4:5], op0=ALU.mult, op1=ALU.add), reads=[zp, w_sb], writes=[acc])
        for k in range(3):
            P.op("dve", lambda e, k=k, a0=a0, a1=a1: e.scalar_tensor_tensor(out=acc[0:C, a0:a1], in0=zp[0:C, k + a0:k + a1], scalar=w_sb[0:C, k:k + 1],
                                                                          in1=acc[0:C, a0:a1], op0=ALU.mult, op1=ALU.add),
                 reads=[zp, w_sb, acc], writes=[acc])
    for (ob, oap) in outs:
        P.op("act", lambda e, oap=oap: e.activation(out=oap, in_=acc[0:C, :], func=AF.Silu), reads=[acc], writes=[ob])


def emit_softplus(P, eng_dve, out_b, out_ap, x_b, x_ap, t1_b, t1_ap, shape_p):
    P.op("act", lambda e: e.activation(out=t1_ap, in_=x_ap, func=AF.Abs), reads=[x_b], writes=[t1_b])
    P.op("act", lambda e: e.activation(out=t1_ap, in_=t1_ap, func=AF.Exp, scale=-1.0), reads=[t1_b], writes=[t1_b])
    P.op("act", lambda e: e.activation(out=t1_ap, in_=t1_ap, func=AF.Ln, bias=1.0, scale=1.0), reads=[t1_b], writes=[t1_b])
    P.op("dve", lambda e: e.scalar_tensor_tensor(out=out_ap, in0=x_ap, scalar=0.0, in1=t1_ap, op0=ALU.max, op1=ALU.add),
         reads=[x_b, t1_b], writes=[out_b])


SSD_NCH = 10 ** 9
SSD_STOP = 99


def emit_ssd(P, xpre_d, bpre_d, cpre_d, cwx_d, cwb_d, cwc_d, dtc_d, sc_d, tri_d, ones_d, idf_d, idb_d, mneg_d, y_d, NH, S=SEQ, pfx="sd"):
    NT = S // 128
    zp = P.sbuf(pfx + "_zp", [128, S + 3], F32)
    acc = P.sbuf(pfx + "_acc", [128, S], F32)
    xsT = P.sbuf(pfx + "_xsT", [64, S], F32)
    BT = P.sbuf(pfx + "_BT", [128, S], BF16)
    CT = P.sbuf(pfx + "_CT", [128, S], BF16)
    yac = P.sbuf(pfx + "_yac", [64, S], F32)
    cwx = P.sbuf(pfx + "_cwx", [64, 5], F32)
    cwb = P.sbuf(pfx + "_cwb", [128, 5], F32)
    cwc = P.sbuf(pfx + "_cwc", [128, 5], F32)
    sc = P.sbuf(pfx + "_sc", [128, 3], F32)
    tri = P.sbuf(pfx + "_tri", [128, 128], F32)
    ones = P.sbuf(pfx + "_ones", [128, 128], F32)
    idf = P.sbuf(pfx + "_idf", [128, 128], F32)
    idb = P.sbuf(pfx + "_idb", [128, 128], BF16)
    mneg = P.sbuf(pfx + "_mneg", [128, 128], BF16)
    dtr = P.sbuf(pfx + "_dtr", [128, NT], F32)
    dtall = P.sbuf(pfx + "_dtall", [128, NT, 6], F32)
    t1 = P.sbuf(pfx + "_t1", [128, NT], F32)
    dt = P.sbuf(pfx + "_dt", [128, NT], F32)
    a_ = P.sbuf(pfx + "_a", [128, NT], F32)
    acum = P.sbuf(pfx + "_acum", [128, NT], F32)
    nacum = P.sbuf(pfx + "_nacum", [128, NT], F32)
    dB = P.sbuf(pfx + "_dB", [128, NT], F32)
    dlast = P.sbuf(pfx + "_dlast", [128, NT], F32)
    Aneg = P.sbuf(pfx + "_Aneg", [128, 1], F32)
    dg = [P.sbuf(pfx + f"_dg{i}", [128, 128], F32) for i in range(2)]
    EB = [P.sbuf(pfx + f"_EB{i}", [128, 128], F32) for i in range(2)]
    LmT = [P.sbuf(pfx + f"_LmT{i}", [128, 128], F32) for i in range(2)]
    SLT = [P.sbuf(pfx + f"_SLT{i}", [128, 128], BF16) for i in range(2)]
    CgT = [P.sbuf(pfx + f"_CgT{i}", [128, 128], BF16) for i in range(2)]
    X = [P.sbuf(pfx + f"_X{i}", [128, 64], BF16) for i in range(2)]
    Bd = [P.sbuf(pfx + f"_Bd{i}", [128, 128], BF16) for i in range(2)]
    hf = P.sbuf(pfx + "_hf", [128, 64], F32)
    hb = [P.sbuf(pfx + f"_hb{i}", [128, 64], BF16) for i in range(2)]
    p_gb = [P.psum(pfx + f"_pgb{i}", [128, 128]) for i in range(2)]
    p_sc = [P.psum(pfx + f"_psc{i}", [128, 128]) for i in range(1)]
    p_tx = [P.psum(pfx + f"_ptx{i}", [128, 64]) for i in range(1)]
    p_tb = [P.psum(pfx + f"_ptb{i}", [128, 128], BF16) for i in range(1)]
    p_y = [P.psum(pfx + f"_py{i}", [64, 128]) for i in range(2)]
    p_h = [P.psum(pfx + f"_ph{i}", [128, 64]) for i in range(1)]
    p_misc = p_gb[0]

    for (sb, d_) in ((tri, tri_d), (ones, ones_d), (idf, idf_d), (idb, idb_d), (mneg, mneg_d)):
        P.dma("sp", sb[:, :], d_[:, :], reads=[d_], writes=[sb])
    P.op("pool", lambda e: e.memset(zp[:, 0:3], 0.0), writes=[zp])
    for h in range(NH):
        P.dma("sp", cwx[:, :], cwx_d[h], reads=[cwx_d.buf], writes=[cwx])
        P.dma("sp", cwb[:, :], cwb_d[h], reads=[cwb_d.buf], writes=[cwb])
        P.dma("sp", cwc[:, :], cwc_d[h], reads=[cwc_d.buf], writes=[cwc])
        P.dma("sp", sc[:, :], sc_d[h], reads=[sc_d.buf], writes=[sc])
        if h == 0:
            P.dma("act", dtall[:, :, :], dtc_d[0], reads=[dtc_d.buf], writes=[dtall])
        emit_conv_silu(P, xpre_d[h], cwx, 64, S, zp, acc, [(xsT, xsT[:, :])], src_buf=xpre_d.buf)
        emit_conv_silu(P, bpre_d[h], cwb, 128, S, zp, acc, [(BT, BT[:, :])], src_buf=bpre_d.buf)
        emit_conv_silu(P, cpre_d[h], cwc, 128, S, zp, acc, [(CT, CT[:, :])], src_buf=cpre_d.buf)
        P.op("dve", lambda e, h=h: e.tensor_scalar(out=dtr[:, :], in0=dtall[:, :, h], scalar1=sc[:, 0:1], scalar2=None, op0=ALU.add), reads=[dtall, sc], writes=[dtr])
        emit_softplus(P, "dve", dt, dt[:, :], dtr, dtr[:, :], t1, t1[:, :], 128)
        P.op("act", lambda e: e.activation(out=Aneg[:, :], in_=sc[:, 1:2], func=AF.Exp), reads=[sc], writes=[Aneg])
        P.op("dve", lambda e: e.tensor_scalar(out=Aneg[:, :], in0=Aneg[:, :], scalar1=-1.0, scalar2=None, op0=ALU.mult), reads=[Aneg], writes=[Aneg])
        P.op("dve", lambda e: e.tensor_scalar(out=a_[:, :], in0=dt[:, :], scalar1=Aneg[:, 0:1], scalar2=None, op0=ALU.mult), reads=[dt, Aneg], writes=[a_])
        P.op("pe", lambda e: e.matmul(p_misc[:, 0:NT], lhsT=tri[:, :], rhs=a_[:, :], start=True, stop=True), reads=[tri, a_], writes=[p_misc])
        P.op("dve", lambda e: e.tensor_copy(out=acum[:, :], in_=p_misc[:, 0:NT]), reads=[p_misc], writes=[acum])
        P.op("dve", lambda e: e.tensor_scalar(out=nacum[:, :], in0=acum[:, :], scalar1=-1.0, scalar2=None, op0=ALU.mult), reads=[acum], writes=[nacum])
        P.op("pe", lambda e: e.matmul(p_misc[:, 0:NT], lhsT=ones[:, :], rhs=a_[:, :], start=True, stop=True), reads=[ones, a_], writes=[p_misc])
        P.op("act", lambda e: e.activation(out=dlast[:, :], in_=p_misc[:, 0:NT], func=AF.Exp), reads=[p_misc], writes=[dlast])
        P.op("dve", lambda e: e.tensor_tensor(out=dB[:, :], in0=p_misc[:, 0:NT], in1=acum[:, :], op=ALU.subtract), reads=[p_misc, acum], writes=[dB])
        P.op("act", lambda e: e.activation(out=dB[:, :], in_=dB[:, :], func=AF.Exp), reads=[dB], writes=[dB])
        P.op("dve", lambda e: e.memset(hf[:, :], 0.0), writes=[hf])
        P.op("pool", lambda e: e.memset(hb[0][:, :], 0.0), writes=[hb[0]])
        for c in range(min(NT, SSD_NCH)):
            cs = slice(c * 128, (c + 1) * 128)
            i2 = c % 2
            gb = p_gb[i2]
            P.op("pool", lambda e, c=c, i2=i2: e.tensor_scalar(out=dg[i2][:, :], in0=idf[:, :], scalar1=acum[:, c:c + 1], scalar2=None, op0=ALU.mult),
                 reads=[idf, acum], writes=[dg[i2]])
            P.op("pe", lambda e, gb=gb, i2=i2: e.matmul(gb[:, :], lhsT=ones[:, :], rhs=dg[i2][:, :], start=True, stop=True), reads=[ones, dg[i2]], writes=[gb])
            P.op("act", lambda e, gb=gb, i2=i2: e.activation(out=EB[i2][:, :], in_=gb[:, :], func=AF.Exp), reads=[gb], writes=[EB[i2]])
            P.op("pe", lambda e, gb=gb: e.matmul(gb[:, :], lhsT=idb[:, :], rhs=mneg[:, :], start=False, stop=True), reads=[idb, mneg], writes=[gb])
            P.op("act", lambda e, gb=gb, i2=i2, c=c: e.activation(out=LmT[i2][:, :], in_=gb[:, :], func=AF.Exp, bias=nacum[:, c:c + 1], scale=1.0),
                 reads=[gb, nacum], writes=[LmT[i2]])
            ps = p_sc[0]
            P.op("pe", lambda e, ps=ps, cs=cs: e.matmul(ps[:, :], lhsT=BT[:, cs], rhs=CT[:, cs], start=True, stop=True), reads=[BT, CT], writes=[ps])
            P.op("dve", lambda e, ps=ps, i2=i2: e.tensor_tensor(out=SLT[i2][:, :], in0=ps[:, :], in1=LmT[i2][:, :], op=ALU.mult), reads=[ps, LmT[i2]], writes=[SLT[i2]])
            P.op("pool", lambda e, i2=i2, cs=cs: e.tensor_tensor(out=CgT[i2][:, :], in0=CT[:, cs], in1=EB[i2][:, :], op=ALU.mult), reads=[CT, EB[i2]], writes=[CgT[i2]])
            px = p_tx[0]
            P.op("pe", lambda e, px=px, cs=cs: e.transpose(out=px[:, :], in_=xsT[:, cs], identity=idf[0:64, 0:64]), reads=[xsT, idf], writes=[px])
            P.op("dve", lambda e, px=px, i2=i2, c=c: e.tensor_scalar(out=X[i2][:, :], in0=px[:, :], scalar1=dt[:, c:c + 1], scalar2=None, op0=ALU.mult),
                 reads=[px, dt], writes=[X[i2]])
            pb = p_tb[0]
            P.op("pe", lambda e, pb=pb, cs=cs: e.transpose(out=pb[:, :], in_=BT[:, cs], identity=idb[:, :]), reads=[BT, idb], writes=[pb])
            P.op("dve", lambda e, pb=pb, i2=i2, c=c: e.tensor_scalar(out=Bd[i2][:, :], in0=pb[:, :], scalar1=dB[:, c:c + 1], scalar2=None, op0=ALU.mult),
                 reads=[pb, dB], writes=[Bd[i2]])
            py = p_y[i2]
            hcur = hb[c % 2]
            hnxt = hb[(c + 1) % 2]
            P.op("pe", lambda e, py=py, i2=i2: e.matmul(py[:, :], lhsT=X[i2][:, :], rhs=SLT[i2][:, :], start=True, stop=False), reads=[X[i2], SLT[i2]], writes=[py])
            P.op("pe", lambda e, py=py, i2=i2, hcur=hcur: e.matmul(py[:, :], lhsT=hcur[:, :], rhs=CgT[i2][:, :], start=False, stop=True), reads=[hcur, CgT[i2]], writes=[py])
            P.op("dve", lambda e, py=py, cs=cs: e.scalar_tensor_tensor(out=yac[:, cs], in0=xsT[:, cs], scalar=sc[0:64, 2:3], in1=py[:, :], op0=ALU.mult, op1=ALU.add),
                 reads=[xsT, sc, py], writes=[yac])
            ph = p_h[0]
            P.op("pe", lambda e, ph=ph, i2=i2: e.matmul(ph[:, :], lhsT=Bd[i2][:, :], rhs=X[i2][:, :], start=True, stop=True), reads=[Bd[i2], X[i2]], writes=[ph])
            P.op("dve", lambda e, ph=ph, c=c: e.scalar_tensor_tensor(out=hf[:, :], in0=hf[:, :], scalar=dlast[:, c:c + 1], in1=ph[:, :], op0=ALU.mult, op1=ALU.add),
                 reads=[hf, dlast, ph], writes=[hf])
            P.op("act", lambda e, hnxt=hnxt: e.copy(out=hnxt[:, :], in_=hf[:, :]), reads=[hf], writes=[hnxt])
        P.dma("pool", y_d[h], yac[:, :], reads=[yac], writes=[y_d.buf])
        if NT % 2 == 1 or True:
            P.op("pool", lambda e: e.memset(hb[0][:, :], 0.0), writes=[hb[0]])


def rec_consts():
    import ml_dtypes
    bf = ml_dtypes.bfloat16
    s1 = np.arange(128)[:, None]
    s2 = np.arange(128)[None, :]
    tri = (s1 <= s2).astype(np.float32)
    mneg = np.where(s2 < s1, -NEGM, 0.0).astype(np.float32)
    mpos = np.where(s2 >= s1, NEGM, 0.0).astype(np.float32)
    return dict(tri=tri, ones=np.ones((128, 128), np.float32), idf=np.eye(128, dtype=np.float32),
                idb=np.eye(128, dtype=np.float32).astype(bf), mneg=mneg.astype(bf), mpos=mpos.astype(bf))


GDN_NCH = 10 ** 9


def emit_gdn(P, qpre_d, kpre_d, vpre_d, cwq_d, cwk_d, cwv_d, bcol_d, acol_d, sc_d, gate_d, nw_d,
             tri_d, ones_d, idf_d, idb_d, mneg_d, mpos_d, y_d, NH, S=SEQ, pfx="gd"):
    NT = S // 128
    zp = P.sbuf(pfx + "_zp", [128, S + 3], F32)
    acc = P.sbuf(pfx + "_acc", [128, S], F32)
    qn = P.sbuf(pfx + "_qn", [64, S], BF16)
    kn = P.sbuf(pfx + "_kn", [64, S], BF16)
    vT = P.sbuf(pfx + "_vT", [64, S], BF16)
    gt = P.sbuf(pfx + "_gt", [128, NT, 64], F32)
    yall = P.sbuf(pfx + "_yall", [64, S], F32)
    ytm = [P.sbuf(pfx + f"_ytm{i}", [128, 64], F32) for i in range(2)]
    cw = [P.sbuf(pfx + f"_cw{i}", [64, 5], F32) for i in range(3)]
    sc = P.sbuf(pfx + "_sc", [128, 2], F32)
    nw = P.sbuf(pfx + "_nw", [128, 64], F32)
    tri = P.sbuf(pfx + "_tri", [128, 128], F32)
    ones = P.sbuf(pfx + "_ones", [128, 128], F32)
    idf = P.sbuf(pfx + "_idf", [128, 128], F32)
    idb = P.sbuf(pfx + "_idb", [128, 128], BF16)
    mneg = P.sbuf(pfx + "_mneg", [128, 128], BF16)
    mpos = P.sbuf(pfx + "_mpos", [128, 128], BF16)
    col = {n: P.sbuf(pfx + "_c_" + n, [128, NT], F32) for n in
           ("b", "a", "t1", "beta", "nbeta", "g", "gcum", "ngcum", "egc", "bege", "dk", "dlast")}
    Aneg = P.sbuf(pfx + "_Aneg", [128, 1], F32)
    ball = P.sbuf(pfx + "_ball", [128, NT, 6], F32)
    aall = P.sbuf(pfx + "_aall", [128, NT, 6], F32)
    rt = [P.sbuf(pfx + f"_rt{i}", [64, 512], F32) for i in range(2)]
    eps_t = P.sbuf(pfx + "_eps", [128, 1], F32)
    P.op("dve", lambda e: e.memset(eps_t[:, :], 1e-6), writes=[eps_t])

    def f32t(n, k=2):
        return [P.sbuf(pfx + f"_{n}{i}", [128, 128], F32) for i in range(k)]

    def bft(n, shape, k=2):
        return [P.sbuf(pfx + f"_{n}{i}", shape, BF16) for i in range(k)]
    dg = f32t("dg")
    EB = f32t("EB")
    dec = f32t("dec")
    decT = f32t("decT")
    Y = f32t("Y", 2)
    YT = f32t("YT", 2)
    PT = f32t("PT", 2)
    TTb = bft("TTb", [128, 128])
    attnT = bft("attnT", [128, 128])
    qgT = bft("qgT", [64, 128])
    kbg = bft("kbg", [128, 64])
    kd = bft("kd", [128, 64])
    vb = bft("vb", [128, 64])
    wT = bft("wT", [64, 128])
    u_sb = [P.sbuf(pfx + f"_u{i}", [128, 64], F32) for i in range(2)]
    vnew = bft("vnew", [128, 64])
    Sf = P.sbuf(pfx + "_Sf", [64, 64], F32)
    Sb = bft("Sb", [64, 64])
    junk = P.sbuf(pfx + "_junk", [128, 64], F32)
    ssq = [P.sbuf(pfx + f"_ssq{i}", [128, 1], F32) for i in range(2)]
    nwg = [P.sbuf(pfx + f"_nwg{i}", [128, 64], F32) for i in range(2)]
    pg = [P.psum(pfx + f"_pg{i}", [128, 128]) for i in range(2)]
    pa = [P.psum(pfx + f"_pa{i}", [128, 512]) for i in range(3)]
    pb = [P.psum(pfx + f"_pb{i}", [128, 64], BF16) for i in range(1)]
    pc = [P.psum(pfx + f"_pc{i}", [128, 128]) for i in range(2)]
    pai = [0]

    def nxt_pa():
        pai[0] += 1
        return pa[pai[0] % 3]
    pci = [0]

    def nxt_pc():
        pci[0] += 1
        return pc[pci[0] % 2]

    for (sb, d_) in ((tri, tri_d), (ones, ones_d), (idf, idf_d), (idb, idb_d), (mneg, mneg_d), (mpos, mpos_d)):
        P.dma("sp", sb[:, :], d_[:, :], reads=[d_], writes=[sb])
    P.op("pool", lambda e: e.memset(zp[:, 0:3], 0.0), writes=[zp])
    for h in range(NH):
        for i, d_ in enumerate((cwq_d, cwk_d, cwv_d)):
            P.dma("sp", cw[i][:, :], d_[h], reads=[d_.buf], writes=[cw[i]])
        P.dma("sp", sc[:, :], sc_d[h], reads=[sc_d.buf], writes=[sc])
        P.dma("sp", nw[:, :], nw_d[h], reads=[nw_d.buf], writes=[nw])
        if h == 0:
            P.dma("sp", ball[:, :, :], bcol_d[0], reads=[bcol_d.buf], writes=[ball])
            P.dma("sp", aall[:, :, :], acol_d[0], reads=[acol_d.buf], writes=[aall])
        P.op("pool", lambda e, h=h: e.tensor_copy(out=col["b"][:, :], in_=ball[:, :, h]), reads=[ball], writes=[col["b"]])
        P.op("pool", lambda e, h=h: e.tensor_copy(out=col["a"][:, :], in_=aall[:, :, h]), reads=[aall], writes=[col["a"]])
        P.dma("act", gt[:, :, :], gate_d[h], reads=[gate_d.buf], writes=[gt])
        P.op("act", lambda e: e.activation(out=gt[:, :, :], in_=gt[:, :, :], func=AF.Silu), reads=[gt], writes=[gt])
        for which, (src_d, outb) in enumerate(((qpre_d, qn), (kpre_d, kn))):
            emit_conv_silu(P, src_d[h], cw[which], 64, S, zp, acc, [(acc, acc[0:64, :])], src_buf=src_d.buf)
            P.op("act", lambda e: e.activation(out=zp[0:64, 3:S + 3], in_=acc[0:64, :], func=AF.Square), reads=[acc], writes=[zp])
            for j in range(S // 512):
                cs = slice(j * 512, (j + 1) * 512)
                ps = nxt_pa()
                r_ = rt[j % 2]
                P.op("pe", lambda e, ps=ps, j=j: e.matmul(ps[0:64, :], lhsT=ones[0:64, 0:64], rhs=zp[0:64, 3 + j * 512:3 + (j + 1) * 512],
                                                         start=True, stop=True), reads=[ones, zp], writes=[ps])
                P.op("act", lambda e, ps=ps, r_=r_: e.activation(out=r_[:, :], in_=ps[0:64, :], func=AF.Sqrt, bias=eps_t[0:64, 0:1], scale=1.0),
                     reads=[ps, eps_t], writes=[r_])
                P.op("dve", lambda e, r_=r_: e.reciprocal(out=r_[:, :], in_=r_[:, :]), reads=[r_], writes=[r_])
                P.op("dve", lambda e, r_=r_, cs=cs, outb=outb, sc_=(0.125 if which == 0 else 1.0): e.scalar_tensor_tensor(
                    out=outb[:, cs], in0=acc[0:64, cs], scalar=sc_, in1=r_[:, :], op0=ALU.mult, op1=ALU.mult),
                    reads=[acc, r_], writes=[outb])
        emit_conv_silu(P, vpre_d[h], cw[2], 64, S, zp, acc, [(vT, vT[:, :])], src_buf=vpre_d.buf)
        c_ = col
        P.op("act", lambda e: e.activation(out=c_["beta"][:, :], in_=c_["b"][:, :], func=AF.Sigmoid), reads=[c_["b"]], writes=[c_["beta"]])
        P.op("dve", lambda e: e.tensor_scalar(out=c_["nbeta"][:, :], in0=c_["beta"][:, :], scalar1=-1.0, scalar2=None, op0=ALU.mult),
             reads=[c_["beta"]], writes=[c_["nbeta"]])
        P.op("dve", lambda e: e.tensor_scalar(out=c_["a"][:, :], in0=c_["a"][:, :], scalar1=sc[:, 0:1], scalar2=None, op0=ALU.add),
             reads=[c_["a"], sc], writes=[c_["a"]])
        emit_softplus(P, "dve", c_["g"], c_["g"][:, :], c_["a"], c_["a"][:, :], c_["t1"], c_["t1"][:, :], 128)
        P.op("act", lambda e: e.activation(out=Aneg[:, :], in_=sc[:, 1:2], func=AF.Exp), reads=[sc], writes=[Aneg])
        P.op("dve", lambda e: e.tensor_scalar(out=Aneg[:, :], in0=Aneg[:, :], scalar1=-1.0, scalar2=None, op0=ALU.mult), reads=[Aneg], writes=[Aneg])
        P.op("dve", lambda e: e.tensor_scalar(out=c_["g"][:, :], in0=c_["g"][:, :], scalar1=Aneg[:, 0:1], scalar2=None, op0=ALU.mult),
             reads=[c_["g"], Aneg], writes=[c_["g"]])
        pm = pg[0]
        P.op("pe", lambda e: e.matmul(pm[:, 0:NT], lhsT=tri[:, :], rhs=c_["g"][:, :], start=True, stop=True), reads=[tri, c_["g"]], writes=[pm])
        P.op("dve", lambda e: e.tensor_copy(out=c_["gcum"][:, :], in_=pm[:, 0:NT]), reads=[pm], writes=[c_["gcum"]])
        P.op("dve", lambda e: e.tensor_scalar(out=c_["ngcum"][:, :], in0=c_["gcum"][:, :], scalar1=-1.0, scalar2=None, op0=ALU.mult),
             reads=[c_["gcum"]], writes=[c_["ngcum"]])
        P.op("act", lambda e: e.activation(out=c_["egc"][:, :], in_=c_["gcum"][:, :], func=AF.Exp), reads=[c_["gcum"]], writes=[c_["egc"]])
        P.op("dve", lambda e: e.tensor_tensor(out=c_["bege"][:, :], in0=c_["egc"][:, :], in1=c_["beta"][:, :], op=ALU.mult),
             reads=[c_["egc"], c_["beta"]], writes=[c_["bege"]])
        P.op("pe", lambda e: e.matmul(pm[:, 0:NT], lhsT=ones[:, :], rhs=c_["g"][:, :], start=True, stop=True), reads=[ones, c_["g"]], writes=[pm])
        P.op("act", lambda e: e.activation(out=c_["dlast"][:, :], in_=pm[:, 0:NT], func=AF.Exp), reads=[pm], writes=[c_["dlast"]])
        P.op("dve", lambda e: e.tensor_tensor(out=c_["dk"][:, :], in0=pm[:, 0:NT], in1=c_["gcum"][:, :], op=ALU.subtract),
             reads=[pm, c_["gcum"]], writes=[c_["dk"]])
        P.op("act", lambda e: e.activation(out=c_["dk"][:, :], in_=c_["dk"][:, :], func=AF.Exp), reads=[c_["dk"]], writes=[c_["dk"]])
        P.op("dve", lambda e: e.memset(Sf[:, :], 0.0), writes=[Sf])
        P.op("pool", lambda e: e.memset(Sb[0][:, :], 0.0), writes=[Sb[0]])
        for c in range(min(NT, GDN_NCH)):
            cs = slice(c * 128, (c + 1) * 128)
            i2 = c % 2
            g1, g2 = pg[0], pg[1]
            P.op("pool", lambda e, c=c, i2=i2: e.tensor_scalar(out=dg[i2][:, :], in0=idf[:, :], scalar1=c_["gcum"][:, c:c + 1], scalar2=None, op0=ALU.mult),
                 reads=[idf, c_["gcum"]], writes=[dg[i2]])
            P.op("pe", lambda e, i2=i2: e.matmul(g1[:, :], lhsT=ones[:, :], rhs=dg[i2][:, :], start=True, stop=True), reads=[ones, dg[i2]], writes=[g1])
            P.op("pe", lambda e, i2=i2: e.matmul(g2[:, :], lhsT=ones[:, :], rhs=dg[i2][:, :], start=True, stop=True), reads=[ones, dg[i2]], writes=[g2])
            P.op("act", lambda e, i2=i2: e.activation(out=EB[i2][0:64, :], in_=g1[0:64, :], func=AF.Exp), reads=[g1], writes=[EB[i2]])
            P.op("pe", lambda e: e.matmul(g1[:, :], lhsT=idb[:, :], rhs=mpos[:, :], start=False, stop=True), reads=[idb, mpos], writes=[g1])
            P.op("pe", lambda e: e.matmul(g2[:, :], lhsT=idb[:, :], rhs=mneg[:, :], start=False, stop=True), reads=[idb, mneg], writes=[g2])
            P.op("act", lambda e, i2=i2, c=c: e.activation(out=dec[i2][:, :], in_=g1[:, :], func=AF.Exp, bias=c_["gcum"][:, c:c + 1], scale=-1.0),
                 reads=[g1, c_["gcum"]], writes=[dec[i2]])
            P.op("act", lambda e, i2=i2, c=c: e.activation(out=decT[i2][:, :], in_=g2[:, :], func=AF.Exp, bias=c_["ngcum"][:, c:c + 1], scale=1.0),
                 reads=[g2, c_["ngcum"]], writes=[decT[i2]])
            P.op("pool", lambda e, i2=i2, cs=cs: e.tensor_tensor(out=qgT[i2][:, :], in0=qn[:, cs], in1=EB[i2][0:64, :], op=ALU.mult),
                 reads=[qn, EB[i2]], writes=[qgT[i2]])
            pk = nxt_pa()
            P.op("pe", lambda e, pk=pk, cs=cs: e.matmul(pk[:, 0:128], lhsT=kn[:, cs], rhs=kn[:, cs], start=True, stop=True), reads=[kn], writes=[pk])
            Yc, YTc, PTc = Y[0], YT[0], PT[0]
            P.op("dve", lambda e, pk=pk, i2=i2, c=c, Yc=Yc: e.scalar_tensor_tensor(out=Yc[:, :], in0=pk[:, 0:128], scalar=c_["nbeta"][:, c:c + 1],
                                                                                 in1=dec[i2][:, :], op0=ALU.mult, op1=ALU.mult),
                 reads=[pk, c_["nbeta"], dec[i2]], writes=[Yc])
            pt_ = nxt_pa()
            P.op("pe", lambda e, pt_=pt_, Yc=Yc: e.transpose(out=pt_[:, 0:128], in_=Yc[:, :], identity=idf[:, :]), reads=[Yc, idf], writes=[pt_])
            P.op("act", lambda e, pt_=pt_, YTc=YTc: e.copy(out=YTc[:, :], in_=pt_[:, 0:128]), reads=[pt_], writes=[YTc])
            P.op("dve", lambda e, pt_=pt_, PTc=PTc: e.tensor_tensor(out=PTc[:, :], in0=pt_[:, 0:128], in1=idf[:, :], op=ALU.add), reads=[pt_, idf], writes=[PTc])
            pq = nxt_pa()
            P.op("pe", lambda e, pq=pq, cs=cs: e.matmul(pq[:, 0:128], lhsT=kn[:, cs], rhs=qn[:, cs], start=True, stop=True), reads=[kn, qn], writes=[pq])
            P.op("dve", lambda e, pq=pq, i2=i2: e.tensor_tensor(out=attnT[i2][:, :], in0=pq[:, 0:128], in1=decT[i2][:, :], op=ALU.mult),
                 reads=[pq, decT[i2]], writes=[attnT[i2]])
            cur = 0
            for lev in range(6):
                nx = 1 - cur
                p1 = nxt_pa()
                P.op("pe", lambda e, p1=p1, cur=cur: e.matmul(p1[:, 0:128], lhsT=YT[cur][:, :], rhs=Y[cur][:, :], start=True, stop=True),
                     reads=[YT[cur], Y[cur]], writes=[p1])
                P.op("act", lambda e, p1=p1, nx=nx: e.copy(out=Y[nx][:, :], in_=p1[:, 0:128]), reads=[p1], writes=[Y[nx]])
                if lev < 5:
                    p2 = nxt_pa()
                    P.op("pe", lambda e, p2=p2, cur=cur: e.matmul(p2[:, 0:128], lhsT=Y[cur][:, :], rhs=YT[cur][:, :], start=True, stop=True),
                         reads=[YT[cur], Y[cur]], writes=[p2])
                    P.op("dve", lambda e, p2=p2, nx=nx: e.tensor_copy(out=YT[nx][:, :], in_=p2[:, 0:128]), reads=[p2], writes=[YT[nx]])
                p3 = nxt_pa()
                P.op("pe", lambda e, p3=p3, nx=nx, cur=cur: e.matmul(p3[:, 0:128], lhsT=Y[nx][:, :], rhs=PT[cur][:, :], start=True, stop=True),
                     reads=[Y[nx], PT[cur]], writes=[p3])
                P.op("dve", lambda e, p3=p3, nx=nx, cur=cur: e.tensor_tensor(out=PT[nx][:, :], in0=p3[:, 0:128], in1=PT[cur][:, :], op=ALU.add),
                     reads=[p3, PT[cur]], writes=[PT[nx]])
                cur = nx
            P.op("act", lambda e, i2=i2, cur=cur: e.copy(out=TTb[i2][:, :], in_=PT[cur][:, :]), reads=[PT[cur]], writes=[TTb[i2]])
            pkt = pb[0]
            P.op("pe", lambda e, cs=cs: e.transpose(out=pkt[:, :], in_=kn[:, cs], identity=idb[0:64, 0:64]), reads=[kn, idb], writes=[pkt])
            P.op("dve", lambda e, i2=i2, c=c: e.tensor_scalar(out=kbg[i2][:, :], in0=pkt[:, :], scalar1=c_["bege"][:, c:c + 1], scalar2=None, op0=ALU.mult),
                 reads=[pkt, c_["bege"]], writes=[kbg[i2]])
            P.op("dve", lambda e, i2=i2, c=c: e.tensor_scalar(out=kd[i2][:, :], in0=pkt[:, :], scalar1=c_["dk"][:, c:c + 1], scalar2=None, op0=ALU.mult),
                 reads=[pkt, c_["dk"]], writes=[kd[i2]])
            P.op("pe", lambda e, cs=cs: e.transpose(out=pkt[:, :], in_=vT[:, cs], identity=idb[0:64, 0:64]), reads=[vT, idb], writes=[pkt])
            P.op("dve", lambda e, i2=i2, c=c: e.tensor_scalar(out=vb[i2][:, :], in0=pkt[:, :], scalar1=c_["beta"][:, c:c + 1], scalar2=None, op0=ALU.mult),
                 reads=[pkt, c_["beta"]], writes=[vb[i2]])
            pu = nxt_pc()
            P.op("pe", lambda e, pu=pu, i2=i2: e.matmul(pu[:, 0:64], lhsT=TTb[i2][:, :], rhs=vb[i2][:, :], start=True, stop=True), reads=[TTb[i2], vb[i2]], writes=[pu])
            P.op("act", lambda e, pu=pu, i2=i2: e.copy(out=u_sb[i2][:, :], in_=pu[:, 0:64]), reads=[pu], writes=[u_sb[i2]])
            pw = nxt_pc()
            P.op("pe", lambda e, pw=pw, i2=i2: e.matmul(pw[0:64, :], lhsT=kbg[i2][:, :], rhs=TTb[i2][:, :], start=True, stop=True), reads=[kbg[i2], TTb[i2]], writes=[pw])
            P.op("act", lambda e, pw=pw, i2=i2: e.copy(out=wT[i2][:, :], in_=pw[0:64, :]), reads=[pw], writes=[wT[i2]])
            Scur, Snxt = Sb[c % 2], Sb[(c + 1) % 2]
            pv = nxt_pc()
            P.op("pe", lambda e, pv=pv, i2=i2, Scur=Scur: e.matmul(pv[:, 0:64], lhsT=wT[i2][:, :], rhs=Scur[:, :], start=True, stop=True), reads=[wT[i2], Scur], writes=[pv])
            P.op("dve", lambda e, pv=pv, i2=i2: e.tensor_tensor(out=vnew[i2][:, :], in0=u_sb[i2][:, :], in1=pv[:, 0:64], op=ALU.subtract),
                 reads=[u_sb[i2], pv], writes=[vnew[i2]])
            po = nxt_pc()
            P.op("pe", lambda e, po=po, i2=i2, Scur=Scur: e.matmul(po[:, 0:64], lhsT=qgT[i2][:, :], rhs=Scur[:, :], start=True, stop=False), reads=[qgT[i2], Scur], writes=[po])
            P.op("pe", lambda e, po=po, i2=i2: e.matmul(po[:, 0:64], lhsT=attnT[i2][:, :], rhs=vnew[i2][:, :], start=False, stop=True), reads=[attnT[i2], vnew[i2]], writes=[po])
            pS = nxt_pc()
            P.op("pe", lambda e, pS=pS, i2=i2: e.matmul(pS[0:64, 0:64], lhsT=kd[i2][:, :], rhs=vnew[i2][:, :], start=True, stop=True), reads=[kd[i2], vnew[i2]], writes=[pS])
            P.op("dve", lambda e, pS=pS, c=c: e.scalar_tensor_tensor(out=Sf[:, :], in0=Sf[:, :], scalar=c_["dlast"][0:64, c:c + 1], in1=pS[0:64, 0:64],
                                                                    op0=ALU.mult, op1=ALU.add), reads=[Sf, c_["dlast"], pS], writes=[Sf])
            P.op("act", lambda e, Snxt=Snxt: e.copy(out=Snxt[:, :], in_=Sf[:, :]), reads=[Sf], writes=[Snxt])
            P.op("act", lambda e, po=po, i2=i2: e.activation(out=junk[:, :], in_=po[:, 0:64], func=AF.Square, accum_out=ssq[i2][:, 0:1]),
                 reads=[po], writes=[junk, ssq[i2]])
            P.op("act", lambda e, i2=i2: e.activation(out=ssq[i2][:, :], in_=ssq[i2][:, :], func=AF.Sqrt, bias=eps_t[:, 0:1], scale=1.0 / 64.0),
                 reads=[ssq[i2], eps_t], writes=[ssq[i2]])
            P.op("dve", lambda e, i2=i2: e.reciprocal(out=ssq[i2][:, :], in_=ssq[i2][:, :]), reads=[ssq[i2]], writes=[ssq[i2]])
            P.op("pool", lambda e, i2=i2, c=c: e.tensor_tensor(out=nwg[i2][:, :], in0=gt[:, c, :], in1=nw[:, :], op=ALU.mult), reads=[gt, nw], writes=[nwg[i2]])
            P.op("dve", lambda e, po=po, i2=i2, c=c: e.scalar_tensor_tensor(out=ytm[i2][:, :], in0=po[:, 0:64], scalar=ssq[i2][:, 0:1], in1=nwg[i2][:, :],
                                                                           op0=ALU.mult, op1=ALU.mult), reads=[po, ssq[i2], nwg[i2]], writes=[ytm[i2]])
            pyt = nxt_pa()
            P.op("pe", lambda e, pyt=pyt, i2=i2: e.transpose(out=pyt[0:64, 0:128], in_=ytm[i2][:, :], identity=idf[:, :]), reads=[ytm[i2], idf], writes=[pyt])
            P.op("act", lambda e, pyt=pyt, cs=cs: e.copy(out=yall[:, cs], in_=pyt[0:64, 0:128]), reads=[pyt], writes=[yall])
        P.dma("pool", y_d[h], yall[:, :], reads=[yall], writes=[y_d.buf])
        P.op("pool", lambda e: e.memset(Sb[0][:, :], 0.0), writes=[Sb[0]])
        if min(NT, GDN_NCH) % 2 == 1:
            P.op("pool", lambda e: e.memset(Sb[1][:, :], 0.0), writes=[Sb[1]])


D_MIX = 1536
ALPHA = (2.0 * 2) ** 0.25
N_FM = 4736
N_TM = 1170
FM_AQ, FM_AK, FM_AG, FM_SX, FM_SB, FM_SC, FM_SZ, FM_CQ, FM_CK, FM_CG, FM_DQ, FM_DK, FM_DV = (
    0, 384, 768, 1152, 1536, 1792, 2048, 2432, 2816, 3200, 3584, 3968, 4352)
TM_AV, TM_CV, TM_DG, TM_DT, TM_DB, TM_DA = 0, 384, 768, 1152, 1158, 1164


def in_col_perm():
    r = lambda a, b: list(range(a, b))
    fm = r(0, 768) + r(1152, 1536) + r(1536, 2816) + r(2822, 3590) + r(3974, 4358) + r(4358, 5510)
    tm = r(768, 1152) + r(3590, 3974) + r(5510, 5894) + r(2816, 2822) + r(5894, 5900) + r(5900, 5906)
    assert len(fm) == N_FM and len(tm) == N_TM
    return np.array(fm + tm)


def emit_projF(P, x_src, src_bf16, w_ap, w_buf, projT, projtm, S=SEQ, pfx="pj"):
    KC = D_MODEL // 128
    C = IN_COLS
    TB = 1024
    wbf = P.sbuf(pfx + "_wbf", [128, KC, C], BF16)
    HW = (C + 1) // 2
    wst = [P.sbuf(pfx + f"_wst{i}", [128, HW], F32) for i in range(2)]
    xbf = [P.sbuf(pfx + f"_xbf{i}", [128, KC, TB], BF16) for i in range(2)]
    xst = [P.sbuf(pfx + f"_xst{i}", [128, TB], F32) for i in range(2)]
    ost = [P.sbuf(pfx + f"_ost{i}", [128, 512], F32) for i in range(3)]
    ot2 = [P.sbuf(pfx + f"_ot2{i}", [128, N_TM], F32) for i in range(2)]
    ps = [P.psum(pfx + f"_ps{i}", [128, 512]) for i in range(6)]
    n = 0
    for k in range(KC):
        for h in range(2):
            c0, c1 = h * HW, min(C, (h + 1) * HW)
            b = wst[n % 2]
            P.dma("sp" if n % 2 == 0 else "act", b[:, :c1 - c0], w_ap[k * 128:(k + 1) * 128, c0:c1], reads=[w_buf], writes=[b])
            eng = "dve" if n % 2 == 0 else "pool"
            P.op(eng, lambda e, b=b, k=k, c0=c0, c1=c1: e.tensor_copy(out=wbf[:, k, c0:c1], in_=b[:, :c1 - c0]), reads=[b], writes=[wbf])
            n += 1
    ci = 0
    oi = 0
    for blk in range(S // TB):
        xb = xbf[blk % 2]
        t0 = blk * TB
        if src_bf16:
            P.dma("sp", xb[:, :, :], x_src.t.rearrange("(k p) t -> p k t", p=128)[:, :, t0:t0 + TB], reads=[x_src], writes=[xb])
        else:
            for k in range(KC):
                b = xst[k % 2]
                P.dma("sp", b[:, :], x_src[k * 128:(k + 1) * 128, t0:t0 + TB], reads=[x_src], writes=[b])
                P.op("pool", lambda e, b=b, k=k, xb=xb: e.tensor_copy(out=xb[:, k, :], in_=b[:, :]), reads=[b], writes=[xb])
        for sub in range(TB // 512):
            for cc in range(N_FM // 128):
                p = ps[ci % 6]
                o = ost[ci % 3]
                for k in range(KC):
                    P.op("pe", lambda e, p=p, k=k, cc=cc, sub=sub, xb=xb: e.matmul(
                        p[:, :], lhsT=wbf[:, k, cc * 128:(cc + 1) * 128], rhs=xb[:, k, sub * 512:(sub + 1) * 512],
                        start=(k == 0), stop=(k == KC - 1)), reads=[wbf, xb], writes=[p])
                if ci % 2 == 0:
                    P.op("dve", lambda e, p=p, o=o: e.tensor_copy(out=o[:, :], in_=p[:, :]), reads=[p], writes=[o])
                else:
                    P.op("act", lambda e, p=p, o=o: e.copy(out=o[:, :], in_=p[:, :]), reads=[p], writes=[o])
                P.dma("pool" if ci % 2 == 0 else "sp", projT[cc * 128:(cc + 1) * 128, t0 + sub * 512:t0 + (sub + 1) * 512], o[:, :],
                      reads=[o], writes=[projT])
                ci += 1
        for tt in range(TB // 128):
            o2 = ot2[oi % 2]
            oi += 1
            for c0 in range(0, N_TM, 512):
                c1 = min(N_TM, c0 + 512)
                p = ps[ci % 6]
                for k in range(KC):
                    P.op("pe", lambda e, p=p, k=k, tt=tt, c0=c0, c1=c1, xb=xb: e.matmul(
                        p[:, :c1 - c0], lhsT=xb[:, k, tt * 128:(tt + 1) * 128], rhs=wbf[:, k, N_FM + c0:N_FM + c1],
                        start=(k == 0), stop=(k == KC - 1)), reads=[wbf, xb], writes=[p])
                if ci % 2 == 0:
                    P.op("dve", lambda e, p=p, o2=o2, c0=c0, c1=c1: e.tensor_copy(out=o2[:, c0:c1], in_=p[:, :c1 - c0]), reads=[p], writes=[o2])
                else:
                    P.op("act", lambda e, p=p, o2=o2, c0=c0, c1=c1: e.copy(out=o2[:, c0:c1], in_=p[:, :c1 - c0]), reads=[p], writes=[o2])
                ci += 1
            P.dma("pool", projtm[t0 + tt * 128:t0 + (tt + 1) * 128, :], o2[:, :], reads=[o2], writes=[projtm])


def emit_outlnF(P, mix_d, projT, snw_ap, w_ap, lng_ap, lnb_ap, ones_ap, idf_ap, cbuf, x_d, xo_d, xoT_d, S=SEQ, TBLK=2048, pfx="ol"):
    FC = D_MIX // 128
    T = TBLK
    wbf = P.sbuf(pfx + "_wbf", [128, FC, D_MODEL], BF16)
    mix = P.sbuf(pfx + "_mix", [128, FC, T], BF16)
    wst = [P.sbuf(pfx + f"_wst{i}", [128, D_MODEL], F32) for i in range(2)]
    mst = [P.sbuf(pfx + f"_mst{i}", [128, T], F32) for i in range(2)]
    zst = [P.sbuf(pfx + f"_zst{i}", [128, T], F32) for i in range(2)]
    gz = P.sbuf(pfx + "_gz", [128, 3, T], F32)
    sq = P.sbuf(pfx + "_sq", [128, T], F32)
    rs = P.sbuf(pfx + "_rs", [128, T], F32)
    snw = P.sbuf(pfx + "_snw", [128, 3], F32)
    lng = P.sbuf(pfx + "_lng", [128, D_MODEL], F32)
    lnb = P.sbuf(pfx + "_lnb", [128, D_MODEL], F32)
    ones = P.sbuf(pfx + "_ones", [128, 128], F32)
    idf = P.sbuf(pfx + "_idf", [128, 128], F32)
    eps6 = P.sbuf(pfx + "_eps6", [128, 1], F32)
    eps5 = P.sbuf(pfx + "_eps5", [128, 1], F32)
    xt = [P.sbuf(pfx + f"_xt{i}", [128, D_MODEL], F32) for i in range(2)]
    zt = [P.sbuf(pfx + f"_zt{i}", [128, D_MODEL], F32) for i in range(2)]
    xTt = [P.sbuf(pfx + f"_xTt{i}", [128, 8, 128], BF16) for i in range(2)]
    st = [P.sbuf(pfx + f"_st{i}", [128, 2, 6], F32) for i in range(2)]
    mv = [P.sbuf(pfx + f"_mv{i}", [128, 2], F32) for i in range(2)]
    pss = [P.psum(pfx + f"_pss{i}", [128, 512]) for i in range(2)]
    po = [P.psum(pfx + f"_po{i}", [128, 512]) for i in range(4)]
    ptr = [P.psum(pfx + f"_ptr{i}", [128, 512]) for i in range(2)]
    P.op("dve", lambda e: e.memset(eps6[:, :], 1e-6), writes=[eps6])
    P.op("dve", lambda e: e.memset(eps5[:, :], 1e-5), writes=[eps5])
    P.dma("sp", snw[:, :], snw_ap, reads=[cbuf], writes=[snw])
    P.dma("sp", lng[:, :], lng_ap, reads=[cbuf], writes=[lng])
    P.dma("sp", lnb[:, :], lnb_ap, reads=[cbuf], writes=[lnb])
    P.dma("sp", ones[:, :], ones_ap, reads=[cbuf], writes=[ones])
    P.dma("sp", idf[:, :], idf_ap, reads=[cbuf], writes=[idf])
    for k in range(FC):
        b = wst[k % 2]
        P.dma("act", b[:, :], w_ap[k * 128:(k + 1) * 128, :], reads=[cbuf], writes=[b])
        P.op("pool", lambda e, b=b, k=k: e.tensor_copy(out=wbf[:, k, :], in_=b[:, :]), reads=[b], writes=[wbf])
    n = 0
    ti = 0
    for blk in range(S // T):
        b0 = blk * T
        bsl = slice(b0, b0 + T)
        for f0 in (0, 6, 9):
            for k in range(3):
                b = mst[n % 2]
                n += 1
                P.dma("sp", b[:, :], mix_d[(f0 + k) * 128:(f0 + k + 1) * 128, bsl], reads=[mix_d], writes=[b])
                P.op("dve", lambda e, b=b, fk=f0 + k: e.tensor_copy(out=mix[:, fk, :], in_=b[:, :]), reads=[b], writes=[mix])
        for k in range(3):
            b = mst[n % 2]
            zb = zst[k % 2]
            n += 1
            P.dma("sp", b[:, :], mix_d[(3 + k) * 128:(4 + k) * 128, bsl], reads=[mix_d], writes=[b])
            P.dma("sp", zb[:, :], projT[FM_SZ + k * 128:FM_SZ + (k + 1) * 128, bsl], reads=[projT], writes=[zb])
            P.op("act", lambda e, zb=zb: e.activation(out=zb[:, :], in_=zb[:, :], func=AF.Silu), reads=[zb], writes=[zb])
            P.op("dve", lambda e, b=b, zb=zb, k=k: e.tensor_tensor(out=gz[:, k, :], in0=b[:, :], in1=zb[:, :], op=ALU.mult), reads=[b, zb], writes=[gz])
        for j in range(T // 512):
            cs = slice(j * 512, (j + 1) * 512)
            ps = pss[j % 2]
            for k in range(3):
                P.op("act", lambda e, k=k, cs=cs: e.activation(out=sq[:, cs], in_=gz[:, k, cs], func=AF.Square), reads=[gz], writes=[sq])
                P.op("pe", lambda e, ps=ps, k=k, cs=cs: e.matmul(ps[:, :], lhsT=ones[:, :], rhs=sq[:, cs], start=(k == 0), stop=(k == 2)),
                     reads=[ones, sq], writes=[ps])
            P.op("act", lambda e, ps=ps, cs=cs: e.activation(out=rs[:, cs], in_=ps[:, :], func=AF.Sqrt, bias=eps6[:, 0:1], scale=1.0 / 384.0),
                 reads=[ps, eps6], writes=[rs])
            P.op("dve", lambda e, cs=cs: e.reciprocal(out=rs[:, cs], in_=rs[:, cs]), reads=[rs], writes=[rs])
            for k in range(3):
                P.op("dve", lambda e, k=k, cs=cs: e.scalar_tensor_tensor(out=mix[:, 3 + k, cs], in0=gz[:, k, cs], scalar=snw[:, k:k + 1], in1=rs[:, cs],
                                                                        op0=ALU.mult, op1=ALU.mult), reads=[gz, snw, rs], writes=[mix])
        for tt in range(T // 128):
            x_ = xt[ti % 2]
            z_ = zt[ti % 2]
            s_ = st[ti % 2]
            m_ = mv[ti % 2]
            xT_ = xTt[ti % 2]
            r0 = b0 + tt * 128
            P.dma("sp", x_[:, :], x_d[r0:r0 + 128, :], reads=[x_d], writes=[x_])
            for hh in range(2):
                p = po[(ti * 2 + hh) % 4]
                for k in range(FC):
                    P.op("pe", lambda e, p=p, k=k, tt=tt, hh=hh: e.matmul(p[:, :], lhsT=mix[:, k, tt * 128:(tt + 1) * 128],
                                                                          rhs=wbf[:, k, hh * 512:(hh + 1) * 512], start=(k == 0), stop=(k == FC - 1)),
                         reads=[mix, wbf], writes=[p])
                P.op("dve", lambda e, p=p, hh=hh, x_=x_, z_=z_: e.scalar_tensor_tensor(out=z_[:, hh * 512:(hh + 1) * 512], in0=x_[:, hh * 512:(hh + 1) * 512],
                                                                                     scalar=ALPHA, in1=p[:, :], op0=ALU.mult, op1=ALU.add),
                     reads=[x_, p], writes=[z_])
                P.op("dve", lambda e, hh=hh, z_=z_, s_=s_: e.bn_stats(out=s_[:, hh, :], in_=z_[:, hh * 512:(hh + 1) * 512]), reads=[z_], writes=[s_])
            P.op("dve", lambda e, s_=s_, m_=m_: e.bn_aggr(out=m_[:, :], in_=s_[:, :, :]), reads=[s_], writes=[m_])
            P.op("act", lambda e, m_=m_: e.activation(out=m_[:, 1:2], in_=m_[:, 1:2], func=AF.Sqrt, bias=eps5[:, 0:1], scale=1.0), reads=[m_, eps5], writes=[m_])
            P.op("dve", lambda e, m_=m_: e.reciprocal(out=m_[:, 1:2], in_=m_[:, 1:2]), reads=[m_], writes=[m_])
            P.op("dve", lambda e, z_=z_, m_=m_: e.tensor_scalar(out=z_[:, :], in0=z_[:, :], scalar1=m_[:, 0:1], scalar2=m_[:, 1:2], op0=ALU.subtract, op1=ALU.mult),
                 reads=[z_, m_], writes=[z_])
            P.op("pool", lambda e, z_=z_: e.tensor_tensor(out=z_[:, :], in0=z_[:, :], in1=lng[:, :], op=ALU.mult), reads=[z_, lng], writes=[z_])
            P.op("pool", lambda e, z_=z_: e.tensor_tensor(out=z_[:, :], in0=z_[:, :], in1=lnb[:, :], op=ALU.add), reads=[z_, lnb], writes=[z_])
            P.dma("pool", xo_d[r0:r0 + 128, :], z_[:, :], reads=[z_], writes=[xo_d])
            if xoT_d is not None:
                for half in range(2):
                    pt_ = ptr[(ti * 2 + half) % 2]
                    for kk in range(4):
                        k = half * 4 + kk
                        P.op("pe", lambda e, pt_=pt_, kk=kk, k=k, z_=z_: e.transpose(out=pt_[:, kk * 128:(kk + 1) * 128], in_=z_[:, k * 128:(k + 1) * 128],
                                                                                  identity=idf[:, :]), reads=[z_, idf], writes=[pt_])
                    P.op("act", lambda e, pt_=pt_, half=half, xT_=xT_: e.copy(out=xT_[:, half * 4:(half + 1) * 4, :],
                                                                             in_=pt_[:, :].rearrange("p (a b) -> p a b", b=128)),
                         reads=[pt_], writes=[xT_])
                P.dma("sp", xoT_d.t.rearrange("(k p) t -> p k t", p=128)[:, :, r0:r0 + 128], xT_[:, :, :], reads=[xT_], writes=[xoT_d])
            ti += 1


DEBUG_OUT = False
N_LAYERS_BUILD = 2
STAGES = "PABCDO"


def build_fused(S=SEQ):
    nc = bass.Bass("TRN2", target_bir_lowering=False)
    P = Prog(nc)
    NT = S // 128
    L = DEPTH
    EI = "ExternalInput"
    xT0 = P.dram("xT0", [D_MODEL, S], F32, EI)
    x0 = P.dram("x0", [S, D_MODEL], F32, EI)
    w_in = P.dram("w_in", [L, D_MODEL, IN_COLS], F32, EI)
    w_out = P.dram("w_out", [L, D_MIX, D_MODEL], F32, EI)
    ab = P.dram("ab", [6, 128, 64], F32, EI)
    bi = P.dram("bi", [32, S], BF16, EI)
    cb = P.dram("cb", [4, 128, 512], BF16, EI)
    idb = P.dram("idb", [128, 128], BF16, EI)
    idf = P.dram("idf", [128, 128], F32, EI)
    dbias = P.dram("dbias", [6, 3, 2, 128, 512], F32, EI)
    tri = P.dram("tri", [128, 128], F32, EI)
    ones = P.dram("ones", [128, 128], F32, EI)
    mneg = P.dram("mneg", [128, 128], BF16, EI)
    mpos = P.dram("mpos", [128, 128], BF16, EI)
    s_cwx = P.dram("s_cwx", [L, 6, 64, 5], F32, EI)
    s_cwb = P.dram("s_cwb", [L, 6, 128, 5], F32, EI)
    s_cwc = P.dram("s_cwc", [L, 6, 128, 5], F32, EI)
    s_sc = P.dram("s_sc", [L, 6, 128, 3], F32, EI)
    d_cwq = P.dram("d_cwq", [L, 6, 64, 5], F32, EI)
    d_cwk = P.dram("d_cwk", [L, 6, 64, 5], F32, EI)
    d_cwv = P.dram("d_cwv", [L, 6, 64, 5], F32, EI)
    d_sc = P.dram("d_sc", [L, 6, 128, 2], F32, EI)
    d_nw = P.dram("d_nw", [L, 128, 64], F32, EI)
    o_snw = P.dram("o_snw", [L, 128, 3], F32, EI)
    o_lng = P.dram("o_lng", [L, 128, D_MODEL], F32, EI)
    o_lnb = P.dram("o_lnb", [L, 128, D_MODEL], F32, EI)
    out = P.dram("out", [S, D_MODEL], F32, "ExternalOutput")
    sk = "ExternalOutput" if DEBUG_OUT else None
    projT = P.dram("projT", [N_FM, S], F32, sk)
    projtm = P.dram("projtm", [S, N_TM], F32, sk)
    mixT = P.dram("mixT", [D_MIX, S], F32, sk)
    x1 = P.dram("x1", [S, D_MODEL], F32, sk)
    x1T = P.dram("x1T", [D_MODEL, S], BF16, None)

    def fm(off):
        return Src(projT, lambda h, off=off: projT[off + h * 64: off + (h + 1) * 64, :])

    def mixrows(off):
        return Src(mixT, lambda h, off=off: mixT[off + h * 64: off + (h + 1) * 64, :])

    for l in range(N_LAYERS_BUILD):
        if "P" in STAGES:
            with P.scope():
                emit_projF(P, xT0 if l == 0 else x1T, l != 0, w_in.t[l], w_in, projT, projtm, S)
        if "A" in STAGES:
            with P.scope():
                emit_moba(P, fm(FM_AQ), fm(FM_AK),
                          Src(projtm, lambda h: projtm.t[:, TM_AV + h * 64: TM_AV + (h + 1) * 64].rearrange("(t p) d -> p t d", p=128)),
                          fm(FM_AG), Src(ab, lambda h: ab.t[h]), bi, cb, idb, idf, mixrows(0), 6, S)
        if "B" in STAGES:
            with P.scope():
                emit_ssd(P, fm(FM_SX),
                         Src(projT, lambda h: projT[FM_SB + (h // 3) * 128: FM_SB + (h // 3 + 1) * 128, :]),
                         Src(projT, lambda h: projT[FM_SC + (h // 3) * 128: FM_SC + (h // 3 + 1) * 128, :]),
                         Src(s_cwx, lambda h, l=l: s_cwx.t[l, h]), Src(s_cwb, lambda h, l=l: s_cwb.t[l, h]), Src(s_cwc, lambda h, l=l: s_cwc.t[l, h]),
                         Src(projtm, lambda h: projtm.t[:, TM_DT:TM_DT + 6].rearrange("(t p) c -> p t c", p=128)),
                         Src(s_sc, lambda h, l=l: s_sc.t[l, h]), tri, ones, idf, idb, mneg, mixrows(384), 6, S)
        if "C" in STAGES:
            with P.scope():
                emit_dil(P, fm(FM_CQ), fm(FM_CK),
                         Src(projtm, lambda hg: projtm.t[:, TM_CV + hg[0] * 64: TM_CV + (hg[0] + 1) * 64].rearrange(
                             "(j p r) c -> p r j c", p=128, r=DIL_D[hg[1]])),
                         fm(FM_CG), Src(dbias, lambda h: dbias.t[h].rearrange("g o p q -> p (g o) q")), mixrows(768), 6, S)
        if "D" in STAGES:
            with P.scope():
                emit_gdn(P, fm(FM_DQ), fm(FM_DK), fm(FM_DV),
                         Src(d_cwq, lambda h, l=l: d_cwq.t[l, h]), Src(d_cwk, lambda h, l=l: d_cwk.t[l, h]), Src(d_cwv, lambda h, l=l: d_cwv.t[l, h]),
                         Src(projtm, lambda h: projtm.t[:, TM_DB:TM_DB + 6].rearrange("(t p) c -> p t c", p=128)),
                         Src(projtm, lambda h: projtm.t[:, TM_DA:TM_DA + 6].rearrange("(t p) c -> p t c", p=128)),
                         Src(d_sc, lambda h, l=l: d_sc.t[l, h]),
                         Src(projtm, lambda h: projtm.t[:, TM_DG + h * 64: TM_DG + (h + 1) * 64].rearrange("(t p) d -> p t d", p=128)),
                         Src(d_nw, lambda h, l=l: d_nw.t[l]), tri, ones, idf, idb, mneg, mpos, mixrows(1152), 6, S)
        if "O" in STAGES:
            with P.scope():
                last = (l == DEPTH - 1)
                emit_outlnF(P, mixT, projT, o_snw.t[l], w_out.t[l], o_lng.t[l], o_lnb.t[l], ones.t, idf.t, w_out,
                            x0 if l == 0 else x1, out if last else x1, None if last else x1T, S)
    return P.finish()


from concourse.bass_utils import run_bass_kernel_spmd

BATCH = 2
_NC = {}


def _host_inputs(p, b):
    perm = in_col_perm()
    CA = moba_consts()
    CR = rec_consts()
    L = DEPTH
    f32 = np.float32
    m = {}
    m["xT0"] = np.ascontiguousarray(p["x"][b].T)
    m["x0"] = np.ascontiguousarray(p["x"][b])
    m["w_in"] = np.ascontiguousarray(p["w_in"][:, :, perm])
    m["w_out"] = np.ascontiguousarray(p["w_out"])
    m["ab"] = CA["ab"]
    m["bi"] = CA["bi"]
    m["cb"] = CA["cb"]
    m["idb"] = CA["idb"]
    m["idf"] = CA["idf"]
    m["dbias"] = dil_consts()
    m["tri"] = CR["tri"]
    m["ones"] = CR["ones"]
    m["mneg"] = CR["mneg"]
    m["mpos"] = CR["mpos"]
    cw5 = [np.concatenate([p["ssm_conv_w"][l].T, p["ssm_conv_b"][l][:, None]], axis=1).astype(f32) for l in range(L)]
    m["s_cwx"] = np.ascontiguousarray(np.stack([np.stack([cw5[l][h * 64:(h + 1) * 64] for h in range(6)]) for l in range(L)]))
    m["s_cwb"] = np.ascontiguousarray(np.stack([np.stack([cw5[l][384 + (h // 3) * 128: 384 + (h // 3 + 1) * 128] for h in range(6)]) for l in range(L)]))
    m["s_cwc"] = np.ascontiguousarray(np.stack([np.stack([cw5[l][640 + (h // 3) * 128: 640 + (h // 3 + 1) * 128] for h in range(6)]) for l in range(L)]))
    m["s_sc"] = np.ascontiguousarray(np.stack([np.stack([np.tile(np.stack([p["ssm_dt_bias"][l, h], p["ssm_A_log"][l, h], p["ssm_D"][l, h]])[None].astype(f32),
                                                                (128, 1)) for h in range(6)]) for l in range(L)]))
    cd5 = [np.concatenate([p["dn_conv_w"][l].T, p["dn_conv_b"][l][:, None]], axis=1).astype(f32) for l in range(L)]
    for nm, off in (("q", 0), ("k", 384), ("v", 768)):
        m["d_cw" + nm] = np.ascontiguousarray(np.stack([np.stack([cd5[l][off + h * 64: off + (h + 1) * 64] for h in range(6)]) for l in range(L)]))
    m["d_sc"] = np.ascontiguousarray(np.stack([np.stack([np.tile(np.stack([p["dn_dt_bias"][l, h], p["dn_A_log"][l, h]])[None].astype(f32), (128, 1))
                                                        for h in range(6)]) for l in range(L)]))
    m["d_nw"] = np.ascontiguousarray(np.stack([np.tile(p["dn_norm_w"][l][None].astype(f32), (128, 1)) for l in range(L)]))
    m["o_snw"] = np.ascontiguousarray(np.stack([p["ssm_norm_w"][l].reshape(3, 128).T.astype(f32) for l in range(L)]))
    m["o_lng"] = np.ascontiguousarray(np.stack([np.tile(p["ln_g"][l][None].astype(f32), (128, 1)) for l in range(L)]))
    m["o_lnb"] = np.ascontiguousarray(np.stack([np.tile(p["ln_b"][l][None].astype(f32), (128, 1)) for l in range(L)]))
    return m


def kernel(**inputs):
    p = {k: np.asarray(v, dtype=np.float32) for k, v in inputs.items()}
    if "nc" not in _NC:
        _NC["nc"] = build_fused()
    in_maps = [_host_inputs(p, b) for b in range(BATCH)]
    res = run_bass_kernel_spmd(_NC["nc"], in_maps, core_ids=list(range(BATCH)))
    _NC["res"] = res
    return np.stack([res.results[b]["out"] for b in range(BATCH)]).astype(np.float32)
```

```python
import contextlib
import numpy as np
import concourse.bass as bass
import concourse.mybir as mybir

F32 = mybir.dt.float32
BF16 = mybir.dt.bfloat16
I32 = mybir.dt.int32
U8 = mybir.dt.uint8
AF = mybir.ActivationFunctionType
ALU = mybir.AluOpType
AX = mybir.AxisListType


class Buf:
    __slots__ = ("name", "t", "writer", "readers", "excl")

    def __init__(self, name, t, excl=False):
        self.name = name
        self.t = t
        self.excl = excl
        self.writer = None
        self.readers = []

    def __getitem__(self, idx):
        return self.t[idx]


class Src:
    def __init__(self, buf, fn):
        self.buf = buf
        self.fn = fn

    def __getitem__(self, k):
        return self.fn(k)


class Prog:
    ENG = ("pe", "dve", "act", "pool", "sp")
    NDMA = 6

    def __init__(self, nc, same_engine_sync=True):
        self.nc = nc
        self.stack = contextlib.ExitStack()
        self.ops = {e: [] for e in self.ENG}
        self.cnt = {e: 0 for e in self.ENG}
        self.sems = {}
        for e in self.ENG:
            self.sems[e] = self.stack.enter_context(nc.semaphore("s_" + e))
        self.dq = {}
        for q in ("sp", "pool", "act"):
            self.dq[q] = {"n": 0, "sems": []}
            for i in range(self.NDMA):
                s = self.stack.enter_context(nc.semaphore(f"d_{q}{i}"))
                self.sems[f"d_{q}{i}"] = s
                self.dq[q]["sems"].append(f"d_{q}{i}")
        self.waited = {e: {} for e in self.ENG}
        self.same = same_engine_sync
        self.nbuf = 0
        self.ARENA_BYTES = 206 * 1024
        self.arena = self.stack.enter_context(nc.sbuf_tensor("arena", [128, self.ARENA_BYTES // 4], F32))
        self.arena_off = 0
        self.banks = [self.stack.enter_context(nc.psum_tensor(f"bank{i}", [128, 512], F32)) for i in range(8)]
        self.banks_used = 0

    @staticmethod
    def _view(base, shape, dt, off_bytes):
        esz = mybir.dt.size(dt)
        n = 1
        for d in shape[1:]:
            n *= d
        t = base if dt == F32 else base.bitcast(dt)
        o = off_bytes // esz
        ap = t[0:shape[0], o:o + n]
        if len(shape) == 3:
            ap = ap.rearrange("p (a b) -> p a b", b=shape[2])
        elif len(shape) == 4:
            ap = ap.rearrange("p (a b c) -> p a b c", b=shape[2], c=shape[3])
        return ap, n * esz

    def sbuf(self, name, shape, dt):
        ap, nb = self._view(self.arena, list(shape), dt, self.arena_off)
        self.arena_off += (nb + 63) // 64 * 64
        assert self.arena_off <= self.ARENA_BYTES, f"SBUF arena overflow at {name}: {self.arena_off}"
        return Buf(name, ap)

    def psum(self, name, shape, dt=F32):
        assert self.banks_used < 8, "out of PSUM banks at " + name
        ap, nb = self._view(self.banks[self.banks_used], list(shape), dt, 0)
        assert nb <= 2048
        self.banks_used += 1
        return Buf(name, ap, excl=True)

    @contextlib.contextmanager
    def scope(self):
        sv = (self.arena_off, self.banks_used)
        try:
            yield
        finally:
            self.barrier()
            self.arena_off, self.banks_used = sv

    def _all_deps(self):
        deps = []
        for q, dq in self.dq.items():
            n = dq["n"]
            for i in range(min(n, self.NDMA)):
                cnt_i = (n - 1 - i) // self.NDMA + 1
                deps.append((dq["sems"][i], 16 * cnt_i))
        for e in self.ENG:
            if self.cnt[e] > 0:
                deps.append((e, self.cnt[e]))
        return deps

    def barrier(self):
        deps = self._all_deps()
        for e in self.ENG:
            self._need(e, [d for d in deps if d[0] != e])

    def dram(self, name, shape, dt, kind=None):
        if kind is None:
            t = self.nc.dram_tensor(name, list(shape), dt)
        else:
            t = self.nc.dram_tensor(name, list(shape), dt, kind=kind)
        return Buf(name, t.ap())

    def alias(self, name, t):
        return Buf(name, t)

    def _need(self, eng, deps):
        w = self.waited[eng]
        for (k, v) in deps:
            if k == eng and (eng == "pe" or not self.same):
                continue
            if w.get(k, 0) >= v:
                continue
            w[k] = v
            sem = self.sems[k]
            self.ops[eng].append(lambda e, sem=sem, v=v: e.wait_ge(sem, v))

    def _deps(self, reads, writes, eng=None):
        deps = []
        for b in reads:
            if b.writer is not None:
                deps.append(b.writer)
            if b.excl:
                deps.extend(r for r in b.readers if r[0] != eng)
        for b in writes:
            if b.writer is not None:
                deps.append(b.writer)
            deps.extend(b.readers)
        return deps

    def _mark(self, reads, writes, tag):
        for b in reads:
            b.readers.append(tag)
            if len(b.readers) > 64:
                m = {}
                for (k, v) in b.readers:
                    if m.get(k, 0) < v:
                        m[k] = v
                b.readers = list(m.items())
        for b in writes:
            b.writer = tag
            b.readers = []

    def op(self, eng, fn, reads=(), writes=()):
        self._need(eng, self._deps(reads, writes, eng))
        self.cnt[eng] += 1
        v = self.cnt[eng]
        sem = self.sems[eng]
        self.ops[eng].append(lambda e, fn=fn, sem=sem: fn(e).then_inc(sem, 1))
        self._mark(reads, writes, (eng, v))

    def dma(self, q, out_ap, in_ap, reads=(), writes=(), **kw):
        dq = self.dq[q]
        n = dq["n"]
        dq["n"] += 1
        key = dq["sems"][n % self.NDMA]
        val = 16 * (n // self.NDMA + 1)
        deps = self._deps(reads, writes)
        if val > 16:
            deps.append((key, val - 16))
        self._need(q, deps)
        sem = self.sems[key]
        self.ops[q].append(lambda e, sem=sem, o=out_ap, i=in_ap, kw=kw: e.dma_start(out=o, in_=i, **kw).then_inc(sem, 16))
        self._mark(reads, writes, (key, val))

    def coll(self, kind, alu, groups, in_b, out_b):
        q = "pool"
        if "cc" not in self.sems:
            self.sems["cc"] = self.stack.enter_context(self.nc.semaphore("s_cc"))
            self.ncc = 0
        self.ncc += 1
        val = self.ncc
        deps = self._deps([in_b], [out_b])
        if val > 1:
            deps.append(("cc", val - 1))
        self._need(q, deps)
        sem = self.sems["cc"]
        self.ops[q].append(lambda e, sem=sem: e.collective_compute(kind, alu, replica_groups=groups, ins=[in_b.t.opt()],
                                                                   outs=[out_b.t.opt()]).then_inc(sem, 1))
        self._mark([in_b], [out_b], ("cc", val))

    def finish(self, final_bufs=()):
        deps = []
        for q, dq in self.dq.items():
            n = dq["n"]
            for i in range(min(n, self.NDMA)):
                cnt_i = (n - 1 - i) // self.NDMA + 1
                deps.append((dq["sems"][i], 16 * cnt_i))
        for e in self.ENG:
            if e != "sp" and self.cnt[e] > 0:
                deps.append((e, self.cnt[e]))
        self._need("sp", deps)
        nc = self.nc
        with nc.Block() as block:
            @block.tensor
            def _(e):
                for f in self.ops["pe"]:
                    f(e)

            @block.vector
            def _(e):
                for f in self.ops["dve"]:
                    f(e)

            @block.scalar
            def _(e):
                for f in self.ops["act"]:
                    f(e)

            @block.gpsimd
            def _(e):
                for f in self.ops["pool"]:
                    f(e)

            @block.sync
            def _(e):
                for f in self.ops["sp"]:
                    f(e)
        self.stack.close()
        return nc


D_MODEL = 1024
IN_COLS = 5906
DEPTH = 2


SEQ = 8192
NEGM = 30000.0


def emit_moba(P, qT_d, kT_d, v_d, gT_d, ab_d, bi_d, cb_d, idb_d, idf_d, y_d, NH, S=SEQ, pfx="mb"):
    NT = S // 128
    NQ = S // 512
    NB = S // 256
    qf = P.sbuf(pfx + "_qf", [64, S], F32)
    kf = P.sbuf(pfx + "_kf", [64, S], F32)
    qa = P.sbuf(pfx + "_qa", [96, S], BF16)
    ka = P.sbuf(pfx + "_ka", [96, S], BF16)
    vf = P.sbuf(pfx + "_vf", [128, NT, 64], F32)
    va = P.sbuf(pfx + "_va", [128, NT, 128], BF16)
    cb = P.sbuf(pfx + "_cb", [128, 4, 512], BF16)
    idb = P.sbuf(pfx + "_idb", [128, 128], BF16)
    idf = P.sbuf(pfx + "_idf", [128, 128], F32)
    ab = P.sbuf(pfx + "_ab", [128, 64], F32)
    km = P.sbuf(pfx + "_km", [64, NB], F32)
    KS = 3
    gsb = [P.sbuf(pfx + f"_gsb{i}", [128, 32], F32) for i in range(KS)]
    mx8 = [P.sbuf(pfx + f"_mx8{i}", [128, 8], F32) for i in range(KS)]
    mb = [P.sbuf(pfx + f"_mbs{i}", [128, 32], F32) for i in range(KS)]
    KM = 4
    pts = [P.sbuf(pfx + f"_pt{i}", [128, 512], BF16) for i in range(KM + 1)]
    gch = [P.sbuf(pfx + f"_gch{i}", [64, 512], F32) for i in range(2)]
    rden = [P.sbuf(pfx + f"_rden{i}", [64, 512], F32) for i in range(2)]
    yo = [P.sbuf(pfx + f"_yo{i}", [64, 512], F32) for i in range(2)]
    acc = [P.psum(pfx + f"_acc{i}", [128, 512]) for i in range(2)]
    sps = [P.psum(pfx + f"_sps{i}", [128, 512]) for i in range(KM + 1)]

    P.dma("sp", cb[:, :, :], cb_d.t.rearrange("k p q -> p k q"), reads=[cb_d], writes=[cb])
    P.dma("sp", idb[:, :], idb_d[:, :], reads=[idb_d], writes=[idb])
    P.dma("sp", idf[:, :], idf_d[:, :], reads=[idf_d], writes=[idf])
    P.dma("sp", ka[64:96, :], bi_d[:, :], reads=[bi_d], writes=[ka])
    P.op("pool", lambda e: e.memset(va[:, :, 64:128], 1.0), writes=[va])
    si = 0
    for h in range(NH):
        P.dma("sp", qf[:, :], qT_d[h], reads=[qT_d.buf], writes=[qf])
        P.dma("act", kf[:, :], kT_d[h], reads=[kT_d.buf], writes=[kf])
        P.dma("pool", vf[:, :, :], v_d[h], reads=[v_d.buf], writes=[vf])
        P.dma("sp", ab[:, :], ab_d[h], reads=[ab_d.buf], writes=[ab])
        P.op("act", lambda e: e.mul(qa[0:64, :], qf[:, :], 0.125), reads=[qf], writes=[qa])
        P.op("pool", lambda e: e.tensor_copy(out=ka[0:64, :], in_=kf[:, :]), reads=[kf], writes=[ka])
        P.op("pool", lambda e: e.tensor_copy(out=va[:, :, 0:64], in_=vf[:, :, :]), reads=[vf], writes=[va])
        P.op("dve", lambda e: e.tensor_reduce(out=km[:, :], in_=kf.t[:, :].rearrange("p (n l) -> p n l", l=256),
                                              axis=AX.X, op=ALU.add), reads=[kf], writes=[km])
        P.op("dve", lambda e: e.tensor_scalar(out=km[:, :], in0=km[:, :], scalar1=1.0 / 256.0, scalar2=None, op0=ALU.mult),
             reads=[km], writes=[km])
        def sel_gen(t):
            qb = t // 2
            sl = t % KS
            bk = sps[sl]
            gp = bk[:, 0:32]
            tp = bk[0:32, 128:256]
            g_, x_, m_ = gsb[sl], mx8[sl], mb[sl]
            P.op("pe", lambda e: e.matmul(gp, lhsT=qf[:, t * 128:(t + 1) * 128], rhs=km[:, :], start=True, stop=True, skip_group_check=True),
                 reads=[qf, km], writes=[bk])
            P.op("pool", lambda e: e.memset(g_[:, :], -1e30), writes=[g_])
            yield
            if qb > 0:
                P.op("dve", lambda e: e.tensor_copy(out=g_[:, 0:qb], in_=bk[:, 0:qb]), reads=[bk], writes=[g_])
            P.op("dve", lambda e: e.max(out=x_[:, :], in_=g_[:, :]), reads=[g_], writes=[x_])
            yield
            P.op("dve", lambda e: e.tensor_scalar(out=m_[:, :], in0=g_[:, :], scalar1=x_[:, 2:3], scalar2=NEGM,
                                                  op0=ALU.is_ge, op1=ALU.mult), reads=[g_, x_], writes=[m_])
            P.op("dve", lambda e: e.memset(m_[:, qb:qb + 1], NEGM), writes=[m_])
            if qb + 1 < 32:
                P.op("dve", lambda e: e.memset(m_[:, qb + 1:32], 0.0), writes=[m_])
            yield
            P.op("dve", lambda e: e.tensor_scalar(out=m_[:, :], in0=m_[:, :], scalar1=-NEGM, scalar2=None, op0=ALU.add),
                 reads=[m_], writes=[m_])
            yield
            P.op("pe", lambda e: e.transpose(out=tp, in_=m_[:, :], identity=idf[:, :]), reads=[m_, idf], writes=[bk])
            yield
            P.op("act", lambda e: e.copy(out=qa[64:96, t * 128:(t + 1) * 128], in_=tp), reads=[bk], writes=[qa])

        run_pipeline([sel_gen(t) for t in range(NT)], KS)
        def att_gen(qt, kt, idx):
            ac = acc[qt % 2]
            nk = 4 * qt + 4
            sp_ = sps[idx % (KM + 1)]
            pt = pts[idx % (KM + 1)]
            diag = kt >= 4 * qt
            P.op("pe", lambda e: e.matmul(sp_[:, :], lhsT=ka[:, kt * 128:(kt + 1) * 128], rhs=qa[:, qt * 512:(qt + 1) * 512],
                                          start=True, stop=not diag), reads=[ka, qa], writes=[sp_])
            if diag:
                P.op("pe", lambda e: e.matmul(sp_[:, :], lhsT=idb[:, :], rhs=cb[:, kt - 4 * qt, :], start=False, stop=True), reads=[idb, cb], writes=[sp_])
            yield
            ri = kt - 4 * qt + 60
            P.op("act", lambda e: e.activation(out=pt[:, :], in_=sp_[:, :], func=AF.Exp, bias=ab[:, ri:ri + 1], scale=1.0),
                 reads=[sp_, ab], writes=[pt])
            yield
            P.op("pe", lambda e: e.matmul(ac[:, :], lhsT=va[:, kt, :], rhs=pt[:, :], start=(kt == 0), stop=(kt == nk - 1)),
                 reads=[va, pt], writes=[ac])
            if kt == nk - 1:
                g = gch[qt % 2]
                rd = rden[qt % 2]
                y = yo[qt % 2]
                P.dma("sp", g[:, :], gT_d[h][:, qt * 512:(qt + 1) * 512], reads=[gT_d.buf], writes=[g])
                P.op("act", lambda e: e.activation(out=g[:, :], in_=g[:, :], func=AF.Silu), reads=[g], writes=[g])
                yield
                P.op("dve", lambda e: e.reciprocal(out=rd[:, :], in_=ac[64:128, :]), reads=[ac], writes=[rd])
                P.op("dve", lambda e: e.tensor_tensor(out=y[:, :], in0=ac[0:64, :], in1=rd[:, :], op=ALU.mult),
                     reads=[ac, rd], writes=[y])
                yield
                P.op("pool", lambda e: e.tensor_tensor(out=y[:, :], in0=y[:, :], in1=g[:, :], op=ALU.mult),
                     reads=[y, g], writes=[y])
                P.dma("pool", y_d[h][:, qt * 512:(qt + 1) * 512], y[:, :], reads=[y], writes=[y_d.buf])

        items = [(qt, kt) for qt in range(NQ) for kt in range(4 * qt + 4)]
        run_pipeline([att_gen(qt, kt, i) for i, (qt, kt) in enumerate(items)], KM)


def moba_consts(S=SEQ):
    import ml_dtypes
    bf = ml_dtypes.bfloat16
    bi = np.zeros((32, S), np.float32)
    for j in range(S // 256):
        bi[j, j * 256:(j + 1) * 256] = 1.0
    cbm = np.zeros((4, 128, 512), np.float32)
    for r in range(4):
        for p in range(128):
            tk = r * 128 + p
            for blk in range(2):
                if tk // 256 == blk:
                    q = np.arange(blk * 256, (blk + 1) * 256)
                    cbm[r, p, q] = np.where(tk > q, -NEGM, 0.0)
    n = 12
    s = 2.0 ** (-8.0 * (np.arange(n) + 1) / n)
    slopes_moba = s[6:]
    ab = np.zeros((6, 128, 64), np.float32)
    for h in range(6):
        for ri in range(64):
            rel = ri - 60
            ab[h, :, ri] = slopes_moba[h] * (128.0 * rel + np.arange(128))
    return dict(bi=bi.astype(bf), cb=cbm.astype(bf), idb=np.eye(128, dtype=np.float32).astype(bf),
                idf=np.eye(128, dtype=np.float32), ab=ab)


DIL_D = (1, 4, 16)
DIL_GROUPS = (0, 1, 2)
DBG = None


def emit_dil(P, qT_d, kT_d, vp_d, gT_d, bias_d, y_d, NH, S=SEQ, pfx="dl"):
    NT = S // 128
    qf = P.sbuf(pfx + "_qf", [64, S], F32)
    kf = qf
    qa = P.sbuf(pfx + "_qa", [64, S], BF16)
    ka = P.sbuf(pfx + "_ka", [64, S], BF16)
    vf = P.sbuf(pfx + "_vf", [128, NT, 64], F32)
    va = [P.sbuf(pfx + f"_va{g}", [128, NT, 128], BF16) for g in range(3)]
    bs = P.sbuf(pfx + "_bias", [128, 6, 512], F32)
    num = P.sbuf(pfx + "_num", [128, S], F32)
    tmp = [P.sbuf(pfx + f"_tmp{i}", [128, 512], F32) for i in range(2)]
    pts = [P.sbuf(pfx + f"_pt{i}", [128, 512], BF16) for i in range(4)]
    gch = [P.sbuf(pfx + f"_gch{i}", [64, 512], F32) for i in range(2)]
    rden = [P.sbuf(pfx + f"_rden{i}", [64, 512], F32) for i in range(2)]
    yo = [P.sbuf(pfx + f"_yo{i}", [64, 512], F32) for i in range(2)]
    sps = [P.psum(pfx + f"_sps{i}", [128, 512]) for i in range(3)]
    ops_ = [P.psum(pfx + f"_ops{i}", [128, 512]) for i in range(2)]
    for g in range(3):
        P.op("pool", lambda e, g=g: e.memset(va[g][:, :, 64:128], 1.0), writes=[va[g]])
    si = 0
    oi = 0
    for h in range(NH):
        P.dma("sp", qf[:, :], qT_d[h], reads=[qT_d.buf], writes=[qf])
        P.op("act", lambda e: e.mul(qa[:, :], qf[:, :], 0.125), reads=[qf], writes=[qa])
        P.dma("sp", kf[:, :], kT_d[h], reads=[kT_d.buf], writes=[kf])
        P.op("pool", lambda e: e.tensor_copy(out=ka[:, :], in_=kf[:, :]), reads=[kf], writes=[ka])
        P.dma("sp", bs[:, :, :], bias_d[h], reads=[bias_d.buf], writes=[bs])
        for g in range(3):
            P.dma("pool", vf.t.rearrange("p (r j) c -> p r j c", r=DIL_D[g]), vp_d[(h, g)], reads=[vp_d.buf], writes=[vf])
            P.op("pool", lambda e, g=g: e.tensor_copy(out=va[g][:, :, 0:64], in_=vf[:, :, :]), reads=[vf], writes=[va[g]])
        for g, d in enumerate(DIL_D):
            if g not in DIL_GROUPS:
                continue
            NTd = NT // d
            qv = qa.t[:, :].rearrange("p (u d) -> p u d", d=d)
            kv = ka.t[:, :].rearrange("p (u d) -> p u d", d=d)
            nv = num.t[:, :].rearrange("p (u d) -> p u d", d=d)
            for r in range(d):
                for jb in range(NTd // 4):
                    op_ = ops_[oi % 2]
                    oi += 1
                    ptl = {}
                    for o in (1, 0):
                        i0 = 1 if (o == 1 and jb == 0) else 0
                        sp_ = sps[si % 3]
                        tm = tmp[si % 2]
                        pt = pts[si % 4]
                        ptl[o] = pt
                        si += 1
                        for i in range(i0, 4):
                            jq = jb * 4 + i
                            jk = jq - o
                            P.op("pe", lambda e, sp_=sp_, i=i, jq=jq, jk=jk, r=r, kv=kv, qv=qv: e.matmul(
                                sp_[:, i * 128:(i + 1) * 128], lhsT=kv[:, jk * 128:(jk + 1) * 128, r],
                                rhs=qv[:, jq * 128:(jq + 1) * 128, r], start=True, stop=True,
                                skip_group_check=True), reads=[ka, qa], writes=[sp_])
                        c0 = i0 * 128
                        P.op("dve", lambda e, sp_=sp_, tm=tm, c0=c0, go=g * 2 + o: e.tensor_tensor(
                            out=tm[:, c0:512], in0=sp_[:, c0:512], in1=bs[:, go, c0:512], op=ALU.add),
                            reads=[sp_, bs], writes=[tm])
                        if DBG is not None and h == 0 and jb == 0 and r == 0 and o == 0 and g == DIL_GROUPS[0]:
                            P.dma("sp", DBG[:, :], tm[:, :], reads=[tm], writes=[DBG])
                        P.op("act", lambda e, tm=tm, pt=pt, c0=c0: e.activation(out=pt[:, c0:512], in_=tm[:, c0:512], func=AF.Exp),
                             reads=[tm], writes=[pt])
                    for i in range(4):
                        os_ = (0,) if (jb == 0 and i == 0) else (1, 0)
                        for o in os_:
                            jk = jb * 4 + i - o
                            P.op("pe", lambda e, op_=op_, pt=ptl[o], i=i, tl=r * NTd + jk, g=g, fp=(o == os_[0]), last=(o == 0): e.matmul(
                                op_[:, i * 128:(i + 1) * 128], lhsT=va[g][:, tl, :], rhs=pt[:, i * 128:(i + 1) * 128],
                                start=fp, stop=last, skip_group_check=True), reads=[va[g], pt], writes=[op_])
                    u0 = jb * 512
                    if g == DIL_GROUPS[0]:
                        P.op("dve", lambda e, op_=op_, u0=u0, r=r, nv=nv: e.tensor_copy(out=nv[:, u0:u0 + 512, r], in_=op_[:, :]),
                             reads=[op_], writes=[num])
                    else:
                        P.op("dve", lambda e, op_=op_, u0=u0, r=r, nv=nv: e.tensor_tensor(
                            out=nv[:, u0:u0 + 512, r], in0=op_[:, :], in1=nv[:, u0:u0 + 512, r], op=ALU.add),
                            reads=[op_, num], writes=[num])
        for qt in range(S // 512):
            g_ = gch[qt % 2]
            rd = rden[qt % 2]
            y = yo[qt % 2]
            cs = slice(qt * 512, (qt + 1) * 512)
            P.dma("sp", g_[:, :], gT_d[h][:, cs], reads=[gT_d.buf], writes=[g_])
            P.op("act", lambda e, g_=g_: e.activation(out=g_[:, :], in_=g_[:, :], func=AF.Silu), reads=[g_], writes=[g_])
            P.op("dve", lambda e, rd=rd, cs=cs: e.reciprocal(out=rd[:, :], in_=num[64:128, cs]), reads=[num], writes=[rd])
            P.op("dve", lambda e, y=y, rd=rd, cs=cs: e.tensor_tensor(out=y[:, :], in0=num[0:64, cs], in1=rd[:, :], op=ALU.mult),
                 reads=[num, rd], writes=[y])
            P.op("pool", lambda e, y=y, g_=g_: e.tensor_tensor(out=y[:, :], in0=y[:, :], in1=g_[:, :], op=ALU.mult),
                 reads=[y, g_], writes=[y])
            P.dma("pool", y_d[h][:, cs], y[:, :], reads=[y], writes=[y_d.buf])


def dil_consts():
    n = 12
    s = 2.0 ** (-8.0 * (np.arange(n) + 1) / n)
    slopes = s[:6]
    bias = np.zeros((6, 3, 2, 128, 512), np.float32)
    p = np.arange(128)[:, None]
    x = np.arange(128)[None, :]
    for h in range(6):
        for g, d in enumerate(DIL_D):
            for o in range(2):
                nn = (x - p) + 128 * o
                b = np.where((nn >= 0) & (nn <= 128), -slopes[h] * d * nn, -NEGM).astype(np.float32)
                bias[h, g, o] = np.tile(b, (1, 4))
    return bias


def dil_perm(S=SEQ):
    out = []
    for d in DIL_D:
        u = np.arange(S // d)
        out.append(np.concatenate([u * d + r for r in range(d)]))
    return out


def emit_conv_silu(P, src_d, w_sb, C, S, zp, acc, outs, q="sp", src_buf=None):
    P.dma(q, zp[0:C, 3:S + 3], src_d, reads=([src_buf] if src_buf is not None else []), writes=[zp])
    H = S // 2
    for hh in range(2):
        a0, a1 = hh * H, (hh + 1) * H
        P.op("dve", lambda e, a0=a0, a1=a1: e.tensor_scalar(out=acc[0:C, a0:a1], in0=zp[0:C, 3 + a0:3 + a1], scalar1=w_sb[0:C, 3:4],
                                                          scalar2=w_sb[0:C, 4:5], op0=ALU.mult, op1=ALU.add), reads=[zp, w_sb], writes=[acc])
        for k in range(3):
            P.op("dve", lambda e, k=k, a0=a0, a1=a1: e.scalar_tensor_tensor(out=acc[0:C, a0:a1], in0=zp[0:C, k + a0:k + a1], scalar=w_sb[0:C, k:k + 1],
                                                                          in1=acc[0:C, a0:a1], op0=ALU.mult, op1=ALU.add),
                 reads=[zp, w_sb, acc], writes=[acc])
    for (ob, oap) in outs:
        P.op("act", lambda e, oap=oap: e.activation(out=oap, in_=acc[0:C, :], func=AF.Silu), reads=[acc], writes=[ob])


def emit_softplus(P, eng_dve, out_b, out_ap, x_b, x_ap, t1_b, t1_ap, shape_p):
    P.op("act", lambda e: e.activation(out=t1_ap, in_=x_ap, func=AF.Abs), reads=[x_b], writes=[t1_b])
    P.op("act", lambda e: e.activation(out=t1_ap, in_=t1_ap, func=AF.Exp, scale=-1.0), reads=[t1_b], writes=[t1_b])
    P.op("act", lambda e: e.activation(out=t1_ap, in_=t1_ap, func=AF.Ln, bias=1.0, scale=1.0), reads=[t1_b], writes=[t1_b])
    P.op("dve", lambda e: e.scalar_tensor_tensor(out=out_ap, in0=x_ap, scalar=0.0, in1=t1_ap, op0=ALU.max, op1=ALU.add),
         reads=[x_b, t1_b], writes=[out_b])


SSD_NCH = 10 ** 9
SSD_STOP = 99


def emit_ssd(P, xpre_d, bpre_d, cpre_d, cwx_d, cwb_d, cwc_d, dtc_d, sc_d, tri_d, ones_d, idf_d, idb_d, mneg_d, y_d, NH, S=SEQ, pfx="sd"):
    NT = S // 128
    zp = P.sbuf(pfx + "_zp", [128, S + 3], F32)
    acc = P.sbuf(pfx + "_acc", [128, S], F32)
    xsT = P.sbuf(pfx + "_xsT", [64, S], F32)
    BT = P.sbuf(pfx + "_BT", [128, S], BF16)
    CT = P.sbuf(pfx + "_CT", [128, S], BF16)
    yac = P.sbuf(pfx + "_yac", [64, S], F32)
    cwx = P.sbuf(pfx + "_cwx", [64, 5], F32)
    cwb = P.sbuf(pfx + "_cwb", [128, 5], F32)
    cwc = P.sbuf(pfx + "_cwc", [128, 5], F32)
    sc = P.sbuf(pfx + "_sc", [128, 3], F32)
    tri = P.sbuf(pfx + "_tri", [128, 128], F32)
    ones = P.sbuf(pfx + "_ones", [128, 128], F32)
    idf = P.sbuf(pfx + "_idf", [128, 128], F32)
    idb = P.sbuf(pfx + "_idb", [128, 128], BF16)
    mneg = P.sbuf(pfx + "_mneg", [128, 128], BF16)
    dtr = P.sbuf(pfx + "_dtr", [128, NT], F32)
    dtall = P.sbuf(pfx + "_dtall", [128, NT, 6], F32)
    t1 = P.sbuf(pfx + "_t1", [128, NT], F32)
    dt = P.sbuf(pfx + "_dt", [128, NT], F32)
    a_ = P.sbuf(pfx + "_a", [128, NT], F32)
    acum = P.sbuf(pfx + "_acum", [128, NT], F32)
    nacum = P.sbuf(pfx + "_nacum", [128, NT], F32)
    dB = P.sbuf(pfx + "_dB", [128, NT], F32)
    dlast = P.sbuf(pfx + "_dlast", [128, NT], F32)
    Aneg = P.sbuf(pfx + "_Aneg", [128, 1], F32)
    dg = [P.sbuf(pfx + f"_dg{i}", [128, 128], F32) for i in range(2)]
    EB = [P.sbuf(pfx + f"_EB{i}", [128, 128], F32) for i in range(2)]
    LmT = [P.sbuf(pfx + f"_LmT{i}", [128, 128], F32) for i in range(2)]
    SLT = [P.sbuf(pfx + f"_SLT{i}", [128, 128], BF16) for i in range(2)]
    CgT = [P.sbuf(pfx + f"_CgT{i}", [128, 128], BF16) for i in range(2)]
    X = [P.sbuf(pfx + f"_X{i}", [128, 64], BF16) for i in range(2)]
    Bd = [P.sbuf(pfx + f"_Bd{i}", [128, 128], BF16) for i in range(2)]
    hf = P.sbuf(pfx + "_hf", [128, 64], F32)
    hb = [P.sbuf(pfx + f"_hb{i}", [128, 64], BF16) for i in range(2)]
    p_gb = [P.psum(pfx + f"_pgb{i}", [128, 128]) for i in range(2)]
    p_sc = [P.psum(pfx + f"_psc{i}", [128, 128]) for i in range(1)]
    p_tx = [P.psum(pfx + f"_ptx{i}", [128, 64]) for i in range(1)]
    p_tb = [P.psum(pfx + f"_ptb{i}", [128, 128], BF16) for i in range(1)]
    p_y = [P.psum(pfx + f"_py{i}", [64, 128]) for i in range(2)]
    p_h = [P.psum(pfx + f"_ph{i}", [128, 64]) for i in range(1)]
    p_misc = p_gb[0]

    for (sb, d_) in ((tri, tri_d), (ones, ones_d), (idf, idf_d), (idb, idb_d), (mneg, mneg_d)):
        P.dma("sp", sb[:, :], d_[:, :], reads=[d_], writes=[sb])
    P.op("pool", lambda e: e.memset(zp[:, 0:3], 0.0), writes=[zp])
    for h in range(NH):
        P.dma("sp", cwx[:, :], cwx_d[h], reads=[cwx_d.buf], writes=[cwx])
        P.dma("sp", cwb[:, :], cwb_d[h], reads=[cwb_d.buf], writes=[cwb])
        P.dma("sp", cwc[:, :], cwc_d[h], reads=[cwc_d.buf], writes=[cwc])
        P.dma("sp", sc[:, :], sc_d[h], reads=[sc_d.buf], writes=[sc])
        if h == 0:
            P.dma("act", dtall[:, :, :], dtc_d[0], reads=[dtc_d.buf], writes=[dtall])
        emit_conv_silu(P, xpre_d[h], cwx, 64, S, zp, acc, [(xsT, xsT[:, :])], src_buf=xpre_d.buf)
        emit_conv_silu(P, bpre_d[h], cwb, 128, S, zp, acc, [(BT, BT[:, :])], src_buf=bpre_d.buf)
        emit_conv_silu(P, cpre_d[h], cwc, 128, S, zp, acc, [(CT, CT[:, :])], src_buf=cpre_d.buf)
        P.op("dve", lambda e, h=h: e.tensor_scalar(out=dtr[:, :], in0=dtall[:, :, h], scalar1=sc[:, 0:1], scalar2=None, op0=ALU.add), reads=[dtall, sc], writes=[dtr])
        emit_softplus(P, "dve", dt, dt[:, :], dtr, dtr[:, :], t1, t1[:, :], 128)
        P.op("act", lambda e: e.activation(out=Aneg[:, :], in_=sc[:, 1:2], func=AF.Exp), reads=[sc], writes=[Aneg])
        P.op("dve", lambda e: e.tensor_scalar(out=Aneg[:, :], in0=Aneg[:, :], scalar1=-1.0, scalar2=None, op0=ALU.mult), reads=[Aneg], writes=[Aneg])
        P.op("dve", lambda e: e.tensor_scalar(out=a_[:, :], in0=dt[:, :], scalar1=Aneg[:, 0:1], scalar2=None, op0=ALU.mult), reads=[dt, Aneg], writes=[a_])
        P.op("pe", lambda e: e.matmul(p_misc[:, 0:NT], lhsT=tri[:, :], rhs=a_[:, :], start=True, stop=True), reads=[tri, a_], writes=[p_misc])
        P.op("dve", lambda e: e.tensor_copy(out=acum[:, :], in_=p_misc[:, 0:NT]), reads=[p_misc], writes=[acum])
        P.op("dve", lambda e: e.tensor_scalar(out=nacum[:, :], in0=acum[:, :], scalar1=-1.0, scalar2=None, op0=ALU.mult), reads=[acum], writes=[nacum])
        P.op("pe", lambda e: e.matmul(p_misc[:, 0:NT], lhsT=ones[:, :], rhs=a_[:, :], start=True, stop=True), reads=[ones, a_], writes=[p_misc])
        P.op("act", lambda e: e.activation(out=dlast[:, :], in_=p_misc[:, 0:NT], func=AF.Exp), reads=[p_misc], writes=[dlast])
        P.op("dve", lambda e: e.tensor_tensor(out=dB[:, :], in0=p_misc[:, 0:NT], in1=acum[:, :], op=ALU.subtract), reads=[p_misc, acum], writes=[dB])
        P.op("act", lambda e: e.activation(out=dB[:, :], in_=dB[:, :], func=AF.Exp), reads=[dB], writes=[dB])
        P.op("dve", lambda e: e.memset(hf[:, :], 0.0), writes=[hf])
        P.op("pool", lambda e: e.memset(hb[0][:, :], 0.0), writes=[hb[0]])
        for c in range(min(NT, SSD_NCH)):
            cs = slice(c * 128, (c + 1) * 128)
            i2 = c % 2
            gb = p_gb[i2]
            P.op("pool", lambda e, c=c, i2=i2: e.tensor_scalar(out=dg[i2][:, :], in0=idf[:, :], scalar1=acum[:, c:c + 1], scalar2=None, op0=ALU.mult),
                 reads=[idf, acum], writes=[dg[i2]])
            P.op("pe", lambda e, gb=gb, i2=i2: e.matmul(gb[:, :], lhsT=ones[:, :], rhs=dg[i2][:, :], start=True, stop=True), reads=[ones, dg[i2]], writes=[gb])
            P.op("act", lambda e, gb=gb, i2=i2: e.activation(out=EB[i2][:, :], in_=gb[:, :], func=AF.Exp), reads=[gb], writes=[EB[i2]])
            P.op("pe", lambda e, gb=gb: e.matmul(gb[:, :], lhsT=idb[:, :], rhs=mneg[:, :], start=False, stop=True), reads=[idb, mneg], writes=[gb])
            P.op("act", lambda e, gb=gb, i2=i2, c=c: e.activation(out=LmT[i2][:, :], in_=gb[:, :], func=AF.Exp, bias=nacum[:, c:c + 1], scale=1.0),
                 reads=[gb, nacum], writes=[LmT[i2]])
            ps = p_sc[0]
            P.op("pe", lambda e, ps=ps, cs=cs: e.matmul(ps[:, :], lhsT=BT[:, cs], rhs=CT[:, cs], start=True, stop=True), reads=[BT, CT], writes=[ps])
            P.op("dve", lambda e, ps=ps, i2=i2: e.tensor_tensor(out=SLT[i2][:, :], in0=ps[:, :], in1=LmT[i2][:, :], op=ALU.mult), reads=[ps, LmT[i2]], writes=[SLT[i2]])
            P.op("pool", lambda e, i2=i2, cs=cs: e.tensor_tensor(out=CgT[i2][:, :], in0=CT[:, cs], in1=EB[i2][:, :], op=ALU.mult), reads=[CT, EB[i2]], writes=[CgT[i2]])
            px = p_tx[0]
            P.op("pe", lambda e, px=px, cs=cs: e.transpose(out=px[:, :], in_=xsT[:, cs], identity=idf[0:64, 0:64]), reads=[xsT, idf], writes=[px])
            P.op("dve", lambda e, px=px, i2=i2, c=c: e.tensor_scalar(out=X[i2][:, :], in0=px[:, :], scalar1=dt[:, c:c + 1], scalar2=None, op0=ALU.mult),
                 reads=[px, dt], writes=[X[i2]])
            pb = p_tb[0]
            P.op("pe", lambda e, pb=pb, cs=cs: e.transpose(out=pb[:, :], in_=BT[:, cs], identity=idb[:, :]), reads=[BT, idb], writes=[pb])
            P.op("dve", lambda e, pb=pb, i2=i2, c=c: e.tensor_scalar(out=Bd[i2][:, :], in0=pb[:, :], scalar1=dB[:, c:c + 1], scalar2=None, op0=ALU.mult),
                 reads=[pb, dB], writes=[Bd[i2]])
            py = p_y[i2]
            hcur = hb[c % 2]
            hnxt = hb[(c + 1) % 2]
            P.op("pe", lambda e, py=py, i2=i2: e.matmul(py[:, :], lhsT=X[i2][:, :], rhs=SLT[i2][:, :], start=True, stop=False), reads=[X[i2], SLT[i2]], writes=[py])
            P.op("pe", lambda e, py=py, i2=i2, hcur=hcur: e.matmul(py[:, :], lhsT=hcur[:, :], rhs=CgT[i2][:, :], start=False, stop=True), reads=[hcur, CgT[i2]], writes=[py])
            P.op("dve", lambda e, py=py, cs=cs: e.scalar_tensor_tensor(out=yac[:, cs], in0=xsT[:, cs], scalar=sc[0:64, 2:3], in1=py[:, :], op0=ALU.mult, op1=ALU.add),
                 reads=[xsT, sc, py], writes=[yac])
            ph = p_h[0]
            P.op("pe", lambda e, ph=ph, i2=i2: e.matmul(ph[:, :], lhsT=Bd[i2][:, :], rhs=X[i2][:, :], start=True, stop=True), reads=[Bd[i2], X[i2]], writes=[ph])
            P.op("dve", lambda e, ph=ph, c=c: e.scalar_tensor_tensor(out=hf[:, :], in0=hf[:, :], scalar=dlast[:, c:c + 1], in1=ph[:, :], op0=ALU.mult, op1=ALU.add),
                 reads=[hf, dlast, ph], writes=[hf])
            P.op("act", lambda e, hnxt=hnxt: e.copy(out=hnxt[:, :], in_=hf[:, :]), reads=[hf], writes=[hnxt])
        P.dma("pool", y_d[h], yac[:, :], reads=[yac], writes=[y_d.buf])
        if NT % 2 == 1 or True:
            P.op("pool", lambda e: e.memset(hb[0][:, :], 0.0), writes=[hb[0]])


def rec_consts():
    import ml_dtypes
    bf = ml_dtypes.bfloat16
    s1 = np.arange(128)[:, None]
    s2 = np.arange(128)[None, :]
    tri = (s1 <= s2).astype(np.float32)
    mneg = np.where(s2 < s1, -NEGM, 0.0).astype(np.float32)
    mpos = np.where(s2 >= s1, NEGM, 0.0).astype(np.float32)
    return dict(tri=tri, ones=np.ones((128, 128), np.float32), idf=np.eye(128, dtype=np.float32),
                idb=np.eye(128, dtype=np.float32).astype(bf), mneg=mneg.astype(bf), mpos=mpos.astype(bf))


GDN_NCH = 10 ** 9
GDN_K = 3


def run_pipeline(gens, K):
    active = []
    it = iter(gens)
    pending = True
    while True:
        if pending and len(active) < K:
            g = next(it, None)
            if g is None:
                pending = False
            else:
                active.append(g)
        if not active:
            if not pending:
                break
            continue
        for g in list(active):
            try:
                next(g)
            except StopIteration:
                active.remove(g)


def emit_gdn(P, qpre_d, kpre_d, vpre_d, cwq_d, cwk_d, cwv_d, bcol_d, acol_d, sc_d, gate_d, nw_d,
             tri_d, ones_d, idf_d, idb_d, mneg_d, mpos_d, y_d, NH, S=SEQ, pfx="gd"):
    NT = S // 128
    zp = P.sbuf(pfx + "_zp", [128, S + 3], F32)
    acc = P.sbuf(pfx + "_acc", [128, S], F32)
    qn = P.sbuf(pfx + "_qn", [64, S], BF16)
    kn = P.sbuf(pfx + "_kn", [64, S], BF16)
    vT = P.sbuf(pfx + "_vT", [64, S], BF16)
    gt = P.sbuf(pfx + "_gt", [128, NT, 64], F32)
    yall = P.sbuf(pfx + "_yall", [64, S], F32)
    ytm = [P.sbuf(pfx + f"_ytm{i}", [128, 64], F32) for i in range(GDN_K + 1)]
    cw = [P.sbuf(pfx + f"_cw{i}", [64, 5], F32) for i in range(3)]
    sc = P.sbuf(pfx + "_sc", [128, 2], F32)
    nw = P.sbuf(pfx + "_nw", [128, 64], F32)
    tri = P.sbuf(pfx + "_tri", [128, 128], F32)
    ones = P.sbuf(pfx + "_ones", [128, 128], F32)
    idf = P.sbuf(pfx + "_idf", [128, 128], F32)
    idb = P.sbuf(pfx + "_idb", [128, 128], BF16)
    mneg = P.sbuf(pfx + "_mneg", [128, 128], BF16)
    mpos = P.sbuf(pfx + "_mpos", [128, 128], BF16)
    col = {n: P.sbuf(pfx + "_c_" + n, [128, NT], F32) for n in
           ("b", "a", "t1", "beta", "nbeta", "g", "gcum", "ngcum", "egc", "bege", "dk", "dlast")}
    Aneg = P.sbuf(pfx + "_Aneg", [128, 1], F32)
    ball = P.sbuf(pfx + "_ball", [128, NT, 6], F32)
    aall = P.sbuf(pfx + "_aall", [128, NT, 6], F32)
    rt = [P.sbuf(pfx + f"_rt{i}", [64, 512], F32) for i in range(2)]
    eps_t = P.sbuf(pfx + "_eps", [128, 1], F32)
    P.op("dve", lambda e: e.memset(eps_t[:, :], 1e-6), writes=[eps_t])

    def f32t(n, k=2):
        return [P.sbuf(pfx + f"_{n}{i}", [128, 128], F32) for i in range(k)]

    def bft(n, shape, k=2):
        return [P.sbuf(pfx + f"_{n}{i}", shape, BF16) for i in range(k)]
    NB = GDN_K + 1
    dg = f32t("dg", NB)
    EB = f32t("EB", NB)
    dec = f32t("dec", NB)
    decT = f32t("decT", NB)
    Ys = [f32t(f"Y{j}_", 2) for j in range(NB)]
    YTs = [f32t(f"YT{j}_", 2) for j in range(NB)]
    PTs = [f32t(f"PT{j}_", 2) for j in range(NB)]
    TTb = bft("TTb", [128, 128], NB)
    attnT = bft("attnT", [128, 128], NB)
    qgT = bft("qgT", [64, 128], NB)
    kbg = bft("kbg", [128, 64], NB)
    kd = bft("kd", [128, 64], NB)
    vb = bft("vb", [128, 64], NB)
    wT = bft("wT", [64, 128], NB)
    u_sb = [P.sbuf(pfx + f"_u{i}", [128, 64], F32) for i in range(NB)]
    vnew = bft("vnew", [128, 64], NB)
    Sf = P.sbuf(pfx + "_Sf", [64, 64], F32)
    Sb = bft("Sb", [64, 64])
    junk = [P.sbuf(pfx + f"_junk{i}", [128, 64], F32) for i in range(NB)]
    ssq = [P.sbuf(pfx + f"_ssq{i}", [128, 1], F32) for i in range(NB)]
    nwg = [P.sbuf(pfx + f"_nwg{i}", [128, 64], F32) for i in range(NB)]
    pa = [P.psum(pfx + f"_pa{i}", [128, 512]) for i in range(6)]
    pbs = [P.psum(pfx + f"_pb{i}", [128, 512], BF16) for i in range(1)]
    pg = [pa[0], pa[1]]
    pai = [0]

    def nxt_pa():
        pai[0] += 1
        return pa[pai[0] % 6]
    pci = [0]

    def nxt_pc():
        pci[0] += 1
        return pc[pci[0] % 2]

    for (sb, d_) in ((tri, tri_d), (ones, ones_d), (idf, idf_d), (idb, idb_d), (mneg, mneg_d), (mpos, mpos_d)):
        P.dma("sp", sb[:, :], d_[:, :], reads=[d_], writes=[sb])
    P.op("pool", lambda e: e.memset(zp[:, 0:3], 0.0), writes=[zp])
    for h in range(NH):
        for i, d_ in enumerate((cwq_d, cwk_d, cwv_d)):
            P.dma("sp", cw[i][:, :], d_[h], reads=[d_.buf], writes=[cw[i]])
        P.dma("sp", sc[:, :], sc_d[h], reads=[sc_d.buf], writes=[sc])
        P.dma("sp", nw[:, :], nw_d[h], reads=[nw_d.buf], writes=[nw])
        if h == 0:
            P.dma("sp", ball[:, :, :], bcol_d[0], reads=[bcol_d.buf], writes=[ball])
            P.dma("sp", aall[:, :, :], acol_d[0], reads=[acol_d.buf], writes=[aall])
        P.op("pool", lambda e, h=h: e.tensor_copy(out=col["b"][:, :], in_=ball[:, :, h]), reads=[ball], writes=[col["b"]])
        P.op("pool", lambda e, h=h: e.tensor_copy(out=col["a"][:, :], in_=aall[:, :, h]), reads=[aall], writes=[col["a"]])
        P.dma("act", gt[:, :, :], gate_d[h], reads=[gate_d.buf], writes=[gt])
        P.op("act", lambda e: e.activation(out=gt[:, :, :], in_=gt[:, :, :], func=AF.Silu), reads=[gt], writes=[gt])
        for which, (src_d, outb) in enumerate(((qpre_d, qn), (kpre_d, kn))):
            emit_conv_silu(P, src_d[h], cw[which], 64, S, zp, acc, [(acc, acc[0:64, :])], src_buf=src_d.buf)
            P.op("act", lambda e: e.activation(out=zp[0:64, 3:S + 3], in_=acc[0:64, :], func=AF.Square), reads=[acc], writes=[zp])
            for j in range(S // 512):
                cs = slice(j * 512, (j + 1) * 512)
                ps = nxt_pa()
                r_ = rt[j % 2]
                P.op("pe", lambda e, ps=ps, j=j: e.matmul(ps[0:64, :], lhsT=ones[0:64, 0:64], rhs=zp[0:64, 3 + j * 512:3 + (j + 1) * 512],
                                                         start=True, stop=True), reads=[ones, zp], writes=[ps])
                P.op("act", lambda e, ps=ps, r_=r_: e.activation(out=r_[:, :], in_=ps[0:64, :], func=AF.Sqrt, bias=eps_t[0:64, 0:1], scale=1.0),
                     reads=[ps, eps_t], writes=[r_])
                P.op("dve", lambda e, r_=r_: e.reciprocal(out=r_[:, :], in_=r_[:, :]), reads=[r_], writes=[r_])
                P.op("dve", lambda e, r_=r_, cs=cs, outb=outb, sc_=(0.125 if which == 0 else 1.0): e.scalar_tensor_tensor(
                    out=outb[:, cs], in0=acc[0:64, cs], scalar=sc_, in1=r_[:, :], op0=ALU.mult, op1=ALU.mult),
                    reads=[acc, r_], writes=[outb])
        emit_conv_silu(P, vpre_d[h], cw[2], 64, S, zp, acc, [(vT, vT[:, :])], src_buf=vpre_d.buf)
        c_ = col
        P.op("act", lambda e: e.activation(out=c_["beta"][:, :], in_=c_["b"][:, :], func=AF.Sigmoid), reads=[c_["b"]], writes=[c_["beta"]])
        P.op("dve", lambda e: e.tensor_scalar(out=c_["nbeta"][:, :], in0=c_["beta"][:, :], scalar1=-1.0, scalar2=None, op0=ALU.mult),
             reads=[c_["beta"]], writes=[c_["nbeta"]])
        P.op("dve", lambda e: e.tensor_scalar(out=c_["a"][:, :], in0=c_["a"][:, :], scalar1=sc[:, 0:1], scalar2=None, op0=ALU.add),
             reads=[c_["a"], sc], writes=[c_["a"]])
        emit_softplus(P, "dve", c_["g"], c_["g"][:, :], c_["a"], c_["a"][:, :], c_["t1"], c_["t1"][:, :], 128)
        P.op("act", lambda e: e.activation(out=Aneg[:, :], in_=sc[:, 1:2], func=AF.Exp), reads=[sc], writes=[Aneg])
        P.op("dve", lambda e: e.tensor_scalar(out=Aneg[:, :], in0=Aneg[:, :], scalar1=-1.0, scalar2=None, op0=ALU.mult), reads=[Aneg], writes=[Aneg])
        P.op("dve", lambda e: e.tensor_scalar(out=c_["g"][:, :], in0=c_["g"][:, :], scalar1=Aneg[:, 0:1], scalar2=None, op0=ALU.mult),
             reads=[c_["g"], Aneg], writes=[c_["g"]])
        pm = pg[0]
        P.op("pe", lambda e: e.matmul(pm[:, 0:NT], lhsT=tri[:, :], rhs=c_["g"][:, :], start=True, stop=True), reads=[tri, c_["g"]], writes=[pm])
        P.op("dve", lambda e: e.tensor_copy(out=c_["gcum"][:, :], in_=pm[:, 0:NT]), reads=[pm], writes=[c_["gcum"]])
        P.op("dve", lambda e: e.tensor_scalar(out=c_["ngcum"][:, :], in0=c_["gcum"][:, :], scalar1=-1.0, scalar2=None, op0=ALU.mult),
             reads=[c_["gcum"]], writes=[c_["ngcum"]])
        P.op("act", lambda e: e.activation(out=c_["egc"][:, :], in_=c_["gcum"][:, :], func=AF.Exp), reads=[c_["gcum"]], writes=[c_["egc"]])
        P.op("dve", lambda e: e.tensor_tensor(out=c_["bege"][:, :], in0=c_["egc"][:, :], in1=c_["beta"][:, :], op=ALU.mult),
             reads=[c_["egc"], c_["beta"]], writes=[c_["bege"]])
        P.op("pe", lambda e: e.matmul(pm[:, 0:NT], lhsT=ones[:, :], rhs=c_["g"][:, :], start=True, stop=True), reads=[ones, c_["g"]], writes=[pm])
        P.op("act", lambda e: e.activation(out=c_["dlast"][:, :], in_=pm[:, 0:NT], func=AF.Exp), reads=[pm], writes=[c_["dlast"]])
        P.op("dve", lambda e: e.tensor_tensor(out=c_["dk"][:, :], in0=pm[:, 0:NT], in1=c_["gcum"][:, :], op=ALU.subtract),
             reads=[pm, c_["gcum"]], writes=[c_["dk"]])
        P.op("act", lambda e: e.activation(out=c_["dk"][:, :], in_=c_["dk"][:, :], func=AF.Exp), reads=[c_["dk"]], writes=[c_["dk"]])
        P.op("dve", lambda e: e.memset(Sf[:, :], 0.0), writes=[Sf])
        P.op("pool", lambda e: e.memset(Sb[0][:, :], 0.0), writes=[Sb[0]])
        seq_state = {"next": 0}

        def chunk_gen(c):
            cs = slice(c * 128, (c + 1) * 128)
            i2 = c % NB
            slot = c % GDN_K
            bA, bB = pa[2 * slot], pa[2 * slot + 1]
            rA = [slice(0, 128), slice(128, 256), slice(256, 384), slice(384, 512)]
            Y, YT, PT = Ys[i2], YTs[i2], PTs[i2]
            P.op("pool", lambda e, c=c, i2=i2: e.tensor_scalar(out=dg[i2][:, :], in0=idf[:, :], scalar1=c_["gcum"][:, c:c + 1], scalar2=None, op0=ALU.mult),
                 reads=[idf, c_["gcum"]], writes=[dg[i2]])
            yield
            P.op("pe", lambda e, i2=i2: e.matmul(bA[:, rA[0]], lhsT=ones[:, :], rhs=dg[i2][:, :], start=True, stop=False, skip_group_check=True), reads=[ones, dg[i2]], writes=[bA])
            P.op("pe", lambda e: e.matmul(bA[:, rA[0]], lhsT=idb[:, :], rhs=mpos[:, :], start=False, stop=True, skip_group_check=True), reads=[idb, mpos], writes=[bA])
            P.op("pe", lambda e, i2=i2: e.matmul(bA[:, rA[1]], lhsT=ones[:, :], rhs=dg[i2][:, :], start=True, stop=False, skip_group_check=True), reads=[ones, dg[i2]], writes=[bA])
            P.op("pe", lambda e: e.matmul(bA[:, rA[1]], lhsT=idb[:, :], rhs=mneg[:, :], start=False, stop=True, skip_group_check=True), reads=[idb, mneg], writes=[bA])
            P.op("pe", lambda e, i2=i2: e.matmul(bA[:, rA[2]], lhsT=ones[:, :], rhs=dg[i2][:, :], start=True, stop=True, skip_group_check=True), reads=[ones, dg[i2]], writes=[bA])
            P.op("pe", lambda e, cs=cs: e.matmul(bA[:, rA[3]], lhsT=kn[:, cs], rhs=kn[:, cs], start=True, stop=True, skip_group_check=True), reads=[kn], writes=[bA])
            P.op("pe", lambda e, cs=cs: e.matmul(bB[:, rA[0]], lhsT=kn[:, cs], rhs=qn[:, cs], start=True, stop=True, skip_group_check=True), reads=[kn, qn], writes=[bB])
            yield
            P.op("act", lambda e, i2=i2: e.activation(out=EB[i2][0:64, :], in_=bA[0:64, rA[2]], func=AF.Exp), reads=[bA], writes=[EB[i2]])
            P.op("act", lambda e, i2=i2, c=c: e.activation(out=dec[i2][:, :], in_=bA[:, rA[0]], func=AF.Exp, bias=c_["gcum"][:, c:c + 1], scale=-1.0),
                 reads=[bA, c_["gcum"]], writes=[dec[i2]])
            P.op("act", lambda e, i2=i2, c=c: e.activation(out=decT[i2][:, :], in_=bA[:, rA[1]], func=AF.Exp, bias=c_["ngcum"][:, c:c + 1], scale=1.0),
                 reads=[bA, c_["ngcum"]], writes=[decT[i2]])
            yield
            P.op("pool", lambda e, i2=i2, cs=cs: e.tensor_tensor(out=qgT[i2][:, :], in0=qn[:, cs], in1=EB[i2][0:64, :], op=ALU.mult),
                 reads=[qn, EB[i2]], writes=[qgT[i2]])
            yield
            Yc, YTc, PTc = Y[0], YT[0], PT[0]
            P.op("dve", lambda e, i2=i2, c=c, Yc=Yc: e.scalar_tensor_tensor(out=Yc[:, :], in0=bA[:, rA[3]], scalar=c_["nbeta"][:, c:c + 1],
                                                                          in1=dec[i2][:, :], op0=ALU.mult, op1=ALU.mult),
                 reads=[bA, c_["nbeta"], dec[i2]], writes=[Yc])
            P.op("dve", lambda e, i2=i2: e.tensor_tensor(out=attnT[i2][:, :], in0=bB[:, rA[0]], in1=decT[i2][:, :], op=ALU.mult),
                 reads=[bB, decT[i2]], writes=[attnT[i2]])
            yield
            P.op("pe", lambda e, Yc=Yc: e.transpose(out=bB[:, rA[1]], in_=Yc[:, :], identity=idf[:, :]), reads=[Yc, idf], writes=[bB])
            pkt = pbs[0]
            k0 = slot * 128
            P.op("pe", lambda e, cs=cs, pkt=pkt, k0=k0: e.transpose(out=pkt[:, k0:k0 + 64], in_=kn[:, cs], identity=idb[0:64, 0:64]), reads=[kn, idb], writes=[pkt])
            P.op("pe", lambda e, cs=cs, pkt=pkt, k0=k0: e.transpose(out=pkt[:, k0 + 64:k0 + 128], in_=vT[:, cs], identity=idb[0:64, 0:64]), reads=[vT, idb], writes=[pkt])
            yield
            P.op("act", lambda e, YTc=YTc: e.copy(out=YTc[:, :], in_=bB[:, rA[1]]), reads=[bB], writes=[YTc])
            P.op("act", lambda e, PTc=PTc: e.activation(out=PTc[:, :], in_=bB[:, rA[1]], func=AF.Identity), reads=[bB], writes=[PTc])
            P.op("dve", lambda e, i2=i2, c=c, pkt=pkt: e.tensor_scalar(out=kbg[i2][:, :], in0=pkt[:, slot * 128:slot * 128 + 64], scalar1=c_["bege"][:, c:c + 1], scalar2=None, op0=ALU.mult),
                 reads=[pkt, c_["bege"]], writes=[kbg[i2]])
            P.op("dve", lambda e, i2=i2, c=c, pkt=pkt: e.tensor_scalar(out=kd[i2][:, :], in0=pkt[:, slot * 128:slot * 128 + 64], scalar1=c_["dk"][:, c:c + 1], scalar2=None, op0=ALU.mult),
                 reads=[pkt, c_["dk"]], writes=[kd[i2]])
            P.op("dve", lambda e, i2=i2, c=c, pkt=pkt: e.tensor_scalar(out=vb[i2][:, :], in0=pkt[:, slot * 128 + 64:slot * 128 + 128], scalar1=c_["beta"][:, c:c + 1], scalar2=None, op0=ALU.mult),
                 reads=[pkt, c_["beta"]], writes=[vb[i2]])
            P.op("pool", lambda e, PTc=PTc: e.tensor_tensor(out=PTc[:, :], in0=PTc[:, :], in1=idf[:, :], op=ALU.add), reads=[PTc, idf], writes=[PTc])
            yield
            cur = 0
            for lev in range(6):
                nx = 1 - cur
                P.op("pe", lambda e, cur=cur: e.matmul(bB[:, rA[1]], lhsT=YT[cur][:, :], rhs=Y[cur][:, :], start=True, stop=True, skip_group_check=True),
                     reads=[YT[cur], Y[cur]], writes=[bB])
                if lev < 5:
                    P.op("pe", lambda e, cur=cur: e.matmul(bB[:, rA[2]], lhsT=Y[cur][:, :], rhs=YT[cur][:, :], start=True, stop=True, skip_group_check=True),
                         reads=[YT[cur], Y[cur]], writes=[bB])
                yield
                P.op("act", lambda e, nx=nx: e.copy(out=Y[nx][:, :], in_=bB[:, rA[1]]), reads=[bB], writes=[Y[nx]])
                if lev < 5:
                    P.op("act", lambda e, nx=nx: e.copy(out=YT[nx][:, :], in_=bB[:, rA[2]]), reads=[bB], writes=[YT[nx]])
                yield
                P.op("pe", lambda e, nx=nx, cur=cur: e.matmul(bB[:, rA[3]], lhsT=Y[nx][:, :], rhs=PT[cur][:, :], start=True, stop=True, skip_group_check=True),
                     reads=[Y[nx], PT[cur]], writes=[bB])
                yield
                P.op("dve", lambda e, nx=nx, cur=cur: e.tensor_tensor(out=PT[nx][:, :], in0=bB[:, rA[3]], in1=PT[cur][:, :], op=ALU.add),
                     reads=[bB, PT[cur]], writes=[PT[nx]])
                yield
                cur = nx
            P.op("act", lambda e, i2=i2, cur=cur: e.copy(out=TTb[i2][:, :], in_=PT[cur][:, :]), reads=[PT[cur]], writes=[TTb[i2]])
            yield
            P.op("pe", lambda e, i2=i2: e.matmul(bA[:, 0:64], lhsT=TTb[i2][:, :], rhs=vb[i2][:, :], start=True, stop=True, skip_group_check=True), reads=[TTb[i2], vb[i2]], writes=[bA])
            P.op("pe", lambda e, i2=i2: e.matmul(bA[0:64, 128:256], lhsT=kbg[i2][:, :], rhs=TTb[i2][:, :], start=True, stop=True, skip_group_check=True), reads=[kbg[i2], TTb[i2]], writes=[bA])
            yield
            P.op("act", lambda e, i2=i2: e.copy(out=u_sb[i2][:, :], in_=bA[:, 0:64]), reads=[bA], writes=[u_sb[i2]])
            P.op("act", lambda e, i2=i2: e.copy(out=wT[i2][:, :], in_=bA[0:64, 128:256]), reads=[bA], writes=[wT[i2]])
            P.op("pool", lambda e, i2=i2, c=c: e.tensor_tensor(out=nwg[i2][:, :], in0=gt[:, c, :], in1=nw[:, :], op=ALU.mult), reads=[gt, nw], writes=[nwg[i2]])
            yield
            while seq_state["next"] != c:
                yield
            Scur, Snxt = Sb[c % 2], Sb[(c + 1) % 2]
            P.op("pe", lambda e, i2=i2, Scur=Scur: e.matmul(bA[:, 256:320], lhsT=wT[i2][:, :], rhs=Scur[:, :], start=True, stop=True, skip_group_check=True), reads=[wT[i2], Scur], writes=[bA])
            P.op("pe", lambda e, i2=i2, Scur=Scur: e.matmul(bB[:, 0:64], lhsT=qgT[i2][:, :], rhs=Scur[:, :], start=True, stop=False, skip_group_check=True), reads=[qgT[i2], Scur], writes=[bB])
            P.op("dve", lambda e, i2=i2: e.tensor_tensor(out=vnew[i2][:, :], in0=u_sb[i2][:, :], in1=bA[:, 256:320], op=ALU.subtract),
                 reads=[u_sb[i2], bA], writes=[vnew[i2]])
            P.op("pe", lambda e, i2=i2: e.matmul(bA[0:64, 384:448], lhsT=kd[i2][:, :], rhs=vnew[i2][:, :], start=True, stop=True, skip_group_check=True), reads=[kd[i2], vnew[i2]], writes=[bA])
            P.op("pe", lambda e, i2=i2: e.matmul(bB[:, 0:64], lhsT=attnT[i2][:, :], rhs=vnew[i2][:, :], start=False, stop=True, skip_group_check=True), reads=[attnT[i2], vnew[i2]], writes=[bB])
            P.op("dve", lambda e, c=c: e.scalar_tensor_tensor(out=Sf[:, :], in0=Sf[:, :], scalar=c_["dlast"][0:64, c:c + 1], in1=bA[0:64, 384:448],
                                                             op0=ALU.mult, op1=ALU.add), reads=[Sf, c_["dlast"], bA], writes=[Sf])
            P.op("act", lambda e, Snxt=Snxt: e.copy(out=Snxt[:, :], in_=Sf[:, :]), reads=[Sf], writes=[Snxt])
            seq_state["next"] = c + 1
            yield
            P.op("act", lambda e, i2=i2: e.activation(out=junk[i2][:, :], in_=bB[:, 0:64], func=AF.Square, accum_out=ssq[i2][:, 0:1]),
                 reads=[bB], writes=[junk[i2], ssq[i2]])
            P.op("act", lambda e, i2=i2: e.activation(out=ssq[i2][:, :], in_=ssq[i2][:, :], func=AF.Sqrt, bias=eps_t[:, 0:1], scale=1.0 / 64.0),
                 reads=[ssq[i2], eps_t], writes=[ssq[i2]])
            yield
            P.op("dve", lambda e, i2=i2: e.reciprocal(out=ssq[i2][:, :], in_=ssq[i2][:, :]), reads=[ssq[i2]], writes=[ssq[i2]])
            P.op("dve", lambda e, i2=i2, c=c: e.scalar_tensor_tensor(out=ytm[i2][:, :], in0=bB[:, 0:64], scalar=ssq[i2][:, 0:1], in1=nwg[i2][:, :],
                                                                    op0=ALU.mult, op1=ALU.mult), reads=[bB, ssq[i2], nwg[i2]], writes=[ytm[i2]])
            yield
            P.op("pe", lambda e, i2=i2: e.transpose(out=bB[0:64, 128:256], in_=ytm[i2][:, :], identity=idf[:, :]), reads=[ytm[i2], idf], writes=[bB])
            yield
            P.op("act", lambda e, cs=cs: e.copy(out=yall[:, cs], in_=bB[0:64, 128:256]), reads=[bB], writes=[yall])

        run_pipeline([chunk_gen(c) for c in range(min(NT, GDN_NCH))], GDN_K)
        P.dma("pool", y_d[h], yall[:, :], reads=[yall], writes=[y_d.buf])
        P.op("pool", lambda e: e.memset(Sb[0][:, :], 0.0), writes=[Sb[0]])
        if min(NT, GDN_NCH) % 2 == 1:
            P.op("pool", lambda e: e.memset(Sb[1][:, :], 0.0), writes=[Sb[1]])


D_MIX = 1536
ALPHA = (2.0 * 2) ** 0.25
N_FM = 4736
N_TM = 1170
FM_AQ, FM_AK, FM_AG, FM_SX, FM_SB, FM_SC, FM_SZ, FM_CQ, FM_CK, FM_CG, FM_DQ, FM_DK, FM_DV = (
    0, 384, 768, 1152, 1536, 1792, 2048, 2432, 2816, 3200, 3584, 3968, 4352)
TM_AV, TM_CV, TM_DG, TM_DT, TM_DB, TM_DA = 0, 384, 768, 1152, 1158, 1164


def in_col_perm():
    r = lambda a, b: list(range(a, b))
    fm = r(0, 768) + r(1152, 1536) + r(1536, 2816) + r(2822, 3590) + r(3974, 4358) + r(4358, 5510)
    tm = r(768, 1152) + r(3590, 3974) + r(5510, 5894) + r(2816, 2822) + r(5894, 5900) + r(5900, 5906)
    assert len(fm) == N_FM and len(tm) == N_TM
    return np.array(fm + tm)


def emit_projF(P, x_src, src_bf16, w_ap, w_buf, projT, projtm, S=SEQ, pfx="pj"):
    KC = D_MODEL // 128
    C = IN_COLS
    TB = 1024
    wbf = P.sbuf(pfx + "_wbf", [128, KC, C], BF16)
    HW = (C + 1) // 2
    wst = [P.sbuf(pfx + f"_wst{i}", [128, HW], F32) for i in range(2)]
    xbf = [P.sbuf(pfx + f"_xbf{i}", [128, KC, TB], BF16) for i in range(2)]
    xst = [P.sbuf(pfx + f"_xst{i}", [128, TB], F32) for i in range(2)]
    ost = [P.sbuf(pfx + f"_ost{i}", [128, 512], F32) for i in range(3)]
    ot2 = [P.sbuf(pfx + f"_ot2{i}", [128, N_TM], F32) for i in range(2)]
    ps = [P.psum(pfx + f"_ps{i}", [128, 512]) for i in range(6)]
    n = 0
    for k in range(KC):
        for h in range(2):
            c0, c1 = h * HW, min(C, (h + 1) * HW)
            b = wst[n % 2]
            P.dma("sp" if n % 2 == 0 else "act", b[:, :c1 - c0], w_ap[k * 128:(k + 1) * 128, c0:c1], reads=[w_buf], writes=[b])
            eng = "dve" if n % 2 == 0 else "pool"
            P.op(eng, lambda e, b=b, k=k, c0=c0, c1=c1: e.tensor_copy(out=wbf[:, k, c0:c1], in_=b[:, :c1 - c0]), reads=[b], writes=[wbf])
            n += 1
    ci = 0
    oi = 0
    for blk in range(S // TB):
        xb = xbf[blk % 2]
        t0 = blk * TB
        if src_bf16:
            P.dma("sp", xb[:, :, :], x_src.t.rearrange("(k p) t -> p k t", p=128)[:, :, t0:t0 + TB], reads=[x_src], writes=[xb])
        else:
            for k in range(KC):
                b = xst[k % 2]
                P.dma("sp", b[:, :], x_src[k * 128:(k + 1) * 128, t0:t0 + TB], reads=[x_src], writes=[b])
                P.op("pool", lambda e, b=b, k=k, xb=xb: e.tensor_copy(out=xb[:, k, :], in_=b[:, :]), reads=[b], writes=[xb])
        for sub in range(TB // 512):
            for cc in range(N_FM // 128):
                p = ps[ci % 6]
                o = ost[ci % 3]
                for k in range(KC):
                    P.op("pe", lambda e, p=p, k=k, cc=cc, sub=sub, xb=xb: e.matmul(
                        p[:, :], lhsT=wbf[:, k, cc * 128:(cc + 1) * 128], rhs=xb[:, k, sub * 512:(sub + 1) * 512],
                        start=(k == 0), stop=(k == KC - 1)), reads=[wbf, xb], writes=[p])
                if ci % 2 == 0:
                    P.op("dve", lambda e, p=p, o=o: e.tensor_copy(out=o[:, :], in_=p[:, :]), reads=[p], writes=[o])
                else:
                    P.op("act", lambda e, p=p, o=o: e.copy(out=o[:, :], in_=p[:, :]), reads=[p], writes=[o])
                P.dma("pool" if ci % 2 == 0 else "sp", projT[cc * 128:(cc + 1) * 128, t0 + sub * 512:t0 + (sub + 1) * 512], o[:, :],
                      reads=[o], writes=[projT])
                ci += 1
        for tt in range(TB // 128):
            o2 = ot2[oi % 2]
            oi += 1
            for c0 in range(0, N_TM, 512):
                c1 = min(N_TM, c0 + 512)
                p = ps[ci % 6]
                for k in range(KC):
                    P.op("pe", lambda e, p=p, k=k, tt=tt, c0=c0, c1=c1, xb=xb: e.matmul(
                        p[:, :c1 - c0], lhsT=xb[:, k, tt * 128:(tt + 1) * 128], rhs=wbf[:, k, N_FM + c0:N_FM + c1],
                        start=(k == 0), stop=(k == KC - 1)), reads=[wbf, xb], writes=[p])
                if ci % 2 == 0:
                    P.op("dve", lambda e, p=p, o2=o2, c0=c0, c1=c1: e.tensor_copy(out=o2[:, c0:c1], in_=p[:, :c1 - c0]), reads=[p], writes=[o2])
                else:
                    P.op("act", lambda e, p=p, o2=o2, c0=c0, c1=c1: e.copy(out=o2[:, c0:c1], in_=p[:, :c1 - c0]), reads=[p], writes=[o2])
                ci += 1
            P.dma("pool", projtm[t0 + tt * 128:t0 + (tt + 1) * 128, :], o2[:, :], reads=[o2], writes=[projtm])


def emit_outlnF(P, mix_d, projT, snw_ap, w_ap, lng_ap, lnb_ap, ones_ap, idf_ap, cbuf, x_d, xo_d, xoT_d, S=SEQ, TBLK=2048, pfx="ol"):
    FC = D_MIX // 128
    T = TBLK
    wbf = P.sbuf(pfx + "_wbf", [128, FC, D_MODEL], BF16)
    mix = P.sbuf(pfx + "_mix", [128, FC, T], BF16)
    wst = [P.sbuf(pfx + f"_wst{i}", [128, D_MODEL], F32) for i in range(2)]
    mst = [P.sbuf(pfx + f"_mst{i}", [128, T], F32) for i in range(2)]
    zst = [P.sbuf(pfx + f"_zst{i}", [128, T], F32) for i in range(2)]
    gz = P.sbuf(pfx + "_gz", [128, 3, T], F32)
    sq = P.sbuf(pfx + "_sq", [128, T], F32)
    rs = P.sbuf(pfx + "_rs", [128, T], F32)
    snw = P.sbuf(pfx + "_snw", [128, 3], F32)
    lng = P.sbuf(pfx + "_lng", [128, D_MODEL], F32)
    lnb = P.sbuf(pfx + "_lnb", [128, D_MODEL], F32)
    ones = P.sbuf(pfx + "_ones", [128, 128], F32)
    idf = P.sbuf(pfx + "_idf", [128, 128], F32)
    eps6 = P.sbuf(pfx + "_eps6", [128, 1], F32)
    eps5 = P.sbuf(pfx + "_eps5", [128, 1], F32)
    xt = [P.sbuf(pfx + f"_xt{i}", [128, D_MODEL], F32) for i in range(2)]
    zt = [P.sbuf(pfx + f"_zt{i}", [128, D_MODEL], F32) for i in range(2)]
    xTt = [P.sbuf(pfx + f"_xTt{i}", [128, 8, 128], BF16) for i in range(2)]
    st = [P.sbuf(pfx + f"_st{i}", [128, 2, 6], F32) for i in range(2)]
    mv = [P.sbuf(pfx + f"_mv{i}", [128, 2], F32) for i in range(2)]
    pss = [P.psum(pfx + f"_pss{i}", [128, 512]) for i in range(2)]
    po = [P.psum(pfx + f"_po{i}", [128, 512]) for i in range(4)]
    ptr = [P.psum(pfx + f"_ptr{i}", [128, 512]) for i in range(2)]
    P.op("dve", lambda e: e.memset(eps6[:, :], 1e-6), writes=[eps6])
    P.op("dve", lambda e: e.memset(eps5[:, :], 1e-5), writes=[eps5])
    P.dma("sp", snw[:, :], snw_ap, reads=[cbuf], writes=[snw])
    P.dma("sp", lng[:, :], lng_ap, reads=[cbuf], writes=[lng])
    P.dma("sp", lnb[:, :], lnb_ap, reads=[cbuf], writes=[lnb])
    P.dma("sp", ones[:, :], ones_ap, reads=[cbuf], writes=[ones])
    P.dma("sp", idf[:, :], idf_ap, reads=[cbuf], writes=[idf])
    for k in range(FC):
        b = wst[k % 2]
        P.dma("act", b[:, :], w_ap[k * 128:(k + 1) * 128, :], reads=[cbuf], writes=[b])
        P.op("pool", lambda e, b=b, k=k: e.tensor_copy(out=wbf[:, k, :], in_=b[:, :]), reads=[b], writes=[wbf])
    n = 0
    ti = 0
    for blk in range(S // T):
        b0 = blk * T
        bsl = slice(b0, b0 + T)
        for f0 in (0, 6, 9):
            for k in range(3):
                b = mst[n % 2]
                n += 1
                P.dma("sp", b[:, :], mix_d[(f0 + k) * 128:(f0 + k + 1) * 128, bsl], reads=[mix_d], writes=[b])
                P.op("dve", lambda e, b=b, fk=f0 + k: e.tensor_copy(out=mix[:, fk, :], in_=b[:, :]), reads=[b], writes=[mix])
        for k in range(3):
            b = mst[n % 2]
            zb = zst[k % 2]
            n += 1
            P.dma("sp", b[:, :], mix_d[(3 + k) * 128:(4 + k) * 128, bsl], reads=[mix_d], writes=[b])
            P.dma("sp", zb[:, :], projT[FM_SZ + k * 128:FM_SZ + (k + 1) * 128, bsl], reads=[projT], writes=[zb])
            P.op("act", lambda e, zb=zb: e.activation(out=zb[:, :], in_=zb[:, :], func=AF.Silu), reads=[zb], writes=[zb])
            P.op("dve", lambda e, b=b, zb=zb, k=k: e.tensor_tensor(out=gz[:, k, :], in0=b[:, :], in1=zb[:, :], op=ALU.mult), reads=[b, zb], writes=[gz])
        for j in range(T // 512):
            cs = slice(j * 512, (j + 1) * 512)
            ps = pss[j % 2]
            for k in range(3):
                P.op("act", lambda e, k=k, cs=cs: e.activation(out=sq[:, cs], in_=gz[:, k, cs], func=AF.Square), reads=[gz], writes=[sq])
                P.op("pe", lambda e, ps=ps, k=k, cs=cs: e.matmul(ps[:, :], lhsT=ones[:, :], rhs=sq[:, cs], start=(k == 0), stop=(k == 2)),
                     reads=[ones, sq], writes=[ps])
            P.op("act", lambda e, ps=ps, cs=cs: e.activation(out=rs[:, cs], in_=ps[:, :], func=AF.Sqrt, bias=eps6[:, 0:1], scale=1.0 / 384.0),
                 reads=[ps, eps6], writes=[rs])
            P.op("dve", lambda e, cs=cs: e.reciprocal(out=rs[:, cs], in_=rs[:, cs]), reads=[rs], writes=[rs])
            for k in range(3):
                P.op("dve", lambda e, k=k, cs=cs: e.scalar_tensor_tensor(out=mix[:, 3 + k, cs], in0=gz[:, k, cs], scalar=snw[:, k:k + 1], in1=rs[:, cs],
                                                                        op0=ALU.mult, op1=ALU.mult), reads=[gz, snw, rs], writes=[mix])
        for tt in range(T // 128):
            x_ = xt[ti % 2]
            z_ = zt[ti % 2]
            s_ = st[ti % 2]
            m_ = mv[ti % 2]
            xT_ = xTt[ti % 2]
            r0 = b0 + tt * 128
            P.dma("sp", x_[:, :], x_d[r0:r0 + 128, :], reads=[x_d], writes=[x_])
            for hh in range(2):
                p = po[(ti * 2 + hh) % 4]
                for k in range(FC):
                    P.op("pe", lambda e, p=p, k=k, tt=tt, hh=hh: e.matmul(p[:, :], lhsT=mix[:, k, tt * 128:(tt + 1) * 128],
                                                                          rhs=wbf[:, k, hh * 512:(hh + 1) * 512], start=(k == 0), stop=(k == FC - 1)),
                         reads=[mix, wbf], writes=[p])
                P.op("dve", lambda e, p=p, hh=hh, x_=x_, z_=z_: e.scalar_tensor_tensor(out=z_[:, hh * 512:(hh + 1) * 512], in0=x_[:, hh * 512:(hh + 1) * 512],
                                                                                     scalar=ALPHA, in1=p[:, :], op0=ALU.mult, op1=ALU.add),
                     reads=[x_, p], writes=[z_])
                P.op("dve", lambda e, hh=hh, z_=z_, s_=s_: e.bn_stats(out=s_[:, hh, :], in_=z_[:, hh * 512:(hh + 1) * 512]), reads=[z_], writes=[s_])
            P.op("dve", lambda e, s_=s_, m_=m_: e.bn_aggr(out=m_[:, :], in_=s_[:, :, :]), reads=[s_], writes=[m_])
            P.op("act", lambda e, m_=m_: e.activation(out=m_[:, 1:2], in_=m_[:, 1:2], func=AF.Sqrt, bias=eps5[:, 0:1], scale=1.0), reads=[m_, eps5], writes=[m_])
            P.op("dve", lambda e, m_=m_: e.reciprocal(out=m_[:, 1:2], in_=m_[:, 1:2]), reads=[m_], writes=[m_])
            P.op("dve", lambda e, z_=z_, m_=m_: e.tensor_scalar(out=z_[:, :], in0=z_[:, :], scalar1=m_[:, 0:1], scalar2=m_[:, 1:2], op0=ALU.subtract, op1=ALU.mult),
                 reads=[z_, m_], writes=[z_])
            P.op("pool", lambda e, z_=z_: e.tensor_tensor(out=z_[:, :], in0=z_[:, :], in1=lng[:, :], op=ALU.mult), reads=[z_, lng], writes=[z_])
            P.op("pool", lambda e, z_=z_: e.tensor_tensor(out=z_[:, :], in0=z_[:, :], in1=lnb[:, :], op=ALU.add), reads=[z_, lnb], writes=[z_])
            P.dma("pool", xo_d[r0:r0 + 128, :], z_[:, :], reads=[z_], writes=[xo_d])
            if xoT_d is not None:
                for half in range(2):
                    pt_ = ptr[(ti * 2 + half) % 2]
                    for kk in range(4):
                        k = half * 4 + kk
                        P.op("pe", lambda e, pt_=pt_, kk=kk, k=k, z_=z_: e.transpose(out=pt_[:, kk * 128:(kk + 1) * 128], in_=z_[:, k * 128:(k + 1) * 128],
                                                                                  identity=idf[:, :]), reads=[z_, idf], writes=[pt_])
                    P.op("act", lambda e, pt_=pt_, half=half, xT_=xT_: e.copy(out=xT_[:, half * 4:(half + 1) * 4, :],
                                                                             in_=pt_[:, :].rearrange("p (a b) -> p a b", b=128)),
                         reads=[pt_], writes=[xT_])
                P.dma("sp", xoT_d.t.rearrange("(k p) t -> p k t", p=128)[:, :, r0:r0 + 128], xT_[:, :, :], reads=[xT_], writes=[xoT_d])
            ti += 1


DEBUG_OUT = False
SAME_ENGINE_SYNC = True
N_LAYERS_BUILD = 2
STAGES = "PABCDO"


def build_fused(S=SEQ):
    nc = bass.Bass("TRN2", target_bir_lowering=False)
    P = Prog(nc, same_engine_sync=SAME_ENGINE_SYNC)
    NT = S // 128
    L = DEPTH
    EI = "ExternalInput"
    xT0 = P.dram("xT0", [D_MODEL, S], F32, EI)
    x0 = P.dram("x0", [S, D_MODEL], F32, EI)
    w_in = P.dram("w_in", [L, D_MODEL, IN_COLS], F32, EI)
    w_out = P.dram("w_out", [L, D_MIX, D_MODEL], F32, EI)
    ab = P.dram("ab", [6, 128, 64], F32, EI)
    bi = P.dram("bi", [32, S], BF16, EI)
    cb = P.dram("cb", [4, 128, 512], BF16, EI)
    idb = P.dram("idb", [128, 128], BF16, EI)
    idf = P.dram("idf", [128, 128], F32, EI)
    dbias = P.dram("dbias", [6, 3, 2, 128, 512], F32, EI)
    tri = P.dram("tri", [128, 128], F32, EI)
    ones = P.dram("ones", [128, 128], F32, EI)
    mneg = P.dram("mneg", [128, 128], BF16, EI)
    mpos = P.dram("mpos", [128, 128], BF16, EI)
    s_cwx = P.dram("s_cwx", [L, 6, 64, 5], F32, EI)
    s_cwb = P.dram("s_cwb", [L, 6, 128, 5], F32, EI)
    s_cwc = P.dram("s_cwc", [L, 6, 128, 5], F32, EI)
    s_sc = P.dram("s_sc", [L, 6, 128, 3], F32, EI)
    d_cwq = P.dram("d_cwq", [L, 6, 64, 5], F32, EI)
    d_cwk = P.dram("d_cwk", [L, 6, 64, 5], F32, EI)
    d_cwv = P.dram("d_cwv", [L, 6, 64, 5], F32, EI)
    d_sc = P.dram("d_sc", [L, 6, 128, 2], F32, EI)
    d_nw = P.dram("d_nw", [L, 128, 64], F32, EI)
    o_snw = P.dram("o_snw", [L, 128, 3], F32, EI)
    o_lng = P.dram("o_lng", [L, 128, D_MODEL], F32, EI)
    o_lnb = P.dram("o_lnb", [L, 128, D_MODEL], F32, EI)
    out = P.dram("out", [S, D_MODEL], F32, "ExternalOutput")
    sk = "ExternalOutput" if DEBUG_OUT else None
    projT = P.dram("projT", [N_FM, S], F32, sk)
    projtm = P.dram("projtm", [S, N_TM], F32, sk)
    mixT = P.dram("mixT", [D_MIX, S], F32, sk)
    x1 = P.dram("x1", [S, D_MODEL], F32, sk)
    x1T = P.dram("x1T", [D_MODEL, S], BF16, None)

    def fm(off):
        return Src(projT, lambda h, off=off: projT[off + h * 64: off + (h + 1) * 64, :])

    def mixrows(off):
        return Src(mixT, lambda h, off=off: mixT[off + h * 64: off + (h + 1) * 64, :])

    for l in range(N_LAYERS_BUILD):
        if "P" in STAGES:
            with P.scope():
                emit_projF(P, xT0 if l == 0 else x1T, l != 0, w_in.t[l], w_in, projT, projtm, S)
        if "A" in STAGES:
            with P.scope():
                emit_moba(P, fm(FM_AQ), fm(FM_AK),
                          Src(projtm, lambda h: projtm.t[:, TM_AV + h * 64: TM_AV + (h + 1) * 64].rearrange("(t p) d -> p t d", p=128)),
                          fm(FM_AG), Src(ab, lambda h: ab.t[h]), bi, cb, idb, idf, mixrows(0), 6, S)
        if "B" in STAGES:
            with P.scope():
                emit_ssd(P, fm(FM_SX),
                         Src(projT, lambda h: projT[FM_SB + (h // 3) * 128: FM_SB + (h // 3 + 1) * 128, :]),
                         Src(projT, lambda h: projT[FM_SC + (h // 3) * 128: FM_SC + (h // 3 + 1) * 128, :]),
                         Src(s_cwx, lambda h, l=l: s_cwx.t[l, h]), Src(s_cwb, lambda h, l=l: s_cwb.t[l, h]), Src(s_cwc, lambda h, l=l: s_cwc.t[l, h]),
                         Src(projtm, lambda h: projtm.t[:, TM_DT:TM_DT + 6].rearrange("(t p) c -> p t c", p=128)),
                         Src(s_sc, lambda h, l=l: s_sc.t[l, h]), tri, ones, idf, idb, mneg, mixrows(384), 6, S)
        if "C" in STAGES:
            with P.scope():
                emit_dil(P, fm(FM_CQ), fm(FM_CK),
                         Src(projtm, lambda hg: projtm.t[:, TM_CV + hg[0] * 64: TM_CV + (hg[0] + 1) * 64].rearrange(
                             "(j p r) c -> p r j c", p=128, r=DIL_D[hg[1]])),
                         fm(FM_CG), Src(dbias, lambda h: dbias.t[h].rearrange("g o p q -> p (g o) q")), mixrows(768), 6, S)
        if "D" in STAGES:
            with P.scope():
                emit_gdn(P, fm(FM_DQ), fm(FM_DK), fm(FM_DV),
                         Src(d_cwq, lambda h, l=l: d_cwq.t[l, h]), Src(d_cwk, lambda h, l=l: d_cwk.t[l, h]), Src(d_cwv, lambda h, l=l: d_cwv.t[l, h]),
                         Src(projtm, lambda h: projtm.t[:, TM_DB:TM_DB + 6].rearrange("(t p) c -> p t c", p=128)),
                         Src(projtm, lambda h: projtm.t[:, TM_DA:TM_DA + 6].rearrange("(t p) c -> p t c", p=128)),
                         Src(d_sc, lambda h, l=l: d_sc.t[l, h]),
                         Src(projtm, lambda h: projtm.t[:, TM_DG + h * 64: TM_DG + (h + 1) * 64].rearrange("(t p) d -> p t d", p=128)),
                         Src(d_nw, lambda h, l=l: d_nw.t[l]), tri, ones, idf, idb, mneg, mpos, mixrows(1152), 6, S)
        if "O" in STAGES:
            with P.scope():
                last = (l == DEPTH - 1)
                emit_outlnF(P, mixT, projT, o_snw.t[l], w_out.t[l], o_lng.t[l], o_lnb.t[l], ones.t, idf.t, w_out,
                            x0 if l == 0 else x1, out if last else x1, None if last else x1T, S)
    return P.finish()


from concourse.bass_utils import run_bass_kernel_spmd

BATCH = 2
_NC = {}


def _host_inputs(p, b):
    perm = in_col_perm()
    CA = moba_consts()
    CR = rec_consts()
    L = DEPTH
    f32 = np.float32
    m = {}
    m["xT0"] = np.ascontiguousarray(p["x"][b].T)
    m["x0"] = np.ascontiguousarray(p["x"][b])
    m["w_in"] = np.ascontiguousarray(p["w_in"][:, :, perm])
    m["w_out"] = np.ascontiguousarray(p["w_out"])
    m["ab"] = CA["ab"]
    m["bi"] = CA["bi"]
    m["cb"] = CA["cb"]
    m["idb"] = CA["idb"]
    m["idf"] = CA["idf"]
    m["dbias"] = dil_consts()
    m["tri"] = CR["tri"]
    m["ones"] = CR["ones"]
    m["mneg"] = CR["mneg"]
    m["mpos"] = CR["mpos"]
    cw5 = [np.concatenate([p["ssm_conv_w"][l].T, p["ssm_conv_b"][l][:, None]], axis=1).astype(f32) for l in range(L)]
    m["s_cwx"] = np.ascontiguousarray(np.stack([np.stack([cw5[l][h * 64:(h + 1) * 64] for h in range(6)]) for l in range(L)]))
    m["s_cwb"] = np.ascontiguousarray(np.stack([np.stack([cw5[l][384 + (h // 3) * 128: 384 + (h // 3 + 1) * 128] for h in range(6)]) for l in range(L)]))
    m["s_cwc"] = np.ascontiguousarray(np.stack([np.stack([cw5[l][640 + (h // 3) * 128: 640 + (h // 3 + 1) * 128] for h in range(6)]) for l in range(L)]))
    m["s_sc"] = np.ascontiguousarray(np.stack([np.stack([np.tile(np.stack([p["ssm_dt_bias"][l, h], p["ssm_A_log"][l, h], p["ssm_D"][l, h]])[None].astype(f32),
                                                                (128, 1)) for h in range(6)]) for l in range(L)]))
    cd5 = [np.concatenate([p["dn_conv_w"][l].T, p["dn_conv_b"][l][:, None]], axis=1).astype(f32) for l in range(L)]
    for nm, off in (("q", 0), ("k", 384), ("v", 768)):
        m["d_cw" + nm] = np.ascontiguousarray(np.stack([np.stack([cd5[l][off + h * 64: off + (h + 1) * 64] for h in range(6)]) for l in range(L)]))
    m["d_sc"] = np.ascontiguousarray(np.stack([np.stack([np.tile(np.stack([p["dn_dt_bias"][l, h], p["dn_A_log"][l, h]])[None].astype(f32), (128, 1))
                                                        for h in range(6)]) for l in range(L)]))
    m["d_nw"] = np.ascontiguousarray(np.stack([np.tile(p["dn_norm_w"][l][None].astype(f32), (128, 1)) for l in range(L)]))
    m["o_snw"] = np.ascontiguousarray(np.stack([p["ssm_norm_w"][l].reshape(3, 128).T.astype(f32) for l in range(L)]))
    m["o_lng"] = np.ascontiguousarray(np.stack([np.tile(p["ln_g"][l][None].astype(f32), (128, 1)) for l in range(L)]))
    m["o_lnb"] = np.ascontiguousarray(np.stack([np.tile(p["ln_b"][l][None].astype(f32), (128, 1)) for l in range(L)]))
    return m


def kernel(**inputs):
    p = {k: np.asarray(v, dtype=np.float32) for k, v in inputs.items()}
    if "nc" not in _NC:
        _NC["nc"] = build_fused()
    in_maps = [_host_inputs(p, b) for b in range(BATCH)]
    res = run_bass_kernel_spmd(_NC["nc"], in_maps, core_ids=list(range(BATCH)))
    _NC["res"] = res
    return np.stack([res.results[b]["out"] for b in range(BATCH)]).astype(np.float32)
```

```python
import contextlib
import numpy as np
import concourse.bass as bass
import concourse.mybir as mybir

F32 = mybir.dt.float32
BF16 = mybir.dt.bfloat16
I32 = mybir.dt.int32
U8 = mybir.dt.uint8
AF = mybir.ActivationFunctionType
ALU = mybir.AluOpType
AX = mybir.AxisListType


class Buf:
    __slots__ = ("name", "t", "writer", "readers", "excl")

    def __init__(self, name, t, excl=False):
        self.name = name
        self.t = t
        self.excl = excl
        self.writer = None
        self.readers = []

    def __getitem__(self, idx):
        return self.t[idx]


class Src:
    def __init__(self, buf, fn):
        self.buf = buf
        self.fn = fn

    def __getitem__(self, k):
        return self.fn(k)


class Prog:
    ENG = ("pe", "dve", "act", "pool", "sp")
    NDMA = 6

    def __init__(self, nc, same_engine_sync=True):
        self.nc = nc
        self.stack = contextlib.ExitStack()
        self.ops = {e: [] for e in self.ENG}
        self.cnt = {e: 0 for e in self.ENG}
        self.sems = {}
        for e in self.ENG:
            self.sems[e] = self.stack.enter_context(nc.semaphore("s_" + e))
        self.dq = {}
        for q in ("sp", "pool", "act"):
            self.dq[q] = {"n": 0, "sems": []}
            for i in range(self.NDMA):
                s = self.stack.enter_context(nc.semaphore(f"d_{q}{i}"))
                self.sems[f"d_{q}{i}"] = s
                self.dq[q]["sems"].append(f"d_{q}{i}")
        self.waited = {e: {} for e in self.ENG}
        self.same = same_engine_sync
        self.nbuf = 0
        self.ARENA_BYTES = 206 * 1024
        self.arena = self.stack.enter_context(nc.sbuf_tensor("arena", [128, self.ARENA_BYTES // 4], F32))
        self.arena_off = 0
        self.banks = [self.stack.enter_context(nc.psum_tensor(f"bank{i}", [128, 512], F32)) for i in range(8)]
        self.banks_used = 0

    @staticmethod
    def _view(base, shape, dt, off_bytes):
        esz = mybir.dt.size(dt)
        n = 1
        for d in shape[1:]:
            n *= d
        t = base if dt == F32 else base.bitcast(dt)
        o = off_bytes // esz
        ap = t[0:shape[0], o:o + n]
        if len(shape) == 3:
            ap = ap.rearrange("p (a b) -> p a b", b=shape[2])
        elif len(shape) == 4:
            ap = ap.rearrange("p (a b c) -> p a b c", b=shape[2], c=shape[3])
        return ap, n * esz

    def sbuf(self, name, shape, dt):
        ap, nb = self._view(self.arena, list(shape), dt, self.arena_off)
        self.arena_off += (nb + 63) // 64 * 64
        assert self.arena_off <= self.ARENA_BYTES, f"SBUF arena overflow at {name}: {self.arena_off}"
        return Buf(name, ap)

    def psum(self, name, shape, dt=F32):
        assert self.banks_used < 8, "out of PSUM banks at " + name
        ap, nb = self._view(self.banks[self.banks_used], list(shape), dt, 0)
        assert nb <= 2048
        self.banks_used += 1
        return Buf(name, ap, excl=True)

    @contextlib.contextmanager
    def scope(self):
        sv = (self.arena_off, self.banks_used)
        try:
            yield
        finally:
            self.barrier()
            self.arena_off, self.banks_used = sv

    def _all_deps(self):
        deps = []
        for q, dq in self.dq.items():
            n = dq["n"]
            for i in range(min(n, self.NDMA)):
                cnt_i = (n - 1 - i) // self.NDMA + 1
                deps.append((dq["sems"][i], 16 * cnt_i))
        for e in self.ENG:
            if self.cnt[e] > 0:
                deps.append((e, self.cnt[e]))
        return deps

    def barrier(self):
        deps = self._all_deps()
        for e in self.ENG:
            self._need(e, [d for d in deps if d[0] != e])

    def dram(self, name, shape, dt, kind=None):
        if kind is None:
            t = self.nc.dram_tensor(name, list(shape), dt)
        else:
            t = self.nc.dram_tensor(name, list(shape), dt, kind=kind)
        return Buf(name, t.ap())

    def alias(self, name, t):
        return Buf(name, t)

    def _need(self, eng, deps):
        w = self.waited[eng]
        for (k, v) in deps:
            if k == eng and (eng == "pe" or not self.same):
                continue
            if w.get(k, 0) >= v:
                continue
            w[k] = v
            sem = self.sems[k]
            self.ops[eng].append(lambda e, sem=sem, v=v: e.wait_ge(sem, v))

    def _deps(self, reads, writes, eng=None):
        deps = []
        for b in reads:
            if b.writer is not None:
                deps.append(b.writer)
            if b.excl:
                deps.extend(r for r in b.readers if r[0] != eng)
        for b in writes:
            if b.writer is not None:
                deps.append(b.writer)
            deps.extend(b.readers)
        return deps

    def _mark(self, reads, writes, tag):
        for b in reads:
            b.readers.append(tag)
            if len(b.readers) > 64:
                m = {}
                for (k, v) in b.readers:
                    if m.get(k, 0) < v:
                        m[k] = v
                b.readers = list(m.items())
        for b in writes:
            b.writer = tag
            b.readers = []

    def op(self, eng, fn, reads=(), writes=()):
        self._need(eng, self._deps(reads, writes, eng))
        self.cnt[eng] += 1
        v = self.cnt[eng]
        sem = self.sems[eng]
        self.ops[eng].append(lambda e, fn=fn, sem=sem: fn(e).then_inc(sem, 1))
        self._mark(reads, writes, (eng, v))

    def dma(self, q, out_ap, in_ap, reads=(), writes=(), **kw):
        dq = self.dq[q]
        n = dq["n"]
        dq["n"] += 1
        key = dq["sems"][n % self.NDMA]
        val = 16 * (n // self.NDMA + 1)
        deps = self._deps(reads, writes)
        if val > 16:
            deps.append((key, val - 16))
        self._need(q, deps)
        sem = self.sems[key]
        self.ops[q].append(lambda e, sem=sem, o=out_ap, i=in_ap, kw=kw: e.dma_start(out=o, in_=i, **kw).then_inc(sem, 16))
        self._mark(reads, writes, (key, val))

    def coll(self, kind, alu, groups, in_b, out_b):
        q = "pool"
        if "cc" not in self.sems:
            self.sems["cc"] = self.stack.enter_context(self.nc.semaphore("s_cc"))
            self.ncc = 0
        self.ncc += 1
        val = self.ncc
        deps = self._deps([in_b], [out_b])
        if val > 1:
            deps.append(("cc", val - 1))
        self._need(q, deps)
        sem = self.sems["cc"]
        self.ops[q].append(lambda e, sem=sem: e.collective_compute(kind, alu, replica_groups=groups, ins=[in_b.t.opt()],
                                                                   outs=[out_b.t.opt()]).then_inc(sem, 1))
        self._mark([in_b], [out_b], ("cc", val))

    def finish(self, final_bufs=()):
        deps = []
        for q, dq in self.dq.items():
            n = dq["n"]
            for i in range(min(n, self.NDMA)):
                cnt_i = (n - 1 - i) // self.NDMA + 1
                deps.append((dq["sems"][i], 16 * cnt_i))
        for e in self.ENG:
            if e != "sp" and self.cnt[e] > 0:
                deps.append((e, self.cnt[e]))
        self._need("sp", deps)
        nc = self.nc
        with nc.Block() as block:
            @block.tensor
            def _(e):
                for f in self.ops["pe"]:
                    f(e)

            @block.vector
            def _(e):
                for f in self.ops["dve"]:
                    f(e)

            @block.scalar
            def _(e):
                for f in self.ops["act"]:
                    f(e)

            @block.gpsimd
            def _(e):
                for f in self.ops["pool"]:
                    f(e)

            @block.sync
            def _(e):
                for f in self.ops["sp"]:
                    f(e)
        self.stack.close()
        return nc


D_MODEL = 1024
IN_COLS = 5906
DEPTH = 2


SEQ = 8192
NEGM = 30000.0


def emit_moba(P, qT_d, kT_d, v_d, gT_d, ab_d, bi_d, cb_d, idb_d, idf_d, y_d, NH, S=SEQ, pfx="mb"):
    NT = S // 128
    NQ = S // 512
    NB = S // 256
    qf = P.sbuf(pfx + "_qf", [64, S], F32)
    kf = P.sbuf(pfx + "_kf", [64, S], F32)
    qa = P.sbuf(pfx + "_qa", [96, S], BF16)
    ka = P.sbuf(pfx + "_ka", [96, S], BF16)
    vf = P.sbuf(pfx + "_vf", [128, NT, 64], F32)
    va = P.sbuf(pfx + "_va", [128, NT, 128], BF16)
    cb = P.sbuf(pfx + "_cb", [128, 4, 512], BF16)
    idb = P.sbuf(pfx + "_idb", [128, 128], BF16)
    idf = P.sbuf(pfx + "_idf", [128, 128], F32)
    ab = P.sbuf(pfx + "_ab", [128, 64], F32)
    km = P.sbuf(pfx + "_km", [64, NB], F32)
    KS = 3
    gsb = [P.sbuf(pfx + f"_gsb{i}", [128, 32], F32) for i in range(KS)]
    mx8 = [P.sbuf(pfx + f"_mx8{i}", [128, 8], F32) for i in range(KS)]
    mb = [P.sbuf(pfx + f"_mbs{i}", [128, 32], F32) for i in range(KS)]
    KM = 4
    pts = [P.sbuf(pfx + f"_pt{i}", [128, 512], BF16) for i in range(KM + 1)]
    gch = [P.sbuf(pfx + f"_gch{i}", [64, 512], F32) for i in range(2)]
    rden = [P.sbuf(pfx + f"_rden{i}", [64, 512], F32) for i in range(2)]
    yo = [P.sbuf(pfx + f"_yo{i}", [64, 512], F32) for i in range(2)]
    acc = [P.psum(pfx + f"_acc{i}", [128, 512]) for i in range(2)]
    sps = [P.psum(pfx + f"_sps{i}", [128, 512]) for i in range(KM + 1)]

    P.dma("sp", cb[:, :, :], cb_d.t.rearrange("k p q -> p k q"), reads=[cb_d], writes=[cb])
    P.dma("sp", idb[:, :], idb_d[:, :], reads=[idb_d], writes=[idb])
    P.dma("sp", idf[:, :], idf_d[:, :], reads=[idf_d], writes=[idf])
    P.dma("sp", ka[64:96, :], bi_d[:, :], reads=[bi_d], writes=[ka])
    P.op("pool", lambda e: e.memset(va[:, :, 64:128], 1.0), writes=[va])
    si = 0
    for h in range(NH):
        P.dma("sp", qf[:, :], qT_d[h], reads=[qT_d.buf], writes=[qf])
        P.dma("act", kf[:, :], kT_d[h], reads=[kT_d.buf], writes=[kf])
        P.dma("pool", vf[:, :, :], v_d[h], reads=[v_d.buf], writes=[vf])
        P.dma("sp", ab[:, :], ab_d[h], reads=[ab_d.buf], writes=[ab])
        P.op("act", lambda e: e.mul(qa[0:64, :], qf[:, :], 0.125), reads=[qf], writes=[qa])
        P.op("pool", lambda e: e.tensor_copy(out=ka[0:64, :], in_=kf[:, :]), reads=[kf], writes=[ka])
        P.op("pool", lambda e: e.tensor_copy(out=va[:, :, 0:64], in_=vf[:, :, :]), reads=[vf], writes=[va])
        P.op("dve", lambda e: e.tensor_reduce(out=km[:, :], in_=kf.t[:, :].rearrange("p (n l) -> p n l", l=256),
                                              axis=AX.X, op=ALU.add), reads=[kf], writes=[km])
        P.op("dve", lambda e: e.tensor_scalar(out=km[:, :], in0=km[:, :], scalar1=1.0 / 256.0, scalar2=None, op0=ALU.mult),
             reads=[km], writes=[km])
        def sel_gen(t):
            qb = t // 2
            sl = t % KS
            bk = sps[sl]
            gp = bk[:, 0:32]
            tp = bk[0:32, 128:256]
            g_, x_, m_ = gsb[sl], mx8[sl], mb[sl]
            P.op("pe", lambda e: e.matmul(gp, lhsT=qf[:, t * 128:(t + 1) * 128], rhs=km[:, :], start=True, stop=True, skip_group_check=True),
                 reads=[qf, km], writes=[bk])
            P.op("pool", lambda e: e.memset(g_[:, :], -1e30), writes=[g_])
            yield
            if qb > 0:
                P.op("dve", lambda e: e.tensor_copy(out=g_[:, 0:qb], in_=bk[:, 0:qb]), reads=[bk], writes=[g_])
            P.op("dve", lambda e: e.max(out=x_[:, :], in_=g_[:, :]), reads=[g_], writes=[x_])
            yield
            P.op("dve", lambda e: e.tensor_scalar(out=m_[:, :], in0=g_[:, :], scalar1=x_[:, 2:3], scalar2=NEGM,
                                                  op0=ALU.is_ge, op1=ALU.mult), reads=[g_, x_], writes=[m_])
            P.op("dve", lambda e: e.memset(m_[:, qb:qb + 1], NEGM), writes=[m_])
            if qb + 1 < 32:
                P.op("dve", lambda e: e.memset(m_[:, qb + 1:32], 0.0), writes=[m_])
            yield
            P.op("dve", lambda e: e.tensor_scalar(out=m_[:, :], in0=m_[:, :], scalar1=-NEGM, scalar2=None, op0=ALU.add),
                 reads=[m_], writes=[m_])
            yield
            P.op("pe", lambda e: e.transpose(out=tp, in_=m_[:, :], identity=idf[:, :]), reads=[m_, idf], writes=[bk])
            yield
            P.op("act", lambda e: e.copy(out=qa[64:96, t * 128:(t + 1) * 128], in_=tp), reads=[bk], writes=[qa])

        run_pipeline([sel_gen(t) for t in range(NT)], KS)
        def att_gen(qt, kt, idx):
            ac = acc[qt % 2]
            nk = 4 * qt + 4
            sp_ = sps[idx % (KM + 1)]
            pt = pts[idx % (KM + 1)]
            diag = kt >= 4 * qt
            P.op("pe", lambda e: e.matmul(sp_[:, :], lhsT=ka[:, kt * 128:(kt + 1) * 128], rhs=qa[:, qt * 512:(qt + 1) * 512],
                                          start=True, stop=not diag), reads=[ka, qa], writes=[sp_])
            if diag:
                P.op("pe", lambda e: e.matmul(sp_[:, :], lhsT=idb[:, :], rhs=cb[:, kt - 4 * qt, :], start=False, stop=True), reads=[idb, cb], writes=[sp_])
            yield
            ri = kt - 4 * qt + 60
            P.op("act", lambda e: e.activation(out=pt[:, :], in_=sp_[:, :], func=AF.Exp, bias=ab[:, ri:ri + 1], scale=1.0),
                 reads=[sp_, ab], writes=[pt])
            yield
            P.op("pe", lambda e: e.matmul(ac[:, :], lhsT=va[:, kt, :], rhs=pt[:, :], start=(kt == 0), stop=(kt == nk - 1)),
                 reads=[va, pt], writes=[ac])
            if kt == nk - 1:
                g = gch[qt % 2]
                rd = rden[qt % 2]
                y = yo[qt % 2]
                P.dma("sp", g[:, :], gT_d[h][:, qt * 512:(qt + 1) * 512], reads=[gT_d.buf], writes=[g])
                P.op("act", lambda e: e.activation(out=g[:, :], in_=g[:, :], func=AF.Silu), reads=[g], writes=[g])
                yield
                P.op("dve", lambda e: e.reciprocal(out=rd[:, :], in_=ac[64:128, :]), reads=[ac], writes=[rd])
                P.op("dve", lambda e: e.tensor_tensor(out=y[:, :], in0=ac[0:64, :], in1=rd[:, :], op=ALU.mult),
                     reads=[ac, rd], writes=[y])
                yield
                P.op("pool", lambda e: e.tensor_tensor(out=y[:, :], in0=y[:, :], in1=g[:, :], op=ALU.mult),
                     reads=[y, g], writes=[y])
                P.dma("pool", y_d[h][:, qt * 512:(qt + 1) * 512], y[:, :], reads=[y], writes=[y_d.buf])

        items = [(qt, kt) for qt in range(NQ) for kt in range(4 * qt + 4)]
        run_pipeline([att_gen(qt, kt, i) for i, (qt, kt) in enumerate(items)], KM)


def moba_consts(S=SEQ):
    import ml_dtypes
    bf = ml_dtypes.bfloat16
    bi = np.zeros((32, S), np.float32)
    for j in range(S // 256):
        bi[j, j * 256:(j + 1) * 256] = 1.0
    cbm = np.zeros((4, 128, 512), np.float32)
    for r in range(4):
        for p in range(128):
            tk = r * 128 + p
            for blk in range(2):
                if tk // 256 == blk:
                    q = np.arange(blk * 256, (blk + 1) * 256)
                    cbm[r, p, q] = np.where(tk > q, -NEGM, 0.0)
    n = 12
    s = 2.0 ** (-8.0 * (np.arange(n) + 1) / n)
    slopes_moba = s[6:]
    ab = np.zeros((6, 128, 64), np.float32)
    for h in range(6):
        for ri in range(64):
            rel = ri - 60
            ab[h, :, ri] = slopes_moba[h] * (128.0 * rel + np.arange(128))
    return dict(bi=bi.astype(bf), cb=cbm.astype(bf), idb=np.eye(128, dtype=np.float32).astype(bf),
                idf=np.eye(128, dtype=np.float32), ab=ab)


DIL_D = (1, 4, 16)
DIL_GROUPS = (0, 1, 2)
DBG = None


def emit_dil(P, qT_d, kT_d, vp_d, gT_d, bias_d, y_d, NH, S=SEQ, pfx="dl"):
    NT = S // 128
    qf = P.sbuf(pfx + "_qf", [64, S], F32)
    kf = qf
    qa = P.sbuf(pfx + "_qa", [64, S], BF16)
    ka = P.sbuf(pfx + "_ka", [64, S], BF16)
    vf = P.sbuf(pfx + "_vf", [128, NT, 64], F32)
    va = [P.sbuf(pfx + f"_va{g}", [128, NT, 128], BF16) for g in range(3)]
    bs = P.sbuf(pfx + "_bias", [128, 6, 512], F32)
    num = P.sbuf(pfx + "_num", [128, S], F32)
    tmp = [P.sbuf(pfx + f"_tmp{i}", [128, 512], F32) for i in range(4)]
    pts = [P.sbuf(pfx + f"_pt{i}", [128, 512], BF16) for i in range(4)]
    gch = [P.sbuf(pfx + f"_gch{i}", [64, 512], F32) for i in range(2)]
    rden = [P.sbuf(pfx + f"_rden{i}", [64, 512], F32) for i in range(2)]
    yo = [P.sbuf(pfx + f"_yo{i}", [64, 512], F32) for i in range(2)]
    sps = [P.psum(pfx + f"_sps{i}", [128, 512]) for i in range(4)]
    ops_ = [P.psum(pfx + f"_ops{i}", [128, 512]) for i in range(3)]
    for g in range(3):
        P.op("pool", lambda e, g=g: e.memset(va[g][:, :, 64:128], 1.0), writes=[va[g]])
    si = 0
    oi = 0
    for h in range(NH):
        P.dma("sp", qf[:, :], qT_d[h], reads=[qT_d.buf], writes=[qf])
        P.op("act", lambda e: e.mul(qa[:, :], qf[:, :], 0.125), reads=[qf], writes=[qa])
        P.dma("sp", kf[:, :], kT_d[h], reads=[kT_d.buf], writes=[kf])
        P.op("pool", lambda e: e.tensor_copy(out=ka[:, :], in_=kf[:, :]), reads=[kf], writes=[ka])
        P.dma("sp", bs[:, :, :], bias_d[h], reads=[bias_d.buf], writes=[bs])
        for g in range(3):
            P.dma("pool", vf.t.rearrange("p (r j) c -> p r j c", r=DIL_D[g]), vp_d[(h, g)], reads=[vp_d.buf], writes=[vf])
            P.op("pool", lambda e, g=g: e.tensor_copy(out=va[g][:, :, 0:64], in_=vf[:, :, :]), reads=[vf], writes=[va[g]])
        def dil_gen(g, d, r, jb, ti):
            NTd = NT // d
            qv = qa.t[:, :].rearrange("p (u d) -> p u d", d=d)
            kv = ka.t[:, :].rearrange("p (u d) -> p u d", d=d)
            nv = num.t[:, :].rearrange("p (u d) -> p u d", d=d)
            sl = ti % 2
            op_ = ops_[ti % 3]
            sp2 = {1: sps[2 * sl], 0: sps[2 * sl + 1]}
            tm2 = {1: tmp[2 * sl], 0: tmp[2 * sl + 1]}
            pt2 = {1: pts[2 * sl], 0: pts[2 * sl + 1]}
            i0s = {1: (1 if jb == 0 else 0), 0: 0}
            for o in (1, 0):
                sp_ = sp2[o]
                for i in range(i0s[o], 4):
                    jq = jb * 4 + i
                    jk = jq - o
                    P.op("pe", lambda e, sp_=sp_, i=i, jq=jq, jk=jk: e.matmul(
                        sp_[:, i * 128:(i + 1) * 128], lhsT=kv[:, jk * 128:(jk + 1) * 128, r],
                        rhs=qv[:, jq * 128:(jq + 1) * 128, r], start=True, stop=True,
                        skip_group_check=True), reads=[ka, qa], writes=[sp_])
            yield
            for o in (1, 0):
                c0 = i0s[o] * 128
                P.op("dve", lambda e, sp_=sp2[o], tm=tm2[o], c0=c0, go=g * 2 + o: e.tensor_tensor(
                    out=tm[:, c0:512], in0=sp_[:, c0:512], in1=bs[:, go, c0:512], op=ALU.add),
                    reads=[sp2[o], bs], writes=[tm2[o]])
            yield
            for o in (1, 0):
                c0 = i0s[o] * 128
                P.op("act", lambda e, tm=tm2[o], pt=pt2[o], c0=c0: e.activation(out=pt[:, c0:512], in_=tm[:, c0:512], func=AF.Exp),
                     reads=[tm2[o]], writes=[pt2[o]])
            yield
            for i in range(4):
                os_ = (0,) if (jb == 0 and i == 0) else (1, 0)
                for o in os_:
                    jk = jb * 4 + i - o
                    P.op("pe", lambda e, pt=pt2[o], i=i, tl=r * NTd + jk, fp=(o == os_[0]), last=(o == 0): e.matmul(
                        op_[:, i * 128:(i + 1) * 128], lhsT=va[g][:, tl, :], rhs=pt[:, i * 128:(i + 1) * 128],
                        start=fp, stop=last, skip_group_check=True), reads=[va[g], pt2[o]], writes=[op_])
            yield
            u0 = jb * 512
            if g == DIL_GROUPS[0]:
                P.op("dve", lambda e: e.tensor_copy(out=nv[:, u0:u0 + 512, r], in_=op_[:, :]),
                     reads=[op_], writes=[num])
            else:
                P.op("dve", lambda e: e.tensor_tensor(
                    out=nv[:, u0:u0 + 512, r], in0=op_[:, :], in1=nv[:, u0:u0 + 512, r], op=ALU.add),
                    reads=[op_, num], writes=[num])

        tasks = []
        for g, d in enumerate(DIL_D):
            if g not in DIL_GROUPS:
                continue
            for r in range(d):
                for jb in range(NT // d // 4):
                    tasks.append((g, d, r, jb))
        run_pipeline([dil_gen(g, d, r, jb, ti) for ti, (g, d, r, jb) in enumerate(tasks)], 2)
        for qt in range(S // 512):
            g_ = gch[qt % 2]
            rd = rden[qt % 2]
            y = yo[qt % 2]
            cs = slice(qt * 512, (qt + 1) * 512)
            P.dma("sp", g_[:, :], gT_d[h][:, cs], reads=[gT_d.buf], writes=[g_])
            P.op("act", lambda e, g_=g_: e.activation(out=g_[:, :], in_=g_[:, :], func=AF.Silu), reads=[g_], writes=[g_])
            P.op("dve", lambda e, rd=rd, cs=cs: e.reciprocal(out=rd[:, :], in_=num[64:128, cs]), reads=[num], writes=[rd])
            P.op("dve", lambda e, y=y, rd=rd, cs=cs: e.tensor_tensor(out=y[:, :], in0=num[0:64, cs], in1=rd[:, :], op=ALU.mult),
                 reads=[num, rd], writes=[y])
            P.op("pool", lambda e, y=y, g_=g_: e.tensor_tensor(out=y[:, :], in0=y[:, :], in1=g_[:, :], op=ALU.mult),
                 reads=[y, g_], writes=[y])
            P.dma("pool", y_d[h][:, cs], y[:, :], reads=[y], writes=[y_d.buf])


def dil_consts():
    n = 12
    s = 2.0 ** (-8.0 * (np.arange(n) + 1) / n)
    slopes = s[:6]
    bias = np.zeros((6, 3, 2, 128, 512), np.float32)
    p = np.arange(128)[:, None]
    x = np.arange(128)[None, :]
    for h in range(6):
        for g, d in enumerate(DIL_D):
            for o in range(2):
                nn = (x - p) + 128 * o
                b = np.where((nn >= 0) & (nn <= 128), -slopes[h] * d * nn, -NEGM).astype(np.float32)
                bias[h, g, o] = np.tile(b, (1, 4))
    return bias


def dil_perm(S=SEQ):
    out = []
    for d in DIL_D:
        u = np.arange(S // d)
        out.append(np.concatenate([u * d + r for r in range(d)]))
    return out


def emit_conv_silu(P, src_d, w_sb, C, S, zp, acc, outs, q="sp", src_buf=None):
    P.dma(q, zp[0:C, 3:S + 3], src_d, reads=([src_buf] if src_buf is not None else []), writes=[zp])
    H = S // 2
    for hh in range(2):
        a0, a1 = hh * H, (hh + 1) * H
        P.op("dve", lambda e, a0=a0, a1=a1: e.tensor_scalar(out=acc[0:C, a0:a1], in0=zp[0:C, 3 + a0:3 + a1], scalar1=w_sb[0:C, 3:4],
                                                          scalar2=w_sb[0:C, 4:5], op0=ALU.mult, op1=ALU.add), reads=[zp, w_sb], writes=[acc])
        for k in range(3):
            P.op("dve", lambda e, k=k, a0=a0, a1=a1: e.scalar_tensor_tensor(out=acc[0:C, a0:a1], in0=zp[0:C, k + a0:k + a1], scalar=w_sb[0:C, k:k + 1],
                                                                          in1=acc[0:C, a0:a1], op0=ALU.mult, op1=ALU.add),
                 reads=[zp, w_sb, acc], writes=[acc])
    for (ob, oap) in outs:
        P.op("act", lambda e, oap=oap: e.activation(out=oap, in_=acc[0:C, :], func=AF.Silu), reads=[acc], writes=[ob])


def emit_softplus(P, eng_dve, out_b, out_ap, x_b, x_ap, t1_b, t1_ap, shape_p):
    P.op("act", lambda e: e.activation(out=t1_ap, in_=x_ap, func=AF.Abs), reads=[x_b], writes=[t1_b])
    P.op("act", lambda e: e.activation(out=t1_ap, in_=t1_ap, func=AF.Exp, scale=-1.0), reads=[t1_b], writes=[t1_b])
    P.op("act", lambda e: e.activation(out=t1_ap, in_=t1_ap, func=AF.Ln, bias=1.0, scale=1.0), reads=[t1_b], writes=[t1_b])
    P.op("dve", lambda e: e.scalar_tensor_tensor(out=out_ap, in0=x_ap, scalar=0.0, in1=t1_ap, op0=ALU.max, op1=ALU.add),
         reads=[x_b, t1_b], writes=[out_b])


SSD_NCH = 10 ** 9
SSD_STOP = 99


def emit_ssd(P, xpre_d, bpre_d, cpre_d, cwx_d, cwb_d, cwc_d, dtc_d, sc_d, tri_d, ones_d, idf_d, idb_d, mneg_d, y_d, NH, S=SEQ, pfx="sd"):
    NT = S // 128
    zp = P.sbuf(pfx + "_zp", [128, S + 3], F32)
    acc = P.sbuf(pfx + "_acc", [128, S], F32)
    xsT = P.sbuf(pfx + "_xsT", [64, S], F32)
    BT = P.sbuf(pfx + "_BT", [128, S], BF16)
    CT = P.sbuf(pfx + "_CT", [128, S], BF16)
    yac = P.sbuf(pfx + "_yac", [64, S], F32)
    cwx = P.sbuf(pfx + "_cwx", [64, 5], F32)
    cwb = P.sbuf(pfx + "_cwb", [128, 5], F32)
    cwc = P.sbuf(pfx + "_cwc", [128, 5], F32)
    sc = P.sbuf(pfx + "_sc", [128, 3], F32)
    tri = P.sbuf(pfx + "_tri", [128, 128], F32)
    ones = P.sbuf(pfx + "_ones", [128, 128], F32)
    idf = P.sbuf(pfx + "_idf", [128, 128], F32)
    idb = P.sbuf(pfx + "_idb", [128, 128], BF16)
    mneg = P.sbuf(pfx + "_mneg", [128, 128], BF16)
    dtr = P.sbuf(pfx + "_dtr", [128, NT], F32)
    dtall = P.sbuf(pfx + "_dtall", [128, NT, 6], F32)
    t1 = P.sbuf(pfx + "_t1", [128, NT], F32)
    dt = P.sbuf(pfx + "_dt", [128, NT], F32)
    a_ = P.sbuf(pfx + "_a", [128, NT], F32)
    acum = P.sbuf(pfx + "_acum", [128, NT], F32)
    nacum = P.sbuf(pfx + "_nacum", [128, NT], F32)
    dB = P.sbuf(pfx + "_dB", [128, NT], F32)
    dlast = P.sbuf(pfx + "_dlast", [128, NT], F32)
    Aneg = P.sbuf(pfx + "_Aneg", [128, 1], F32)
    SK = 5
    dg = [P.sbuf(pfx + f"_dg{i}", [128, 128], F32) for i in range(SK)]
    EB = [P.sbuf(pfx + f"_EB{i}", [128, 128], F32) for i in range(SK)]
    LmT = [P.sbuf(pfx + f"_LmT{i}", [128, 128], F32) for i in range(SK)]
    SLT = [P.sbuf(pfx + f"_SLT{i}", [128, 128], BF16) for i in range(SK)]
    CgT = [P.sbuf(pfx + f"_CgT{i}", [128, 128], BF16) for i in range(SK)]
    X = [P.sbuf(pfx + f"_X{i}", [128, 64], BF16) for i in range(SK)]
    Bd = [P.sbuf(pfx + f"_Bd{i}", [128, 128], BF16) for i in range(SK)]
    hf = P.sbuf(pfx + "_hf", [128, 64], F32)
    hb = [P.sbuf(pfx + f"_hb{i}", [128, 64], BF16) for i in range(2)]
    pk_ = [P.psum(pfx + f"_pk{i}", [128, 512]) for i in range(SK)]
    pbf = P.psum(pfx + "_pbf", [128, 1024], BF16)
    p_gb = [P.psum(pfx + f"_pgb{i}", [128, 128]) for i in range(1)]
    p_misc = p_gb[0]

    for (sb, d_) in ((tri, tri_d), (ones, ones_d), (idf, idf_d), (idb, idb_d), (mneg, mneg_d)):
        P.dma("sp", sb[:, :], d_[:, :], reads=[d_], writes=[sb])
    P.op("pool", lambda e: e.memset(zp[:, 0:3], 0.0), writes=[zp])
    for h in range(NH):
        P.dma("sp", cwx[:, :], cwx_d[h], reads=[cwx_d.buf], writes=[cwx])
        P.dma("sp", cwb[:, :], cwb_d[h], reads=[cwb_d.buf], writes=[cwb])
        P.dma("sp", cwc[:, :], cwc_d[h], reads=[cwc_d.buf], writes=[cwc])
        P.dma("sp", sc[:, :], sc_d[h], reads=[sc_d.buf], writes=[sc])
        if h == 0:
            P.dma("act", dtall[:, :, :], dtc_d[0], reads=[dtc_d.buf], writes=[dtall])
        emit_conv_silu(P, xpre_d[h], cwx, 64, S, zp, acc, [(xsT, xsT[:, :])], src_buf=xpre_d.buf)
        emit_conv_silu(P, bpre_d[h], cwb, 128, S, zp, acc, [(BT, BT[:, :])], src_buf=bpre_d.buf)
        emit_conv_silu(P, cpre_d[h], cwc, 128, S, zp, acc, [(CT, CT[:, :])], src_buf=cpre_d.buf)
        P.op("dve", lambda e, h=h: e.tensor_scalar(out=dtr[:, :], in0=dtall[:, :, h], scalar1=sc[:, 0:1], scalar2=None, op0=ALU.add), reads=[dtall, sc], writes=[dtr])
        emit_softplus(P, "dve", dt, dt[:, :], dtr, dtr[:, :], t1, t1[:, :], 128)
        P.op("act", lambda e: e.activation(out=Aneg[:, :], in_=sc[:, 1:2], func=AF.Exp), reads=[sc], writes=[Aneg])
        P.op("dve", lambda e: e.tensor_scalar(out=Aneg[:, :], in0=Aneg[:, :], scalar1=-1.0, scalar2=None, op0=ALU.mult), reads=[Aneg], writes=[Aneg])
        P.op("dve", lambda e: e.tensor_scalar(out=a_[:, :], in0=dt[:, :], scalar1=Aneg[:, 0:1], scalar2=None, op0=ALU.mult), reads=[dt, Aneg], writes=[a_])
        P.op("pe", lambda e: e.matmul(p_misc[:, 0:NT], lhsT=tri[:, :], rhs=a_[:, :], start=True, stop=True), reads=[tri, a_], writes=[p_misc])
        P.op("dve", lambda e: e.tensor_copy(out=acum[:, :], in_=p_misc[:, 0:NT]), reads=[p_misc], writes=[acum])
        P.op("dve", lambda e: e.tensor_scalar(out=nacum[:, :], in0=acum[:, :], scalar1=-1.0, scalar2=None, op0=ALU.mult), reads=[acum], writes=[nacum])
        P.op("pe", lambda e: e.matmul(p_misc[:, 0:NT], lhsT=ones[:, :], rhs=a_[:, :], start=True, stop=True), reads=[ones, a_], writes=[p_misc])
        P.op("act", lambda e: e.activation(out=dlast[:, :], in_=p_misc[:, 0:NT], func=AF.Exp), reads=[p_misc], writes=[dlast])
        P.op("dve", lambda e: e.tensor_tensor(out=dB[:, :], in0=p_misc[:, 0:NT], in1=acum[:, :], op=ALU.subtract), reads=[p_misc, acum], writes=[dB])
        P.op("act", lambda e: e.activation(out=dB[:, :], in_=dB[:, :], func=AF.Exp), reads=[dB], writes=[dB])
        P.op("dve", lambda e: e.memset(hf[:, :], 0.0), writes=[hf])
        P.op("pool", lambda e: e.memset(hb[0][:, :], 0.0), writes=[hb[0]])
        seq_state = {"next": 0}

        def chunk_gen(c):
            cs = slice(c * 128, (c + 1) * 128)
            i2 = c % SK
            bk = pk_[i2]
            R = [slice(0, 128), slice(128, 256), slice(256, 384), slice(384, 512)]
            b0 = i2 * 128
            P.op("pool", lambda e: e.tensor_scalar(out=dg[i2][:, :], in0=idf[:, :], scalar1=acum[:, c:c + 1], scalar2=None, op0=ALU.mult),
                 reads=[idf, acum], writes=[dg[i2]])
            yield
            P.op("pe", lambda e: e.matmul(bk[:, R[0]], lhsT=ones[:, :], rhs=dg[i2][:, :], start=True, stop=False, skip_group_check=True), reads=[ones, dg[i2]], writes=[bk])
            P.op("pe", lambda e: e.matmul(bk[:, R[0]], lhsT=idb[:, :], rhs=mneg[:, :], start=False, stop=True, skip_group_check=True), reads=[idb, mneg], writes=[bk])
            P.op("pe", lambda e: e.matmul(bk[:, R[1]], lhsT=ones[:, :], rhs=dg[i2][:, :], start=True, stop=True, skip_group_check=True), reads=[ones, dg[i2]], writes=[bk])
            P.op("pe", lambda e: e.matmul(bk[:, R[2]], lhsT=BT[:, cs], rhs=CT[:, cs], start=True, stop=True, skip_group_check=True), reads=[BT, CT], writes=[bk])
            P.op("pe", lambda e: e.transpose(out=bk[:, 384:448], in_=xsT[:, cs], identity=idf[0:64, 0:64]), reads=[xsT, idf], writes=[bk])
            P.op("pe", lambda e: e.transpose(out=pbf[:, b0:b0 + 128], in_=BT[:, cs], identity=idb[:, :]), reads=[BT, idb], writes=[pbf])
            yield
            P.op("act", lambda e: e.activation(out=EB[i2][:, :], in_=bk[:, R[1]], func=AF.Exp), reads=[bk], writes=[EB[i2]])
            P.op("act", lambda e: e.activation(out=LmT[i2][:, :], in_=bk[:, R[0]], func=AF.Exp, bias=nacum[:, c:c + 1], scale=1.0),
                 reads=[bk, nacum], writes=[LmT[i2]])
            yield
            P.op("dve", lambda e: e.tensor_tensor(out=SLT[i2][:, :], in0=bk[:, R[2]], in1=LmT[i2][:, :], op=ALU.mult), reads=[bk, LmT[i2]], writes=[SLT[i2]])
            P.op("dve", lambda e: e.tensor_scalar(out=X[i2][:, :], in0=bk[:, 384:448], scalar1=dt[:, c:c + 1], scalar2=None, op0=ALU.mult),
                 reads=[bk, dt], writes=[X[i2]])
            P.op("dve", lambda e: e.tensor_scalar(out=Bd[i2][:, :], in0=pbf[:, b0:b0 + 128], scalar1=dB[:, c:c + 1], scalar2=None, op0=ALU.mult),
                 reads=[pbf, dB], writes=[Bd[i2]])
            P.op("pool", lambda e: e.tensor_tensor(out=CgT[i2][:, :], in0=CT[:, cs], in1=EB[i2][:, :], op=ALU.mult), reads=[CT, EB[i2]], writes=[CgT[i2]])
            yield
            while seq_state["next"] != c:
                yield
            hcur = hb[c % 2]
            hnxt = hb[(c + 1) % 2]
            P.op("pe", lambda e: e.matmul(bk[0:64, R[0]], lhsT=X[i2][:, :], rhs=SLT[i2][:, :], start=True, stop=False, skip_group_check=True), reads=[X[i2], SLT[i2]], writes=[bk])
            P.op("pe", lambda e: e.matmul(bk[0:64, R[0]], lhsT=hcur[:, :], rhs=CgT[i2][:, :], start=False, stop=True, skip_group_check=True), reads=[hcur, CgT[i2]], writes=[bk])
            P.op("pe", lambda e: e.matmul(bk[:, 128:192], lhsT=Bd[i2][:, :], rhs=X[i2][:, :], start=True, stop=True, skip_group_check=True), reads=[Bd[i2], X[i2]], writes=[bk])
            seq_state["next"] = c + 1
            yield
            P.op("dve", lambda e: e.scalar_tensor_tensor(out=hf[:, :], in0=hf[:, :], scalar=dlast[:, c:c + 1], in1=bk[:, 128:192], op0=ALU.mult, op1=ALU.add),
                 reads=[hf, dlast, bk], writes=[hf])
            P.op("act", lambda e: e.copy(out=hnxt[:, :], in_=hf[:, :]), reads=[hf], writes=[hnxt])
            P.op("dve", lambda e: e.scalar_tensor_tensor(out=yac[:, cs], in0=xsT[:, cs], scalar=sc[0:64, 2:3], in1=bk[0:64, R[0]], op0=ALU.mult, op1=ALU.add),
                 reads=[xsT, sc, bk], writes=[yac])

        run_pipeline([chunk_gen(c) for c in range(min(NT, SSD_NCH))], SK)
        P.dma("pool", y_d[h], yac[:, :], reads=[yac], writes=[y_d.buf])
        if NT % 2 == 1 or True:
            P.op("pool", lambda e: e.memset(hb[0][:, :], 0.0), writes=[hb[0]])


def rec_consts():
    import ml_dtypes
    bf = ml_dtypes.bfloat16
    s1 = np.arange(128)[:, None]
    s2 = np.arange(128)[None, :]
    tri = (s1 <= s2).astype(np.float32)
    mneg = np.where(s2 < s1, -NEGM, 0.0).astype(np.float32)
    mpos = np.where(s2 >= s1, NEGM, 0.0).astype(np.float32)
    return dict(tri=tri, ones=np.ones((128, 128), np.float32), idf=np.eye(128, dtype=np.float32),
                idb=np.eye(128, dtype=np.float32).astype(bf), mneg=mneg.astype(bf), mpos=mpos.astype(bf))


GDN_NCH = 10 ** 9
GDN_K = 5


def run_pipeline(gens, K):
    active = []
    it = iter(gens)
    pending = True
    while True:
        if pending and len(active) < K:
            g = next(it, None)
            if g is None:
                pending = False
            else:
                active.append(g)
        if not active:
            if not pending:
                break
            continue
        for g in list(active):
            try:
                next(g)
            except StopIteration:
                active.remove(g)


def emit_gdn(P, qpre_d, kpre_d, vpre_d, cwq_d, cwk_d, cwv_d, bcol_d, acol_d, sc_d, gate_d, nw_d,
             tri_d, ones_d, idf_d, idb_d, mneg_d, mpos_d, y_d, NH, S=SEQ, pfx="gd"):
    NT = S // 128
    zp = P.sbuf(pfx + "_zp", [128, S + 3], F32)
    acc = P.sbuf(pfx + "_acc", [128, S], F32)
    qn = P.sbuf(pfx + "_qn", [64, S], BF16)
    kn = P.sbuf(pfx + "_kn", [64, S], BF16)
    vT = P.sbuf(pfx + "_vT", [64, S], BF16)
    gt = P.sbuf(pfx + "_gt", [128, NT, 64], F32)
    yall = [P.sbuf(pfx + f"_yall{i}", [64, 2048], F32) for i in range(2)]
    ytm = [P.sbuf(pfx + f"_ytm{i}", [128, 64], F32) for i in range(GDN_K)]
    cw = [P.sbuf(pfx + f"_cw{i}", [64, 5], F32) for i in range(3)]
    sc = P.sbuf(pfx + "_sc", [128, 2], F32)
    nw = P.sbuf(pfx + "_nw", [128, 64], F32)
    tri = P.sbuf(pfx + "_tri", [128, 128], F32)
    ones = P.sbuf(pfx + "_ones", [128, 128], F32)
    idf = P.sbuf(pfx + "_idf", [128, 128], F32)
    idb = P.sbuf(pfx + "_idb", [128, 128], BF16)
    mneg = P.sbuf(pfx + "_mneg", [128, 128], BF16)
    mpos = P.sbuf(pfx + "_mpos", [128, 128], BF16)
    col = {n: P.sbuf(pfx + "_c_" + n, [128, NT], F32) for n in
           ("b", "a", "t1", "beta", "nbeta", "g", "gcum", "ngcum", "egc", "bege", "dk", "dlast")}
    Aneg = P.sbuf(pfx + "_Aneg", [128, 1], F32)
    ball = P.sbuf(pfx + "_ball", [128, NT, 6], F32)
    aall = P.sbuf(pfx + "_aall", [128, NT, 6], F32)
    rt = [P.sbuf(pfx + f"_rt{i}", [64, 512], F32) for i in range(2)]
    eps_t = P.sbuf(pfx + "_eps", [128, 1], F32)
    P.op("dve", lambda e: e.memset(eps_t[:, :], 1e-6), writes=[eps_t])

    def f32t(n, k=2):
        return [P.sbuf(pfx + f"_{n}{i}", [128, 128], F32) for i in range(k)]

    def bft(n, shape, k=2):
        return [P.sbuf(pfx + f"_{n}{i}", shape, BF16) for i in range(k)]
    NB = GDN_K
    dg = f32t("dg", NB)
    EB = f32t("EB", NB)
    dec = f32t("dec", NB)
    decT = f32t("decT", NB)
    Ys = [f32t(f"Y{j}_", 2) for j in range(NB)]
    YTs = [f32t(f"YT{j}_", 2) for j in range(NB)]
    PTs = [f32t(f"PT{j}_", 2) for j in range(NB)]
    TTb = bft("TTb", [128, 128], NB)
    attnT = bft("attnT", [128, 128], NB)
    qgT = bft("qgT", [64, 128], NB)
    kbg = bft("kbg", [128, 64], NB)
    kd = bft("kd", [128, 64], NB)
    vb = bft("vb", [128, 64], NB)
    wT = bft("wT", [64, 128], NB)
    u_sb = [P.sbuf(pfx + f"_u{i}", [128, 64], F32) for i in range(NB)]
    vnew = bft("vnew", [128, 64], NB)
    Sf = P.sbuf(pfx + "_Sf", [64, 64], F32)
    Sb = bft("Sb", [64, 64])
    junk = [P.sbuf(pfx + f"_junk{i}", [128, 64], F32) for i in range(NB)]
    ssq = [P.sbuf(pfx + f"_ssq{i}", [128, 1], F32) for i in range(NB)]
    nwg = [P.sbuf(pfx + f"_nwg{i}", [128, 64], F32) for i in range(NB)]
    pa = [P.psum(pfx + f"_pa{i}", [128, 512]) for i in range(7)]
    pbs = [P.psum(pfx + f"_pb{i}", [128, 1024], BF16) for i in range(1)]
    pg = [pa[5], pa[6]]
    pai = [0]

    def nxt_pa():
        pai[0] += 1
        return pa[5 + pai[0] % 2]
    pci = [0]

    def nxt_pc():
        pci[0] += 1
        return pc[pci[0] % 2]

    for (sb, d_) in ((tri, tri_d), (ones, ones_d), (idf, idf_d), (idb, idb_d), (mneg, mneg_d), (mpos, mpos_d)):
        P.dma("sp", sb[:, :], d_[:, :], reads=[d_], writes=[sb])
    P.op("pool", lambda e: e.memset(zp[:, 0:3], 0.0), writes=[zp])
    for h in range(NH):
        for i, d_ in enumerate((cwq_d, cwk_d, cwv_d)):
            P.dma("sp", cw[i][:, :], d_[h], reads=[d_.buf], writes=[cw[i]])
        P.dma("sp", sc[:, :], sc_d[h], reads=[sc_d.buf], writes=[sc])
        P.dma("sp", nw[:, :], nw_d[h], reads=[nw_d.buf], writes=[nw])
        if h == 0:
            P.dma("sp", ball[:, :, :], bcol_d[0], reads=[bcol_d.buf], writes=[ball])
            P.dma("sp", aall[:, :, :], acol_d[0], reads=[acol_d.buf], writes=[aall])
        P.op("pool", lambda e, h=h: e.tensor_copy(out=col["b"][:, :], in_=ball[:, :, h]), reads=[ball], writes=[col["b"]])
        P.op("pool", lambda e, h=h: e.tensor_copy(out=col["a"][:, :], in_=aall[:, :, h]), reads=[aall], writes=[col["a"]])
        P.dma("act", gt[:, :, :], gate_d[h], reads=[gate_d.buf], writes=[gt])
        P.op("act", lambda e: e.activation(out=gt[:, :, :], in_=gt[:, :, :], func=AF.Silu), reads=[gt], writes=[gt])
        for which, (src_d, outb) in enumerate(((qpre_d, qn), (kpre_d, kn))):
            emit_conv_silu(P, src_d[h], cw[which], 64, S, zp, acc, [(acc, acc[0:64, :])], src_buf=src_d.buf)
            P.op("act", lambda e: e.activation(out=zp[0:64, 3:S + 3], in_=acc[0:64, :], func=AF.Square), reads=[acc], writes=[zp])
            for j in range(S // 512):
                cs = slice(j * 512, (j + 1) * 512)
                ps = nxt_pa()
                r_ = rt[j % 2]
                P.op("pe", lambda e, ps=ps, j=j: e.matmul(ps[0:64, :], lhsT=ones[0:64, 0:64], rhs=zp[0:64, 3 + j * 512:3 + (j + 1) * 512],
                                                         start=True, stop=True), reads=[ones, zp], writes=[ps])
                P.op("act", lambda e, ps=ps, r_=r_: e.activation(out=r_[:, :], in_=ps[0:64, :], func=AF.Sqrt, bias=eps_t[0:64, 0:1], scale=1.0),
                     reads=[ps, eps_t], writes=[r_])
                P.op("dve", lambda e, r_=r_: e.reciprocal(out=r_[:, :], in_=r_[:, :]), reads=[r_], writes=[r_])
                P.op("dve", lambda e, r_=r_, cs=cs, outb=outb, sc_=(0.125 if which == 0 else 1.0): e.scalar_tensor_tensor(
                    out=outb[:, cs], in0=acc[0:64, cs], scalar=sc_, in1=r_[:, :], op0=ALU.mult, op1=ALU.mult),
                    reads=[acc, r_], writes=[outb])
        emit_conv_silu(P, vpre_d[h], cw[2], 64, S, zp, acc, [(vT, vT[:, :])], src_buf=vpre_d.buf)
        c_ = col
        P.op("act", lambda e: e.activation(out=c_["beta"][:, :], in_=c_["b"][:, :], func=AF.Sigmoid), reads=[c_["b"]], writes=[c_["beta"]])
        P.op("dve", lambda e: e.tensor_scalar(out=c_["nbeta"][:, :], in0=c_["beta"][:, :], scalar1=-1.0, scalar2=None, op0=ALU.mult),
             reads=[c_["beta"]], writes=[c_["nbeta"]])
        P.op("dve", lambda e: e.tensor_scalar(out=c_["a"][:, :], in0=c_["a"][:, :], scalar1=sc[:, 0:1], scalar2=None, op0=ALU.add),
             reads=[c_["a"], sc], writes=[c_["a"]])
        emit_softplus(P, "dve", c_["g"], c_["g"][:, :], c_["a"], c_["a"][:, :], c_["t1"], c_["t1"][:, :], 128)
        P.op("act", lambda e: e.activation(out=Aneg[:, :], in_=sc[:, 1:2], func=AF.Exp), reads=[sc], writes=[Aneg])
        P.op("dve", lambda e: e.tensor_scalar(out=Aneg[:, :], in0=Aneg[:, :], scalar1=-1.0, scalar2=None, op0=ALU.mult), reads=[Aneg], writes=[Aneg])
        P.op("dve", lambda e: e.tensor_scalar(out=c_["g"][:, :], in0=c_["g"][:, :], scalar1=Aneg[:, 0:1], scalar2=None, op0=ALU.mult),
             reads=[c_["g"], Aneg], writes=[c_["g"]])
        pm = pg[0]
        P.op("pe", lambda e: e.matmul(pm[:, 0:NT], lhsT=tri[:, :], rhs=c_["g"][:, :], start=True, stop=True), reads=[tri, c_["g"]], writes=[pm])
        P.op("dve", lambda e: e.tensor_copy(out=c_["gcum"][:, :], in_=pm[:, 0:NT]), reads=[pm], writes=[c_["gcum"]])
        P.op("dve", lambda e: e.tensor_scalar(out=c_["ngcum"][:, :], in0=c_["gcum"][:, :], scalar1=-1.0, scalar2=None, op0=ALU.mult),
             reads=[c_["gcum"]], writes=[c_["ngcum"]])
        P.op("act", lambda e: e.activation(out=c_["egc"][:, :], in_=c_["gcum"][:, :], func=AF.Exp), reads=[c_["gcum"]], writes=[c_["egc"]])
        P.op("dve", lambda e: e.tensor_tensor(out=c_["bege"][:, :], in0=c_["egc"][:, :], in1=c_["beta"][:, :], op=ALU.mult),
             reads=[c_["egc"], c_["beta"]], writes=[c_["bege"]])
        P.op("pe", lambda e: e.matmul(pm[:, 0:NT], lhsT=ones[:, :], rhs=c_["g"][:, :], start=True, stop=True), reads=[ones, c_["g"]], writes=[pm])
        P.op("act", lambda e: e.activation(out=c_["dlast"][:, :], in_=pm[:, 0:NT], func=AF.Exp), reads=[pm], writes=[c_["dlast"]])
        P.op("dve", lambda e: e.tensor_tensor(out=c_["dk"][:, :], in0=pm[:, 0:NT], in1=c_["gcum"][:, :], op=ALU.subtract),
             reads=[pm, c_["gcum"]], writes=[c_["dk"]])
        P.op("act", lambda e: e.activation(out=c_["dk"][:, :], in_=c_["dk"][:, :], func=AF.Exp), reads=[c_["dk"]], writes=[c_["dk"]])
        P.op("dve", lambda e: e.memset(Sf[:, :], 0.0), writes=[Sf])
        P.op("pool", lambda e: e.memset(Sb[0][:, :], 0.0), writes=[Sb[0]])
        seq_state = {"next": 0}

        def chunk_gen(c):
            cs = slice(c * 128, (c + 1) * 128)
            i2 = c % NB
            slot = c % GDN_K
            bA = pa[slot]
            bB = bA
            rA = [slice(0, 128), slice(128, 256), slice(256, 384), slice(384, 512)]
            Y, YT, PT = Ys[i2], YTs[i2], PTs[i2]
            P.op("pool", lambda e, c=c, i2=i2: e.tensor_scalar(out=dg[i2][:, :], in0=idf[:, :], scalar1=c_["gcum"][:, c:c + 1], scalar2=None, op0=ALU.mult),
                 reads=[idf, c_["gcum"]], writes=[dg[i2]])
            yield
            P.op("pe", lambda e, i2=i2: e.matmul(bA[:, rA[0]], lhsT=ones[:, :], rhs=dg[i2][:, :], start=True, stop=False, skip_group_check=True), reads=[ones, dg[i2]], writes=[bA])
            P.op("pe", lambda e: e.matmul(bA[:, rA[0]], lhsT=idb[:, :], rhs=mpos[:, :], start=False, stop=True, skip_group_check=True), reads=[idb, mpos], writes=[bA])
            P.op("pe", lambda e, i2=i2: e.matmul(bA[:, rA[1]], lhsT=ones[:, :], rhs=dg[i2][:, :], start=True, stop=False, skip_group_check=True), reads=[ones, dg[i2]], writes=[bA])
            P.op("pe", lambda e: e.matmul(bA[:, rA[1]], lhsT=idb[:, :], rhs=mneg[:, :], start=False, stop=True, skip_group_check=True), reads=[idb, mneg], writes=[bA])
            P.op("pe", lambda e, i2=i2: e.matmul(bA[:, rA[2]], lhsT=ones[:, :], rhs=dg[i2][:, :], start=True, stop=True, skip_group_check=True), reads=[ones, dg[i2]], writes=[bA])
            P.op("pe", lambda e, cs=cs: e.matmul(bA[:, rA[3]], lhsT=kn[:, cs], rhs=kn[:, cs], start=True, stop=True, skip_group_check=True), reads=[kn], writes=[bA])
            yield
            P.op("act", lambda e, i2=i2: e.activation(out=EB[i2][0:64, :], in_=bA[0:64, rA[2]], func=AF.Exp), reads=[bA], writes=[EB[i2]])
            P.op("act", lambda e, i2=i2, c=c: e.activation(out=dec[i2][:, :], in_=bA[:, rA[0]], func=AF.Exp, bias=c_["gcum"][:, c:c + 1], scale=-1.0),
                 reads=[bA, c_["gcum"]], writes=[dec[i2]])
            P.op("act", lambda e, i2=i2, c=c: e.activation(out=decT[i2][:, :], in_=bA[:, rA[1]], func=AF.Exp, bias=c_["ngcum"][:, c:c + 1], scale=1.0),
                 reads=[bA, c_["ngcum"]], writes=[decT[i2]])
            yield
            P.op("pool", lambda e, i2=i2, cs=cs: e.tensor_tensor(out=qgT[i2][:, :], in0=qn[:, cs], in1=EB[i2][0:64, :], op=ALU.mult),
                 reads=[qn, EB[i2]], writes=[qgT[i2]])
            yield
            Yc, YTc, PTc = Y[0], YT[0], PT[0]
            P.op("dve", lambda e, i2=i2, c=c, Yc=Yc: e.scalar_tensor_tensor(out=Yc[:, :], in0=bA[:, rA[3]], scalar=c_["nbeta"][:, c:c + 1],
                                                                          in1=dec[i2][:, :], op0=ALU.mult, op1=ALU.mult),
                 reads=[bA, c_["nbeta"], dec[i2]], writes=[Yc])
            yield
            P.op("pe", lambda e, cs=cs: e.matmul(bB[:, rA[0]], lhsT=kn[:, cs], rhs=qn[:, cs], start=True, stop=True, skip_group_check=True), reads=[kn, qn], writes=[bB])
            P.op("pe", lambda e, Yc=Yc: e.transpose(out=bB[:, rA[1]], in_=Yc[:, :], identity=idf[:, :]), reads=[Yc, idf], writes=[bB])
            pkt = pbs[0]
            k0 = slot * 128
            P.op("pe", lambda e, cs=cs, pkt=pkt, k0=k0: e.transpose(out=pkt[:, k0:k0 + 64], in_=kn[:, cs], identity=idb[0:64, 0:64]), reads=[kn, idb], writes=[pkt])
            P.op("pe", lambda e, cs=cs, pkt=pkt, k0=k0: e.transpose(out=pkt[:, k0 + 64:k0 + 128], in_=vT[:, cs], identity=idb[0:64, 0:64]), reads=[vT, idb], writes=[pkt])
            yield
            P.op("dve", lambda e, i2=i2: e.tensor_tensor(out=attnT[i2][:, :], in0=bB[:, rA[0]], in1=decT[i2][:, :], op=ALU.mult),
                 reads=[bB, decT[i2]], writes=[attnT[i2]])
            P.op("act", lambda e, YTc=YTc: e.copy(out=YTc[:, :], in_=bB[:, rA[1]]), reads=[bB], writes=[YTc])
            P.op("act", lambda e, PTc=PTc: e.activation(out=PTc[:, :], in_=bB[:, rA[1]], func=AF.Identity), reads=[bB], writes=[PTc])
            P.op("dve", lambda e, i2=i2, c=c, pkt=pkt: e.tensor_scalar(out=kbg[i2][:, :], in0=pkt[:, slot * 128:slot * 128 + 64], scalar1=c_["bege"][:, c:c + 1], scalar2=None, op0=ALU.mult),
                 reads=[pkt, c_["bege"]], writes=[kbg[i2]])
            P.op("dve", lambda e, i2=i2, c=c, pkt=pkt: e.tensor_scalar(out=kd[i2][:, :], in0=pkt[:, slot * 128:slot * 128 + 64], scalar1=c_["dk"][:, c:c + 1], scalar2=None, op0=ALU.mult),
                 reads=[pkt, c_["dk"]], writes=[kd[i2]])
            P.op("dve", lambda e, i2=i2, c=c, pkt=pkt: e.tensor_scalar(out=vb[i2][:, :], in0=pkt[:, slot * 128 + 64:slot * 128 + 128], scalar1=c_["beta"][:, c:c + 1], scalar2=None, op0=ALU.mult),
                 reads=[pkt, c_["beta"]], writes=[vb[i2]])
            P.op("pool", lambda e, PTc=PTc: e.tensor_tensor(out=PTc[:, :], in0=PTc[:, :], in1=idf[:, :], op=ALU.add), reads=[PTc, idf], writes=[PTc])
            yield
            cur = 0
            for lev in range(6):
                nx = 1 - cur
                P.op("pe", lambda e, cur=cur: e.matmul(bB[:, rA[2]], lhsT=YT[cur][:, :], rhs=Y[cur][:, :], start=True, stop=True, skip_group_check=True),
                     reads=[YT[cur], Y[cur]], writes=[bB])
                if lev < 5:
                    P.op("pe", lambda e, cur=cur: e.matmul(bB[:, rA[3]], lhsT=Y[cur][:, :], rhs=YT[cur][:, :], start=True, stop=True, skip_group_check=True),
                         reads=[YT[cur], Y[cur]], writes=[bB])
                yield
                P.op("act", lambda e, nx=nx: e.copy(out=Y[nx][:, :], in_=bB[:, rA[2]]), reads=[bB], writes=[Y[nx]])
                if lev < 5:
                    P.op("act", lambda e, nx=nx: e.copy(out=YT[nx][:, :], in_=bB[:, rA[3]]), reads=[bB], writes=[YT[nx]])
                yield
                P.op("pe", lambda e, nx=nx, cur=cur: e.matmul(bB[:, rA[0]], lhsT=Y[nx][:, :], rhs=PT[cur][:, :], start=True, stop=True, skip_group_check=True),
                     reads=[Y[nx], PT[cur]], writes=[bB])
                yield
                P.op("dve", lambda e, nx=nx, cur=cur: e.tensor_tensor(out=PT[nx][:, :], in0=bB[:, rA[0]], in1=PT[cur][:, :], op=ALU.add),
                     reads=[bB, PT[cur]], writes=[PT[nx]])
                yield
                cur = nx
            P.op("act", lambda e, i2=i2, cur=cur: e.copy(out=TTb[i2][:, :], in_=PT[cur][:, :]), reads=[PT[cur]], writes=[TTb[i2]])
            yield
            P.op("pe", lambda e, i2=i2: e.matmul(bA[:, 128:192], lhsT=TTb[i2][:, :], rhs=vb[i2][:, :], start=True, stop=True, skip_group_check=True), reads=[TTb[i2], vb[i2]], writes=[bA])
            P.op("pe", lambda e, i2=i2: e.matmul(bA[0:64, 256:384], lhsT=kbg[i2][:, :], rhs=TTb[i2][:, :], start=True, stop=True, skip_group_check=True), reads=[kbg[i2], TTb[i2]], writes=[bA])
            yield
            P.op("act", lambda e, i2=i2: e.copy(out=u_sb[i2][:, :], in_=bA[:, 128:192]), reads=[bA], writes=[u_sb[i2]])
            P.op("act", lambda e, i2=i2: e.copy(out=wT[i2][:, :], in_=bA[0:64, 256:384]), reads=[bA], writes=[wT[i2]])
            P.op("pool", lambda e, i2=i2, c=c: e.tensor_tensor(out=nwg[i2][:, :], in0=gt[:, c, :], in1=nw[:, :], op=ALU.mult), reads=[gt, nw], writes=[nwg[i2]])
            yield
            while seq_state["next"] != c:
                yield
            Scur, Snxt = Sb[c % 2], Sb[(c + 1) % 2]
            P.op("pe", lambda e, i2=i2, Scur=Scur: e.matmul(bA[:, 384:448], lhsT=wT[i2][:, :], rhs=Scur[:, :], start=True, stop=True, skip_group_check=True), reads=[wT[i2], Scur], writes=[bA])
            P.op("dve", lambda e, i2=i2: e.tensor_tensor(out=vnew[i2][:, :], in0=u_sb[i2][:, :], in1=bA[:, 384:448], op=ALU.subtract),
                 reads=[u_sb[i2], bA], writes=[vnew[i2]])
            P.op("pe", lambda e, i2=i2: e.matmul(bA[0:64, 128:192], lhsT=kd[i2][:, :], rhs=vnew[i2][:, :], start=True, stop=True, skip_group_check=True), reads=[kd[i2], vnew[i2]], writes=[bA])
            P.op("pe", lambda e, i2=i2, Scur=Scur: e.matmul(bB[:, 0:64], lhsT=qgT[i2][:, :], rhs=Scur[:, :], start=True, stop=False, skip_group_check=True), reads=[qgT[i2], Scur], writes=[bB])
            P.op("pe", lambda e, i2=i2: e.matmul(bB[:, 0:64], lhsT=attnT[i2][:, :], rhs=vnew[i2][:, :], start=False, stop=True, skip_group_check=True), reads=[attnT[i2], vnew[i2]], writes=[bB])
            P.op("dve", lambda e, c=c: e.scalar_tensor_tensor(out=Sf[:, :], in0=Sf[:, :], scalar=c_["dlast"][0:64, c:c + 1], in1=bA[0:64, 128:192],
                                                             op0=ALU.mult, op1=ALU.add), reads=[Sf, c_["dlast"], bA], writes=[Sf])
            P.op("act", lambda e, Snxt=Snxt: e.copy(out=Snxt[:, :], in_=Sf[:, :]), reads=[Sf], writes=[Snxt])
            seq_state["next"] = c + 1
            yield
            P.op("act", lambda e, i2=i2: e.activation(out=junk[i2][:, :], in_=bB[:, 0:64], func=AF.Square, accum_out=ssq[i2][:, 0:1]),
                 reads=[bB], writes=[junk[i2], ssq[i2]])
            P.op("act", lambda e, i2=i2: e.activation(out=ssq[i2][:, :], in_=ssq[i2][:, :], func=AF.Sqrt, bias=eps_t[:, 0:1], scale=1.0 / 64.0),
                 reads=[ssq[i2], eps_t], writes=[ssq[i2]])
            yield
            P.op("dve", lambda e, i2=i2: e.reciprocal(out=ssq[i2][:, :], in_=ssq[i2][:, :]), reads=[ssq[i2]], writes=[ssq[i2]])
            P.op("dve", lambda e, i2=i2, c=c: e.scalar_tensor_tensor(out=ytm[i2][:, :], in0=bB[:, 0:64], scalar=ssq[i2][:, 0:1], in1=nwg[i2][:, :],
                                                                    op0=ALU.mult, op1=ALU.mult), reads=[bB, ssq[i2], nwg[i2]], writes=[ytm[i2]])
            yield
            P.op("pe", lambda e, i2=i2: e.transpose(out=bB[0:64, 256:384], in_=ytm[i2][:, :], identity=idf[:, :]), reads=[ytm[i2], idf], writes=[bB])
            yield
            yb_ = yall[(c // 16) % 2]
            P.op("act", lambda e, yb_=yb_: e.copy(out=yb_[:, (c % 16) * 128:(c % 16 + 1) * 128], in_=bB[0:64, 256:384]), reads=[bB], writes=[yb_])
            if c % 16 == 15 or c == min(NT, GDN_NCH) - 1:
                c0 = (c // 16) * 16
                P.dma("pool", y_d[h][:, c0 * 128:(c + 1) * 128], yb_[:, 0:(c - c0 + 1) * 128], reads=[yb_], writes=[y_d.buf])

        run_pipeline([chunk_gen(c) for c in range(min(NT, GDN_NCH))], GDN_K)
        P.op("pool", lambda e: e.memset(Sb[0][:, :], 0.0), writes=[Sb[0]])
        if min(NT, GDN_NCH) % 2 == 1:
            P.op("pool", lambda e: e.memset(Sb[1][:, :], 0.0), writes=[Sb[1]])


D_MIX = 1536
ALPHA = (2.0 * 2) ** 0.25
N_FM = 4736
N_TM = 1170
FM_AQ, FM_AK, FM_AG, FM_SX, FM_SB, FM_SC, FM_SZ, FM_CQ, FM_CK, FM_CG, FM_DQ, FM_DK, FM_DV = (
    0, 384, 768, 1152, 1536, 1792, 2048, 2432, 2816, 3200, 3584, 3968, 4352)
TM_AV, TM_CV, TM_DG, TM_DT, TM_DB, TM_DA = 0, 384, 768, 1152, 1158, 1164


def in_col_perm():
    r = lambda a, b: list(range(a, b))
    fm = r(0, 768) + r(1152, 1536) + r(1536, 2816) + r(2822, 3590) + r(3974, 4358) + r(4358, 5510)
    tm = r(768, 1152) + r(3590, 3974) + r(5510, 5894) + r(2816, 2822) + r(5894, 5900) + r(5900, 5906)
    assert len(fm) == N_FM and len(tm) == N_TM
    return np.array(fm + tm)


def emit_projF(P, x_src, src_bf16, w_ap, w_buf, projT, projtm, S=SEQ, pfx="pj"):
    KC = D_MODEL // 128
    C = IN_COLS
    TB = 1024
    wbf = P.sbuf(pfx + "_wbf", [128, KC, C], BF16)
    HW = (C + 1) // 2
    wst = [P.sbuf(pfx + f"_wst{i}", [128, HW], F32) for i in range(2)]
    xbf = [P.sbuf(pfx + f"_xbf{i}", [128, KC, TB], BF16) for i in range(2)]
    xst = [P.sbuf(pfx + f"_xst{i}", [128, TB], F32) for i in range(2)]
    ost = [P.sbuf(pfx + f"_ost{i}", [128, 512], F32) for i in range(3)]
    ot2 = [P.sbuf(pfx + f"_ot2{i}", [128, N_TM], F32) for i in range(2)]
    ps = [P.psum(pfx + f"_ps{i}", [128, 512]) for i in range(6)]
    n = 0
    for k in range(KC):
        for h in range(2):
            c0, c1 = h * HW, min(C, (h + 1) * HW)
            b = wst[n % 2]
            P.dma("sp" if n % 2 == 0 else "act", b[:, :c1 - c0], w_ap[k * 128:(k + 1) * 128, c0:c1], reads=[w_buf], writes=[b])
            eng = "dve" if n % 2 == 0 else "pool"
            P.op(eng, lambda e, b=b, k=k, c0=c0, c1=c1: e.tensor_copy(out=wbf[:, k, c0:c1], in_=b[:, :c1 - c0]), reads=[b], writes=[wbf])
            n += 1
    ci = 0
    oi = 0
    for blk in range(S // TB):
        xb = xbf[blk % 2]
        t0 = blk * TB
        if src_bf16:
            P.dma("sp", xb[:, :, :], x_src.t.rearrange("(k p) t -> p k t", p=128)[:, :, t0:t0 + TB], reads=[x_src], writes=[xb])
        else:
            for k in range(KC):
                b = xst[k % 2]
                P.dma("sp", b[:, :], x_src[k * 128:(k + 1) * 128, t0:t0 + TB], reads=[x_src], writes=[b])
                P.op("pool", lambda e, b=b, k=k, xb=xb: e.tensor_copy(out=xb[:, k, :], in_=b[:, :]), reads=[b], writes=[xb])
        for sub in range(TB // 512):
            for cc in range(N_FM // 128):
                p = ps[ci % 6]
                o = ost[ci % 3]
                for k in range(KC):
                    P.op("pe", lambda e, p=p, k=k, cc=cc, sub=sub, xb=xb: e.matmul(
                        p[:, :], lhsT=wbf[:, k, cc * 128:(cc + 1) * 128], rhs=xb[:, k, sub * 512:(sub + 1) * 512],
                        start=(k == 0), stop=(k == KC - 1)), reads=[wbf, xb], writes=[p])
                if ci % 2 == 0:
                    P.op("dve", lambda e, p=p, o=o: e.tensor_copy(out=o[:, :], in_=p[:, :]), reads=[p], writes=[o])
                else:
                    P.op("act", lambda e, p=p, o=o: e.copy(out=o[:, :], in_=p[:, :]), reads=[p], writes=[o])
                P.dma("pool" if ci % 2 == 0 else "sp", projT[cc * 128:(cc + 1) * 128, t0 + sub * 512:t0 + (sub + 1) * 512], o[:, :],
                      reads=[o], writes=[projT])
                ci += 1
        for tt in range(TB // 128):
            o2 = ot2[oi % 2]
            oi += 1
            for c0 in range(0, N_TM, 512):
                c1 = min(N_TM, c0 + 512)
                p = ps[ci % 6]
                for k in range(KC):
                    P.op("pe", lambda e, p=p, k=k, tt=tt, c0=c0, c1=c1, xb=xb: e.matmul(
                        p[:, :c1 - c0], lhsT=xb[:, k, tt * 128:(tt + 1) * 128], rhs=wbf[:, k, N_FM + c0:N_FM + c1],
                        start=(k == 0), stop=(k == KC - 1)), reads=[wbf, xb], writes=[p])
                if ci % 2 == 0:
                    P.op("dve", lambda e, p=p, o2=o2, c0=c0, c1=c1: e.tensor_copy(out=o2[:, c0:c1], in_=p[:, :c1 - c0]), reads=[p], writes=[o2])
                else:
                    P.op("act", lambda e, p=p, o2=o2, c0=c0, c1=c1: e.copy(out=o2[:, c0:c1], in_=p[:, :c1 - c0]), reads=[p], writes=[o2])
                ci += 1
            P.dma("pool", projtm[t0 + tt * 128:t0 + (tt + 1) * 128, :], o2[:, :], reads=[o2], writes=[projtm])


def emit_outlnF(P, mix_d, projT, snw_ap, w_ap, lng_ap, lnb_ap, ones_ap, idf_ap, cbuf, x_d, xo_d, xoT_d, S=SEQ, TBLK=2048, pfx="ol"):
    FC = D_MIX // 128
    T = TBLK
    wbf = P.sbuf(pfx + "_wbf", [128, FC, D_MODEL], BF16)
    mix = P.sbuf(pfx + "_mix", [128, FC, T], BF16)
    wst = [P.sbuf(pfx + f"_wst{i}", [128, D_MODEL], F32) for i in range(2)]
    mst = [P.sbuf(pfx + f"_mst{i}", [128, T], F32) for i in range(2)]
    zst = [P.sbuf(pfx + f"_zst{i}", [128, T], F32) for i in range(2)]
    gz = P.sbuf(pfx + "_gz", [128, 3, T], F32)
    sq = P.sbuf(pfx + "_sq", [128, T], F32)
    rs = P.sbuf(pfx + "_rs", [128, T], F32)
    snw = P.sbuf(pfx + "_snw", [128, 3], F32)
    lng = P.sbuf(pfx + "_lng", [128, D_MODEL], F32)
    lnb = P.sbuf(pfx + "_lnb", [128, D_MODEL], F32)
    ones = P.sbuf(pfx + "_ones", [128, 128], F32)
    idf = P.sbuf(pfx + "_idf", [128, 128], F32)
    eps6 = P.sbuf(pfx + "_eps6", [128, 1], F32)
    eps5 = P.sbuf(pfx + "_eps5", [128, 1], F32)
    xt = [P.sbuf(pfx + f"_xt{i}", [128, D_MODEL], F32) for i in range(2)]
    zt = [P.sbuf(pfx + f"_zt{i}", [128, D_MODEL], F32) for i in range(2)]
    xTt = [P.sbuf(pfx + f"_xTt{i}", [128, 8, 128], BF16) for i in range(2)]
    st = [P.sbuf(pfx + f"_st{i}", [128, 2, 6], F32) for i in range(2)]
    mv = [P.sbuf(pfx + f"_mv{i}", [128, 2], F32) for i in range(2)]
    pss = [P.psum(pfx + f"_pss{i}", [128, 512]) for i in range(2)]
    po = [P.psum(pfx + f"_po{i}", [128, 512]) for i in range(4)]
    ptr = [P.psum(pfx + f"_ptr{i}", [128, 512]) for i in range(2)]
    P.op("dve", lambda e: e.memset(eps6[:, :], 1e-6), writes=[eps6])
    P.op("dve", lambda e: e.memset(eps5[:, :], 1e-5), writes=[eps5])
    P.dma("sp", snw[:, :], snw_ap, reads=[cbuf], writes=[snw])
    P.dma("sp", lng[:, :], lng_ap, reads=[cbuf], writes=[lng])
    P.dma("sp", lnb[:, :], lnb_ap, reads=[cbuf], writes=[lnb])
    P.dma("sp", ones[:, :], ones_ap, reads=[cbuf], writes=[ones])
    P.dma("sp", idf[:, :], idf_ap, reads=[cbuf], writes=[idf])
    for k in range(FC):
        b = wst[k % 2]
        P.dma("act", b[:, :], w_ap[k * 128:(k + 1) * 128, :], reads=[cbuf], writes=[b])
        P.op("pool", lambda e, b=b, k=k: e.tensor_copy(out=wbf[:, k, :], in_=b[:, :]), reads=[b], writes=[wbf])
    n = 0
    ti = 0
    for blk in range(S // T):
        b0 = blk * T
        bsl = slice(b0, b0 + T)
        for f0 in (0, 6, 9):
            for k in range(3):
                b = mst[n % 2]
                n += 1
                P.dma("sp", b[:, :], mix_d[(f0 + k) * 128:(f0 + k + 1) * 128, bsl], reads=[mix_d], writes=[b])
                P.op("dve", lambda e, b=b, fk=f0 + k: e.tensor_copy(out=mix[:, fk, :], in_=b[:, :]), reads=[b], writes=[mix])
        for k in range(3):
            b = mst[n % 2]
            zb = zst[k % 2]
            n += 1
            P.dma("sp", b[:, :], mix_d[(3 + k) * 128:(4 + k) * 128, bsl], reads=[mix_d], writes=[b])
            P.dma("sp", zb[:, :], projT[FM_SZ + k * 128:FM_SZ + (k + 1) * 128, bsl], reads=[projT], writes=[zb])
            P.op("act", lambda e, zb=zb: e.activation(out=zb[:, :], in_=zb[:, :], func=AF.Silu), reads=[zb], writes=[zb])
            P.op("dve", lambda e, b=b, zb=zb, k=k: e.tensor_tensor(out=gz[:, k, :], in0=b[:, :], in1=zb[:, :], op=ALU.mult), reads=[b, zb], writes=[gz])
        for j in range(T // 512):
            cs = slice(j * 512, (j + 1) * 512)
            ps = pss[j % 2]
            for k in range(3):
                P.op("act", lambda e, k=k, cs=cs: e.activation(out=sq[:, cs], in_=gz[:, k, cs], func=AF.Square), reads=[gz], writes=[sq])
                P.op("pe", lambda e, ps=ps, k=k, cs=cs: e.matmul(ps[:, :], lhsT=ones[:, :], rhs=sq[:, cs], start=(k == 0), stop=(k == 2)),
                     reads=[ones, sq], writes=[ps])
            P.op("act", lambda e, ps=ps, cs=cs: e.activation(out=rs[:, cs], in_=ps[:, :], func=AF.Sqrt, bias=eps6[:, 0:1], scale=1.0 / 384.0),
                 reads=[ps, eps6], writes=[rs])
            P.op("dve", lambda e, cs=cs: e.reciprocal(out=rs[:, cs], in_=rs[:, cs]), reads=[rs], writes=[rs])
            for k in range(3):
                P.op("dve", lambda e, k=k, cs=cs: e.scalar_tensor_tensor(out=mix[:, 3 + k, cs], in0=gz[:, k, cs], scalar=snw[:, k:k + 1], in1=rs[:, cs],
                                                                        op0=ALU.mult, op1=ALU.mult), reads=[gz, snw, rs], writes=[mix])
        for tt in range(T // 128):
            x_ = xt[ti % 2]
            z_ = zt[ti % 2]
            s_ = st[ti % 2]
            m_ = mv[ti % 2]
            xT_ = xTt[ti % 2]
            r0 = b0 + tt * 128
            P.dma("sp", x_[:, :], x_d[r0:r0 + 128, :], reads=[x_d], writes=[x_])
            for hh in range(2):
                p = po[(ti * 2 + hh) % 4]
                for k in range(FC):
                    P.op("pe", lambda e, p=p, k=k, tt=tt, hh=hh: e.matmul(p[:, :], lhsT=mix[:, k, tt * 128:(tt + 1) * 128],
                                                                          rhs=wbf[:, k, hh * 512:(hh + 1) * 512], start=(k == 0), stop=(k == FC - 1)),
                         reads=[mix, wbf], writes=[p])
                P.op("dve", lambda e, p=p, hh=hh, x_=x_, z_=z_: e.scalar_tensor_tensor(out=z_[:, hh * 512:(hh + 1) * 512], in0=x_[:, hh * 512:(hh + 1) * 512],
                                                                                     scalar=ALPHA, in1=p[:, :], op0=ALU.mult, op1=ALU.add),
                     reads=[x_, p], writes=[z_])
                P.op("dve", lambda e, hh=hh, z_=z_, s_=s_: e.bn_stats(out=s_[:, hh, :], in_=z_[:, hh * 512:(hh + 1) * 512]), reads=[z_], writes=[s_])
            P.op("dve", lambda e, s_=s_, m_=m_: e.bn_aggr(out=m_[:, :], in_=s_[:, :, :]), reads=[s_], writes=[m_])
            P.op("act", lambda e, m_=m_: e.activation(out=m_[:, 1:2], in_=m_[:, 1:2], func=AF.Sqrt, bias=eps5[:, 0:1], scale=1.0), reads=[m_, eps5], writes=[m_])
            P.op("dve", lambda e, m_=m_: e.reciprocal(out=m_[:, 1:2], in_=m_[:, 1:2]), reads=[m_], writes=[m_])
            P.op("dve", lambda e, z_=z_, m_=m_: e.tensor_scalar(out=z_[:, :], in0=z_[:, :], scalar1=m_[:, 0:1], scalar2=m_[:, 1:2], op0=ALU.subtract, op1=ALU.mult),
                 reads=[z_, m_], writes=[z_])
            P.op("pool", lambda e, z_=z_: e.tensor_tensor(out=z_[:, :], in0=z_[:, :], in1=lng[:, :], op=ALU.mult), reads=[z_, lng], writes=[z_])
            P.op("pool", lambda e, z_=z_: e.tensor_tensor(out=z_[:, :], in0=z_[:, :], in1=lnb[:, :], op=ALU.add), reads=[z_, lnb], writes=[z_])
            P.dma("pool", xo_d[r0:r0 + 128, :], z_[:, :], reads=[z_], writes=[xo_d])
            if xoT_d is not None:
                for half in range(2):
                    pt_ = ptr[(ti * 2 + half) % 2]
                    for kk in range(4):
                        k = half * 4 + kk
                        P.op("pe", lambda e, pt_=pt_, kk=kk, k=k, z_=z_: e.transpose(out=pt_[:, kk * 128:(kk + 1) * 128], in_=z_[:, k * 128:(k + 1) * 128],
                                                                                  identity=idf[:, :]), reads=[z_, idf], writes=[pt_])
                    P.op("act", lambda e, pt_=pt_, half=half, xT_=xT_: e.copy(out=xT_[:, half * 4:(half + 1) * 4, :],
                                                                             in_=pt_[:, :].rearrange("p (a b) -> p a b", b=128)),
                         reads=[pt_], writes=[xT_])
                P.dma("sp", xoT_d.t.rearrange("(k p) t -> p k t", p=128)[:, :, r0:r0 + 128], xT_[:, :, :], reads=[xT_], writes=[xoT_d])
            ti += 1


DEBUG_OUT = False
SAME_ENGINE_SYNC = True
N_LAYERS_BUILD = 2
STAGES = "PABCDO"


def build_fused(S=SEQ):
    nc = bass.Bass("TRN2", target_bir_lowering=False)
    P = Prog(nc, same_engine_sync=SAME_ENGINE_SYNC)
    NT = S // 128
    L = DEPTH
    EI = "ExternalInput"
    xT0 = P.dram("xT0", [D_MODEL, S], F32, EI)
    x0 = P.dram("x0", [S, D_MODEL], F32, EI)
    w_in = P.dram("w_in", [L, D_MODEL, IN_COLS], F32, EI)
    w_out = P.dram("w_out", [L, D_MIX, D_MODEL], F32, EI)
    ab = P.dram("ab", [6, 128, 64], F32, EI)
    bi = P.dram("bi", [32, S], BF16, EI)
    cb = P.dram("cb", [4, 128, 512], BF16, EI)
    idb = P.dram("idb", [128, 128], BF16, EI)
    idf = P.dram("idf", [128, 128], F32, EI)
    dbias = P.dram("dbias", [6, 3, 2, 128, 512], F32, EI)
    tri = P.dram("tri", [128, 128], F32, EI)
    ones = P.dram("ones", [128, 128], F32, EI)
    mneg = P.dram("mneg", [128, 128], BF16, EI)
    mpos = P.dram("mpos", [128, 128], BF16, EI)
    s_cwx = P.dram("s_cwx", [L, 6, 64, 5], F32, EI)
    s_cwb = P.dram("s_cwb", [L, 6, 128, 5], F32, EI)
    s_cwc = P.dram("s_cwc", [L, 6, 128, 5], F32, EI)
    s_sc = P.dram("s_sc", [L, 6, 128, 3], F32, EI)
    d_cwq = P.dram("d_cwq", [L, 6, 64, 5], F32, EI)
    d_cwk = P.dram("d_cwk", [L, 6, 64, 5], F32, EI)
    d_cwv = P.dram("d_cwv", [L, 6, 64, 5], F32, EI)
    d_sc = P.dram("d_sc", [L, 6, 128, 2], F32, EI)
    d_nw = P.dram("d_nw", [L, 128, 64], F32, EI)
    o_snw = P.dram("o_snw", [L, 128, 3], F32, EI)
    o_lng = P.dram("o_lng", [L, 128, D_MODEL], F32, EI)
    o_lnb = P.dram("o_lnb", [L, 128, D_MODEL], F32, EI)
    out = P.dram("out", [S, D_MODEL], F32, "ExternalOutput")
    sk = "ExternalOutput" if DEBUG_OUT else None
    projT = P.dram("projT", [N_FM, S], F32, sk)
    projtm = P.dram("projtm", [S, N_TM], F32, sk)
    mixT = P.dram("mixT", [D_MIX, S], F32, sk)
    x1 = P.dram("x1", [S, D_MODEL], F32, sk)
    x1T = P.dram("x1T", [D_MODEL, S], BF16, None)

    def fm(off):
        return Src(projT, lambda h, off=off: projT[off + h * 64: off + (h + 1) * 64, :])

    def mixrows(off):
        return Src(mixT, lambda h, off=off: mixT[off + h * 64: off + (h + 1) * 64, :])

    for l in range(N_LAYERS_BUILD):
        if "P" in STAGES:
            with P.scope():
                emit_projF(P, xT0 if l == 0 else x1T, l != 0, w_in.t[l], w_in, projT, projtm, S)
        if "A" in STAGES:
            with P.scope():
                emit_moba(P, fm(FM_AQ), fm(FM_AK),
                          Src(projtm, lambda h: projtm.t[:, TM_AV + h * 64: TM_AV + (h + 1) * 64].rearrange("(t p) d -> p t d", p=128)),
                          fm(FM_AG), Src(ab, lambda h: ab.t[h]), bi, cb, idb, idf, mixrows(0), 6, S)
        if "B" in STAGES:
            with P.scope():
                emit_ssd(P, fm(FM_SX),
                         Src(projT, lambda h: projT[FM_SB + (h // 3) * 128: FM_SB + (h // 3 + 1) * 128, :]),
                         Src(projT, lambda h: projT[FM_SC + (h // 3) * 128: FM_SC + (h // 3 + 1) * 128, :]),
                         Src(s_cwx, lambda h, l=l: s_cwx.t[l, h]), Src(s_cwb, lambda h, l=l: s_cwb.t[l, h]), Src(s_cwc, lambda h, l=l: s_cwc.t[l, h]),
                         Src(projtm, lambda h: projtm.t[:, TM_DT:TM_DT + 6].rearrange("(t p) c -> p t c", p=128)),
                         Src(s_sc, lambda h, l=l: s_sc.t[l, h]), tri, ones, idf, idb, mneg, mixrows(384), 6, S)
        if "C" in STAGES:
            with P.scope():
                emit_dil(P, fm(FM_CQ), fm(FM_CK),
                         Src(projtm, lambda hg: projtm.t[:, TM_CV + hg[0] * 64: TM_CV + (hg[0] + 1) * 64].rearrange(
                             "(j p r) c -> p r j c", p=128, r=DIL_D[hg[1]])),
                         fm(FM_CG), Src(dbias, lambda h: dbias.t[h].rearrange("g o p q -> p (g o) q")), mixrows(768), 6, S)
        if "D" in STAGES:
            with P.scope():
                emit_gdn(P, fm(FM_DQ), fm(FM_DK), fm(FM_DV),
                         Src(d_cwq, lambda h, l=l: d_cwq.t[l, h]), Src(d_cwk, lambda h, l=l: d_cwk.t[l, h]), Src(d_cwv, lambda h, l=l: d_cwv.t[l, h]),
                         Src(projtm, lambda h: projtm.t[:, TM_DB:TM_DB + 6].rearrange("(t p) c -> p t c", p=128)),
                         Src(projtm, lambda h: projtm.t[:, TM_DA:TM_DA + 6].rearrange("(t p) c -> p t c", p=128)),
                         Src(d_sc, lambda h, l=l: d_sc.t[l, h]),
                         Src(projtm, lambda h: projtm.t[:, TM_DG + h * 64: TM_DG + (h + 1) * 64].rearrange("(t p) d -> p t d", p=128)),
                         Src(d_nw, lambda h, l=l: d_nw.t[l]), tri, ones, idf, idb, mneg, mpos, mixrows(1152), 6, S)
        if "O" in STAGES:
            with P.scope():
                last = (l == DEPTH - 1)
                emit_outlnF(P, mixT, projT, o_snw.t[l], w_out.t[l], o_lng.t[l], o_lnb.t[l], ones.t, idf.t, w_out,
                            x0 if l == 0 else x1, out if last else x1, None if last else x1T, S)
    return P.finish()


from concourse.bass_utils import run_bass_kernel_spmd

BATCH = 2
_NC = {}


def _host_inputs(p, b):
    perm = in_col_perm()
    CA = moba_consts()
    CR = rec_consts()
    L = DEPTH
    f32 = np.float32
    m = {}
    m["xT0"] = np.ascontiguousarray(p["x"][b].T)
    m["x0"] = np.ascontiguousarray(p["x"][b])
    m["w_in"] = np.ascontiguousarray(p["w_in"][:, :, perm])
    m["w_out"] = np.ascontiguousarray(p["w_out"])
    m["ab"] = CA["ab"]
    m["bi"] = CA["bi"]
    m["cb"] = CA["cb"]
    m["idb"] = CA["idb"]
    m["idf"] = CA["idf"]
    m["dbias"] = dil_consts()
    m["tri"] = CR["tri"]
    m["ones"] = CR["ones"]
    m["mneg"] = CR["mneg"]
    m["mpos"] = CR["mpos"]
    cw5 = [np.concatenate([p["ssm_conv_w"][l].T, p["ssm_conv_b"][l][:, None]], axis=1).astype(f32) for l in range(L)]
    m["s_cwx"] = np.ascontiguousarray(np.stack([np.stack([cw5[l][h * 64:(h + 1) * 64] for h in range(6)]) for l in range(L)]))
    m["s_cwb"] = np.ascontiguousarray(np.stack([np.stack([cw5[l][384 + (h // 3) * 128: 384 + (h // 3 + 1) * 128] for h in range(6)]) for l in range(L)]))
    m["s_cwc"] = np.ascontiguousarray(np.stack([np.stack([cw5[l][640 + (h // 3) * 128: 640 + (h // 3 + 1) * 128] for h in range(6)]) for l in range(L)]))
    m["s_sc"] = np.ascontiguousarray(np.stack([np.stack([np.tile(np.stack([p["ssm_dt_bias"][l, h], p["ssm_A_log"][l, h], p["ssm_D"][l, h]])[None].astype(f32),
                                                                (128, 1)) for h in range(6)]) for l in range(L)]))
    cd5 = [np.concatenate([p["dn_conv_w"][l].T, p["dn_conv_b"][l][:, None]], axis=1).astype(f32) for l in range(L)]
    for nm, off in (("q", 0), ("k", 384), ("v", 768)):
        m["d_cw" + nm] = np.ascontiguousarray(np.stack([np.stack([cd5[l][off + h * 64: off + (h + 1) * 64] for h in range(6)]) for l in range(L)]))
    m["d_sc"] = np.ascontiguousarray(np.stack([np.stack([np.tile(np.stack([p["dn_dt_bias"][l, h], p["dn_A_log"][l, h]])[None].astype(f32), (128, 1))
                                                        for h in range(6)]) for l in range(L)]))
    m["d_nw"] = np.ascontiguousarray(np.stack([np.tile(p["dn_norm_w"][l][None].astype(f32), (128, 1)) for l in range(L)]))
    m["o_snw"] = np.ascontiguousarray(np.stack([p["ssm_norm_w"][l].reshape(3, 128).T.astype(f32) for l in range(L)]))
    m["o_lng"] = np.ascontiguousarray(np.stack([np.tile(p["ln_g"][l][None].astype(f32), (128, 1)) for l in range(L)]))
    m["o_lnb"] = np.ascontiguousarray(np.stack([np.tile(p["ln_b"][l][None].astype(f32), (128, 1)) for l in range(L)]))
    return m


def kernel(**inputs):
    p = {k: np.asarray(v, dtype=np.float32) for k, v in inputs.items()}
    if "nc" not in _NC:
        _NC["nc"] = build_fused()
    in_maps = [_host_inputs(p, b) for b in range(BATCH)]
    res = run_bass_kernel_spmd(_NC["nc"], in_maps, core_ids=list(range(BATCH)))
    _NC["res"] = res
    return np.stack([res.results[b]["out"] for b in range(BATCH)]).astype(np.float32)
```

```python
import contextlib
import numpy as np
import concourse.bass as bass
import concourse.mybir as mybir

F32 = mybir.dt.float32
BF16 = mybir.dt.bfloat16
I32 = mybir.dt.int32
U8 = mybir.dt.uint8
AF = mybir.ActivationFunctionType
ALU = mybir.AluOpType
AX = mybir.AxisListType


class Buf:
    __slots__ = ("name", "t", "writer", "readers", "excl")

    def __init__(self, name, t, excl=False):
        self.name = name
        self.t = t
        self.excl = excl
        self.writer = None
        self.readers = []

    def __getitem__(self, idx):
        return self.t[idx]


class Src:
    def __init__(self, buf, fn):
        self.buf = buf
        self.fn = fn

    def __getitem__(self, k):
        return self.fn(k)


class Prog:
    ENG = ("pe", "dve", "act", "pool", "sp")
    NDMA = 6

    def __init__(self, nc, same_engine_sync=True):
        self.nc = nc
        self.stack = contextlib.ExitStack()
        self.ops = {e: [] for e in self.ENG}
        self.cnt = {e: 0 for e in self.ENG}
        self.sems = {}
        for e in self.ENG:
            self.sems[e] = self.stack.enter_context(nc.semaphore("s_" + e))
        self.dq = {}
        for q in ("sp", "pool", "act"):
            self.dq[q] = {"n": 0, "sems": []}
            for i in range(self.NDMA):
                s = self.stack.enter_context(nc.semaphore(f"d_{q}{i}"))
                self.sems[f"d_{q}{i}"] = s
                self.dq[q]["sems"].append(f"d_{q}{i}")
        self.waited = {e: {} for e in self.ENG}
        self.same = same_engine_sync
        self.nbuf = 0
        self.ARENA_BYTES = 206 * 1024
        self.arena = self.stack.enter_context(nc.sbuf_tensor("arena", [128, self.ARENA_BYTES // 4], F32))
        self.arena_off = 0
        self.banks = [self.stack.enter_context(nc.psum_tensor(f"bank{i}", [128, 512], F32)) for i in range(8)]
        self.banks_used = 0

    @staticmethod
    def _view(base, shape, dt, off_bytes):
        esz = mybir.dt.size(dt)
        n = 1
        for d in shape[1:]:
            n *= d
        t = base if dt == F32 else base.bitcast(dt)
        o = off_bytes // esz
        ap = t[0:shape[0], o:o + n]
        if len(shape) == 3:
            ap = ap.rearrange("p (a b) -> p a b", b=shape[2])
        elif len(shape) == 4:
            ap = ap.rearrange("p (a b c) -> p a b c", b=shape[2], c=shape[3])
        return ap, n * esz

    def sbuf(self, name, shape, dt):
        ap, nb = self._view(self.arena, list(shape), dt, self.arena_off)
        self.arena_off += (nb + 63) // 64 * 64
        assert self.arena_off <= self.ARENA_BYTES, f"SBUF arena overflow at {name}: {self.arena_off}"
        return Buf(name, ap)

    def psum(self, name, shape, dt=F32):
        assert self.banks_used < 8, "out of PSUM banks at " + name
        ap, nb = self._view(self.banks[self.banks_used], list(shape), dt, 0)
        assert nb <= 2048
        self.banks_used += 1
        return Buf(name, ap, excl=True)

    @contextlib.contextmanager
    def scope(self):
        sv = (self.arena_off, self.banks_used)
        try:
            yield
        finally:
            self.barrier()
            self.arena_off, self.banks_used = sv

    def _all_deps(self):
        deps = []
        for q, dq in self.dq.items():
            n = dq["n"]
            for i in range(min(n, self.NDMA)):
                cnt_i = (n - 1 - i) // self.NDMA + 1
                deps.append((dq["sems"][i], 16 * cnt_i))
        for e in self.ENG:
            if self.cnt[e] > 0:
                deps.append((e, self.cnt[e]))
        return deps

    def barrier(self):
        deps = self._all_deps()
        for e in self.ENG:
            self._need(e, [d for d in deps if d[0] != e])

    def dram(self, name, shape, dt, kind=None):
        if kind is None:
            t = self.nc.dram_tensor(name, list(shape), dt)
        else:
            t = self.nc.dram_tensor(name, list(shape), dt, kind=kind)
        return Buf(name, t.ap())

    def alias(self, name, t):
        return Buf(name, t)

    def _need(self, eng, deps):
        w = self.waited[eng]
        for (k, v) in deps:
            if k == eng and (eng == "pe" or not self.same):
                continue
            if w.get(k, 0) >= v:
                continue
            w[k] = v
            sem = self.sems[k]
            self.ops[eng].append(lambda e, sem=sem, v=v: e.wait_ge(sem, v))

    def _deps(self, reads, writes, eng=None):
        deps = []
        for b in reads:
            if b.writer is not None:
                deps.append(b.writer)
            if b.excl:
                deps.extend(r for r in b.readers if r[0] != eng)
        for b in writes:
            if b.writer is not None:
                deps.append(b.writer)
            deps.extend(b.readers)
        return deps

    def _mark(self, reads, writes, tag):
        for b in reads:
            b.readers.append(tag)
            if len(b.readers) > 64:
                m = {}
                for (k, v) in b.readers:
                    if m.get(k, 0) < v:
                        m[k] = v
                b.readers = list(m.items())
        for b in writes:
            b.writer = tag
            b.readers = []

    def op(self, eng, fn, reads=(), writes=()):
        self._need(eng, self._deps(reads, writes, eng))
        self.cnt[eng] += 1
        v = self.cnt[eng]
        sem = self.sems[eng]
        self.ops[eng].append(lambda e, fn=fn, sem=sem: fn(e).then_inc(sem, 1))
        self._mark(reads, writes, (eng, v))

    def dma(self, q, out_ap, in_ap, reads=(), writes=(), **kw):
        dq = self.dq[q]
        n = dq["n"]
        dq["n"] += 1
        key = dq["sems"][n % self.NDMA]
        val = 16 * (n // self.NDMA + 1)
        deps = self._deps(reads, writes)
        if val > 16:
            deps.append((key, val - 16))
        self._need(q, deps)
        sem = self.sems[key]
        self.ops[q].append(lambda e, sem=sem, o=out_ap, i=in_ap, kw=kw: e.dma_start(out=o, in_=i, **kw).then_inc(sem, 16))
        self._mark(reads, writes, (key, val))

    def coll(self, kind, alu, groups, in_b, out_b):
        q = "pool"
        if "cc" not in self.sems:
            self.sems["cc"] = self.stack.enter_context(self.nc.semaphore("s_cc"))
            self.ncc = 0
        self.ncc += 1
        val = self.ncc
        deps = self._deps([in_b], [out_b])
        if val > 1:
            deps.append(("cc", val - 1))
        self._need(q, deps)
        sem = self.sems["cc"]
        self.ops[q].append(lambda e, sem=sem: e.collective_compute(kind, alu, replica_groups=groups, ins=[in_b.t.opt()],
                                                                   outs=[out_b.t.opt()]).then_inc(sem, 1))
        self._mark([in_b], [out_b], ("cc", val))

    def finish(self, final_bufs=()):
        deps = []
        for q, dq in self.dq.items():
            n = dq["n"]
            for i in range(min(n, self.NDMA)):
                cnt_i = (n - 1 - i) // self.NDMA + 1
                deps.append((dq["sems"][i], 16 * cnt_i))
        for e in self.ENG:
            if e != "sp" and self.cnt[e] > 0:
                deps.append((e, self.cnt[e]))
        self._need("sp", deps)
        nc = self.nc
        with nc.Block() as block:
            @block.tensor
            def _(e):
                for f in self.ops["pe"]:
                    f(e)

            @block.vector
            def _(e):
                for f in self.ops["dve"]:
                    f(e)

            @block.scalar
            def _(e):
                for f in self.ops["act"]:
                    f(e)

            @block.gpsimd
            def _(e):
                for f in self.ops["pool"]:
                    f(e)

            @block.sync
            def _(e):
                for f in self.ops["sp"]:
                    f(e)
        self.stack.close()
        return nc


D_MODEL = 1024
IN_COLS = 5906
DEPTH = 2


SEQ = 8192
NEGM = 30000.0


def emit_moba(P, qT_d, kT_d, v_d, gT_d, ab_d, bi_d, cb_d, idb_d, idf_d, y_d, NH, S=SEQ, pfx="mb"):
    NT = S // 128
    NQ = S // 512
    NB = S // 256
    qf = P.sbuf(pfx + "_qf", [64, S], F32)
    kf = P.sbuf(pfx + "_kf", [64, S], F32)
    qa = P.sbuf(pfx + "_qa", [96, S], BF16)
    ka = P.sbuf(pfx + "_ka", [96, S], BF16)
    vf = P.sbuf(pfx + "_vf", [128, NT, 64], F32)
    va = P.sbuf(pfx + "_va", [128, NT, 128], BF16)
    cb = P.sbuf(pfx + "_cb", [128, 4, 512], BF16)
    idb = P.sbuf(pfx + "_idb", [128, 128], BF16)
    idf = P.sbuf(pfx + "_idf", [128, 128], F32)
    ab = P.sbuf(pfx + "_ab", [128, 64], F32)
    km = P.sbuf(pfx + "_km", [64, NB], F32)
    KS = 3
    gsb = [P.sbuf(pfx + f"_gsb{i}", [128, 32], F32) for i in range(KS)]
    mx8 = [P.sbuf(pfx + f"_mx8{i}", [128, 8], F32) for i in range(KS)]
    mb = [P.sbuf(pfx + f"_mbs{i}", [128, 32], F32) for i in range(KS)]
    KM = 4
    pts = [P.sbuf(pfx + f"_pt{i}", [128, 512], BF16) for i in range(KM + 1)]
    gch = [P.sbuf(pfx + f"_gch{i}", [64, 512], F32) for i in range(2)]
    rden = [P.sbuf(pfx + f"_rden{i}", [64, 512], F32) for i in range(2)]
    yo = [P.sbuf(pfx + f"_yo{i}", [64, 512], F32) for i in range(2)]
    acc = [P.psum(pfx + f"_acc{i}", [128, 512]) for i in range(2)]
    sps = [P.psum(pfx + f"_sps{i}", [128, 512]) for i in range(KM + 1)]

    P.dma("sp", cb[:, :, :], cb_d.t.rearrange("k p q -> p k q"), reads=[cb_d], writes=[cb])
    P.dma("sp", idb[:, :], idb_d[:, :], reads=[idb_d], writes=[idb])
    P.dma("sp", idf[:, :], idf_d[:, :], reads=[idf_d], writes=[idf])
    P.dma("sp", ka[64:96, :], bi_d[:, :], reads=[bi_d], writes=[ka])
    P.op("pool", lambda e: e.memset(va[:, :, 64:128], 1.0), writes=[va])
    si = 0
    for h in range(NH):
        P.dma("sp", qf[:, :], qT_d[h], reads=[qT_d.buf], writes=[qf])
        P.dma("act", kf[:, :], kT_d[h], reads=[kT_d.buf], writes=[kf])
        P.dma("pool", vf[:, :, :], v_d[h], reads=[v_d.buf], writes=[vf])
        P.dma("sp", ab[:, :], ab_d[h], reads=[ab_d.buf], writes=[ab])
        P.op("act", lambda e: e.mul(qa[0:64, :], qf[:, :], 0.125), reads=[qf], writes=[qa])
        P.op("pool", lambda e: e.tensor_copy(out=ka[0:64, :], in_=kf[:, :]), reads=[kf], writes=[ka])
        P.op("pool", lambda e: e.tensor_copy(out=va[:, :, 0:64], in_=vf[:, :, :]), reads=[vf], writes=[va])
        P.op("dve", lambda e: e.tensor_reduce(out=km[:, :], in_=kf.t[:, :].rearrange("p (n l) -> p n l", l=256),
                                              axis=AX.X, op=ALU.add), reads=[kf], writes=[km])
        P.op("dve", lambda e: e.tensor_scalar(out=km[:, :], in0=km[:, :], scalar1=1.0 / 256.0, scalar2=None, op0=ALU.mult),
             reads=[km], writes=[km])
        def sel_gen(t):
            qb = t // 2
            sl = t % KS
            bk = sps[sl]
            gp = bk[:, 0:32]
            tp = bk[0:32, 128:256]
            g_, x_, m_ = gsb[sl], mx8[sl], mb[sl]
            P.op("pe", lambda e: e.matmul(gp, lhsT=qf[:, t * 128:(t + 1) * 128], rhs=km[:, :], start=True, stop=True, skip_group_check=True),
                 reads=[qf, km], writes=[bk])
            P.op("pool", lambda e: e.memset(g_[:, :], -1e30), writes=[g_])
            yield
            if qb > 0:
                P.op("dve", lambda e: e.tensor_copy(out=g_[:, 0:qb], in_=bk[:, 0:qb]), reads=[bk], writes=[g_])
            P.op("dve", lambda e: e.max(out=x_[:, :], in_=g_[:, :]), reads=[g_], writes=[x_])
            yield
            P.op("dve", lambda e: e.tensor_scalar(out=m_[:, :], in0=g_[:, :], scalar1=x_[:, 2:3], scalar2=NEGM,
                                                  op0=ALU.is_ge, op1=ALU.mult), reads=[g_, x_], writes=[m_])
            P.op("dve", lambda e: e.memset(m_[:, qb:qb + 1], NEGM), writes=[m_])
            if qb + 1 < 32:
                P.op("dve", lambda e: e.memset(m_[:, qb + 1:32], 0.0), writes=[m_])
            yield
            P.op("dve", lambda e: e.tensor_scalar(out=m_[:, :], in0=m_[:, :], scalar1=-NEGM, scalar2=None, op0=ALU.add),
                 reads=[m_], writes=[m_])
            yield
            P.op("pe", lambda e: e.transpose(out=tp, in_=m_[:, :], identity=idf[:, :]), reads=[m_, idf], writes=[bk])
            yield
            P.op("act", lambda e: e.copy(out=qa[64:96, t * 128:(t + 1) * 128], in_=tp), reads=[bk], writes=[qa])

        run_pipeline([sel_gen(t) for t in range(NT)], KS)
        def att_gen(qt, kt, idx):
            ac = acc[qt % 2]
            nk = 4 * qt + 4
            sp_ = sps[idx % (KM + 1)]
            pt = pts[idx % (KM + 1)]
            diag = kt >= 4 * qt
            P.op("pe", lambda e: e.matmul(sp_[:, :], lhsT=ka[:, kt * 128:(kt + 1) * 128], rhs=qa[:, qt * 512:(qt + 1) * 512],
                                          start=True, stop=not diag), reads=[ka, qa], writes=[sp_])
            if diag:
                P.op("pe", lambda e: e.matmul(sp_[:, :], lhsT=idb[:, :], rhs=cb[:, kt - 4 * qt, :], start=False, stop=True), reads=[idb, cb], writes=[sp_])
            yield
            ri = kt - 4 * qt + 60
            P.op("act", lambda e: e.activation(out=pt[:, :], in_=sp_[:, :], func=AF.Exp, bias=ab[:, ri:ri + 1], scale=1.0),
                 reads=[sp_, ab], writes=[pt])
            yield
            P.op("pe", lambda e: e.matmul(ac[:, :], lhsT=va[:, kt, :], rhs=pt[:, :], start=(kt == 0), stop=(kt == nk - 1)),
                 reads=[va, pt], writes=[ac])
            if kt == nk - 1:
                g = gch[qt % 2]
                rd = rden[qt % 2]
                y = yo[qt % 2]
                P.dma("sp", g[:, :], gT_d[h][:, qt * 512:(qt + 1) * 512], reads=[gT_d.buf], writes=[g])
                P.op("act", lambda e: e.activation(out=g[:, :], in_=g[:, :], func=AF.Silu), reads=[g], writes=[g])
                yield
                P.op("dve", lambda e: e.reciprocal(out=rd[:, :], in_=ac[64:128, :]), reads=[ac], writes=[rd])
                P.op("dve", lambda e: e.tensor_tensor(out=y[:, :], in0=ac[0:64, :], in1=rd[:, :], op=ALU.mult),
                     reads=[ac, rd], writes=[y])
                yield
                P.op("pool", lambda e: e.tensor_tensor(out=y[:, :], in0=y[:, :], in1=g[:, :], op=ALU.mult),
                     reads=[y, g], writes=[y])
                P.dma("pool", y_d[h][:, qt * 512:(qt + 1) * 512], y[:, :], reads=[y], writes=[y_d.buf])

        items = [(qt, kt) for qt in range(NQ) for kt in range(4 * qt + 4)]
        run_pipeline([att_gen(qt, kt, i) for i, (qt, kt) in enumerate(items)], KM)


def moba_consts(S=SEQ):
    import ml_dtypes
    bf = ml_dtypes.bfloat16
    bi = np.zeros((32, S), np.float32)
    for j in range(S // 256):
        bi[j, j * 256:(j + 1) * 256] = 1.0
    cbm = np.zeros((4, 128, 512), np.float32)
    for r in range(4):
        for p in range(128):
            tk = r * 128 + p
            for blk in range(2):
                if tk // 256 == blk:
                    q = np.arange(blk * 256, (blk + 1) * 256)
                    cbm[r, p, q] = np.where(tk > q, -NEGM, 0.0)
    n = 12
    s = 2.0 ** (-8.0 * (np.arange(n) + 1) / n)
    slopes_moba = s[6:]
    ab = np.zeros((6, 128, 64), np.float32)
    for h in range(6):
        for ri in range(64):
            rel = ri - 60
            ab[h, :, ri] = slopes_moba[h] * (128.0 * rel + np.arange(128))
    return dict(bi=bi.astype(bf), cb=cbm.astype(bf), idb=np.eye(128, dtype=np.float32).astype(bf),
                idf=np.eye(128, dtype=np.float32), ab=ab)


DIL_D = (1, 4, 16)
DIL_GROUPS = (0, 1, 2)
DBG = None


def emit_dil(P, qT_d, kT_d, vp_d, gT_d, bias_d, y_d, NH, S=SEQ, pfx="dl"):
    NT = S // 128
    qf = P.sbuf(pfx + "_qf", [64, S], F32)
    kf = qf
    qa = P.sbuf(pfx + "_qa", [64, S], BF16)
    ka = P.sbuf(pfx + "_ka", [64, S], BF16)
    vf = P.sbuf(pfx + "_vf", [128, NT, 64], F32)
    va = [P.sbuf(pfx + f"_va{g}", [128, NT, 128], BF16) for g in range(3)]
    bs = P.sbuf(pfx + "_bias", [128, 6, 512], F32)
    num = P.sbuf(pfx + "_num", [128, S], F32)
    tmp = [P.sbuf(pfx + f"_tmp{i}", [128, 512], F32) for i in range(4)]
    pts = [P.sbuf(pfx + f"_pt{i}", [128, 512], BF16) for i in range(4)]
    gch = [P.sbuf(pfx + f"_gch{i}", [64, 512], F32) for i in range(2)]
    rden = [P.sbuf(pfx + f"_rden{i}", [64, 512], F32) for i in range(2)]
    yo = [P.sbuf(pfx + f"_yo{i}", [64, 512], F32) for i in range(2)]
    sps = [P.psum(pfx + f"_sps{i}", [128, 512]) for i in range(4)]
    ops_ = [P.psum(pfx + f"_ops{i}", [128, 512]) for i in range(3)]
    for g in range(3):
        P.op("pool", lambda e, g=g: e.memset(va[g][:, :, 64:128], 1.0), writes=[va[g]])
    si = 0
    oi = 0
    for h in range(NH):
        P.dma("sp", qf[:, :], qT_d[h], reads=[qT_d.buf], writes=[qf])
        P.op("act", lambda e: e.mul(qa[:, :], qf[:, :], 0.125), reads=[qf], writes=[qa])
        P.dma("sp", kf[:, :], kT_d[h], reads=[kT_d.buf], writes=[kf])
        P.op("pool", lambda e: e.tensor_copy(out=ka[:, :], in_=kf[:, :]), reads=[kf], writes=[ka])
        P.dma("sp", bs[:, :, :], bias_d[h], reads=[bias_d.buf], writes=[bs])
        for g in range(3):
            P.dma("pool", vf.t.rearrange("p (r j) c -> p r j c", r=DIL_D[g]), vp_d[(h, g)], reads=[vp_d.buf], writes=[vf])
            P.op("pool", lambda e, g=g: e.tensor_copy(out=va[g][:, :, 0:64], in_=vf[:, :, :]), reads=[vf], writes=[va[g]])
        def dil_gen(g, d, r, jb, ti):
            NTd = NT // d
            qv = qa.t[:, :].rearrange("p (u d) -> p u d", d=d)
            kv = ka.t[:, :].rearrange("p (u d) -> p u d", d=d)
            nv = num.t[:, :].rearrange("p (u d) -> p u d", d=d)
            sl = ti % 2
            op_ = ops_[ti % 3]
            sp2 = {1: sps[2 * sl], 0: sps[2 * sl + 1]}
            tm2 = {1: tmp[2 * sl], 0: tmp[2 * sl + 1]}
            pt2 = {1: pts[2 * sl], 0: pts[2 * sl + 1]}
            i0s = {1: (1 if jb == 0 else 0), 0: 0}
            for o in (1, 0):
                sp_ = sp2[o]
                for i in range(i0s[o], 4):
                    jq = jb * 4 + i
                    jk = jq - o
                    P.op("pe", lambda e, sp_=sp_, i=i, jq=jq, jk=jk: e.matmul(
                        sp_[:, i * 128:(i + 1) * 128], lhsT=kv[:, jk * 128:(jk + 1) * 128, r],
                        rhs=qv[:, jq * 128:(jq + 1) * 128, r], start=True, stop=True,
                        skip_group_check=True), reads=[ka, qa], writes=[sp_])
            yield
            for o in (1, 0):
                c0 = i0s[o] * 128
                P.op("dve", lambda e, sp_=sp2[o], tm=tm2[o], c0=c0, go=g * 2 + o: e.tensor_tensor(
                    out=tm[:, c0:512], in0=sp_[:, c0:512], in1=bs[:, go, c0:512], op=ALU.add),
                    reads=[sp2[o], bs], writes=[tm2[o]])
            yield
            for o in (1, 0):
                c0 = i0s[o] * 128
                P.op("act", lambda e, tm=tm2[o], pt=pt2[o], c0=c0: e.activation(out=pt[:, c0:512], in_=tm[:, c0:512], func=AF.Exp),
                     reads=[tm2[o]], writes=[pt2[o]])
            yield
            for i in range(4):
                os_ = (0,) if (jb == 0 and i == 0) else (1, 0)
                for o in os_:
                    jk = jb * 4 + i - o
                    P.op("pe", lambda e, pt=pt2[o], i=i, tl=r * NTd + jk, fp=(o == os_[0]), last=(o == 0): e.matmul(
                        op_[:, i * 128:(i + 1) * 128], lhsT=va[g][:, tl, :], rhs=pt[:, i * 128:(i + 1) * 128],
                        start=fp, stop=last, skip_group_check=True), reads=[va[g], pt2[o]], writes=[op_])
            yield
            u0 = jb * 512
            if g == DIL_GROUPS[0]:
                P.op("dve", lambda e: e.tensor_copy(out=nv[:, u0:u0 + 512, r], in_=op_[:, :]),
                     reads=[op_], writes=[num])
            else:
                P.op("dve", lambda e: e.tensor_tensor(
                    out=nv[:, u0:u0 + 512, r], in0=op_[:, :], in1=nv[:, u0:u0 + 512, r], op=ALU.add),
                    reads=[op_, num], writes=[num])

        tasks = []
        for g, d in enumerate(DIL_D):
            if g not in DIL_GROUPS:
                continue
            for r in range(d):
                for jb in range(NT // d // 4):
                    tasks.append((g, d, r, jb))
        run_pipeline([dil_gen(g, d, r, jb, ti) for ti, (g, d, r, jb) in enumerate(tasks)], 2)
        for qt in range(S // 512):
            g_ = gch[qt % 2]
            rd = rden[qt % 2]
            y = yo[qt % 2]
            cs = slice(qt * 512, (qt + 1) * 512)
            P.dma("sp", g_[:, :], gT_d[h][:, cs], reads=[gT_d.buf], writes=[g_])
            P.op("act", lambda e, g_=g_: e.activation(out=g_[:, :], in_=g_[:, :], func=AF.Silu), reads=[g_], writes=[g_])
            P.op("dve", lambda e, rd=rd, cs=cs: e.reciprocal(out=rd[:, :], in_=num[64:128, cs]), reads=[num], writes=[rd])
            P.op("dve", lambda e, y=y, rd=rd, cs=cs: e.tensor_tensor(out=y[:, :], in0=num[0:64, cs], in1=rd[:, :], op=ALU.mult),
                 reads=[num, rd], writes=[y])
            P.op("pool", lambda e, y=y, g_=g_: e.tensor_tensor(out=y[:, :], in0=y[:, :], in1=g_[:, :], op=ALU.mult),
                 reads=[y, g_], writes=[y])
            P.dma("pool", y_d[h][:, cs], y[:, :], reads=[y], writes=[y_d.buf])


def dil_consts():
    n = 12
    s = 2.0 ** (-8.0 * (np.arange(n) + 1) / n)
    slopes = s[:6]
    bias = np.zeros((6, 3, 2, 128, 512), np.float32)
    p = np.arange(128)[:, None]
    x = np.arange(128)[None, :]
    for h in range(6):
        for g, d in enumerate(DIL_D):
            for o in range(2):
                nn = (x - p) + 128 * o
                b = np.where((nn >= 0) & (nn <= 128), -slopes[h] * d * nn, -NEGM).astype(np.float32)
                bias[h, g, o] = np.tile(b, (1, 4))
    return bias


def dil_perm(S=SEQ):
    out = []
    for d in DIL_D:
        u = np.arange(S // d)
        out.append(np.concatenate([u * d + r for r in range(d)]))
    return out


def emit_conv_silu(P, src_d, w_sb, C, S, zp, acc, outs, q="sp", src_buf=None):
    P.dma(q, zp[0:C, 3:S + 3], src_d, reads=([src_buf] if src_buf is not None else []), writes=[zp])
    H = S // 2
    for hh in range(2):
        a0, a1 = hh * H, (hh + 1) * H
        P.op("dve", lambda e, a0=a0, a1=a1: e.tensor_scalar(out=acc[0:C, a0:a1], in0=zp[0:C, 3 + a0:3 + a1], scalar1=w_sb[0:C, 3:4],
                                                          scalar2=w_sb[0:C, 4:5], op0=ALU.mult, op1=ALU.add), reads=[zp, w_sb], writes=[acc])
        for k in range(3):
            P.op("dve", lambda e, k=k, a0=a0, a1=a1: e.scalar_tensor_tensor(out=acc[0:C, a0:a1], in0=zp[0:C, k + a0:k + a1], scalar=w_sb[0:C, k:k + 1],
                                                                          in1=acc[0:C, a0:a1], op0=ALU.mult, op1=ALU.add),
                 reads=[zp, w_sb, acc], writes=[acc])
    for (ob, oap) in outs:
        P.op("act", lambda e, oap=oap: e.activation(out=oap, in_=acc[0:C, :], func=AF.Silu), reads=[acc], writes=[ob])


def emit_softplus(P, eng_dve, out_b, out_ap, x_b, x_ap, t1_b, t1_ap, shape_p):
    P.op("act", lambda e: e.activation(out=t1_ap, in_=x_ap, func=AF.Abs), reads=[x_b], writes=[t1_b])
    P.op("act", lambda e: e.activation(out=t1_ap, in_=t1_ap, func=AF.Exp, scale=-1.0), reads=[t1_b], writes=[t1_b])
    P.op("act", lambda e: e.activation(out=t1_ap, in_=t1_ap, func=AF.Ln, bias=1.0, scale=1.0), reads=[t1_b], writes=[t1_b])
    P.op("dve", lambda e: e.scalar_tensor_tensor(out=out_ap, in0=x_ap, scalar=0.0, in1=t1_ap, op0=ALU.max, op1=ALU.add),
         reads=[x_b, t1_b], writes=[out_b])


SSD_NCH = 10 ** 9
SSD_STOP = 99


def emit_ssd(P, xpre_d, bpre_d, cpre_d, cwx_d, cwb_d, cwc_d, dtc_d, sc_d, tri_d, ones_d, idf_d, idb_d, mneg_d, y_d, NH, S=SEQ, pfx="sd"):
    NT = S // 128
    zp = P.sbuf(pfx + "_zp", [128, S + 3], F32)
    acc = P.sbuf(pfx + "_acc", [128, S], F32)
    xsT = P.sbuf(pfx + "_xsT", [64, S], F32)
    BT = P.sbuf(pfx + "_BT", [128, S], BF16)
    CT = P.sbuf(pfx + "_CT", [128, S], BF16)
    yac = P.sbuf(pfx + "_yac", [64, S], F32)
    cwx = P.sbuf(pfx + "_cwx", [64, 5], F32)
    cwb = P.sbuf(pfx + "_cwb", [128, 5], F32)
    cwc = P.sbuf(pfx + "_cwc", [128, 5], F32)
    sc = P.sbuf(pfx + "_sc", [128, 3], F32)
    tri = P.sbuf(pfx + "_tri", [128, 128], F32)
    ones = P.sbuf(pfx + "_ones", [128, 128], F32)
    idf = P.sbuf(pfx + "_idf", [128, 128], F32)
    idb = P.sbuf(pfx + "_idb", [128, 128], BF16)
    mneg = P.sbuf(pfx + "_mneg", [128, 128], BF16)
    dtr = P.sbuf(pfx + "_dtr", [128, NT], F32)
    dtall = P.sbuf(pfx + "_dtall", [128, NT, 6], F32)
    t1 = P.sbuf(pfx + "_t1", [128, NT], F32)
    dt = P.sbuf(pfx + "_dt", [128, NT], F32)
    a_ = P.sbuf(pfx + "_a", [128, NT], F32)
    acum = P.sbuf(pfx + "_acum", [128, NT], F32)
    nacum = P.sbuf(pfx + "_nacum", [128, NT], F32)
    dB = P.sbuf(pfx + "_dB", [128, NT], F32)
    dlast = P.sbuf(pfx + "_dlast", [128, NT], F32)
    Aneg = P.sbuf(pfx + "_Aneg", [128, 1], F32)
    SK = 5
    dg = [P.sbuf(pfx + f"_dg{i}", [128, 128], F32) for i in range(SK)]
    EB = [P.sbuf(pfx + f"_EB{i}", [128, 128], F32) for i in range(SK)]
    LmT = [P.sbuf(pfx + f"_LmT{i}", [128, 128], F32) for i in range(SK)]
    SLT = [P.sbuf(pfx + f"_SLT{i}", [128, 128], BF16) for i in range(SK)]
    CgT = [P.sbuf(pfx + f"_CgT{i}", [128, 128], BF16) for i in range(SK)]
    X = [P.sbuf(pfx + f"_X{i}", [128, 64], BF16) for i in range(SK)]
    Bd = [P.sbuf(pfx + f"_Bd{i}", [128, 128], BF16) for i in range(SK)]
    hf = P.sbuf(pfx + "_hf", [128, 64], F32)
    hb = [P.sbuf(pfx + f"_hb{i}", [128, 64], BF16) for i in range(2)]
    pk_ = [P.psum(pfx + f"_pk{i}", [128, 512]) for i in range(SK)]
    pbf = P.psum(pfx + "_pbf", [128, 1024], BF16)
    p_gb = [P.psum(pfx + f"_pgb{i}", [128, 128]) for i in range(1)]
    p_misc = p_gb[0]

    for (sb, d_) in ((tri, tri_d), (ones, ones_d), (idf, idf_d), (idb, idb_d), (mneg, mneg_d)):
        P.dma("sp", sb[:, :], d_[:, :], reads=[d_], writes=[sb])
    P.op("pool", lambda e: e.memset(zp[:, 0:3], 0.0), writes=[zp])
    for h in range(NH):
        P.dma("sp", cwx[:, :], cwx_d[h], reads=[cwx_d.buf], writes=[cwx])
        P.dma("sp", cwb[:, :], cwb_d[h], reads=[cwb_d.buf], writes=[cwb])
        P.dma("sp", cwc[:, :], cwc_d[h], reads=[cwc_d.buf], writes=[cwc])
        P.dma("sp", sc[:, :], sc_d[h], reads=[sc_d.buf], writes=[sc])
        if h == 0:
            P.dma("act", dtall[:, :, :], dtc_d[0], reads=[dtc_d.buf], writes=[dtall])
        emit_conv_silu(P, xpre_d[h], cwx, 64, S, zp, acc, [(xsT, xsT[:, :])], src_buf=xpre_d.buf)
        if h % 3 == 0:
            emit_conv_silu(P, bpre_d[h], cwb, 128, S, zp, acc, [(BT, BT[:, :])], src_buf=bpre_d.buf)
            emit_conv_silu(P, cpre_d[h], cwc, 128, S, zp, acc, [(CT, CT[:, :])], src_buf=cpre_d.buf)
        P.op("dve", lambda e, h=h: e.tensor_scalar(out=dtr[:, :], in0=dtall[:, :, h], scalar1=sc[:, 0:1], scalar2=None, op0=ALU.add), reads=[dtall, sc], writes=[dtr])
        emit_softplus(P, "dve", dt, dt[:, :], dtr, dtr[:, :], t1, t1[:, :], 128)
        P.op("act", lambda e: e.activation(out=Aneg[:, :], in_=sc[:, 1:2], func=AF.Exp), reads=[sc], writes=[Aneg])
        P.op("dve", lambda e: e.tensor_scalar(out=Aneg[:, :], in0=Aneg[:, :], scalar1=-1.0, scalar2=None, op0=ALU.mult), reads=[Aneg], writes=[Aneg])
        P.op("dve", lambda e: e.tensor_scalar(out=a_[:, :], in0=dt[:, :], scalar1=Aneg[:, 0:1], scalar2=None, op0=ALU.mult), reads=[dt, Aneg], writes=[a_])
        P.op("pe", lambda e: e.matmul(p_misc[:, 0:NT], lhsT=tri[:, :], rhs=a_[:, :], start=True, stop=True), reads=[tri, a_], writes=[p_misc])
        P.op("dve", lambda e: e.tensor_copy(out=acum[:, :], in_=p_misc[:, 0:NT]), reads=[p_misc], writes=[acum])
        P.op("dve", lambda e: e.tensor_scalar(out=nacum[:, :], in0=acum[:, :], scalar1=-1.0, scalar2=None, op0=ALU.mult), reads=[acum], writes=[nacum])
        P.op("pe", lambda e: e.matmul(p_misc[:, 0:NT], lhsT=ones[:, :], rhs=a_[:, :], start=True, stop=True), reads=[ones, a_], writes=[p_misc])
        P.op("act", lambda e: e.activation(out=dlast[:, :], in_=p_misc[:, 0:NT], func=AF.Exp), reads=[p_misc], writes=[dlast])
        P.op("dve", lambda e: e.tensor_tensor(out=dB[:, :], in0=p_misc[:, 0:NT], in1=acum[:, :], op=ALU.subtract), reads=[p_misc, acum], writes=[dB])
        P.op("act", lambda e: e.activation(out=dB[:, :], in_=dB[:, :], func=AF.Exp), reads=[dB], writes=[dB])
        P.op("dve", lambda e: e.memset(hf[:, :], 0.0), writes=[hf])
        P.op("pool", lambda e: e.memset(hb[0][:, :], 0.0), writes=[hb[0]])
        seq_state = {"next": 0}

        def chunk_gen(c):
            cs = slice(c * 128, (c + 1) * 128)
            i2 = c % SK
            bk = pk_[i2]
            R = [slice(0, 128), slice(128, 256), slice(256, 384), slice(384, 512)]
            b0 = i2 * 128
            P.op("pool", lambda e: e.tensor_scalar(out=dg[i2][:, :], in0=idf[:, :], scalar1=acum[:, c:c + 1], scalar2=None, op0=ALU.mult),
                 reads=[idf, acum], writes=[dg[i2]])
            yield
            P.op("pe", lambda e: e.matmul(bk[:, R[0]], lhsT=ones[:, :], rhs=dg[i2][:, :], start=True, stop=False, skip_group_check=True), reads=[ones, dg[i2]], writes=[bk])
            P.op("pe", lambda e: e.matmul(bk[:, R[0]], lhsT=idb[:, :], rhs=mneg[:, :], start=False, stop=True, skip_group_check=True), reads=[idb, mneg], writes=[bk])
            P.op("pe", lambda e: e.matmul(bk[:, R[1]], lhsT=ones[:, :], rhs=dg[i2][:, :], start=True, stop=True, skip_group_check=True), reads=[ones, dg[i2]], writes=[bk])
            P.op("pe", lambda e: e.matmul(bk[:, R[2]], lhsT=BT[:, cs], rhs=CT[:, cs], start=True, stop=True, skip_group_check=True), reads=[BT, CT], writes=[bk])
            P.op("pe", lambda e: e.transpose(out=bk[:, 384:448], in_=xsT[:, cs], identity=idf[0:64, 0:64]), reads=[xsT, idf], writes=[bk])
            P.op("pe", lambda e: e.transpose(out=pbf[:, b0:b0 + 128], in_=BT[:, cs], identity=idb[:, :]), reads=[BT, idb], writes=[pbf])
            yield
            P.op("act", lambda e: e.activation(out=EB[i2][:, :], in_=bk[:, R[1]], func=AF.Exp), reads=[bk], writes=[EB[i2]])
            P.op("act", lambda e: e.activation(out=LmT[i2][:, :], in_=bk[:, R[0]], func=AF.Exp, bias=nacum[:, c:c + 1], scale=1.0),
                 reads=[bk, nacum], writes=[LmT[i2]])
            yield
            P.op("dve", lambda e: e.tensor_tensor(out=SLT[i2][:, :], in0=bk[:, R[2]], in1=LmT[i2][:, :], op=ALU.mult), reads=[bk, LmT[i2]], writes=[SLT[i2]])
            P.op("dve", lambda e: e.tensor_scalar(out=X[i2][:, :], in0=bk[:, 384:448], scalar1=dt[:, c:c + 1], scalar2=None, op0=ALU.mult),
                 reads=[bk, dt], writes=[X[i2]])
            P.op("dve", lambda e: e.tensor_scalar(out=Bd[i2][:, :], in0=pbf[:, b0:b0 + 128], scalar1=dB[:, c:c + 1], scalar2=None, op0=ALU.mult),
                 reads=[pbf, dB], writes=[Bd[i2]])
            P.op("pool", lambda e: e.tensor_tensor(out=CgT[i2][:, :], in0=CT[:, cs], in1=EB[i2][:, :], op=ALU.mult), reads=[CT, EB[i2]], writes=[CgT[i2]])
            yield
            while seq_state["next"] != c:
                yield
            hcur = hb[c % 2]
            hnxt = hb[(c + 1) % 2]
            P.op("pe", lambda e: e.matmul(bk[0:64, R[0]], lhsT=X[i2][:, :], rhs=SLT[i2][:, :], start=True, stop=False, skip_group_check=True), reads=[X[i2], SLT[i2]], writes=[bk])
            P.op("pe", lambda e: e.matmul(bk[0:64, R[0]], lhsT=hcur[:, :], rhs=CgT[i2][:, :], start=False, stop=True, skip_group_check=True), reads=[hcur, CgT[i2]], writes=[bk])
            P.op("pe", lambda e: e.matmul(bk[:, 128:192], lhsT=Bd[i2][:, :], rhs=X[i2][:, :], start=True, stop=True, skip_group_check=True), reads=[Bd[i2], X[i2]], writes=[bk])
            seq_state["next"] = c + 1
            yield
            P.op("dve", lambda e: e.scalar_tensor_tensor(out=hf[:, :], in0=hf[:, :], scalar=dlast[:, c:c + 1], in1=bk[:, 128:192], op0=ALU.mult, op1=ALU.add),
                 reads=[hf, dlast, bk], writes=[hf])
            P.op("act", lambda e: e.copy(out=hnxt[:, :], in_=hf[:, :]), reads=[hf], writes=[hnxt])
            P.op("dve", lambda e: e.scalar_tensor_tensor(out=yac[:, cs], in0=xsT[:, cs], scalar=sc[0:64, 2:3], in1=bk[0:64, R[0]], op0=ALU.mult, op1=ALU.add),
                 reads=[xsT, sc, bk], writes=[yac])

        run_pipeline([chunk_gen(c) for c in range(min(NT, SSD_NCH))], SK)
        P.dma("pool", y_d[h], yac[:, :], reads=[yac], writes=[y_d.buf])
        if NT % 2 == 1 or True:
            P.op("pool", lambda e: e.memset(hb[0][:, :], 0.0), writes=[hb[0]])


def rec_consts():
    import ml_dtypes
    bf = ml_dtypes.bfloat16
    s1 = np.arange(128)[:, None]
    s2 = np.arange(128)[None, :]
    tri = (s1 <= s2).astype(np.float32)
    mneg = np.where(s2 < s1, -NEGM, 0.0).astype(np.float32)
    mpos = np.where(s2 >= s1, NEGM, 0.0).astype(np.float32)
    return dict(tri=tri, ones=np.ones((128, 128), np.float32), idf=np.eye(128, dtype=np.float32),
                idb=np.eye(128, dtype=np.float32).astype(bf), mneg=mneg.astype(bf), mpos=mpos.astype(bf))


GDN_NCH = 10 ** 9
GDN_K = 5
GDN_CHAIN_BF16 = False


def run_pipeline(gens, K):
    active = []
    it = iter(gens)
    pending = True
    while True:
        if pending and len(active) < K:
            g = next(it, None)
            if g is None:
                pending = False
            else:
                active.append(g)
        if not active:
            if not pending:
                break
            continue
        for g in list(active):
            try:
                next(g)
            except StopIteration:
                active.remove(g)


def emit_gdn(P, qpre_d, kpre_d, vpre_d, cwq_d, cwk_d, cwv_d, bcol_d, acol_d, sc_d, gate_d, nw_d,
             tri_d, ones_d, idf_d, idb_d, mneg_d, mpos_d, y_d, NH, S=SEQ, pfx="gd"):
    NT = S // 128
    zp = P.sbuf(pfx + "_zp", [128, S + 3], F32)
    acc = P.sbuf(pfx + "_acc", [128, S], F32)
    qn = P.sbuf(pfx + "_qn", [64, S], BF16)
    kn = P.sbuf(pfx + "_kn", [64, S], BF16)
    vT = P.sbuf(pfx + "_vT", [64, S], BF16)
    gt = P.sbuf(pfx + "_gt", [128, NT, 64], F32)
    yall = [P.sbuf(pfx + f"_yall{i}", [64, 2048], F32) for i in range(2)]
    ytm = [P.sbuf(pfx + f"_ytm{i}", [128, 64], F32) for i in range(GDN_K)]
    cw = [P.sbuf(pfx + f"_cw{i}", [64, 5], F32) for i in range(3)]
    sc = P.sbuf(pfx + "_sc", [128, 2], F32)
    nw = P.sbuf(pfx + "_nw", [128, 64], F32)
    tri = P.sbuf(pfx + "_tri", [128, 128], F32)
    ones = P.sbuf(pfx + "_ones", [128, 128], F32)
    idf = P.sbuf(pfx + "_idf", [128, 128], F32)
    idb = P.sbuf(pfx + "_idb", [128, 128], BF16)
    mneg = P.sbuf(pfx + "_mneg", [128, 128], BF16)
    mpos = P.sbuf(pfx + "_mpos", [128, 128], BF16)
    col = {n: P.sbuf(pfx + "_c_" + n, [128, NT], F32) for n in
           ("b", "a", "t1", "beta", "nbeta", "g", "gcum", "ngcum", "egc", "bege", "dk", "dlast")}
    Aneg = P.sbuf(pfx + "_Aneg", [128, 1], F32)
    ball = P.sbuf(pfx + "_ball", [128, NT, 6], F32)
    aall = P.sbuf(pfx + "_aall", [128, NT, 6], F32)
    rt = [P.sbuf(pfx + f"_rt{i}", [64, 512], F32) for i in range(2)]
    eps_t = P.sbuf(pfx + "_eps", [128, 1], F32)
    P.op("dve", lambda e: e.memset(eps_t[:, :], 1e-6), writes=[eps_t])

    def f32t(n, k=2):
        return [P.sbuf(pfx + f"_{n}{i}", [128, 128], F32) for i in range(k)]

    def bft(n, shape, k=2):
        return [P.sbuf(pfx + f"_{n}{i}", shape, BF16) for i in range(k)]
    NB = GDN_K
    dg = f32t("dg", NB)
    EB = f32t("EB", NB)
    dec = f32t("dec", NB)
    decT = f32t("decT", NB)
    if GDN_CHAIN_BF16:
        Ys = [bft(f"Y{j}_", [128, 128], 2) for j in range(NB)]
        YTs = [bft(f"YT{j}_", [128, 128], 2) for j in range(NB)]
        PTs = [f32t(f"PT{j}_", 2) for j in range(NB)]
        PTbs = [bft(f"PTb{j}_", [128, 128], 2) for j in range(NB)]
        YYs = Ys
    else:
        YYs = [[P.sbuf(pfx + f"_YY{j}_{i}", [128, 256], F32) for i in range(2)] for j in range(NB)]
        Ys = [[Buf(y.name + "Y", y[:, 0:128]) for y in yy] for yy in YYs]
        YTs = [[Buf(y.name + "T", y[:, 128:256]) for y in yy] for yy in YYs]
        PTs = [f32t(f"PT{j}_", 2) for j in range(NB)]
        PTbs = PTs
    TTb = bft("TTb", [128, 128], NB)
    attnT = bft("attnT", [128, 128], NB)
    qgT = bft("qgT", [64, 128], NB)
    kbg = bft("kbg", [128, 64], NB)
    kd = bft("kd", [128, 64], NB)
    vb = bft("vb", [128, 64], NB)
    wT = bft("wT", [64, 128], NB)
    u_sb = [P.sbuf(pfx + f"_u{i}", [128, 64], F32) for i in range(NB)]
    vnew = bft("vnew", [128, 64], NB)
    Sf = P.sbuf(pfx + "_Sf", [64, 64], F32)
    Sb = bft("Sb", [64, 64])
    junk = [P.sbuf(pfx + f"_junk{i}", [128, 64], F32) for i in range(NB)]
    ssq = [P.sbuf(pfx + f"_ssq{i}", [128, 1], F32) for i in range(NB)]
    nwg = [P.sbuf(pfx + f"_nwg{i}", [128, 64], F32) for i in range(NB)]
    pa = [P.psum(pfx + f"_pa{i}", [128, 512]) for i in range(7)]
    pbs = [P.psum(pfx + f"_pb{i}", [128, 1024], BF16) for i in range(1)]
    pg = [pa[5], pa[6]]
    pai = [0]

    def nxt_pa():
        pai[0] += 1
        return pa[5 + pai[0] % 2]
    pci = [0]

    def nxt_pc():
        pci[0] += 1
        return pc[pci[0] % 2]

    for (sb, d_) in ((tri, tri_d), (ones, ones_d), (idf, idf_d), (idb, idb_d), (mneg, mneg_d), (mpos, mpos_d)):
        P.dma("sp", sb[:, :], d_[:, :], reads=[d_], writes=[sb])
    P.op("pool", lambda e: e.memset(zp[:, 0:3], 0.0), writes=[zp])
    for h in range(NH):
        for i, d_ in enumerate((cwq_d, cwk_d, cwv_d)):
            P.dma("sp", cw[i][:, :], d_[h], reads=[d_.buf], writes=[cw[i]])
        P.dma("sp", sc[:, :], sc_d[h], reads=[sc_d.buf], writes=[sc])
        P.dma("sp", nw[:, :], nw_d[h], reads=[nw_d.buf], writes=[nw])
        if h == 0:
            P.dma("sp", ball[:, :, :], bcol_d[0], reads=[bcol_d.buf], writes=[ball])
            P.dma("sp", aall[:, :, :], acol_d[0], reads=[acol_d.buf], writes=[aall])
        P.op("pool", lambda e, h=h: e.tensor_copy(out=col["b"][:, :], in_=ball[:, :, h]), reads=[ball], writes=[col["b"]])
        P.op("pool", lambda e, h=h: e.tensor_copy(out=col["a"][:, :], in_=aall[:, :, h]), reads=[aall], writes=[col["a"]])
        P.dma("act", gt[:, :, :], gate_d[h], reads=[gate_d.buf], writes=[gt])
        P.op("act", lambda e: e.activation(out=gt[:, :, :], in_=gt[:, :, :], func=AF.Silu), reads=[gt], writes=[gt])
        for which, (src_d, outb) in enumerate(((qpre_d, qn), (kpre_d, kn))):
            emit_conv_silu(P, src_d[h], cw[which], 64, S, zp, acc, [(acc, acc[0:64, :])], src_buf=src_d.buf)
            P.op("act", lambda e: e.activation(out=zp[0:64, 3:S + 3], in_=acc[0:64, :], func=AF.Square), reads=[acc], writes=[zp])
            for j in range(S // 512):
                cs = slice(j * 512, (j + 1) * 512)
                ps = nxt_pa()
                r_ = rt[j % 2]
                P.op("pe", lambda e, ps=ps, j=j: e.matmul(ps[0:64, :], lhsT=ones[0:64, 0:64], rhs=zp[0:64, 3 + j * 512:3 + (j + 1) * 512],
                                                         start=True, stop=True), reads=[ones, zp], writes=[ps])
                P.op("act", lambda e, ps=ps, r_=r_: e.activation(out=r_[:, :], in_=ps[0:64, :], func=AF.Sqrt, bias=eps_t[0:64, 0:1], scale=1.0),
                     reads=[ps, eps_t], writes=[r_])
                P.op("dve", lambda e, r_=r_: e.reciprocal(out=r_[:, :], in_=r_[:, :]), reads=[r_], writes=[r_])
                P.op("dve", lambda e, r_=r_, cs=cs, outb=outb, sc_=(0.125 if which == 0 else 1.0): e.scalar_tensor_tensor(
                    out=outb[:, cs], in0=acc[0:64, cs], scalar=sc_, in1=r_[:, :], op0=ALU.mult, op1=ALU.mult),
                    reads=[acc, r_], writes=[outb])
        emit_conv_silu(P, vpre_d[h], cw[2], 64, S, zp, acc, [(vT, vT[:, :])], src_buf=vpre_d.buf)
        c_ = col
        P.op("act", lambda e: e.activation(out=c_["beta"][:, :], in_=c_["b"][:, :], func=AF.Sigmoid), reads=[c_["b"]], writes=[c_["beta"]])
        P.op("dve", lambda e: e.tensor_scalar(out=c_["nbeta"][:, :], in0=c_["beta"][:, :], scalar1=-1.0, scalar2=None, op0=ALU.mult),
             reads=[c_["beta"]], writes=[c_["nbeta"]])
        P.op("dve", lambda e: e.tensor_scalar(out=c_["a"][:, :], in0=c_["a"][:, :], scalar1=sc[:, 0:1], scalar2=None, op0=ALU.add),
             reads=[c_["a"], sc], writes=[c_["a"]])
        emit_softplus(P, "dve", c_["g"], c_["g"][:, :], c_["a"], c_["a"][:, :], c_["t1"], c_["t1"][:, :], 128)
        P.op("act", lambda e: e.activation(out=Aneg[:, :], in_=sc[:, 1:2], func=AF.Exp), reads=[sc], writes=[Aneg])
        P.op("dve", lambda e: e.tensor_scalar(out=Aneg[:, :], in0=Aneg[:, :], scalar1=-1.0, scalar2=None, op0=ALU.mult), reads=[Aneg], writes=[Aneg])
        P.op("dve", lambda e: e.tensor_scalar(out=c_["g"][:, :], in0=c_["g"][:, :], scalar1=Aneg[:, 0:1], scalar2=None, op0=ALU.mult),
             reads=[c_["g"], Aneg], writes=[c_["g"]])
        pm = pg[0]
        P.op("pe", lambda e: e.matmul(pm[:, 0:NT], lhsT=tri[:, :], rhs=c_["g"][:, :], start=True, stop=True), reads=[tri, c_["g"]], writes=[pm])
        P.op("dve", lambda e: e.tensor_copy(out=c_["gcum"][:, :], in_=pm[:, 0:NT]), reads=[pm], writes=[c_["gcum"]])
        P.op("dve", lambda e: e.tensor_scalar(out=c_["ngcum"][:, :], in0=c_["gcum"][:, :], scalar1=-1.0, scalar2=None, op0=ALU.mult),
             reads=[c_["gcum"]], writes=[c_["ngcum"]])
        P.op("act", lambda e: e.activation(out=c_["egc"][:, :], in_=c_["gcum"][:, :], func=AF.Exp), reads=[c_["gcum"]], writes=[c_["egc"]])
        P.op("dve", lambda e: e.tensor_tensor(out=c_["bege"][:, :], in0=c_["egc"][:, :], in1=c_["beta"][:, :], op=ALU.mult),
             reads=[c_["egc"], c_["beta"]], writes=[c_["bege"]])
        P.op("pe", lambda e: e.matmul(pm[:, 0:NT], lhsT=ones[:, :], rhs=c_["g"][:, :], start=True, stop=True), reads=[ones, c_["g"]], writes=[pm])
        P.op("act", lambda e: e.activation(out=c_["dlast"][:, :], in_=pm[:, 0:NT], func=AF.Exp), reads=[pm], writes=[c_["dlast"]])
        P.op("dve", lambda e: e.tensor_tensor(out=c_["dk"][:, :], in0=pm[:, 0:NT], in1=c_["gcum"][:, :], op=ALU.subtract),
             reads=[pm, c_["gcum"]], writes=[c_["dk"]])
        P.op("act", lambda e: e.activation(out=c_["dk"][:, :], in_=c_["dk"][:, :], func=AF.Exp), reads=[c_["dk"]], writes=[c_["dk"]])
        P.op("dve", lambda e: e.memset(Sf[:, :], 0.0), writes=[Sf])
        P.op("pool", lambda e: e.memset(Sb[0][:, :], 0.0), writes=[Sb[0]])
        seq_state = {"next": 0}

        def chunk_gen(c):
            cs = slice(c * 128, (c + 1) * 128)
            i2 = c % NB
            slot = c % GDN_K
            bA = pa[slot]
            bB = bA
            rA = [slice(0, 128), slice(128, 256), slice(256, 384), slice(384, 512)]
            Y, YT, PT, PTb = Ys[i2], YTs[i2], PTs[i2], PTbs[i2]
            YY = YYs[i2]
            Yv = lambda i: YY[i][:, 0:128]
            YTv = lambda i: YY[i][:, 128:256]
            bkb = bA.t.bitcast(BF16)
            trg = bkb[:, 256:384] if GDN_CHAIN_BF16 else bA[:, rA[1]]
            idt = idb if GDN_CHAIN_BF16 else idf
            P.op("pool", lambda e, c=c, i2=i2: e.tensor_scalar(out=dg[i2][:, :], in0=idf[:, :], scalar1=c_["gcum"][:, c:c + 1], scalar2=None, op0=ALU.mult),
                 reads=[idf, c_["gcum"]], writes=[dg[i2]])
            yield
            P.op("pe", lambda e, i2=i2: e.matmul(bA[:, rA[0]], lhsT=ones[:, :], rhs=dg[i2][:, :], start=True, stop=False, skip_group_check=True), reads=[ones, dg[i2]], writes=[bA])
            P.op("pe", lambda e: e.matmul(bA[:, rA[0]], lhsT=idb[:, :], rhs=mpos[:, :], start=False, stop=True, skip_group_check=True), reads=[idb, mpos], writes=[bA])
            P.op("pe", lambda e, i2=i2: e.matmul(bA[:, rA[1]], lhsT=ones[:, :], rhs=dg[i2][:, :], start=True, stop=False, skip_group_check=True), reads=[ones, dg[i2]], writes=[bA])
            P.op("pe", lambda e: e.matmul(bA[:, rA[1]], lhsT=idb[:, :], rhs=mneg[:, :], start=False, stop=True, skip_group_check=True), reads=[idb, mneg], writes=[bA])
            P.op("pe", lambda e, i2=i2: e.matmul(bA[:, rA[2]], lhsT=ones[:, :], rhs=dg[i2][:, :], start=True, stop=True, skip_group_check=True), reads=[ones, dg[i2]], writes=[bA])
            P.op("pe", lambda e, cs=cs: e.matmul(bA[:, rA[3]], lhsT=kn[:, cs], rhs=kn[:, cs], start=True, stop=True, skip_group_check=True), reads=[kn], writes=[bA])
            yield
            P.op("act", lambda e, i2=i2: e.activation(out=EB[i2][0:64, :], in_=bA[0:64, rA[2]], func=AF.Exp), reads=[bA], writes=[EB[i2]])
            P.op("act", lambda e, i2=i2, c=c: e.activation(out=dec[i2][:, :], in_=bA[:, rA[0]], func=AF.Exp, bias=c_["gcum"][:, c:c + 1], scale=-1.0),
                 reads=[bA, c_["gcum"]], writes=[dec[i2]])
            P.op("act", lambda e, i2=i2, c=c: e.activation(out=decT[i2][:, :], in_=bA[:, rA[1]], func=AF.Exp, bias=c_["ngcum"][:, c:c + 1], scale=1.0),
                 reads=[bA, c_["ngcum"]], writes=[decT[i2]])
            yield
            P.op("pool", lambda e, i2=i2, cs=cs: e.tensor_tensor(out=qgT[i2][:, :], in0=qn[:, cs], in1=EB[i2][0:64, :], op=ALU.mult),
                 reads=[qn, EB[i2]], writes=[qgT[i2]])
            yield
            PTc = PT[0]
            P.op("dve", lambda e, i2=i2, c=c: e.scalar_tensor_tensor(out=Yv(0), in0=bA[:, rA[3]], scalar=c_["nbeta"][:, c:c + 1],
                                                                          in1=dec[i2][:, :], op0=ALU.mult, op1=ALU.mult),
                 reads=[bA, c_["nbeta"], dec[i2]], writes=[YY[0]])
            yield
            P.op("pe", lambda e, cs=cs: e.matmul(bB[:, rA[0]], lhsT=kn[:, cs], rhs=qn[:, cs], start=True, stop=True, skip_group_check=True), reads=[kn, qn], writes=[bB])
            P.op("pe", lambda e: e.transpose(out=trg, in_=Yv(0), identity=idt[:, :]), reads=[YY[0], idt], writes=[bB])
            pkt = pbs[0]
            k0 = slot * 128
            P.op("pe", lambda e, cs=cs, pkt=pkt, k0=k0: e.transpose(out=pkt[:, k0:k0 + 64], in_=kn[:, cs], identity=idb[0:64, 0:64]), reads=[kn, idb], writes=[pkt])
            P.op("pe", lambda e, cs=cs, pkt=pkt, k0=k0: e.transpose(out=pkt[:, k0 + 64:k0 + 128], in_=vT[:, cs], identity=idb[0:64, 0:64]), reads=[vT, idb], writes=[pkt])
            yield
            P.op("dve", lambda e, i2=i2: e.tensor_tensor(out=attnT[i2][:, :], in0=bB[:, rA[0]], in1=decT[i2][:, :], op=ALU.mult),
                 reads=[bB, decT[i2]], writes=[attnT[i2]])
            P.op("act", lambda e: e.copy(out=YTv(0), in_=trg), reads=[bB], writes=[YY[0]])
            P.op("act", lambda e, PTc=PTc: e.activation(out=PTc[:, :], in_=trg, func=AF.Identity), reads=[bB], writes=[PTc])
            P.op("dve", lambda e, i2=i2, c=c, pkt=pkt: e.tensor_scalar(out=kbg[i2][:, :], in0=pkt[:, slot * 128:slot * 128 + 64], scalar1=c_["bege"][:, c:c + 1], scalar2=None, op0=ALU.mult),
                 reads=[pkt, c_["bege"]], writes=[kbg[i2]])
            P.op("dve", lambda e, i2=i2, c=c, pkt=pkt: e.tensor_scalar(out=kd[i2][:, :], in0=pkt[:, slot * 128:slot * 128 + 64], scalar1=c_["dk"][:, c:c + 1], scalar2=None, op0=ALU.mult),
                 reads=[pkt, c_["dk"]], writes=[kd[i2]])
            P.op("dve", lambda e, i2=i2, c=c, pkt=pkt: e.tensor_scalar(out=vb[i2][:, :], in0=pkt[:, slot * 128 + 64:slot * 128 + 128], scalar1=c_["beta"][:, c:c + 1], scalar2=None, op0=ALU.mult),
                 reads=[pkt, c_["beta"]], writes=[vb[i2]])
            P.op("pool", lambda e, PTc=PTc: e.tensor_tensor(out=PTc[:, :], in0=PTc[:, :], in1=idf[:, :], op=ALU.add), reads=[PTc, idf], writes=[PTc])
            if GDN_CHAIN_BF16:
                P.op("pool", lambda e, PTc=PTc: e.tensor_copy(out=PTb[0][:, :], in_=PTc[:, :]), reads=[PTc], writes=[PTb[0]])
            yield
            cur = 0
            for lev in range(6):
                nx = 1 - cur
                P.op("pe", lambda e, cur=cur: e.matmul(bB[:, rA[2]], lhsT=YTv(cur), rhs=Yv(cur), start=True, stop=True, skip_group_check=True),
                     reads=[YY[cur]], writes=[bB])
                if lev < 5:
                    P.op("pe", lambda e, cur=cur: e.matmul(bB[:, rA[3]], lhsT=Yv(cur), rhs=YTv(cur), start=True, stop=True, skip_group_check=True),
                         reads=[YY[cur]], writes=[bB])
                yield
                if lev < 5 and not GDN_CHAIN_BF16:
                    P.op("act", lambda e, nx=nx: e.copy(out=YY[nx][:, :], in_=bB[:, 256:512]), reads=[bB], writes=[YY[nx]])
                else:
                    P.op("act", lambda e, nx=nx: e.copy(out=Yv(nx), in_=bB[:, rA[2]]), reads=[bB], writes=[YY[nx]])
                    if lev < 5:
                        P.op("act", lambda e, nx=nx: e.copy(out=YTv(nx), in_=bB[:, rA[3]]), reads=[bB], writes=[YY[nx]])
                yield
                P.op("pe", lambda e, nx=nx, cur=cur: e.matmul(bB[:, rA[0]], lhsT=Yv(nx), rhs=PTb[cur][:, :], start=True, stop=True, skip_group_check=True),
                     reads=[YY[nx], PTb[cur]], writes=[bB])
                yield
                if lev == 5:
                    P.op("dve", lambda e, cur=cur: e.tensor_tensor(out=TTb[i2][:, :], in0=bB[:, rA[0]], in1=PT[cur][:, :], op=ALU.add),
                         reads=[bB, PT[cur]], writes=[TTb[i2]])
                else:
                    P.op("dve", lambda e, nx=nx, cur=cur: e.tensor_tensor(out=PT[nx][:, :], in0=bB[:, rA[0]], in1=PT[cur][:, :], op=ALU.add),
                         reads=[bB, PT[cur]], writes=[PT[nx]])
                if GDN_CHAIN_BF16 and lev < 5:
                    P.op("pool", lambda e, nx=nx: e.tensor_copy(out=PTb[nx][:, :], in_=PT[nx][:, :]), reads=[PT[nx]], writes=[PTb[nx]])
                yield
                cur = nx
            P.op("pe", lambda e, i2=i2: e.matmul(bA[:, 128:192], lhsT=TTb[i2][:, :], rhs=vb[i2][:, :], start=True, stop=True, skip_group_check=True), reads=[TTb[i2], vb[i2]], writes=[bA])
            P.op("pe", lambda e, i2=i2: e.matmul(bA[0:64, 256:384], lhsT=kbg[i2][:, :], rhs=TTb[i2][:, :], start=True, stop=True, skip_group_check=True), reads=[kbg[i2], TTb[i2]], writes=[bA])
            yield
            P.op("act", lambda e, i2=i2: e.copy(out=u_sb[i2][:, :], in_=bA[:, 128:192]), reads=[bA], writes=[u_sb[i2]])
            P.op("act", lambda e, i2=i2: e.copy(out=wT[i2][:, :], in_=bA[0:64, 256:384]), reads=[bA], writes=[wT[i2]])
            P.op("pool", lambda e, i2=i2, c=c: e.tensor_tensor(out=nwg[i2][:, :], in0=gt[:, c, :], in1=nw[:, :], op=ALU.mult), reads=[gt, nw], writes=[nwg[i2]])
            yield
            while seq_state["next"] != c:
                yield
            Scur, Snxt = Sb[c % 2], Sb[(c + 1) % 2]
            P.op("pe", lambda e, i2=i2, Scur=Scur: e.matmul(bA[:, 384:448], lhsT=wT[i2][:, :], rhs=Scur[:, :], start=True, stop=True, skip_group_check=True), reads=[wT[i2], Scur], writes=[bA])
            P.op("dve", lambda e, i2=i2: e.tensor_tensor(out=vnew[i2][:, :], in0=u_sb[i2][:, :], in1=bA[:, 384:448], op=ALU.subtract),
                 reads=[u_sb[i2], bA], writes=[vnew[i2]])
            P.op("pe", lambda e, i2=i2: e.matmul(bA[0:64, 128:192], lhsT=kd[i2][:, :], rhs=vnew[i2][:, :], start=True, stop=True, skip_group_check=True), reads=[kd[i2], vnew[i2]], writes=[bA])
            P.op("pe", lambda e, i2=i2, Scur=Scur: e.matmul(bB[:, 0:64], lhsT=qgT[i2][:, :], rhs=Scur[:, :], start=True, stop=False, skip_group_check=True), reads=[qgT[i2], Scur], writes=[bB])
            P.op("pe", lambda e, i2=i2: e.matmul(bB[:, 0:64], lhsT=attnT[i2][:, :], rhs=vnew[i2][:, :], start=False, stop=True, skip_group_check=True), reads=[attnT[i2], vnew[i2]], writes=[bB])
            P.op("dve", lambda e, c=c: e.scalar_tensor_tensor(out=Sf[:, :], in0=Sf[:, :], scalar=c_["dlast"][0:64, c:c + 1], in1=bA[0:64, 128:192],
                                                             op0=ALU.mult, op1=ALU.add), reads=[Sf, c_["dlast"], bA], writes=[Sf])
            P.op("act", lambda e, Snxt=Snxt: e.copy(out=Snxt[:, :], in_=Sf[:, :]), reads=[Sf], writes=[Snxt])
            seq_state["next"] = c + 1
            yield
            P.op("act", lambda e, i2=i2: e.activation(out=junk[i2][:, :], in_=bB[:, 0:64], func=AF.Square, accum_out=ssq[i2][:, 0:1]),
                 reads=[bB], writes=[junk[i2], ssq[i2]])
            P.op("act", lambda e, i2=i2: e.activation(out=ssq[i2][:, :], in_=ssq[i2][:, :], func=AF.Sqrt, bias=eps_t[:, 0:1], scale=1.0 / 64.0),
                 reads=[ssq[i2], eps_t], writes=[ssq[i2]])
            yield
            P.op("dve", lambda e, i2=i2: e.reciprocal(out=ssq[i2][:, :], in_=ssq[i2][:, :]), reads=[ssq[i2]], writes=[ssq[i2]])
            P.op("dve", lambda e, i2=i2, c=c: e.scalar_tensor_tensor(out=ytm[i2][:, :], in0=bB[:, 0:64], scalar=ssq[i2][:, 0:1], in1=nwg[i2][:, :],
                                                                    op0=ALU.mult, op1=ALU.mult), reads=[bB, ssq[i2], nwg[i2]], writes=[ytm[i2]])
            yield
            P.op("pe", lambda e, i2=i2: e.transpose(out=bB[0:64, 256:384], in_=ytm[i2][:, :], identity=idf[:, :]), reads=[ytm[i2], idf], writes=[bB])
            yield
            yb_ = yall[(c // 16) % 2]
            P.op("act", lambda e, yb_=yb_: e.copy(out=yb_[:, (c % 16) * 128:(c % 16 + 1) * 128], in_=bB[0:64, 256:384]), reads=[bB], writes=[yb_])
            if c % 16 == 15 or c == min(NT, GDN_NCH) - 1:
                c0 = (c // 16) * 16
                P.dma("pool", y_d[h][:, c0 * 128:(c + 1) * 128], yb_[:, 0:(c - c0 + 1) * 128], reads=[yb_], writes=[y_d.buf])

        run_pipeline([chunk_gen(c) for c in range(min(NT, GDN_NCH))], GDN_K)
        P.op("pool", lambda e: e.memset(Sb[0][:, :], 0.0), writes=[Sb[0]])
        if min(NT, GDN_NCH) % 2 == 1:
            P.op("pool", lambda e: e.memset(Sb[1][:, :], 0.0), writes=[Sb[1]])


D_MIX = 1536
ALPHA = (2.0 * 2) ** 0.25
N_FM = 4736
N_TM = 1170
FM_AQ, FM_AK, FM_AG, FM_SX, FM_SB, FM_SC, FM_SZ, FM_CQ, FM_CK, FM_CG, FM_DQ, FM_DK, FM_DV = (
    0, 384, 768, 1152, 1536, 1792, 2048, 2432, 2816, 3200, 3584, 3968, 4352)
TM_AV, TM_CV, TM_DG, TM_DT, TM_DB, TM_DA = 0, 384, 768, 1152, 1158, 1164


def in_col_perm():
    r = lambda a, b: list(range(a, b))
    fm = r(0, 768) + r(1152, 1536) + r(1536, 2816) + r(2822, 3590) + r(3974, 4358) + r(4358, 5510)
    tm = r(768, 1152) + r(3590, 3974) + r(5510, 5894) + r(2816, 2822) + r(5894, 5900) + r(5900, 5906)
    assert len(fm) == N_FM and len(tm) == N_TM
    return np.array(fm + tm)


def emit_projF(P, x_src, src_bf16, w_ap, w_buf, projT, projtm, S=SEQ, pfx="pj"):
    KC = D_MODEL // 128
    C = IN_COLS
    TB = 1024
    wbf = P.sbuf(pfx + "_wbf", [128, KC, C], BF16)
    HW = (C + 1) // 2
    wst = [P.sbuf(pfx + f"_wst{i}", [128, HW], F32) for i in range(2)]
    xbf = [P.sbuf(pfx + f"_xbf{i}", [128, KC, TB], BF16) for i in range(2)]
    xst = [P.sbuf(pfx + f"_xst{i}", [128, TB], F32) for i in range(2)]
    ost = [P.sbuf(pfx + f"_ost{i}", [128, 512], F32) for i in range(3)]
    ot2 = [P.sbuf(pfx + f"_ot2{i}", [128, N_TM], F32) for i in range(2)]
    ps = [P.psum(pfx + f"_ps{i}", [128, 512]) for i in range(6)]
    n = 0
    for k in range(KC):
        for h in range(2):
            c0, c1 = h * HW, min(C, (h + 1) * HW)
            b = wst[n % 2]
            P.dma("sp" if n % 2 == 0 else "act", b[:, :c1 - c0], w_ap[k * 128:(k + 1) * 128, c0:c1], reads=[w_buf], writes=[b])
            eng = "dve" if n % 2 == 0 else "pool"
            P.op(eng, lambda e, b=b, k=k, c0=c0, c1=c1: e.tensor_copy(out=wbf[:, k, c0:c1], in_=b[:, :c1 - c0]), reads=[b], writes=[wbf])
            n += 1
    ci = 0
    oi = 0
    for blk in range(S // TB):
        xb = xbf[blk % 2]
        t0 = blk * TB
        if src_bf16:
            P.dma("sp", xb[:, :, :], x_src.t.rearrange("(k p) t -> p k t", p=128)[:, :, t0:t0 + TB], reads=[x_src], writes=[xb])
        else:
            for k in range(KC):
                b = xst[k % 2]
                P.dma("sp", b[:, :], x_src[k * 128:(k + 1) * 128, t0:t0 + TB], reads=[x_src], writes=[b])
                P.op("pool", lambda e, b=b, k=k, xb=xb: e.tensor_copy(out=xb[:, k, :], in_=b[:, :]), reads=[b], writes=[xb])
        for sub in range(TB // 512):
            for cc in range(N_FM // 128):
                p = ps[ci % 6]
                o = ost[ci % 3]
                for k in range(KC):
                    P.op("pe", lambda e, p=p, k=k, cc=cc, sub=sub, xb=xb: e.matmul(
                        p[:, :], lhsT=wbf[:, k, cc * 128:(cc + 1) * 128], rhs=xb[:, k, sub * 512:(sub + 1) * 512],
                        start=(k == 0), stop=(k == KC - 1)), reads=[wbf, xb], writes=[p])
                if ci % 2 == 0:
                    P.op("dve", lambda e, p=p, o=o: e.tensor_copy(out=o[:, :], in_=p[:, :]), reads=[p], writes=[o])
                else:
                    P.op("act", lambda e, p=p, o=o: e.copy(out=o[:, :], in_=p[:, :]), reads=[p], writes=[o])
                P.dma("pool" if ci % 2 == 0 else "sp", projT[cc * 128:(cc + 1) * 128, t0 + sub * 512:t0 + (sub + 1) * 512], o[:, :],
                      reads=[o], writes=[projT])
                ci += 1
        for tt in range(TB // 128):
            o2 = ot2[oi % 2]
            oi += 1
            for c0 in range(0, N_TM, 512):
                c1 = min(N_TM, c0 + 512)
                p = ps[ci % 6]
                for k in range(KC):
                    P.op("pe", lambda e, p=p, k=k, tt=tt, c0=c0, c1=c1, xb=xb: e.matmul(
                        p[:, :c1 - c0], lhsT=xb[:, k, tt * 128:(tt + 1) * 128], rhs=wbf[:, k, N_FM + c0:N_FM + c1],
                        start=(k == 0), stop=(k == KC - 1)), reads=[wbf, xb], writes=[p])
                if ci % 2 == 0:
                    P.op("dve", lambda e, p=p, o2=o2, c0=c0, c1=c1: e.tensor_copy(out=o2[:, c0:c1], in_=p[:, :c1 - c0]), reads=[p], writes=[o2])
                else:
                    P.op("act", lambda e, p=p, o2=o2, c0=c0, c1=c1: e.copy(out=o2[:, c0:c1], in_=p[:, :c1 - c0]), reads=[p], writes=[o2])
                ci += 1
            P.dma("pool", projtm[t0 + tt * 128:t0 + (tt + 1) * 128, :], o2[:, :], reads=[o2], writes=[projtm])


def emit_outlnF(P, mix_d, projT, snw_ap, w_ap, lng_ap, lnb_ap, ones_ap, idf_ap, cbuf, x_d, xo_d, xoT_d, S=SEQ, TBLK=2048, pfx="ol"):
    FC = D_MIX // 128
    T = TBLK
    wbf = P.sbuf(pfx + "_wbf", [128, FC, D_MODEL], BF16)
    mix = P.sbuf(pfx + "_mix", [128, FC, T], BF16)
    wst = [P.sbuf(pfx + f"_wst{i}", [128, D_MODEL], F32) for i in range(2)]
    mst = [P.sbuf(pfx + f"_mst{i}", [128, T], F32) for i in range(2)]
    zst = [P.sbuf(pfx + f"_zst{i}", [128, T], F32) for i in range(2)]
    gz = P.sbuf(pfx + "_gz", [128, 3, T], F32)
    sq = P.sbuf(pfx + "_sq", [128, T], F32)
    rs = P.sbuf(pfx + "_rs", [128, T], F32)
    snw = P.sbuf(pfx + "_snw", [128, 3], F32)
    lng = P.sbuf(pfx + "_lng", [128, D_MODEL], F32)
    lnb = P.sbuf(pfx + "_lnb", [128, D_MODEL], F32)
    ones = P.sbuf(pfx + "_ones", [128, 128], F32)
    idf = P.sbuf(pfx + "_idf", [128, 128], F32)
    eps6 = P.sbuf(pfx + "_eps6", [128, 1], F32)
    eps5 = P.sbuf(pfx + "_eps5", [128, 1], F32)
    xt = [P.sbuf(pfx + f"_xt{i}", [128, D_MODEL], F32) for i in range(2)]
    zt = [P.sbuf(pfx + f"_zt{i}", [128, D_MODEL], F32) for i in range(2)]
    xTt = [P.sbuf(pfx + f"_xTt{i}", [128, 8, 128], BF16) for i in range(2)]
    st = [P.sbuf(pfx + f"_st{i}", [128, 2, 6], F32) for i in range(2)]
    mv = [P.sbuf(pfx + f"_mv{i}", [128, 2], F32) for i in range(2)]
    pss = [P.psum(pfx + f"_pss{i}", [128, 512]) for i in range(2)]
    po = [P.psum(pfx + f"_po{i}", [128, 512]) for i in range(4)]
    ptr = [P.psum(pfx + f"_ptr{i}", [128, 512]) for i in range(2)]
    P.op("dve", lambda e: e.memset(eps6[:, :], 1e-6), writes=[eps6])
    P.op("dve", lambda e: e.memset(eps5[:, :], 1e-5), writes=[eps5])
    P.dma("sp", snw[:, :], snw_ap, reads=[cbuf], writes=[snw])
    P.dma("sp", lng[:, :], lng_ap, reads=[cbuf], writes=[lng])
    P.dma("sp", lnb[:, :], lnb_ap, reads=[cbuf], writes=[lnb])
    P.dma("sp", ones[:, :], ones_ap, reads=[cbuf], writes=[ones])
    P.dma("sp", idf[:, :], idf_ap, reads=[cbuf], writes=[idf])
    for k in range(FC):
        b = wst[k % 2]
        P.dma("act", b[:, :], w_ap[k * 128:(k + 1) * 128, :], reads=[cbuf], writes=[b])
        P.op("pool", lambda e, b=b, k=k: e.tensor_copy(out=wbf[:, k, :], in_=b[:, :]), reads=[b], writes=[wbf])
    n = 0
    ti = 0
    for blk in range(S // T):
        b0 = blk * T
        bsl = slice(b0, b0 + T)
        for f0 in (0, 6, 9):
            for k in range(3):
                b = mst[n % 2]
                n += 1
                P.dma("sp", b[:, :], mix_d[(f0 + k) * 128:(f0 + k + 1) * 128, bsl], reads=[mix_d], writes=[b])
                P.op("dve", lambda e, b=b, fk=f0 + k: e.tensor_copy(out=mix[:, fk, :], in_=b[:, :]), reads=[b], writes=[mix])
        for k in range(3):
            b = mst[n % 2]
            zb = zst[k % 2]
            n += 1
            P.dma("sp", b[:, :], mix_d[(3 + k) * 128:(4 + k) * 128, bsl], reads=[mix_d], writes=[b])
            P.dma("sp", zb[:, :], projT[FM_SZ + k * 128:FM_SZ + (k + 1) * 128, bsl], reads=[projT], writes=[zb])
            P.op("act", lambda e, zb=zb: e.activation(out=zb[:, :], in_=zb[:, :], func=AF.Silu), reads=[zb], writes=[zb])
            P.op("dve", lambda e, b=b, zb=zb, k=k: e.tensor_tensor(out=gz[:, k, :], in0=b[:, :], in1=zb[:, :], op=ALU.mult), reads=[b, zb], writes=[gz])
        for j in range(T // 512):
            cs = slice(j * 512, (j + 1) * 512)
            ps = pss[j % 2]
            for k in range(3):
                P.op("act", lambda e, k=k, cs=cs: e.activation(out=sq[:, cs], in_=gz[:, k, cs], func=AF.Square), reads=[gz], writes=[sq])
                P.op("pe", lambda e, ps=ps, k=k, cs=cs: e.matmul(ps[:, :], lhsT=ones[:, :], rhs=sq[:, cs], start=(k == 0), stop=(k == 2)),
                     reads=[ones, sq], writes=[ps])
            P.op("act", lambda e, ps=ps, cs=cs: e.activation(out=rs[:, cs], in_=ps[:, :], func=AF.Sqrt, bias=eps6[:, 0:1], scale=1.0 / 384.0),
                 reads=[ps, eps6], writes=[rs])
            P.op("dve", lambda e, cs=cs: e.reciprocal(out=rs[:, cs], in_=rs[:, cs]), reads=[rs], writes=[rs])
            for k in range(3):
                P.op("dve", lambda e, k=k, cs=cs: e.scalar_tensor_tensor(out=mix[:, 3 + k, cs], in0=gz[:, k, cs], scalar=snw[:, k:k + 1], in1=rs[:, cs],
                                                                        op0=ALU.mult, op1=ALU.mult), reads=[gz, snw, rs], writes=[mix])
        for tt in range(T // 128):
            x_ = xt[ti % 2]
            z_ = zt[ti % 2]
            s_ = st[ti % 2]
            m_ = mv[ti % 2]
            xT_ = xTt[ti % 2]
            r0 = b0 + tt * 128
            P.dma("sp", x_[:, :], x_d[r0:r0 + 128, :], reads=[x_d], writes=[x_])
            for hh in range(2):
                p = po[(ti * 2 + hh) % 4]
                for k in range(FC):
                    P.op("pe", lambda e, p=p, k=k, tt=tt, hh=hh: e.matmul(p[:, :], lhsT=mix[:, k, tt * 128:(tt + 1) * 128],
                                                                          rhs=wbf[:, k, hh * 512:(hh + 1) * 512], start=(k == 0), stop=(k == FC - 1)),
                         reads=[mix, wbf], writes=[p])
                P.op("dve", lambda e, p=p, hh=hh, x_=x_, z_=z_: e.scalar_tensor_tensor(out=z_[:, hh * 512:(hh + 1) * 512], in0=x_[:, hh * 512:(hh + 1) * 512],
                                                                                     scalar=ALPHA, in1=p[:, :], op0=ALU.mult, op1=ALU.add),
                     reads=[x_, p], writes=[z_])
                P.op("dve", lambda e, hh=hh, z_=z_, s_=s_: e.bn_stats(out=s_[:, hh, :], in_=z_[:, hh * 512:(hh + 1) * 512]), reads=[z_], writes=[s_])
            P.op("dve", lambda e, s_=s_, m_=m_: e.bn_aggr(out=m_[:, :], in_=s_[:, :, :]), reads=[s_], writes=[m_])
            P.op("act", lambda e, m_=m_: e.activation(out=m_[:, 1:2], in_=m_[:, 1:2], func=AF.Sqrt, bias=eps5[:, 0:1], scale=1.0), reads=[m_, eps5], writes=[m_])
            P.op("dve", lambda e, m_=m_: e.reciprocal(out=m_[:, 1:2], in_=m_[:, 1:2]), reads=[m_], writes=[m_])
            P.op("dve", lambda e, z_=z_, m_=m_: e.tensor_scalar(out=z_[:, :], in0=z_[:, :], scalar1=m_[:, 0:1], scalar2=m_[:, 1:2], op0=ALU.subtract, op1=ALU.mult),
                 reads=[z_, m_], writes=[z_])
            P.op("pool", lambda e, z_=z_: e.tensor_tensor(out=z_[:, :], in0=z_[:, :], in1=lng[:, :], op=ALU.mult), reads=[z_, lng], writes=[z_])
            P.op("pool", lambda e, z_=z_: e.tensor_tensor(out=z_[:, :], in0=z_[:, :], in1=lnb[:, :], op=ALU.add), reads=[z_, lnb], writes=[z_])
            P.dma("pool", xo_d[r0:r0 + 128, :], z_[:, :], reads=[z_], writes=[xo_d])
            if xoT_d is not None:
                for half in range(2):
                    pt_ = ptr[(ti * 2 + half) % 2]
                    for kk in range(4):
                        k = half * 4 + kk
                        P.op("pe", lambda e, pt_=pt_, kk=kk, k=k, z_=z_: e.transpose(out=pt_[:, kk * 128:(kk + 1) * 128], in_=z_[:, k * 128:(k + 1) * 128],
                                                                                  identity=idf[:, :]), reads=[z_, idf], writes=[pt_])
                    P.op("act", lambda e, pt_=pt_, half=half, xT_=xT_: e.copy(out=xT_[:, half * 4:(half + 1) * 4, :],
                                                                             in_=pt_[:, :].rearrange("p (a b) -> p a b", b=128)),
                         reads=[pt_], writes=[xT_])
                P.dma("sp", xoT_d.t.rearrange("(k p) t -> p k t", p=128)[:, :, r0:r0 + 128], xT_[:, :, :], reads=[xT_], writes=[xoT_d])
            ti += 1


DEBUG_OUT = False
SAME_ENGINE_SYNC = True
N_LAYERS_BUILD = 2
STAGES = "PABCDO"


def build_fused(S=SEQ):
    nc = bass.Bass("TRN2", target_bir_lowering=False)
    P = Prog(nc, same_engine_sync=SAME_ENGINE_SYNC)
    NT = S // 128
    L = DEPTH
    EI = "ExternalInput"
    xT0 = P.dram("xT0", [D_MODEL, S], F32, EI)
    x0 = P.dram("x0", [S, D_MODEL], F32, EI)
    w_in = P.dram("w_in", [L, D_MODEL, IN_COLS], F32, EI)
    w_out = P.dram("w_out", [L, D_MIX, D_MODEL], F32, EI)
    ab = P.dram("ab", [6, 128, 64], F32, EI)
    bi = P.dram("bi", [32, S], BF16, EI)
    cb = P.dram("cb", [4, 128, 512], BF16, EI)
    idb = P.dram("idb", [128, 128], BF16, EI)
    idf = P.dram("idf", [128, 128], F32, EI)
    dbias = P.dram("dbias", [6, 3, 2, 128, 512], F32, EI)
    tri = P.dram("tri", [128, 128], F32, EI)
    ones = P.dram("ones", [128, 128], F32, EI)
    mneg = P.dram("mneg", [128, 128], BF16, EI)
    mpos = P.dram("mpos", [128, 128], BF16, EI)
    s_cwx = P.dram("s_cwx", [L, 6, 64, 5], F32, EI)
    s_cwb = P.dram("s_cwb", [L, 6, 128, 5], F32, EI)
    s_cwc = P.dram("s_cwc", [L, 6, 128, 5], F32, EI)
    s_sc = P.dram("s_sc", [L, 6, 128, 3], F32, EI)
    d_cwq = P.dram("d_cwq", [L, 6, 64, 5], F32, EI)
    d_cwk = P.dram("d_cwk", [L, 6, 64, 5], F32, EI)
    d_cwv = P.dram("d_cwv", [L, 6, 64, 5], F32, EI)
    d_sc = P.dram("d_sc", [L, 6, 128, 2], F32, EI)
    d_nw = P.dram("d_nw", [L, 128, 64], F32, EI)
    o_snw = P.dram("o_snw", [L, 128, 3], F32, EI)
    o_lng = P.dram("o_lng", [L, 128, D_MODEL], F32, EI)
    o_lnb = P.dram("o_lnb", [L, 128, D_MODEL], F32, EI)
    out = P.dram("out", [S, D_MODEL], F32, "ExternalOutput")
    sk = "ExternalOutput" if DEBUG_OUT else None
    projT = P.dram("projT", [N_FM, S], F32, sk)
    projtm = P.dram("projtm", [S, N_TM], F32, sk)
    mixT = P.dram("mixT", [D_MIX, S], F32, sk)
    x1 = P.dram("x1", [S, D_MODEL], F32, sk)
    x1T = P.dram("x1T", [D_MODEL, S], BF16, None)

    def fm(off):
        return Src(projT, lambda h, off=off: projT[off + h * 64: off + (h + 1) * 64, :])

    def mixrows(off):
        return Src(mixT, lambda h, off=off: mixT[off + h * 64: off + (h + 1) * 64, :])

    for l in range(N_LAYERS_BUILD):
        if "P" in STAGES:
            with P.scope():
                emit_projF(P, xT0 if l == 0 else x1T, l != 0, w_in.t[l], w_in, projT, projtm, S)
        if "A" in STAGES:
            with P.scope():
                emit_moba(P, fm(FM_AQ), fm(FM_AK),
                          Src(projtm, lambda h: projtm.t[:, TM_AV + h * 64: TM_AV + (h + 1) * 64].rearrange("(t p) d -> p t d", p=128)),
                          fm(FM_AG), Src(ab, lambda h: ab.t[h]), bi, cb, idb, idf, mixrows(0), 6, S)
        if "B" in STAGES:
            with P.scope():
                emit_ssd(P, fm(FM_SX),
                         Src(projT, lambda h: projT[FM_SB + (h // 3) * 128: FM_SB + (h // 3 + 1) * 128, :]),
                         Src(projT, lambda h: projT[FM_SC + (h // 3) * 128: FM_SC + (h // 3 + 1) * 128, :]),
                         Src(s_cwx, lambda h, l=l: s_cwx.t[l, h]), Src(s_cwb, lambda h, l=l: s_cwb.t[l, h]), Src(s_cwc, lambda h, l=l: s_cwc.t[l, h]),
                         Src(projtm, lambda h: projtm.t[:, TM_DT:TM_DT + 6].rearrange("(t p) c -> p t c", p=128)),
                         Src(s_sc, lambda h, l=l: s_sc.t[l, h]), tri, ones, idf, idb, mneg, mixrows(384), 6, S)
        if "C" in STAGES:
            with P.scope():
                emit_dil(P, fm(FM_CQ), fm(FM_CK),
                         Src(projtm, lambda hg: projtm.t[:, TM_CV + hg[0] * 64: TM_CV + (hg[0] + 1) * 64].rearrange(
                             "(j p r) c -> p r j c", p=128, r=DIL_D[hg[1]])),
                         fm(FM_CG), Src(dbias, lambda h: dbias.t[h].rearrange("g o p q -> p (g o) q")), mixrows(768), 6, S)
        if "D" in STAGES:
            with P.scope():
                emit_gdn(P, fm(FM_DQ), fm(FM_DK), fm(FM_DV),
                         Src(d_cwq, lambda h, l=l: d_cwq.t[l, h]), Src(d_cwk, lambda h, l=l: d_cwk.t[l, h]), Src(d_cwv, lambda h, l=l: d_cwv.t[l, h]),
                         Src(projtm, lambda h: projtm.t[:, TM_DB:TM_DB + 6].rearrange("(t p) c -> p t c", p=128)),
                         Src(projtm, lambda h: projtm.t[:, TM_DA:TM_DA + 6].rearrange("(t p) c -> p t c", p=128)),
                         Src(d_sc, lambda h, l=l: d_sc.t[l, h]),
                         Src(projtm, lambda h: projtm.t[:, TM_DG + h * 64: TM_DG + (h + 1) * 64].rearrange("(t p) d -> p t d", p=128)),
                         Src(d_nw, lambda h, l=l: d_nw.t[l]), tri, ones, idf, idb, mneg, mpos, mixrows(1152), 6, S)
        if "O" in STAGES:
            with P.scope():
                last = (l == DEPTH - 1)
                emit_outlnF(P, mixT, projT, o_snw.t[l], w_out.t[l], o_lng.t[l], o_lnb.t[l], ones.t, idf.t, w_out,
                            x0 if l == 0 else x1, out if last else x1, None if last else x1T, S)
    return P.finish()


from concourse.bass_utils import run_bass_kernel_spmd

BATCH = 2
_NC = {}


def _host_inputs(p, b):
    perm = in_col_perm()
    CA = moba_consts()
    CR = rec_consts()
    L = DEPTH
    f32 = np.float32
    m = {}
    m["xT0"] = np.ascontiguousarray(p["x"][b].T)
    m["x0"] = np.ascontiguousarray(p["x"][b])
    m["w_in"] = np.ascontiguousarray(p["w_in"][:, :, perm])
    m["w_out"] = np.ascontiguousarray(p["w_out"])
    m["ab"] = CA["ab"]
    m["bi"] = CA["bi"]
    m["cb"] = CA["cb"]
    m["idb"] = CA["idb"]
    m["idf"] = CA["idf"]
    m["dbias"] = dil_consts()
    m["tri"] = CR["tri"]
    m["ones"] = CR["ones"]
    m["mneg"] = CR["mneg"]
    m["mpos"] = CR["mpos"]
    cw5 = [np.concatenate([p["ssm_conv_w"][l].T, p["ssm_conv_b"][l][:, None]], axis=1).astype(f32) for l in range(L)]
    m["s_cwx"] = np.ascontiguousarray(np.stack([np.stack([cw5[l][h * 64:(h + 1) * 64] for h in range(6)]) for l in range(L)]))
    m["s_cwb"] = np.ascontiguousarray(np.stack([np.stack([cw5[l][384 + (h // 3) * 128: 384 + (h // 3 + 1) * 128] for h in range(6)]) for l in range(L)]))
    m["s_cwc"] = np.ascontiguousarray(np.stack([np.stack([cw5[l][640 + (h // 3) * 128: 640 + (h // 3 + 1) * 128] for h in range(6)]) for l in range(L)]))
    m["s_sc"] = np.ascontiguousarray(np.stack([np.stack([np.tile(np.stack([p["ssm_dt_bias"][l, h], p["ssm_A_log"][l, h], p["ssm_D"][l, h]])[None].astype(f32),
                                                                (128, 1)) for h in range(6)]) for l in range(L)]))
    cd5 = [np.concatenate([p["dn_conv_w"][l].T, p["dn_conv_b"][l][:, None]], axis=1).astype(f32) for l in range(L)]
    for nm, off in (("q", 0), ("k", 384), ("v", 768)):
        m["d_cw" + nm] = np.ascontiguousarray(np.stack([np.stack([cd5[l][off + h * 64: off + (h + 1) * 64] for h in range(6)]) for l in range(L)]))
    m["d_sc"] = np.ascontiguousarray(np.stack([np.stack([np.tile(np.stack([p["dn_dt_bias"][l, h], p["dn_A_log"][l, h]])[None].astype(f32), (128, 1))
                                                        for h in range(6)]) for l in range(L)]))
    m["d_nw"] = np.ascontiguousarray(np.stack([np.tile(p["dn_norm_w"][l][None].astype(f32), (128, 1)) for l in range(L)]))
    m["o_snw"] = np.ascontiguousarray(np.stack([p["ssm_norm_w"][l].reshape(3, 128).T.astype(f32) for l in range(L)]))
    m["o_lng"] = np.ascontiguousarray(np.stack([np.tile(p["ln_g"][l][None].astype(f32), (128, 1)) for l in range(L)]))
    m["o_lnb"] = np.ascontiguousarray(np.stack([np.tile(p["ln_b"][l][None].astype(f32), (128, 1)) for l in range(L)]))
    return m


def kernel(**inputs):
    p = {k: np.asarray(v, dtype=np.float32) for k, v in inputs.items()}
    if "nc" not in _NC:
        _NC["nc"] = build_fused()
    in_maps = [_host_inputs(p, b) for b in range(BATCH)]
    res = run_bass_kernel_spmd(_NC["nc"], in_maps, core_ids=list(range(BATCH)))
    _NC["res"] = res
    return np.stack([res.results[b]["out"] for b in range(BATCH)]).astype(np.float32)
```

```python
import contextlib
import numpy as np
import concourse.bass as bass
import concourse.mybir as mybir

F32 = mybir.dt.float32
BF16 = mybir.dt.bfloat16
I32 = mybir.dt.int32
U8 = mybir.dt.uint8
AF = mybir.ActivationFunctionType
ALU = mybir.AluOpType
AX = mybir.AxisListType


class Buf:
    __slots__ = ("name", "t", "writer", "readers", "excl")

    def __init__(self, name, t, excl=False):
        self.name = name
        self.t = t
        self.excl = excl
        self.writer = None
        self.readers = []

    def __getitem__(self, idx):
        return self.t[idx]


class Src:
    def __init__(self, buf, fn):
        self.buf = buf
        self.fn = fn

    def __getitem__(self, k):
        return self.fn(k)


class Prog:
    ENG = ("pe", "dve", "act", "pool", "sp")
    NDMA = 6

    def __init__(self, nc, same_engine_sync=True):
        self.nc = nc
        self.stack = contextlib.ExitStack()
        self.ops = {e: [] for e in self.ENG}
        self.cnt = {e: 0 for e in self.ENG}
        self.sems = {}
        for e in self.ENG:
            self.sems[e] = self.stack.enter_context(nc.semaphore("s_" + e))
        self.dq = {}
        for q in ("sp", "pool", "act"):
            self.dq[q] = {"n": 0, "sems": []}
            for i in range(self.NDMA):
                s = self.stack.enter_context(nc.semaphore(f"d_{q}{i}"))
                self.sems[f"d_{q}{i}"] = s
                self.dq[q]["sems"].append(f"d_{q}{i}")
        self.waited = {e: {} for e in self.ENG}
        self.same = same_engine_sync
        self.nbuf = 0
        self.ARENA_BYTES = 206 * 1024
        self.arena = self.stack.enter_context(nc.sbuf_tensor("arena", [128, self.ARENA_BYTES // 4], F32))
        self.arena_off = 0
        self.banks = [self.stack.enter_context(nc.psum_tensor(f"bank{i}", [128, 512], F32)) for i in range(8)]
        self.banks_used = 0

    @staticmethod
    def _view(base, shape, dt, off_bytes):
        esz = mybir.dt.size(dt)
        n = 1
        for d in shape[1:]:
            n *= d
        t = base if dt == F32 else base.bitcast(dt)
        o = off_bytes // esz
        ap = t[0:shape[0], o:o + n]
        if len(shape) == 3:
            ap = ap.rearrange("p (a b) -> p a b", b=shape[2])
        elif len(shape) == 4:
            ap = ap.rearrange("p (a b c) -> p a b c", b=shape[2], c=shape[3])
        return ap, n * esz

    def sbuf(self, name, shape, dt):
        ap, nb = self._view(self.arena, list(shape), dt, self.arena_off)
        self.arena_off += (nb + 63) // 64 * 64
        assert self.arena_off <= self.ARENA_BYTES, f"SBUF arena overflow at {name}: {self.arena_off}"
        return Buf(name, ap)

    def psum(self, name, shape, dt=F32):
        assert self.banks_used < 8, "out of PSUM banks at " + name
        ap, nb = self._view(self.banks[self.banks_used], list(shape), dt, 0)
        assert nb <= 2048
        self.banks_used += 1
        return Buf(name, ap, excl=True)

    @contextlib.contextmanager
    def scope(self):
        sv = (self.arena_off, self.banks_used)
        try:
            yield
        finally:
            self.barrier()
            self.arena_off, self.banks_used = sv

    def _all_deps(self):
        deps = []
        for q, dq in self.dq.items():
            n = dq["n"]
            for i in range(min(n, self.NDMA)):
                cnt_i = (n - 1 - i) // self.NDMA + 1
                deps.append((dq["sems"][i], 16 * cnt_i))
        for e in self.ENG:
            if self.cnt[e] > 0:
                deps.append((e, self.cnt[e]))
        return deps

    def barrier(self):
        deps = self._all_deps()
        for e in self.ENG:
            self._need(e, [d for d in deps if d[0] != e])

    def dram(self, name, shape, dt, kind=None):
        if kind is None:
            t = self.nc.dram_tensor(name, list(shape), dt)
        else:
            t = self.nc.dram_tensor(name, list(shape), dt, kind=kind)
        return Buf(name, t.ap())

    def alias(self, name, t):
        return Buf(name, t)

    def _need(self, eng, deps):
        w = self.waited[eng]
        for (k, v) in deps:
            if k == eng and (eng == "pe" or not self.same):
                continue
            if w.get(k, 0) >= v:
                continue
            w[k] = v
            sem = self.sems[k]
            self.ops[eng].append(lambda e, sem=sem, v=v: e.wait_ge(sem, v))

    def _deps(self, reads, writes, eng=None):
        deps = []
        for b in reads:
            if b.writer is not None:
                deps.append(b.writer)
            if b.excl:
                deps.extend(r for r in b.readers if r[0] != eng)
        for b in writes:
            if b.writer is not None:
                deps.append(b.writer)
            deps.extend(b.readers)
        return deps

    def _mark(self, reads, writes, tag):
        for b in reads:
            b.readers.append(tag)
            if len(b.readers) > 64:
                m = {}
                for (k, v) in b.readers:
                    if m.get(k, 0) < v:
                        m[k] = v
                b.readers = list(m.items())
        for b in writes:
            b.writer = tag
            b.readers = []

    def op(self, eng, fn, reads=(), writes=()):
        self._need(eng, self._deps(reads, writes, eng))
        self.cnt[eng] += 1
        v = self.cnt[eng]
        sem = self.sems[eng]
        self.ops[eng].append(lambda e, fn=fn, sem=sem: fn(e).then_inc(sem, 1))
        self._mark(reads, writes, (eng, v))

    def dma(self, q, out_ap, in_ap, reads=(), writes=(), **kw):
        dq = self.dq[q]
        n = dq["n"]
        dq["n"] += 1
        key = dq["sems"][n % self.NDMA]
        val = 16 * (n // self.NDMA + 1)
        deps = self._deps(reads, writes)
        if val > 16:
            deps.append((key, val - 16))
        self._need(q, deps)
        sem = self.sems[key]
        self.ops[q].append(lambda e, sem=sem, o=out_ap, i=in_ap, kw=kw: e.dma_start(out=o, in_=i, **kw).then_inc(sem, 16))
        self._mark(reads, writes, (key, val))

    def coll(self, kind, alu, groups, in_b, out_b):
        q = "pool"
        if "cc" not in self.sems:
            self.sems["cc"] = self.stack.enter_context(self.nc.semaphore("s_cc"))
            self.ncc = 0
        self.ncc += 1
        val = self.ncc
        deps = self._deps([in_b], [out_b])
        if val > 1:
            deps.append(("cc", val - 1))
        self._need(q, deps)
        sem = self.sems["cc"]
        self.ops[q].append(lambda e, sem=sem: e.collective_compute(kind, alu, replica_groups=groups, ins=[in_b.t.opt()],
                                                                   outs=[out_b.t.opt()]).then_inc(sem, 1))
        self._mark([in_b], [out_b], ("cc", val))

    def finish(self, final_bufs=()):
        deps = []
        for q, dq in self.dq.items():
            n = dq["n"]
            for i in range(min(n, self.NDMA)):
                cnt_i = (n - 1 - i) // self.NDMA + 1
                deps.append((dq["sems"][i], 16 * cnt_i))
        for e in self.ENG:
            if e != "sp" and self.cnt[e] > 0:
                deps.append((e, self.cnt[e]))
        self._need("sp", deps)
        nc = self.nc
        with nc.Block() as block:
            @block.tensor
            def _(e):
                for f in self.ops["pe"]:
                    f(e)

            @block.vector
            def _(e):
                for f in self.ops["dve"]:
                    f(e)

            @block.scalar
            def _(e):
                for f in self.ops["act"]:
                    f(e)

            @block.gpsimd
            def _(e):
                for f in self.ops["pool"]:
                    f(e)

            @block.sync
            def _(e):
                for f in self.ops["sp"]:
                    f(e)
        self.stack.close()
        return nc


D_MODEL = 1024
IN_COLS = 5906
DEPTH = 2


SEQ = 8192
NEGM = 30000.0


def emit_moba(P, qT_d, kT_d, v_d, gT_d, ab_d, bi_d, cb_d, idb_d, idf_d, y_d, NH, S=SEQ, pfx="mb"):
    NT = S // 128
    NQ = S // 512
    NB = S // 256
    qf = P.sbuf(pfx + "_qf", [64, S], F32)
    kf = P.sbuf(pfx + "_kf", [64, S], F32)
    vf = P.sbuf(pfx + "_vf", [128, NT, 64], F32)
    qas = [P.sbuf(pfx + f"_qa{i}", [96, S], BF16) for i in range(2)]
    kas = [P.sbuf(pfx + f"_ka{i}", [96, S], BF16) for i in range(2)]
    vas = [P.sbuf(pfx + f"_va{i}", [128, NT, 128], BF16) for i in range(2)]
    abs_ = [P.sbuf(pfx + f"_ab{i}", [128, 64], F32) for i in range(2)]
    cb = P.sbuf(pfx + "_cb", [128, 4, 512], BF16)
    idb = P.sbuf(pfx + "_idb", [128, 128], BF16)
    idf = P.sbuf(pfx + "_idf", [128, 128], F32)
    km = P.sbuf(pfx + "_km", [64, NB], F32)
    KS = 2
    gsb = [P.sbuf(pfx + f"_gsb{i}", [128, 32], F32) for i in range(KS)]
    mx8 = [P.sbuf(pfx + f"_mx8{i}", [128, 8], F32) for i in range(KS)]
    mb = [P.sbuf(pfx + f"_mbs{i}", [128, 32], F32) for i in range(KS)]
    KM = 4
    pts = [P.sbuf(pfx + f"_pt{i}", [128, 512], BF16) for i in range(KM)]
    gch = [P.sbuf(pfx + f"_gch{i}", [64, 512], F32) for i in range(2)]
    rden = [P.sbuf(pfx + f"_rden{i}", [64, 512], F32) for i in range(2)]
    yo = [P.sbuf(pfx + f"_yo{i}", [64, 512], F32) for i in range(2)]
    acc = [P.psum(pfx + f"_acc{i}", [128, 512]) for i in range(2)]
    sps = [P.psum(pfx + f"_sps{i}", [128, 512]) for i in range(KM)]
    sls = [P.psum(pfx + f"_sls{i}", [128, 512]) for i in range(KS)]

    P.dma("sp", cb[:, :, :], cb_d.t.rearrange("k p q -> p k q"), reads=[cb_d], writes=[cb])
    P.dma("sp", idb[:, :], idb_d[:, :], reads=[idb_d], writes=[idb])
    P.dma("sp", idf[:, :], idf_d[:, :], reads=[idf_d], writes=[idf])
    for i in range(2):
        P.dma("sp", kas[i][64:96, :], bi_d[:, :], reads=[bi_d], writes=[kas[i]])
        P.op("pool", lambda e, i=i: e.memset(vas[i][:, :, 64:128], 1.0), writes=[vas[i]])

    def prep_gen(h):
        qa, ka, va, ab = qas[h % 2], kas[h % 2], vas[h % 2], abs_[h % 2]
        P.dma("sp", qf[:, :], qT_d[h], reads=[qT_d.buf], writes=[qf])
        P.dma("act", kf[:, :], kT_d[h], reads=[kT_d.buf], writes=[kf])
        P.dma("pool", vf[:, :, :], v_d[h], reads=[v_d.buf], writes=[vf])
        P.dma("sp", ab[:, :], ab_d[h], reads=[ab_d.buf], writes=[ab])
        yield
        P.op("act", lambda e: e.mul(qa[0:64, :], qf[:, :], 0.125), reads=[qf], writes=[qa])
        yield
        P.op("pool", lambda e: e.tensor_copy(out=ka[0:64, :], in_=kf[:, :]), reads=[kf], writes=[ka])
        yield
        P.op("pool", lambda e: e.tensor_copy(out=va[:, :, 0:64], in_=vf[:, :, :]), reads=[vf], writes=[va])
        yield
        P.op("dve", lambda e: e.tensor_reduce(out=km[:, :], in_=kf.t[:, :].rearrange("p (n l) -> p n l", l=256),
                                              axis=AX.X, op=ALU.add), reads=[kf], writes=[km])
        yield
        P.op("dve", lambda e: e.tensor_scalar(out=km[:, :], in0=km[:, :], scalar1=1.0 / 256.0, scalar2=None, op0=ALU.mult),
             reads=[km], writes=[km])

    def sel_gen(h, t):
        qa = qas[h % 2]
        qb = t // 2
        sl = t % KS
        bk = sls[sl]
        gp = bk[:, 0:32]
        tp = bk[0:32, 128:256]
        g_, x_, m_ = gsb[sl], mx8[sl], mb[sl]
        P.op("pe", lambda e: e.matmul(gp, lhsT=qf[:, t * 128:(t + 1) * 128], rhs=km[:, :], start=True, stop=True, skip_group_check=True),
             reads=[qf, km], writes=[bk])
        P.op("pool", lambda e: e.memset(g_[:, :], -1e30), writes=[g_])
        yield
        if qb > 0:
            P.op("dve", lambda e: e.tensor_copy(out=g_[:, 0:qb], in_=bk[:, 0:qb]), reads=[bk], writes=[g_])
        P.op("dve", lambda e: e.max(out=x_[:, :], in_=g_[:, :]), reads=[g_], writes=[x_])
        yield
        P.op("dve", lambda e: e.tensor_scalar(out=m_[:, :], in0=g_[:, :], scalar1=x_[:, 2:3], scalar2=NEGM,
                                              op0=ALU.is_ge, op1=ALU.mult), reads=[g_, x_], writes=[m_])
        P.op("dve", lambda e: e.memset(m_[:, qb:qb + 1], NEGM), writes=[m_])
        if qb + 1 < 32:
            P.op("dve", lambda e: e.memset(m_[:, qb + 1:32], 0.0), writes=[m_])
        yield
        P.op("dve", lambda e: e.tensor_scalar(out=m_[:, :], in0=m_[:, :], scalar1=-NEGM, scalar2=None, op0=ALU.add),
             reads=[m_], writes=[m_])
        yield
        P.op("pe", lambda e: e.transpose(out=tp, in_=m_[:, :], identity=idf[:, :]), reads=[m_, idf], writes=[bk])
        yield
        P.op("act", lambda e: e.copy(out=qa[64:96, t * 128:(t + 1) * 128], in_=tp), reads=[bk], writes=[qa])

    def att_gen(h, qt, kt, idx):
        qa, ka, va, ab = qas[h % 2], kas[h % 2], vas[h % 2], abs_[h % 2]
        ac = acc[qt % 2]
        nk = 4 * qt + 4
        sp_ = sps[idx % KM]
        pt = pts[idx % KM]
        diag = kt >= 4 * qt
        P.op("pe", lambda e: e.matmul(sp_[:, :], lhsT=ka[:, kt * 128:(kt + 1) * 128], rhs=qa[:, qt * 512:(qt + 1) * 512],
                                      start=True, stop=not diag), reads=[ka, qa], writes=[sp_])
        if diag:
            P.op("pe", lambda e: e.matmul(sp_[:, :], lhsT=idb[:, :], rhs=cb[:, kt - 4 * qt, :], start=False, stop=True), reads=[idb, cb], writes=[sp_])
        yield
        ri = kt - 4 * qt + 60
        P.op("act", lambda e: e.activation(out=pt[:, :], in_=sp_[:, :], func=AF.Exp, bias=ab[:, ri:ri + 1], scale=1.0),
             reads=[sp_, ab], writes=[pt])
        yield
        P.op("pe", lambda e: e.matmul(ac[:, :], lhsT=va[:, kt, :], rhs=pt[:, :], start=(kt == 0), stop=(kt == nk - 1)),
             reads=[va, pt], writes=[ac])
        if kt == nk - 1:
            g = gch[qt % 2]
            rd = rden[qt % 2]
            y = yo[qt % 2]
            P.dma("sp", g[:, :], gT_d[h][:, qt * 512:(qt + 1) * 512], reads=[gT_d.buf], writes=[g])
            P.op("act", lambda e: e.activation(out=g[:, :], in_=g[:, :], func=AF.Silu), reads=[g], writes=[g])
            yield
            P.op("dve", lambda e: e.reciprocal(out=rd[:, :], in_=ac[64:128, :]), reads=[ac], writes=[rd])
            P.op("dve", lambda e: e.tensor_tensor(out=y[:, :], in0=ac[0:64, :], in1=rd[:, :], op=ALU.mult),
                 reads=[ac, rd], writes=[y])
            yield
            P.op("pool", lambda e: e.tensor_tensor(out=y[:, :], in0=y[:, :], in1=g[:, :], op=ALU.mult),
                 reads=[y, g], writes=[y])
            P.dma("pool", y_d[h][:, qt * 512:(qt + 1) * 512], y[:, :], reads=[y], writes=[y_d.buf])

    items = [(qt, kt) for qt in range(NQ) for kt in range(4 * qt + 4)]

    def prep_pipes(h):
        return [Pipe([prep_gen(h)], 1), Pipe([sel_gen(h, t) for t in range(NT)], KS)]

    run_chains([prep_pipes(0)])
    for h in range(NH):
        chains = [[Pipe([att_gen(h, qt, kt, i) for i, (qt, kt) in enumerate(items)], KM)]]
        if h + 1 < NH:
            chains.append(prep_pipes(h + 1))
        run_chains(chains)


def moba_consts(S=SEQ):
    import ml_dtypes
    bf = ml_dtypes.bfloat16
    bi = np.zeros((32, S), np.float32)
    for j in range(S // 256):
        bi[j, j * 256:(j + 1) * 256] = 1.0
    cbm = np.zeros((4, 128, 512), np.float32)
    for r in range(4):
        for p in range(128):
            tk = r * 128 + p
            for blk in range(2):
                if tk // 256 == blk:
                    q = np.arange(blk * 256, (blk + 1) * 256)
                    cbm[r, p, q] = np.where(tk > q, -NEGM, 0.0)
    n = 12
    s = 2.0 ** (-8.0 * (np.arange(n) + 1) / n)
    slopes_moba = s[6:]
    ab = np.zeros((6, 128, 64), np.float32)
    for h in range(6):
        for ri in range(64):
            rel = ri - 60
            ab[h, :, ri] = slopes_moba[h] * (128.0 * rel + np.arange(128))
    return dict(bi=bi.astype(bf), cb=cbm.astype(bf), idb=np.eye(128, dtype=np.float32).astype(bf),
                idf=np.eye(128, dtype=np.float32), ab=ab)


DIL_D = (1, 4, 16)
DIL_GROUPS = (0, 1, 2)
DBG = None


def emit_dil(P, qT_d, kT_d, vp_d, gT_d, bias_d, y_d, NH, S=SEQ, pfx="dl"):
    NT = S // 128
    qf = P.sbuf(pfx + "_qf", [64, S], F32)
    kf = qf
    qa = P.sbuf(pfx + "_qa", [64, S], BF16)
    ka = P.sbuf(pfx + "_ka", [64, S], BF16)
    vf = P.sbuf(pfx + "_vf", [128, NT, 64], F32)
    va = [P.sbuf(pfx + f"_va{g}", [128, NT, 128], BF16) for g in range(3)]
    bs = P.sbuf(pfx + "_bias", [128, 6, 512], F32)
    num = P.sbuf(pfx + "_num", [128, S], F32)
    tmp = [P.sbuf(pfx + f"_tmp{i}", [128, 512], F32) for i in range(4)]
    pts = [P.sbuf(pfx + f"_pt{i}", [128, 512], BF16) for i in range(4)]
    gch = [P.sbuf(pfx + f"_gch{i}", [64, 512], F32) for i in range(2)]
    rden = [P.sbuf(pfx + f"_rden{i}", [64, 512], F32) for i in range(2)]
    yo = [P.sbuf(pfx + f"_yo{i}", [64, 512], F32) for i in range(2)]
    sps = [P.psum(pfx + f"_sps{i}", [128, 512]) for i in range(4)]
    ops_ = [P.psum(pfx + f"_ops{i}", [128, 512]) for i in range(3)]
    for g in range(3):
        P.op("pool", lambda e, g=g: e.memset(va[g][:, :, 64:128], 1.0), writes=[va[g]])
    si = 0
    oi = 0
    for h in range(NH):
        P.dma("sp", qf[:, :], qT_d[h], reads=[qT_d.buf], writes=[qf])
        P.op("act", lambda e: e.mul(qa[:, :], qf[:, :], 0.125), reads=[qf], writes=[qa])
        P.dma("sp", kf[:, :], kT_d[h], reads=[kT_d.buf], writes=[kf])
        P.op("pool", lambda e: e.tensor_copy(out=ka[:, :], in_=kf[:, :]), reads=[kf], writes=[ka])
        P.dma("sp", bs[:, :, :], bias_d[h], reads=[bias_d.buf], writes=[bs])
        for g in range(3):
            P.dma("pool", vf.t.rearrange("p (r j) c -> p r j c", r=DIL_D[g]), vp_d[(h, g)], reads=[vp_d.buf], writes=[vf])
            P.op("pool", lambda e, g=g: e.tensor_copy(out=va[g][:, :, 0:64], in_=vf[:, :, :]), reads=[vf], writes=[va[g]])
        def dil_gen(g, d, r, jb, ti):
            NTd = NT // d
            qv = qa.t[:, :].rearrange("p (u d) -> p u d", d=d)
            kv = ka.t[:, :].rearrange("p (u d) -> p u d", d=d)
            nv = num.t[:, :].rearrange("p (u d) -> p u d", d=d)
            sl = ti % 2
            op_ = ops_[ti % 3]
            sp2 = {1: sps[2 * sl], 0: sps[2 * sl + 1]}
            tm2 = {1: tmp[2 * sl], 0: tmp[2 * sl + 1]}
            pt2 = {1: pts[2 * sl], 0: pts[2 * sl + 1]}
            i0s = {1: (1 if jb == 0 else 0), 0: 0}
            for o in (1, 0):
                sp_ = sp2[o]
                for i in range(i0s[o], 4):
                    jq = jb * 4 + i
                    jk = jq - o
                    P.op("pe", lambda e, sp_=sp_, i=i, jq=jq, jk=jk: e.matmul(
                        sp_[:, i * 128:(i + 1) * 128], lhsT=kv[:, jk * 128:(jk + 1) * 128, r],
                        rhs=qv[:, jq * 128:(jq + 1) * 128, r], start=True, stop=True,
                        skip_group_check=True), reads=[ka, qa], writes=[sp_])
            yield
            for o in (1, 0):
                c0 = i0s[o] * 128
                P.op("dve", lambda e, sp_=sp2[o], tm=tm2[o], c0=c0, go=g * 2 + o: e.tensor_tensor(
                    out=tm[:, c0:512], in0=sp_[:, c0:512], in1=bs[:, go, c0:512], op=ALU.add),
                    reads=[sp2[o], bs], writes=[tm2[o]])
            yield
            for o in (1, 0):
                c0 = i0s[o] * 128
                P.op("act", lambda e, tm=tm2[o], pt=pt2[o], c0=c0: e.activation(out=pt[:, c0:512], in_=tm[:, c0:512], func=AF.Exp),
                     reads=[tm2[o]], writes=[pt2[o]])
            yield
            for i in range(4):
                os_ = (0,) if (jb == 0 and i == 0) else (1, 0)
                for o in os_:
                    jk = jb * 4 + i - o
                    P.op("pe", lambda e, pt=pt2[o], i=i, tl=r * NTd + jk, fp=(o == os_[0]), last=(o == 0): e.matmul(
                        op_[:, i * 128:(i + 1) * 128], lhsT=va[g][:, tl, :], rhs=pt[:, i * 128:(i + 1) * 128],
                        start=fp, stop=last, skip_group_check=True), reads=[va[g], pt2[o]], writes=[op_])
            yield
            u0 = jb * 512
            if g == DIL_GROUPS[0]:
                P.op("dve", lambda e: e.tensor_copy(out=nv[:, u0:u0 + 512, r], in_=op_[:, :]),
                     reads=[op_], writes=[num])
            else:
                P.op("dve", lambda e: e.tensor_tensor(
                    out=nv[:, u0:u0 + 512, r], in0=op_[:, :], in1=nv[:, u0:u0 + 512, r], op=ALU.add),
                    reads=[op_, num], writes=[num])

        tasks = []
        for g, d in enumerate(DIL_D):
            if g not in DIL_GROUPS:
                continue
            for r in range(d):
                for jb in range(NT // d // 4):
                    tasks.append((g, d, r, jb))
        run_pipeline([dil_gen(g, d, r, jb, ti) for ti, (g, d, r, jb) in enumerate(tasks)], 2)
        for qt in range(S // 512):
            g_ = gch[qt % 2]
            rd = rden[qt % 2]
            y = yo[qt % 2]
            cs = slice(qt * 512, (qt + 1) * 512)
            P.dma("sp", g_[:, :], gT_d[h][:, cs], reads=[gT_d.buf], writes=[g_])
            P.op("act", lambda e, g_=g_: e.activation(out=g_[:, :], in_=g_[:, :], func=AF.Silu), reads=[g_], writes=[g_])
            P.op("dve", lambda e, rd=rd, cs=cs: e.reciprocal(out=rd[:, :], in_=num[64:128, cs]), reads=[num], writes=[rd])
            P.op("dve", lambda e, y=y, rd=rd, cs=cs: e.tensor_tensor(out=y[:, :], in0=num[0:64, cs], in1=rd[:, :], op=ALU.mult),
                 reads=[num, rd], writes=[y])
            P.op("pool", lambda e, y=y, g_=g_: e.tensor_tensor(out=y[:, :], in0=y[:, :], in1=g_[:, :], op=ALU.mult),
                 reads=[y, g_], writes=[y])
            P.dma("pool", y_d[h][:, cs], y[:, :], reads=[y], writes=[y_d.buf])


def dil_consts():
    n = 12
    s = 2.0 ** (-8.0 * (np.arange(n) + 1) / n)
    slopes = s[:6]
    bias = np.zeros((6, 3, 2, 128, 512), np.float32)
    p = np.arange(128)[:, None]
    x = np.arange(128)[None, :]
    for h in range(6):
        for g, d in enumerate(DIL_D):
            for o in range(2):
                nn = (x - p) + 128 * o
                b = np.where((nn >= 0) & (nn <= 128), -slopes[h] * d * nn, -NEGM).astype(np.float32)
                bias[h, g, o] = np.tile(b, (1, 4))
    return bias


def dil_perm(S=SEQ):
    out = []
    for d in DIL_D:
        u = np.arange(S // d)
        out.append(np.concatenate([u * d + r for r in range(d)]))
    return out


def emit_conv_silu(P, src_d, w_sb, C, S, zps, accs, out_fn, q="sp", src_buf=None):
    H = S // 2
    rd = [src_buf] if src_buf is not None else []
    P.dma(q, zps[0][0:C, 3:H + 3], src_d[:, 0:H], reads=rd, writes=[zps[0]])
    P.dma("act" if q == "sp" else q, zps[1][0:C, 0:H + 3], src_d[:, H - 3:S], reads=rd, writes=[zps[1]])
    for hh in range(2):
        zp, acc = zps[hh], accs[hh]
        P.op("dve", lambda e, zp=zp, acc=acc: e.tensor_scalar(out=acc[0:C, :], in0=zp[0:C, 3:H + 3], scalar1=w_sb[0:C, 3:4],
                                                              scalar2=w_sb[0:C, 4:5], op0=ALU.mult, op1=ALU.add), reads=[zp, w_sb], writes=[acc])
        for k in range(3):
            P.op("dve", lambda e, k=k, zp=zp, acc=acc: e.scalar_tensor_tensor(out=acc[0:C, :], in0=zp[0:C, k:k + H], scalar=w_sb[0:C, k:k + 1],
                                                                              in1=acc[0:C, :], op0=ALU.mult, op1=ALU.add),
                 reads=[zp, w_sb, acc], writes=[acc])
        for (ob, oap) in out_fn(hh):
            P.op("act", lambda e, oap=oap, acc=acc: e.activation(out=oap, in_=acc[0:C, :], func=AF.Silu), reads=[acc], writes=[ob])


def emit_softplus(P, eng_dve, out_b, out_ap, x_b, x_ap, t1_b, t1_ap, shape_p):
    P.op("act", lambda e: e.activation(out=t1_ap, in_=x_ap, func=AF.Abs), reads=[x_b], writes=[t1_b])
    P.op("act", lambda e: e.activation(out=t1_ap, in_=t1_ap, func=AF.Exp, scale=-1.0), reads=[t1_b], writes=[t1_b])
    P.op("act", lambda e: e.activation(out=t1_ap, in_=t1_ap, func=AF.Ln, bias=1.0, scale=1.0), reads=[t1_b], writes=[t1_b])
    P.op("dve", lambda e: e.scalar_tensor_tensor(out=out_ap, in0=x_ap, scalar=0.0, in1=t1_ap, op0=ALU.max, op1=ALU.add),
         reads=[x_b, t1_b], writes=[out_b])


SSD_NCH = 10 ** 9
SSD_STOP = 99


def emit_ssd(P, xpre_d, bpre_d, cpre_d, cwx_d, cwb_d, cwc_d, dtc_d, sc_d, tri_d, ones_d, idf_d, idb_d, mneg_d, y_d, NH, S=SEQ, pfx="sd"):
    NT = S // 128
    HS = S // 2
    zps = [P.sbuf(pfx + f"_zp{i}", [128, HS + 3], F32) for i in range(2)]
    accs = [P.sbuf(pfx + f"_acc{i}", [128, HS], F32) for i in range(2)]
    xsT = P.sbuf(pfx + "_xsT", [64, S], F32)
    BT = P.sbuf(pfx + "_BT", [128, S], BF16)
    CT = P.sbuf(pfx + "_CT", [128, S], BF16)
    yac = P.sbuf(pfx + "_yac", [64, S], F32)
    cwx = P.sbuf(pfx + "_cwx", [64, 5], F32)
    cwb = P.sbuf(pfx + "_cwb", [128, 5], F32)
    cwc = P.sbuf(pfx + "_cwc", [128, 5], F32)
    sc = P.sbuf(pfx + "_sc", [128, 3], F32)
    tri = P.sbuf(pfx + "_tri", [128, 128], F32)
    ones = P.sbuf(pfx + "_ones", [128, 128], F32)
    idf = P.sbuf(pfx + "_idf", [128, 128], F32)
    idb = P.sbuf(pfx + "_idb", [128, 128], BF16)
    mneg = P.sbuf(pfx + "_mneg", [128, 128], BF16)
    dtr = P.sbuf(pfx + "_dtr", [128, NT], F32)
    dtall = P.sbuf(pfx + "_dtall", [128, NT, 6], F32)
    t1 = P.sbuf(pfx + "_t1", [128, NT], F32)
    dt = P.sbuf(pfx + "_dt", [128, NT], F32)
    a_ = P.sbuf(pfx + "_a", [128, NT], F32)
    acum = P.sbuf(pfx + "_acum", [128, NT], F32)
    nacum = P.sbuf(pfx + "_nacum", [128, NT], F32)
    dB = P.sbuf(pfx + "_dB", [128, NT], F32)
    dlast = P.sbuf(pfx + "_dlast", [128, NT], F32)
    Aneg = P.sbuf(pfx + "_Aneg", [128, 1], F32)
    SK = 5
    dg = [P.sbuf(pfx + f"_dg{i}", [128, 128], F32) for i in range(SK)]
    EB = [P.sbuf(pfx + f"_EB{i}", [128, 128], F32) for i in range(SK)]
    LmT = [P.sbuf(pfx + f"_LmT{i}", [128, 128], F32) for i in range(SK)]
    SLT = [P.sbuf(pfx + f"_SLT{i}", [128, 128], BF16) for i in range(SK)]
    CgT = [P.sbuf(pfx + f"_CgT{i}", [128, 128], BF16) for i in range(SK)]
    X = [P.sbuf(pfx + f"_X{i}", [128, 64], BF16) for i in range(SK)]
    Bd = [P.sbuf(pfx + f"_Bd{i}", [128, 128], BF16) for i in range(SK)]
    hf = P.sbuf(pfx + "_hf", [128, 64], F32)
    hb = [P.sbuf(pfx + f"_hb{i}", [128, 64], BF16) for i in range(2)]
    pk_ = [P.psum(pfx + f"_pk{i}", [128, 512]) for i in range(SK)]
    pbf = P.psum(pfx + "_pbf", [128, 1024], BF16)
    p_gb = [P.psum(pfx + f"_pgb{i}", [128, 128]) for i in range(1)]
    p_misc = p_gb[0]

    for (sb, d_) in ((tri, tri_d), (ones, ones_d), (idf, idf_d), (idb, idb_d), (mneg, mneg_d)):
        P.dma("sp", sb[:, :], d_[:, :], reads=[d_], writes=[sb])
    P.op("pool", lambda e: e.memset(zps[0][:, 0:3], 0.0), writes=[zps[0]])
    for h in range(NH):
        P.dma("sp", cwx[:, :], cwx_d[h], reads=[cwx_d.buf], writes=[cwx])
        P.dma("sp", cwb[:, :], cwb_d[h], reads=[cwb_d.buf], writes=[cwb])
        P.dma("sp", cwc[:, :], cwc_d[h], reads=[cwc_d.buf], writes=[cwc])
        P.dma("sp", sc[:, :], sc_d[h], reads=[sc_d.buf], writes=[sc])
        if h == 0:
            P.dma("act", dtall[:, :, :], dtc_d[0], reads=[dtc_d.buf], writes=[dtall])
        emit_conv_silu(P, xpre_d[h], cwx, 64, S, zps, accs, lambda hf: [(xsT, xsT[:, hf * HS:(hf + 1) * HS])], src_buf=xpre_d.buf)
        if h % 3 == 0:
            emit_conv_silu(P, bpre_d[h], cwb, 128, S, zps, accs, lambda hf: [(BT, BT[:, hf * HS:(hf + 1) * HS])], src_buf=bpre_d.buf)
            emit_conv_silu(P, cpre_d[h], cwc, 128, S, zps, accs, lambda hf: [(CT, CT[:, hf * HS:(hf + 1) * HS])], src_buf=cpre_d.buf)
        P.op("dve", lambda e, h=h: e.tensor_scalar(out=dtr[:, :], in0=dtall[:, :, h], scalar1=sc[:, 0:1], scalar2=None, op0=ALU.add), reads=[dtall, sc], writes=[dtr])
        emit_softplus(P, "dve", dt, dt[:, :], dtr, dtr[:, :], t1, t1[:, :], 128)
        P.op("act", lambda e: e.activation(out=Aneg[:, :], in_=sc[:, 1:2], func=AF.Exp), reads=[sc], writes=[Aneg])
        P.op("dve", lambda e: e.tensor_scalar(out=Aneg[:, :], in0=Aneg[:, :], scalar1=-1.0, scalar2=None, op0=ALU.mult), reads=[Aneg], writes=[Aneg])
        P.op("dve", lambda e: e.tensor_scalar(out=a_[:, :], in0=dt[:, :], scalar1=Aneg[:, 0:1], scalar2=None, op0=ALU.mult), reads=[dt, Aneg], writes=[a_])
        P.op("pe", lambda e: e.matmul(p_misc[:, 0:NT], lhsT=tri[:, :], rhs=a_[:, :], start=True, stop=True), reads=[tri, a_], writes=[p_misc])
        P.op("dve", lambda e: e.tensor_copy(out=acum[:, :], in_=p_misc[:, 0:NT]), reads=[p_misc], writes=[acum])
        P.op("dve", lambda e: e.tensor_scalar(out=nacum[:, :], in0=acum[:, :], scalar1=-1.0, scalar2=None, op0=ALU.mult), reads=[acum], writes=[nacum])
        P.op("pe", lambda e: e.matmul(p_misc[:, 0:NT], lhsT=ones[:, :], rhs=a_[:, :], start=True, stop=True), reads=[ones, a_], writes=[p_misc])
        P.op("act", lambda e: e.activation(out=dlast[:, :], in_=p_misc[:, 0:NT], func=AF.Exp), reads=[p_misc], writes=[dlast])
        P.op("dve", lambda e: e.tensor_tensor(out=dB[:, :], in0=p_misc[:, 0:NT], in1=acum[:, :], op=ALU.subtract), reads=[p_misc, acum], writes=[dB])
        P.op("act", lambda e: e.activation(out=dB[:, :], in_=dB[:, :], func=AF.Exp), reads=[dB], writes=[dB])
        P.op("dve", lambda e: e.memset(hf[:, :], 0.0), writes=[hf])
        P.op("pool", lambda e: e.memset(hb[0][:, :], 0.0), writes=[hb[0]])
        seq_state = {"next": 0}

        def chunk_gen(c):
            cs = slice(c * 128, (c + 1) * 128)
            i2 = c % SK
            bk = pk_[i2]
            R = [slice(0, 128), slice(128, 256), slice(256, 384), slice(384, 512)]
            b0 = i2 * 128
            P.op("pool", lambda e: e.tensor_scalar(out=dg[i2][:, :], in0=idf[:, :], scalar1=acum[:, c:c + 1], scalar2=None, op0=ALU.mult),
                 reads=[idf, acum], writes=[dg[i2]])
            yield
            P.op("pe", lambda e: e.matmul(bk[:, R[0]], lhsT=ones[:, :], rhs=dg[i2][:, :], start=True, stop=False, skip_group_check=True), reads=[ones, dg[i2]], writes=[bk])
            P.op("pe", lambda e: e.matmul(bk[:, R[0]], lhsT=idb[:, :], rhs=mneg[:, :], start=False, stop=True, skip_group_check=True), reads=[idb, mneg], writes=[bk])
            P.op("pe", lambda e: e.matmul(bk[:, R[1]], lhsT=ones[:, :], rhs=dg[i2][:, :], start=True, stop=True, skip_group_check=True), reads=[ones, dg[i2]], writes=[bk])
            P.op("pe", lambda e: e.matmul(bk[:, R[2]], lhsT=BT[:, cs], rhs=CT[:, cs], start=True, stop=True, skip_group_check=True), reads=[BT, CT], writes=[bk])
            P.op("pe", lambda e: e.transpose(out=bk[:, 384:448], in_=xsT[:, cs], identity=idf[0:64, 0:64]), reads=[xsT, idf], writes=[bk])
            P.op("pe", lambda e: e.transpose(out=pbf[:, b0:b0 + 128], in_=BT[:, cs], identity=idb[:, :]), reads=[BT, idb], writes=[pbf])
            yield
            P.op("act", lambda e: e.activation(out=EB[i2][:, :], in_=bk[:, R[1]], func=AF.Exp), reads=[bk], writes=[EB[i2]])
            P.op("act", lambda e: e.activation(out=LmT[i2][:, :], in_=bk[:, R[0]], func=AF.Exp, bias=nacum[:, c:c + 1], scale=1.0),
                 reads=[bk, nacum], writes=[LmT[i2]])
            yield
            P.op("dve", lambda e: e.tensor_tensor(out=SLT[i2][:, :], in0=bk[:, R[2]], in1=LmT[i2][:, :], op=ALU.mult), reads=[bk, LmT[i2]], writes=[SLT[i2]])
            P.op("dve", lambda e: e.tensor_scalar(out=X[i2][:, :], in0=bk[:, 384:448], scalar1=dt[:, c:c + 1], scalar2=None, op0=ALU.mult),
                 reads=[bk, dt], writes=[X[i2]])
            P.op("dve", lambda e: e.tensor_scalar(out=Bd[i2][:, :], in0=pbf[:, b0:b0 + 128], scalar1=dB[:, c:c + 1], scalar2=None, op0=ALU.mult),
                 reads=[pbf, dB], writes=[Bd[i2]])
            P.op("pool", lambda e: e.tensor_tensor(out=CgT[i2][:, :], in0=CT[:, cs], in1=EB[i2][:, :], op=ALU.mult), reads=[CT, EB[i2]], writes=[CgT[i2]])
            yield
            while seq_state["next"] != c:
                yield
            hcur = hb[c % 2]
            hnxt = hb[(c + 1) % 2]
            P.op("pe", lambda e: e.matmul(bk[0:64, R[0]], lhsT=X[i2][:, :], rhs=SLT[i2][:, :], start=True, stop=False, skip_group_check=True), reads=[X[i2], SLT[i2]], writes=[bk])
            P.op("pe", lambda e: e.matmul(bk[0:64, R[0]], lhsT=hcur[:, :], rhs=CgT[i2][:, :], start=False, stop=True, skip_group_check=True), reads=[hcur, CgT[i2]], writes=[bk])
            P.op("pe", lambda e: e.matmul(bk[:, 128:192], lhsT=Bd[i2][:, :], rhs=X[i2][:, :], start=True, stop=True, skip_group_check=True), reads=[Bd[i2], X[i2]], writes=[bk])
            seq_state["next"] = c + 1
            yield
            P.op("dve", lambda e: e.scalar_tensor_tensor(out=hf[:, :], in0=hf[:, :], scalar=dlast[:, c:c + 1], in1=bk[:, 128:192], op0=ALU.mult, op1=ALU.add),
                 reads=[hf, dlast, bk], writes=[hf])
            P.op("act", lambda e: e.copy(out=hnxt[:, :], in_=hf[:, :]), reads=[hf], writes=[hnxt])
            P.op("dve", lambda e: e.scalar_tensor_tensor(out=yac[:, cs], in0=xsT[:, cs], scalar=sc[0:64, 2:3], in1=bk[0:64, R[0]], op0=ALU.mult, op1=ALU.add),
                 reads=[xsT, sc, bk], writes=[yac])

        run_pipeline([chunk_gen(c) for c in range(min(NT, SSD_NCH))], SK)
        P.dma("pool", y_d[h], yac[:, :], reads=[yac], writes=[y_d.buf])
        if NT % 2 == 1 or True:
            P.op("pool", lambda e: e.memset(hb[0][:, :], 0.0), writes=[hb[0]])


def rec_consts():
    import ml_dtypes
    bf = ml_dtypes.bfloat16
    s1 = np.arange(128)[:, None]
    s2 = np.arange(128)[None, :]
    tri = (s1 <= s2).astype(np.float32)
    mneg = np.where(s2 < s1, -NEGM, 0.0).astype(np.float32)
    mpos = np.where(s2 >= s1, NEGM, 0.0).astype(np.float32)
    return dict(tri=tri, ones=np.ones((128, 128), np.float32), idf=np.eye(128, dtype=np.float32),
                idb=np.eye(128, dtype=np.float32).astype(bf), mneg=mneg.astype(bf), mpos=mpos.astype(bf))


GDN_NCH = 10 ** 9
GDN_K = 7
GDN_CHAIN_BF16 = False


class Pipe:
    def __init__(self, gens, K):
        self.it = iter(gens)
        self.K = K
        self.active = []
        self.pending = True

    def round(self):
        if self.pending and len(self.active) < self.K:
            g = next(self.it, None)
            if g is None:
                self.pending = False
            else:
                self.active.append(g)
        for g in list(self.active):
            try:
                next(g)
            except StopIteration:
                self.active.remove(g)
        return self.pending or bool(self.active)


def run_chains(chains):
    chains = [list(c) for c in chains]
    while any(chains):
        for c in chains:
            if c and not c[0].round():
                c.pop(0)


def run_pipeline(gens, K):
    active = []
    it = iter(gens)
    pending = True
    while True:
        if pending and len(active) < K:
            g = next(it, None)
            if g is None:
                pending = False
            else:
                active.append(g)
        if not active:
            if not pending:
                break
            continue
        for g in list(active):
            try:
                next(g)
            except StopIteration:
                active.remove(g)


def emit_gdn(P, qpre_d, kpre_d, vpre_d, cwq_d, cwk_d, cwv_d, bcol_d, acol_d, sc_d, gate_d, nw_d,
             tri_d, ones_d, idf_d, idb_d, mneg_d, mpos_d, y_d, NH, S=SEQ, pfx="gd"):
    NT = S // 128
    scratch_off = P.arena_off
    HS = S // 2
    zps = [P.sbuf(pfx + f"_zp{i}", [128, HS + 3], F32) for i in range(2)]
    accs = [P.sbuf(pfx + f"_acc{i}", [128, HS], F32) for i in range(2)]
    scratch_end = P.arena_off
    qn = P.sbuf(pfx + "_qn", [64, S], BF16)
    kn = P.sbuf(pfx + "_kn", [64, S], BF16)
    vT = P.sbuf(pfx + "_vT", [64, S], BF16)
    gt = P.sbuf(pfx + "_gt", [128, NT, 64], F32)
    yall = [P.sbuf(pfx + f"_yall{i}", [64, 2048], F32) for i in range(2)]
    ytm = [P.sbuf(pfx + f"_ytm{i}", [128, 64], F32) for i in range(GDN_K)]
    cw = [P.sbuf(pfx + f"_cw{i}", [64, 5], F32) for i in range(3)]
    sc = P.sbuf(pfx + "_sc", [128, 2], F32)
    nw = P.sbuf(pfx + "_nw", [128, 64], F32)
    tri = P.sbuf(pfx + "_tri", [128, 128], F32)
    ones = P.sbuf(pfx + "_ones", [128, 128], F32)
    idf = P.sbuf(pfx + "_idf", [128, 128], F32)
    idb = P.sbuf(pfx + "_idb", [128, 128], BF16)
    mneg = P.sbuf(pfx + "_mneg", [128, 128], BF16)
    mpos = P.sbuf(pfx + "_mpos", [128, 128], BF16)
    col = {n: P.sbuf(pfx + "_c_" + n, [128, NT], F32) for n in
           ("b", "a", "t1", "beta", "nbeta", "g", "gcum", "ngcum", "egc", "bege", "dk", "dlast")}
    Aneg = P.sbuf(pfx + "_Aneg", [128, 1], F32)
    ball = P.sbuf(pfx + "_ball", [128, NT, 6], F32)
    aall = P.sbuf(pfx + "_aall", [128, NT, 6], F32)
    rt = [P.sbuf(pfx + f"_rt{i}", [64, 512], F32) for i in range(2)]
    eps_t = P.sbuf(pfx + "_eps", [128, 1], F32)
    P.op("dve", lambda e: e.memset(eps_t[:, :], 1e-6), writes=[eps_t])

    def f32t(n, k=2):
        return [P.sbuf(pfx + f"_{n}{i}", [128, 128], F32) for i in range(k)]

    def bft(n, shape, k=2):
        return [P.sbuf(pfx + f"_{n}{i}", shape, BF16) for i in range(k)]
    _save_off = P.arena_off
    P.arena_off = scratch_off
    NB = GDN_K
    dg = f32t("dg", NB)
    EB = f32t("EB", NB)
    dec = f32t("dec", NB)
    decT = f32t("decT", NB)
    ydt = BF16 if GDN_CHAIN_BF16 else F32
    YYs = [[P.sbuf(pfx + f"_YY{j}_{i}", [128, 256], ydt) for i in range(2)] for j in range(NB)]
    Ys = YTs = None
    PTs = [f32t(f"PT{j}_", 2) for j in range(NB)]
    PTbs = [bft(f"PTb{j}_", [128, 128], 2) for j in range(NB)] if GDN_CHAIN_BF16 else PTs
    TTb = bft("TTb", [128, 128], NB)
    attnT = bft("attnT", [128, 128], NB)
    qgT = bft("qgT", [64, 128], NB)
    kbg = bft("kbg", [128, 64], NB)
    kd = bft("kd", [128, 64], NB)
    vb = bft("vb", [128, 64], NB)
    wT = bft("wT", [64, 128], NB)
    u_sb = [P.sbuf(pfx + f"_u{i}", [128, 64], F32) for i in range(NB)]
    vnew = bft("vnew", [128, 64], NB)
    Sf = P.sbuf(pfx + "_Sf", [64, 64], F32)
    Sb = bft("Sb", [64, 64])
    junk = [P.sbuf(pfx + f"_junk{i}", [128, 64], F32) for i in range(NB)]
    ssq = [P.sbuf(pfx + f"_ssq{i}", [128, 1], F32) for i in range(NB)]
    nwg = [P.sbuf(pfx + f"_nwg{i}", [128, 64], F32) for i in range(NB)]
    assert P.arena_off <= scratch_end, ('gdn sets overflow conv scratch', P.arena_off, scratch_end)
    P.arena_off = _save_off
    pa = [P.psum(pfx + f"_pa{i}", [128, 512]) for i in range(7)]
    pbs = [P.psum(pfx + f"_pb{i}", [128, 1024], BF16) for i in range(1)]
    pg = [pa[5], pa[6]]
    pai = [0]

    def nxt_pa():
        pai[0] += 1
        return pa[5 + pai[0] % 2]
    pci = [0]

    def nxt_pc():
        pci[0] += 1
        return pc[pci[0] % 2]

    for (sb, d_) in ((tri, tri_d), (ones, ones_d), (idf, idf_d), (idb, idb_d), (mneg, mneg_d), (mpos, mpos_d)):
        P.dma("sp", sb[:, :], d_[:, :], reads=[d_], writes=[sb])
    for h in range(NH):
        P.op("pool", lambda e: e.memset(zps[0][:, 0:3], 0.0), writes=[zps[0]])
        for i, d_ in enumerate((cwq_d, cwk_d, cwv_d)):
            P.dma("sp", cw[i][:, :], d_[h], reads=[d_.buf], writes=[cw[i]])
        P.dma("sp", sc[:, :], sc_d[h], reads=[sc_d.buf], writes=[sc])
        P.dma("sp", nw[:, :], nw_d[h], reads=[nw_d.buf], writes=[nw])
        if h == 0:
            P.dma("sp", ball[:, :, :], bcol_d[0], reads=[bcol_d.buf], writes=[ball])
            P.dma("sp", aall[:, :, :], acol_d[0], reads=[acol_d.buf], writes=[aall])
        P.op("pool", lambda e, h=h: e.tensor_copy(out=col["b"][:, :], in_=ball[:, :, h]), reads=[ball], writes=[col["b"]])
        P.op("pool", lambda e, h=h: e.tensor_copy(out=col["a"][:, :], in_=aall[:, :, h]), reads=[aall], writes=[col["a"]])
        P.dma("act", gt[:, :, :], gate_d[h], reads=[gate_d.buf], writes=[gt])
        P.op("act", lambda e: e.activation(out=gt[:, :, :], in_=gt[:, :, :], func=AF.Silu), reads=[gt], writes=[gt])
        for which, (src_d, outb) in enumerate(((qpre_d, qn), (kpre_d, kn))):
            emit_conv_silu(P, src_d[h], cw[which], 64, S, zps, accs, lambda hf: [(accs[hf], accs[hf][0:64, :])], src_buf=src_d.buf)
            for hf in range(2):
                zp, acc = zps[hf], accs[hf]
                P.op("act", lambda e, zp=zp, acc=acc: e.activation(out=zp[0:64, 3:HS + 3], in_=acc[0:64, :], func=AF.Square), reads=[acc], writes=[zp])
                for j in range(HS // 512):
                    cs = slice(hf * HS + j * 512, hf * HS + (j + 1) * 512)
                    ls = slice(j * 512, (j + 1) * 512)
                    ps = nxt_pa()
                    r_ = rt[j % 2]
                    P.op("pe", lambda e, ps=ps, j=j, zp=zp: e.matmul(ps[0:64, :], lhsT=ones[0:64, 0:64], rhs=zp[0:64, 3 + j * 512:3 + (j + 1) * 512],
                                                                    start=True, stop=True), reads=[ones, zp], writes=[ps])
                    P.op("act", lambda e, ps=ps, r_=r_: e.activation(out=r_[:, :], in_=ps[0:64, :], func=AF.Sqrt, bias=eps_t[0:64, 0:1], scale=1.0),
                         reads=[ps, eps_t], writes=[r_])
                    P.op("dve", lambda e, r_=r_: e.reciprocal(out=r_[:, :], in_=r_[:, :]), reads=[r_], writes=[r_])
                    P.op("dve", lambda e, r_=r_, cs=cs, ls=ls, acc=acc, outb=outb, sc_=(0.125 if which == 0 else 1.0): e.scalar_tensor_tensor(
                        out=outb[:, cs], in0=acc[0:64, ls], scalar=sc_, in1=r_[:, :], op0=ALU.mult, op1=ALU.mult),
                        reads=[acc, r_], writes=[outb])
        emit_conv_silu(P, vpre_d[h], cw[2], 64, S, zps, accs, lambda hf: [(vT, vT[:, hf * HS:(hf + 1) * HS])], src_buf=vpre_d.buf)
        c_ = col
        P.op("act", lambda e: e.activation(out=c_["beta"][:, :], in_=c_["b"][:, :], func=AF.Sigmoid), reads=[c_["b"]], writes=[c_["beta"]])
        P.op("dve", lambda e: e.tensor_scalar(out=c_["nbeta"][:, :], in0=c_["beta"][:, :], scalar1=-1.0, scalar2=None, op0=ALU.mult),
             reads=[c_["beta"]], writes=[c_["nbeta"]])
        P.op("dve", lambda e: e.tensor_scalar(out=c_["a"][:, :], in0=c_["a"][:, :], scalar1=sc[:, 0:1], scalar2=None, op0=ALU.add),
             reads=[c_["a"], sc], writes=[c_["a"]])
        emit_softplus(P, "dve", c_["g"], c_["g"][:, :], c_["a"], c_["a"][:, :], c_["t1"], c_["t1"][:, :], 128)
        P.op("act", lambda e: e.activation(out=Aneg[:, :], in_=sc[:, 1:2], func=AF.Exp), reads=[sc], writes=[Aneg])
        P.op("dve", lambda e: e.tensor_scalar(out=Aneg[:, :], in0=Aneg[:, :], scalar1=-1.0, scalar2=None, op0=ALU.mult), reads=[Aneg], writes=[Aneg])
        P.op("dve", lambda e: e.tensor_scalar(out=c_["g"][:, :], in0=c_["g"][:, :], scalar1=Aneg[:, 0:1], scalar2=None, op0=ALU.mult),
             reads=[c_["g"], Aneg], writes=[c_["g"]])
        pm = pg[0]
        P.op("pe", lambda e: e.matmul(pm[:, 0:NT], lhsT=tri[:, :], rhs=c_["g"][:, :], start=True, stop=True), reads=[tri, c_["g"]], writes=[pm])
        P.op("dve", lambda e: e.tensor_copy(out=c_["gcum"][:, :], in_=pm[:, 0:NT]), reads=[pm], writes=[c_["gcum"]])
        P.op("dve", lambda e: e.tensor_scalar(out=c_["ngcum"][:, :], in0=c_["gcum"][:, :], scalar1=-1.0, scalar2=None, op0=ALU.mult),
             reads=[c_["gcum"]], writes=[c_["ngcum"]])
        P.op("act", lambda e: e.activation(out=c_["egc"][:, :], in_=c_["gcum"][:, :], func=AF.Exp), reads=[c_["gcum"]], writes=[c_["egc"]])
        P.op("dve", lambda e: e.tensor_tensor(out=c_["bege"][:, :], in0=c_["egc"][:, :], in1=c_["beta"][:, :], op=ALU.mult),
             reads=[c_["egc"], c_["beta"]], writes=[c_["bege"]])
        P.op("pe", lambda e: e.matmul(pm[:, 0:NT], lhsT=ones[:, :], rhs=c_["g"][:, :], start=True, stop=True), reads=[ones, c_["g"]], writes=[pm])
        P.op("act", lambda e: e.activation(out=c_["dlast"][:, :], in_=pm[:, 0:NT], func=AF.Exp), reads=[pm], writes=[c_["dlast"]])
        P.op("dve", lambda e: e.tensor_tensor(out=c_["dk"][:, :], in0=pm[:, 0:NT], in1=c_["gcum"][:, :], op=ALU.subtract),
             reads=[pm, c_["gcum"]], writes=[c_["dk"]])
        P.op("act", lambda e: e.activation(out=c_["dk"][:, :], in_=c_["dk"][:, :], func=AF.Exp), reads=[c_["dk"]], writes=[c_["dk"]])
        P.barrier()
        P.op("dve", lambda e: e.memset(Sf[:, :], 0.0), writes=[Sf])
        P.op("pool", lambda e: e.memset(Sb[0][:, :], 0.0), writes=[Sb[0]])
        seq_state = {"next": 0}

        def chunk_gen(c):
            cs = slice(c * 128, (c + 1) * 128)
            i2 = c % NB
            slot = c % GDN_K
            bA = pa[slot]
            bB = bA
            rA = [slice(0, 128), slice(128, 256), slice(256, 384), slice(384, 512)]
            PT, PTb = PTs[i2], PTbs[i2]
            YY = YYs[i2]
            Yv = lambda i: YY[i][:, 0:128]
            YTv = lambda i: YY[i][:, 128:256]
            bkb = bA.t.bitcast(BF16)
            trg = bkb[:, 256:384] if GDN_CHAIN_BF16 else bA[:, rA[1]]
            idt = idb if GDN_CHAIN_BF16 else idf
            P.op("pool", lambda e, c=c, i2=i2: e.tensor_scalar(out=dg[i2][:, :], in0=idf[:, :], scalar1=c_["gcum"][:, c:c + 1], scalar2=None, op0=ALU.mult),
                 reads=[idf, c_["gcum"]], writes=[dg[i2]])
            yield
            P.op("pe", lambda e, i2=i2: e.matmul(bA[:, rA[0]], lhsT=ones[:, :], rhs=dg[i2][:, :], start=True, stop=False, skip_group_check=True), reads=[ones, dg[i2]], writes=[bA])
            P.op("pe", lambda e: e.matmul(bA[:, rA[0]], lhsT=idb[:, :], rhs=mpos[:, :], start=False, stop=True, skip_group_check=True), reads=[idb, mpos], writes=[bA])
            P.op("pe", lambda e, i2=i2: e.matmul(bA[:, rA[1]], lhsT=ones[:, :], rhs=dg[i2][:, :], start=True, stop=False, skip_group_check=True), reads=[ones, dg[i2]], writes=[bA])
            P.op("pe", lambda e: e.matmul(bA[:, rA[1]], lhsT=idb[:, :], rhs=mneg[:, :], start=False, stop=True, skip_group_check=True), reads=[idb, mneg], writes=[bA])
            P.op("pe", lambda e, i2=i2: e.matmul(bA[:, rA[2]], lhsT=ones[:, :], rhs=dg[i2][:, :], start=True, stop=True, skip_group_check=True), reads=[ones, dg[i2]], writes=[bA])
            P.op("pe", lambda e, cs=cs: e.matmul(bA[:, rA[3]], lhsT=kn[:, cs], rhs=kn[:, cs], start=True, stop=True, skip_group_check=True), reads=[kn], writes=[bA])
            yield
            P.op("act", lambda e, i2=i2: e.activation(out=EB[i2][0:64, :], in_=bA[0:64, rA[2]], func=AF.Exp), reads=[bA], writes=[EB[i2]])
            P.op("act", lambda e, i2=i2, c=c: e.activation(out=dec[i2][:, :], in_=bA[:, rA[0]], func=AF.Exp, bias=c_["gcum"][:, c:c + 1], scale=-1.0),
                 reads=[bA, c_["gcum"]], writes=[dec[i2]])
            P.op("act", lambda e, i2=i2, c=c: e.activation(out=decT[i2][:, :], in_=bA[:, rA[1]], func=AF.Exp, bias=c_["ngcum"][:, c:c + 1], scale=1.0),
                 reads=[bA, c_["ngcum"]], writes=[decT[i2]])
            yield
            P.op("pool", lambda e, i2=i2, cs=cs: e.tensor_tensor(out=qgT[i2][:, :], in0=qn[:, cs], in1=EB[i2][0:64, :], op=ALU.mult),
                 reads=[qn, EB[i2]], writes=[qgT[i2]])
            yield
            PTc = PT[0]
            P.op("dve", lambda e, i2=i2, c=c: e.scalar_tensor_tensor(out=Yv(0), in0=bA[:, rA[3]], scalar=c_["nbeta"][:, c:c + 1],
                                                                          in1=dec[i2][:, :], op0=ALU.mult, op1=ALU.mult),
                 reads=[bA, c_["nbeta"], dec[i2]], writes=[YY[0]])
            yield
            P.op("pe", lambda e, cs=cs: e.matmul(bB[:, rA[0]], lhsT=kn[:, cs], rhs=qn[:, cs], start=True, stop=True, skip_group_check=True), reads=[kn, qn], writes=[bB])
            P.op("pe", lambda e: e.transpose(out=trg, in_=Yv(0), identity=idt[:, :]), reads=[YY[0], idt], writes=[bB])
            pkt = pbs[0]
            k0 = slot * 128
            P.op("pe", lambda e, cs=cs, pkt=pkt, k0=k0: e.transpose(out=pkt[:, k0:k0 + 64], in_=kn[:, cs], identity=idb[0:64, 0:64]), reads=[kn, idb], writes=[pkt])
            P.op("pe", lambda e, cs=cs, pkt=pkt, k0=k0: e.transpose(out=pkt[:, k0 + 64:k0 + 128], in_=vT[:, cs], identity=idb[0:64, 0:64]), reads=[vT, idb], writes=[pkt])
            yield
            P.op("dve", lambda e, i2=i2: e.tensor_tensor(out=attnT[i2][:, :], in0=bB[:, rA[0]], in1=decT[i2][:, :], op=ALU.mult),
                 reads=[bB, decT[i2]], writes=[attnT[i2]])
            P.op("act", lambda e: e.copy(out=YTv(0), in_=trg), reads=[bB], writes=[YY[0]])
            P.op("act", lambda e, PTc=PTc: e.activation(out=PTc[:, :], in_=trg, func=AF.Identity), reads=[bB], writes=[PTc])
            P.op("dve", lambda e, i2=i2, c=c, pkt=pkt: e.tensor_scalar(out=kbg[i2][:, :], in0=pkt[:, slot * 128:slot * 128 + 64], scalar1=c_["bege"][:, c:c + 1], scalar2=None, op0=ALU.mult),
                 reads=[pkt, c_["bege"]], writes=[kbg[i2]])
            P.op("dve", lambda e, i2=i2, c=c, pkt=pkt: e.tensor_scalar(out=kd[i2][:, :], in0=pkt[:, slot * 128:slot * 128 + 64], scalar1=c_["dk"][:, c:c + 1], scalar2=None, op0=ALU.mult),
                 reads=[pkt, c_["dk"]], writes=[kd[i2]])
            P.op("dve", lambda e, i2=i2, c=c, pkt=pkt: e.tensor_scalar(out=vb[i2][:, :], in0=pkt[:, slot * 128 + 64:slot * 128 + 128], scalar1=c_["beta"][:, c:c + 1], scalar2=None, op0=ALU.mult),
                 reads=[pkt, c_["beta"]], writes=[vb[i2]])
            P.op("pool", lambda e, PTc=PTc: e.tensor_tensor(out=PTc[:, :], in0=PTc[:, :], in1=idf[:, :], op=ALU.add), reads=[PTc, idf], writes=[PTc])
            if GDN_CHAIN_BF16:
                P.op("pool", lambda e, PTc=PTc: e.tensor_copy(out=PTb[0][:, :], in_=PTc[:, :]), reads=[PTc], writes=[PTb[0]])
            yield
            cur = 0
            for lev in range(6):
                nx = 1 - cur
                P.op("pe", lambda e, cur=cur: e.matmul(bB[:, rA[2]], lhsT=YTv(cur), rhs=Yv(cur), start=True, stop=True, skip_group_check=True),
                     reads=[YY[cur]], writes=[bB])
                if lev < 5:
                    P.op("pe", lambda e, cur=cur: e.matmul(bB[:, rA[3]], lhsT=Yv(cur), rhs=YTv(cur), start=True, stop=True, skip_group_check=True),
                         reads=[YY[cur]], writes=[bB])
                yield
                if lev < 5:
                    P.op("act", lambda e, nx=nx: e.copy(out=YY[nx][:, :], in_=bB[:, 256:512]), reads=[bB], writes=[YY[nx]])
                else:
                    P.op("act", lambda e, nx=nx: e.copy(out=Yv(nx), in_=bB[:, rA[2]]), reads=[bB], writes=[YY[nx]])
                    if lev < 5:
                        P.op("act", lambda e, nx=nx: e.copy(out=YTv(nx), in_=bB[:, rA[3]]), reads=[bB], writes=[YY[nx]])
                yield
                P.op("pe", lambda e, nx=nx, cur=cur: e.matmul(bB[:, rA[0]], lhsT=Yv(nx), rhs=PTb[cur][:, :], start=True, stop=True, skip_group_check=True),
                     reads=[YY[nx], PTb[cur]], writes=[bB])
                yield
                if lev == 5:
                    P.op("dve", lambda e, cur=cur: e.tensor_tensor(out=TTb[i2][:, :], in0=bB[:, rA[0]], in1=PT[cur][:, :], op=ALU.add),
                         reads=[bB, PT[cur]], writes=[TTb[i2]])
                else:
                    P.op("dve", lambda e, nx=nx, cur=cur: e.tensor_tensor(out=PT[nx][:, :], in0=bB[:, rA[0]], in1=PT[cur][:, :], op=ALU.add),
                         reads=[bB, PT[cur]], writes=[PT[nx]])
                if GDN_CHAIN_BF16 and lev < 5:
                    P.op("pool", lambda e, nx=nx: e.tensor_copy(out=PTb[nx][:, :], in_=PT[nx][:, :]), reads=[PT[nx]], writes=[PTb[nx]])
                yield
                cur = nx
            P.op("pe", lambda e, i2=i2: e.matmul(bA[:, 128:192], lhsT=TTb[i2][:, :], rhs=vb[i2][:, :], start=True, stop=True, skip_group_check=True), reads=[TTb[i2], vb[i2]], writes=[bA])
            P.op("pe", lambda e, i2=i2: e.matmul(bA[0:64, 256:384], lhsT=kbg[i2][:, :], rhs=TTb[i2][:, :], start=True, stop=True, skip_group_check=True), reads=[kbg[i2], TTb[i2]], writes=[bA])
            yield
            P.op("act", lambda e, i2=i2: e.copy(out=u_sb[i2][:, :], in_=bA[:, 128:192]), reads=[bA], writes=[u_sb[i2]])
            P.op("act", lambda e, i2=i2: e.copy(out=wT[i2][:, :], in_=bA[0:64, 256:384]), reads=[bA], writes=[wT[i2]])
            P.op("pool", lambda e, i2=i2, c=c: e.tensor_tensor(out=nwg[i2][:, :], in0=gt[:, c, :], in1=nw[:, :], op=ALU.mult), reads=[gt, nw], writes=[nwg[i2]])
            yield
            while seq_state["next"] != c:
                yield
            Scur, Snxt = Sb[c % 2], Sb[(c + 1) % 2]
            P.op("pe", lambda e, i2=i2, Scur=Scur: e.matmul(bA[:, 384:448], lhsT=wT[i2][:, :], rhs=Scur[:, :], start=True, stop=True, skip_group_check=True), reads=[wT[i2], Scur], writes=[bA])
            P.op("dve", lambda e, i2=i2: e.tensor_tensor(out=vnew[i2][:, :], in0=u_sb[i2][:, :], in1=bA[:, 384:448], op=ALU.subtract),
                 reads=[u_sb[i2], bA], writes=[vnew[i2]])
            P.op("pe", lambda e, i2=i2: e.matmul(bA[0:64, 128:192], lhsT=kd[i2][:, :], rhs=vnew[i2][:, :], start=True, stop=True, skip_group_check=True), reads=[kd[i2], vnew[i2]], writes=[bA])
            P.op("pe", lambda e, i2=i2, Scur=Scur: e.matmul(bB[:, 0:64], lhsT=qgT[i2][:, :], rhs=Scur[:, :], start=True, stop=False, skip_group_check=True), reads=[qgT[i2], Scur], writes=[bB])
            P.op("pe", lambda e, i2=i2: e.matmul(bB[:, 0:64], lhsT=attnT[i2][:, :], rhs=vnew[i2][:, :], start=False, stop=True, skip_group_check=True), reads=[attnT[i2], vnew[i2]], writes=[bB])
            P.op("dve", lambda e, c=c: e.scalar_tensor_tensor(out=Sf[:, :], in0=Sf[:, :], scalar=c_["dlast"][0:64, c:c + 1], in1=bA[0:64, 128:192],
                                                             op0=ALU.mult, op1=ALU.add), reads=[Sf, c_["dlast"], bA], writes=[Sf])
            P.op("act", lambda e, Snxt=Snxt: e.copy(out=Snxt[:, :], in_=Sf[:, :]), reads=[Sf], writes=[Snxt])
            seq_state["next"] = c + 1
            yield
            P.op("act", lambda e, i2=i2: e.activation(out=junk[i2][:, :], in_=bB[:, 0:64], func=AF.Square, accum_out=ssq[i2][:, 0:1]),
                 reads=[bB], writes=[junk[i2], ssq[i2]])
            P.op("act", lambda e, i2=i2: e.activation(out=ssq[i2][:, :], in_=ssq[i2][:, :], func=AF.Sqrt, bias=eps_t[:, 0:1], scale=1.0 / 64.0),
                 reads=[ssq[i2], eps_t], writes=[ssq[i2]])
            yield
            P.op("dve", lambda e, i2=i2: e.reciprocal(out=ssq[i2][:, :], in_=ssq[i2][:, :]), reads=[ssq[i2]], writes=[ssq[i2]])
            P.op("dve", lambda e, i2=i2, c=c: e.scalar_tensor_tensor(out=ytm[i2][:, :], in0=bB[:, 0:64], scalar=ssq[i2][:, 0:1], in1=nwg[i2][:, :],
                                                                    op0=ALU.mult, op1=ALU.mult), reads=[bB, ssq[i2], nwg[i2]], writes=[ytm[i2]])
            yield
            P.op("pe", lambda e, i2=i2: e.transpose(out=bB[0:64, 256:384], in_=ytm[i2][:, :], identity=idf[:, :]), reads=[ytm[i2], idf], writes=[bB])
            yield
            yb_ = yall[(c // 16) % 2]
            P.op("act", lambda e, yb_=yb_: e.copy(out=yb_[:, (c % 16) * 128:(c % 16 + 1) * 128], in_=bB[0:64, 256:384]), reads=[bB], writes=[yb_])
            if c % 16 == 15 or c == min(NT, GDN_NCH) - 1:
                c0 = (c // 16) * 16
                P.dma("pool", y_d[h][:, c0 * 128:(c + 1) * 128], yb_[:, 0:(c - c0 + 1) * 128], reads=[yb_], writes=[y_d.buf])

        run_pipeline([chunk_gen(c) for c in range(min(NT, GDN_NCH))], GDN_K)
        P.barrier()


D_MIX = 1536
ALPHA = (2.0 * 2) ** 0.25
N_FM = 4736
N_TM = 1170
FM_AQ, FM_AK, FM_AG, FM_SX, FM_SB, FM_SC, FM_SZ, FM_CQ, FM_CK, FM_CG, FM_DQ, FM_DK, FM_DV = (
    0, 384, 768, 1152, 1536, 1792, 2048, 2432, 2816, 3200, 3584, 3968, 4352)
TM_AV, TM_CV, TM_DG, TM_DT, TM_DB, TM_DA = 0, 384, 768, 1152, 1158, 1164


def in_col_perm():
    r = lambda a, b: list(range(a, b))
    fm = r(0, 768) + r(1152, 1536) + r(1536, 2816) + r(2822, 3590) + r(3974, 4358) + r(4358, 5510)
    tm = r(768, 1152) + r(3590, 3974) + r(5510, 5894) + r(2816, 2822) + r(5894, 5900) + r(5900, 5906)
    assert len(fm) == N_FM and len(tm) == N_TM
    return np.array(fm + tm)


def emit_projF(P, x_src, src_bf16, w_ap, w_buf, projT, projtm, S=SEQ, pfx="pj"):
    KC = D_MODEL // 128
    C = IN_COLS
    TB = 1024
    wbf = P.sbuf(pfx + "_wbf", [128, KC, C], BF16)
    HW = (C + 1) // 2
    wst = [P.sbuf(pfx + f"_wst{i}", [128, HW], F32) for i in range(2)]
    xbf = [P.sbuf(pfx + f"_xbf{i}", [128, KC, TB], BF16) for i in range(2)]
    xst = [P.sbuf(pfx + f"_xst{i}", [128, TB], F32) for i in range(2)]
    ost = [P.sbuf(pfx + f"_ost{i}", [128, 512], F32) for i in range(3)]
    ot2 = [P.sbuf(pfx + f"_ot2{i}", [128, N_TM], F32) for i in range(2)]
    ps = [P.psum(pfx + f"_ps{i}", [128, 512]) for i in range(6)]
    n = 0
    for k in range(KC):
        for h in range(2):
            c0, c1 = h * HW, min(C, (h + 1) * HW)
            b = wst[n % 2]
            P.dma("sp" if n % 2 == 0 else "act", b[:, :c1 - c0], w_ap[k * 128:(k + 1) * 128, c0:c1], reads=[w_buf], writes=[b])
            eng = "dve" if n % 2 == 0 else "pool"
            P.op(eng, lambda e, b=b, k=k, c0=c0, c1=c1: e.tensor_copy(out=wbf[:, k, c0:c1], in_=b[:, :c1 - c0]), reads=[b], writes=[wbf])
            n += 1
    ci = 0
    oi = 0
    for blk in range(S // TB):
        xb = xbf[blk % 2]
        t0 = blk * TB
        if src_bf16:
            P.dma("sp", xb[:, :, :], x_src.t.rearrange("(k p) t -> p k t", p=128)[:, :, t0:t0 + TB], reads=[x_src], writes=[xb])
        else:
            for k in range(KC):
                b = xst[k % 2]
                P.dma("sp", b[:, :], x_src[k * 128:(k + 1) * 128, t0:t0 + TB], reads=[x_src], writes=[b])
                P.op("pool", lambda e, b=b, k=k, xb=xb: e.tensor_copy(out=xb[:, k, :], in_=b[:, :]), reads=[b], writes=[xb])
        for sub in range(TB // 512):
            for cc in range(N_FM // 128):
                p = ps[ci % 6]
                o = ost[ci % 3]
                for k in range(KC):
                    P.op("pe", lambda e, p=p, k=k, cc=cc, sub=sub, xb=xb: e.matmul(
                        p[:, :], lhsT=wbf[:, k, cc * 128:(cc + 1) * 128], rhs=xb[:, k, sub * 512:(sub + 1) * 512],
                        start=(k == 0), stop=(k == KC - 1)), reads=[wbf, xb], writes=[p])
                if ci % 2 == 0:
                    P.op("dve", lambda e, p=p, o=o: e.tensor_copy(out=o[:, :], in_=p[:, :]), reads=[p], writes=[o])
                else:
                    P.op("act", lambda e, p=p, o=o: e.copy(out=o[:, :], in_=p[:, :]), reads=[p], writes=[o])
                P.dma("pool" if ci % 2 == 0 else "sp", projT[cc * 128:(cc + 1) * 128, t0 + sub * 512:t0 + (sub + 1) * 512], o[:, :],
                      reads=[o], writes=[projT])
                ci += 1
        for tt in range(TB // 128):
            o2 = ot2[oi % 2]
            oi += 1
            for c0 in range(0, N_TM, 512):
                c1 = min(N_TM, c0 + 512)
                p = ps[ci % 6]
                for k in range(KC):
                    P.op("pe", lambda e, p=p, k=k, tt=tt, c0=c0, c1=c1, xb=xb: e.matmul(
                        p[:, :c1 - c0], lhsT=xb[:, k, tt * 128:(tt + 1) * 128], rhs=wbf[:, k, N_FM + c0:N_FM + c1],
                        start=(k == 0), stop=(k == KC - 1)), reads=[wbf, xb], writes=[p])
                if ci % 2 == 0:
                    P.op("dve", lambda e, p=p, o2=o2, c0=c0, c1=c1: e.tensor_copy(out=o2[:, c0:c1], in_=p[:, :c1 - c0]), reads=[p], writes=[o2])
                else:
                    P.op("act", lambda e, p=p, o2=o2, c0=c0, c1=c1: e.copy(out=o2[:, c0:c1], in_=p[:, :c1 - c0]), reads=[p], writes=[o2])
                ci += 1
            P.dma("pool", projtm[t0 + tt * 128:t0 + (tt + 1) * 128, :], o2[:, :], reads=[o2], writes=[projtm])


def emit_outlnF(P, mix_d, projT, snw_ap, w_ap, lng_ap, lnb_ap, ones_ap, idf_ap, cbuf, x_d, xo_d, xoT_d, S=SEQ, TBLK=2048, pfx="ol"):
    FC = D_MIX // 128
    T = TBLK
    wbf = P.sbuf(pfx + "_wbf", [128, FC, D_MODEL], BF16)
    mix = P.sbuf(pfx + "_mix", [128, FC, T], BF16)
    wst = [P.sbuf(pfx + f"_wst{i}", [128, D_MODEL], F32) for i in range(2)]
    mst = [P.sbuf(pfx + f"_mst{i}", [128, T], F32) for i in range(2)]
    zst = [P.sbuf(pfx + f"_zst{i}", [128, T], F32) for i in range(2)]
    gz = P.sbuf(pfx + "_gz", [128, 3, T], F32)
    sq = P.sbuf(pfx + "_sq", [128, T], F32)
    rs = P.sbuf(pfx + "_rs", [128, T], F32)
    snw = P.sbuf(pfx + "_snw", [128, 3], F32)
    lng = P.sbuf(pfx + "_lng", [128, D_MODEL], F32)
    lnb = P.sbuf(pfx + "_lnb", [128, D_MODEL], F32)
    ones = P.sbuf(pfx + "_ones", [128, 128], F32)
    idf = P.sbuf(pfx + "_idf", [128, 128], F32)
    eps6 = P.sbuf(pfx + "_eps6", [128, 1], F32)
    eps5 = P.sbuf(pfx + "_eps5", [128, 1], F32)
    xt = [P.sbuf(pfx + f"_xt{i}", [128, D_MODEL], F32) for i in range(2)]
    zt = [P.sbuf(pfx + f"_zt{i}", [128, D_MODEL], F32) for i in range(2)]
    xTt = [P.sbuf(pfx + f"_xTt{i}", [128, 8, 128], BF16) for i in range(2)]
    st = [P.sbuf(pfx + f"_st{i}", [128, 2, 6], F32) for i in range(2)]
    mv = [P.sbuf(pfx + f"_mv{i}", [128, 2], F32) for i in range(2)]
    pss = [P.psum(pfx + f"_pss{i}", [128, 512]) for i in range(2)]
    po = [P.psum(pfx + f"_po{i}", [128, 512]) for i in range(4)]
    ptr = [P.psum(pfx + f"_ptr{i}", [128, 512]) for i in range(2)]
    P.op("dve", lambda e: e.memset(eps6[:, :], 1e-6), writes=[eps6])
    P.op("dve", lambda e: e.memset(eps5[:, :], 1e-5), writes=[eps5])
    P.dma("sp", snw[:, :], snw_ap, reads=[cbuf], writes=[snw])
    P.dma("sp", lng[:, :], lng_ap, reads=[cbuf], writes=[lng])
    P.dma("sp", lnb[:, :], lnb_ap, reads=[cbuf], writes=[lnb])
    P.dma("sp", ones[:, :], ones_ap, reads=[cbuf], writes=[ones])
    P.dma("sp", idf[:, :], idf_ap, reads=[cbuf], writes=[idf])
    for k in range(FC):
        b = wst[k % 2]
        P.dma("act", b[:, :], w_ap[k * 128:(k + 1) * 128, :], reads=[cbuf], writes=[b])
        P.op("pool", lambda e, b=b, k=k: e.tensor_copy(out=wbf[:, k, :], in_=b[:, :]), reads=[b], writes=[wbf])
    n = 0
    ti = 0
    for blk in range(S // T):
        b0 = blk * T
        bsl = slice(b0, b0 + T)
        for f0 in (0, 6, 9):
            for k in range(3):
                b = mst[n % 2]
                n += 1
                P.dma("sp", b[:, :], mix_d[(f0 + k) * 128:(f0 + k + 1) * 128, bsl], reads=[mix_d], writes=[b])
                P.op("dve", lambda e, b=b, fk=f0 + k: e.tensor_copy(out=mix[:, fk, :], in_=b[:, :]), reads=[b], writes=[mix])
        for k in range(3):
            b = mst[n % 2]
            zb = zst[k % 2]
            n += 1
            P.dma("sp", b[:, :], mix_d[(3 + k) * 128:(4 + k) * 128, bsl], reads=[mix_d], writes=[b])
            P.dma("sp", zb[:, :], projT[FM_SZ + k * 128:FM_SZ + (k + 1) * 128, bsl], reads=[projT], writes=[zb])
            P.op("act", lambda e, zb=zb: e.activation(out=zb[:, :], in_=zb[:, :], func=AF.Silu), reads=[zb], writes=[zb])
            P.op("dve", lambda e, b=b, zb=zb, k=k: e.tensor_tensor(out=gz[:, k, :], in0=b[:, :], in1=zb[:, :], op=ALU.mult), reads=[b, zb], writes=[gz])
        for j in range(T // 512):
            cs = slice(j * 512, (j + 1) * 512)
            ps = pss[j % 2]
            for k in range(3):
                P.op("act", lambda e, k=k, cs=cs: e.activation(out=sq[:, cs], in_=gz[:, k, cs], func=AF.Square), reads=[gz], writes=[sq])
                P.op("pe", lambda e, ps=ps, k=k, cs=cs: e.matmul(ps[:, :], lhsT=ones[:, :], rhs=sq[:, cs], start=(k == 0), stop=(k == 2)),
                     reads=[ones, sq], writes=[ps])
            P.op("act", lambda e, ps=ps, cs=cs: e.activation(out=rs[:, cs], in_=ps[:, :], func=AF.Sqrt, bias=eps6[:, 0:1], scale=1.0 / 384.0),
                 reads=[ps, eps6], writes=[rs])
            P.op("dve", lambda e, cs=cs: e.reciprocal(out=rs[:, cs], in_=rs[:, cs]), reads=[rs], writes=[rs])
            for k in range(3):
                P.op("dve", lambda e, k=k, cs=cs: e.scalar_tensor_tensor(out=mix[:, 3 + k, cs], in0=gz[:, k, cs], scalar=snw[:, k:k + 1], in1=rs[:, cs],
                                                                        op0=ALU.mult, op1=ALU.mult), reads=[gz, snw, rs], writes=[mix])
        for tt in range(T // 128):
            x_ = xt[ti % 2]
            z_ = zt[ti % 2]
            s_ = st[ti % 2]
            m_ = mv[ti % 2]
            xT_ = xTt[ti % 2]
            r0 = b0 + tt * 128
            P.dma("sp", x_[:, :], x_d[r0:r0 + 128, :], reads=[x_d], writes=[x_])
            for hh in range(2):
                p = po[(ti * 2 + hh) % 4]
                for k in range(FC):
                    P.op("pe", lambda e, p=p, k=k, tt=tt, hh=hh: e.matmul(p[:, :], lhsT=mix[:, k, tt * 128:(tt + 1) * 128],
                                                                          rhs=wbf[:, k, hh * 512:(hh + 1) * 512], start=(k == 0), stop=(k == FC - 1)),
                         reads=[mix, wbf], writes=[p])
                P.op("dve", lambda e, p=p, hh=hh, x_=x_, z_=z_: e.scalar_tensor_tensor(out=z_[:, hh * 512:(hh + 1) * 512], in0=x_[:, hh * 512:(hh + 1) * 512],
                                                                                     scalar=ALPHA, in1=p[:, :], op0=ALU.mult, op1=ALU.add),
                     reads=[x_, p], writes=[z_])
                P.op("dve", lambda e, hh=hh, z_=z_, s_=s_: e.bn_stats(out=s_[:, hh, :], in_=z_[:, hh * 512:(hh + 1) * 512]), reads=[z_], writes=[s_])
            P.op("dve", lambda e, s_=s_, m_=m_: e.bn_aggr(out=m_[:, :], in_=s_[:, :, :]), reads=[s_], writes=[m_])
            P.op("act", lambda e, m_=m_: e.activation(out=m_[:, 1:2], in_=m_[:, 1:2], func=AF.Sqrt, bias=eps5[:, 0:1], scale=1.0), reads=[m_, eps5], writes=[m_])
            P.op("dve", lambda e, m_=m_: e.reciprocal(out=m_[:, 1:2], in_=m_[:, 1:2]), reads=[m_], writes=[m_])
            P.op("dve", lambda e, z_=z_, m_=m_: e.tensor_scalar(out=z_[:, :], in0=z_[:, :], scalar1=m_[:, 0:1], scalar2=m_[:, 1:2], op0=ALU.subtract, op1=ALU.mult),
                 reads=[z_, m_], writes=[z_])
            P.op("pool", lambda e, z_=z_: e.tensor_tensor(out=z_[:, :], in0=z_[:, :], in1=lng[:, :], op=ALU.mult), reads=[z_, lng], writes=[z_])
            P.op("pool", lambda e, z_=z_: e.tensor_tensor(out=z_[:, :], in0=z_[:, :], in1=lnb[:, :], op=ALU.add), reads=[z_, lnb], writes=[z_])
            P.dma("pool", xo_d[r0:r0 + 128, :], z_[:, :], reads=[z_], writes=[xo_d])
            if xoT_d is not None:
                for half in range(2):
                    pt_ = ptr[(ti * 2 + half) % 2]
                    for kk in range(4):
                        k = half * 4 + kk
                        P.op("pe", lambda e, pt_=pt_, kk=kk, k=k, z_=z_: e.transpose(out=pt_[:, kk * 128:(kk + 1) * 128], in_=z_[:, k * 128:(k + 1) * 128],
                                                                                  identity=idf[:, :]), reads=[z_, idf], writes=[pt_])
                    P.op("act", lambda e, pt_=pt_, half=half, xT_=xT_: e.copy(out=xT_[:, half * 4:(half + 1) * 4, :],
                                                                             in_=pt_[:, :].rearrange("p (a b) -> p a b", b=128)),
                         reads=[pt_], writes=[xT_])
                P.dma("sp", xoT_d.t.rearrange("(k p) t -> p k t", p=128)[:, :, r0:r0 + 128], xT_[:, :, :], reads=[xT_], writes=[xoT_d])
            ti += 1


DEBUG_OUT = False
SAME_ENGINE_SYNC = True
N_LAYERS_BUILD = 2
STAGES = "PABCDO"


def build_fused(S=SEQ):
    nc = bass.Bass("TRN2", target_bir_lowering=False)
    P = Prog(nc, same_engine_sync=SAME_ENGINE_SYNC)
    NT = S // 128
    L = DEPTH
    EI = "ExternalInput"
    xT0 = P.dram("xT0", [D_MODEL, S], F32, EI)
    x0 = P.dram("x0", [S, D_MODEL], F32, EI)
    w_in = P.dram("w_in", [L, D_MODEL, IN_COLS], F32, EI)
    w_out = P.dram("w_out", [L, D_MIX, D_MODEL], F32, EI)
    ab = P.dram("ab", [6, 128, 64], F32, EI)
    bi = P.dram("bi", [32, S], BF16, EI)
    cb = P.dram("cb", [4, 128, 512], BF16, EI)
    idb = P.dram("idb", [128, 128], BF16, EI)
    idf = P.dram("idf", [128, 128], F32, EI)
    dbias = P.dram("dbias", [6, 3, 2, 128, 512], F32, EI)
    tri = P.dram("tri", [128, 128], F32, EI)
    ones = P.dram("ones", [128, 128], F32, EI)
    mneg = P.dram("mneg", [128, 128], BF16, EI)
    mpos = P.dram("mpos", [128, 128], BF16, EI)
    s_cwx = P.dram("s_cwx", [L, 6, 64, 5], F32, EI)
    s_cwb = P.dram("s_cwb", [L, 6, 128, 5], F32, EI)
    s_cwc = P.dram("s_cwc", [L, 6, 128, 5], F32, EI)
    s_sc = P.dram("s_sc", [L, 6, 128, 3], F32, EI)
    d_cwq = P.dram("d_cwq", [L, 6, 64, 5], F32, EI)
    d_cwk = P.dram("d_cwk", [L, 6, 64, 5], F32, EI)
    d_cwv = P.dram("d_cwv", [L, 6, 64, 5], F32, EI)
    d_sc = P.dram("d_sc", [L, 6, 128, 2], F32, EI)
    d_nw = P.dram("d_nw", [L, 128, 64], F32, EI)
    o_snw = P.dram("o_snw", [L, 128, 3], F32, EI)
    o_lng = P.dram("o_lng", [L, 128, D_MODEL], F32, EI)
    o_lnb = P.dram("o_lnb", [L, 128, D_MODEL], F32, EI)
    out = P.dram("out", [S, D_MODEL], F32, "ExternalOutput")
    sk = "ExternalOutput" if DEBUG_OUT else None
    projT = P.dram("projT", [N_FM, S], F32, sk)
    projtm = P.dram("projtm", [S, N_TM], F32, sk)
    mixT = P.dram("mixT", [D_MIX, S], F32, sk)
    x1 = P.dram("x1", [S, D_MODEL], F32, sk)
    x1T = P.dram("x1T", [D_MODEL, S], BF16, None)

    def fm(off):
        return Src(projT, lambda h, off=off: projT[off + h * 64: off + (h + 1) * 64, :])

    def mixrows(off):
        return Src(mixT, lambda h, off=off: mixT[off + h * 64: off + (h + 1) * 64, :])

    for l in range(N_LAYERS_BUILD):
        if "P" in STAGES:
            with P.scope():
                emit_projF(P, xT0 if l == 0 else x1T, l != 0, w_in.t[l], w_in, projT, projtm, S)
        if "A" in STAGES:
            with P.scope():
                emit_moba(P, fm(FM_AQ), fm(FM_AK),
                          Src(projtm, lambda h: projtm.t[:, TM_AV + h * 64: TM_AV + (h + 1) * 64].rearrange("(t p) d -> p t d", p=128)),
                          fm(FM_AG), Src(ab, lambda h: ab.t[h]), bi, cb, idb, idf, mixrows(0), 6, S)
        if "B" in STAGES:
            with P.scope():
                emit_ssd(P, fm(FM_SX),
                         Src(projT, lambda h: projT[FM_SB + (h // 3) * 128: FM_SB + (h // 3 + 1) * 128, :]),
                         Src(projT, lambda h: projT[FM_SC + (h // 3) * 128: FM_SC + (h // 3 + 1) * 128, :]),
                         Src(s_cwx, lambda h, l=l: s_cwx.t[l, h]), Src(s_cwb, lambda h, l=l: s_cwb.t[l, h]), Src(s_cwc, lambda h, l=l: s_cwc.t[l, h]),
                         Src(projtm, lambda h: projtm.t[:, TM_DT:TM_DT + 6].rearrange("(t p) c -> p t c", p=128)),
                         Src(s_sc, lambda h, l=l: s_sc.t[l, h]), tri, ones, idf, idb, mneg, mixrows(384), 6, S)
        if "C" in STAGES:
            with P.scope():
                emit_dil(P, fm(FM_CQ), fm(FM_CK),
                         Src(projtm, lambda hg: projtm.t[:, TM_CV + hg[0] * 64: TM_CV + (hg[0] + 1) * 64].rearrange(
                             "(j p r) c -> p r j c", p=128, r=DIL_D[hg[1]])),
                         fm(FM_CG), Src(dbias, lambda h: dbias.t[h].rearrange("g o p q -> p (g o) q")), mixrows(768), 6, S)
        if "D" in STAGES:
            with P.scope():
                emit_gdn(P, fm(FM_DQ), fm(FM_DK), fm(FM_DV),
                         Src(d_cwq, lambda h, l=l: d_cwq.t[l, h]), Src(d_cwk, lambda h, l=l: d_cwk.t[l, h]), Src(d_cwv, lambda h, l=l: d_cwv.t[l, h]),
                         Src(projtm, lambda h: projtm.t[:, TM_DB:TM_DB + 6].rearrange("(t p) c -> p t c", p=128)),
                         Src(projtm, lambda h: projtm.t[:, TM_DA:TM_DA + 6].rearrange("(t p) c -> p t c", p=128)),
                         Src(d_sc, lambda h, l=l: d_sc.t[l, h]),
                         Src(projtm, lambda h: projtm.t[:, TM_DG + h * 64: TM_DG + (h + 1) * 64].rearrange("(t p) d -> p t d", p=128)),
                         Src(d_nw, lambda h, l=l: d_nw.t[l]), tri, ones, idf, idb, mneg, mpos, mixrows(1152), 6, S)
        if "O" in STAGES:
            with P.scope():
                last = (l == DEPTH - 1)
                emit_outlnF(P, mixT, projT, o_snw.t[l], w_out.t[l], o_lng.t[l], o_lnb.t[l], ones.t, idf.t, w_out,
                            x0 if l == 0 else x1, out if last else x1, None if last else x1T, S)
    return P.finish()


from concourse.bass_utils import run_bass_kernel_spmd

BATCH = 2
_NC = {}


def _host_inputs(p, b):
    perm = in_col_perm()
    CA = moba_consts()
    CR = rec_consts()
    L = DEPTH
    f32 = np.float32
    m = {}
    m["xT0"] = np.ascontiguousarray(p["x"][b].T)
    m["x0"] = np.ascontiguousarray(p["x"][b])
    m["w_in"] = np.ascontiguousarray(p["w_in"][:, :, perm])
    m["w_out"] = np.ascontiguousarray(p["w_out"])
    m["ab"] = CA["ab"]
    m["bi"] = CA["bi"]
    m["cb"] = CA["cb"]
    m["idb"] = CA["idb"]
    m["idf"] = CA["idf"]
    m["dbias"] = dil_consts()
    m["tri"] = CR["tri"]
    m["ones"] = CR["ones"]
    m["mneg"] = CR["mneg"]
    m["mpos"] = CR["mpos"]
    cw5 = [np.concatenate([p["ssm_conv_w"][l].T, p["ssm_conv_b"][l][:, None]], axis=1).astype(f32) for l in range(L)]
    m["s_cwx"] = np.ascontiguousarray(np.stack([np.stack([cw5[l][h * 64:(h + 1) * 64] for h in range(6)]) for l in range(L)]))
    m["s_cwb"] = np.ascontiguousarray(np.stack([np.stack([cw5[l][384 + (h // 3) * 128: 384 + (h // 3 + 1) * 128] for h in range(6)]) for l in range(L)]))
    m["s_cwc"] = np.ascontiguousarray(np.stack([np.stack([cw5[l][640 + (h // 3) * 128: 640 + (h // 3 + 1) * 128] for h in range(6)]) for l in range(L)]))
    m["s_sc"] = np.ascontiguousarray(np.stack([np.stack([np.tile(np.stack([p["ssm_dt_bias"][l, h], p["ssm_A_log"][l, h], p["ssm_D"][l, h]])[None].astype(f32),
                                                                (128, 1)) for h in range(6)]) for l in range(L)]))
    cd5 = [np.concatenate([p["dn_conv_w"][l].T, p["dn_conv_b"][l][:, None]], axis=1).astype(f32) for l in range(L)]
    for nm, off in (("q", 0), ("k", 384), ("v", 768)):
        m["d_cw" + nm] = np.ascontiguousarray(np.stack([np.stack([cd5[l][off + h * 64: off + (h + 1) * 64] for h in range(6)]) for l in range(L)]))
    m["d_sc"] = np.ascontiguousarray(np.stack([np.stack([np.tile(np.stack([p["dn_dt_bias"][l, h], p["dn_A_log"][l, h]])[None].astype(f32), (128, 1))
                                                        for h in range(6)]) for l in range(L)]))
    m["d_nw"] = np.ascontiguousarray(np.stack([np.tile(p["dn_norm_w"][l][None].astype(f32), (128, 1)) for l in range(L)]))
    m["o_snw"] = np.ascontiguousarray(np.stack([p["ssm_norm_w"][l].reshape(3, 128).T.astype(f32) for l in range(L)]))
    m["o_lng"] = np.ascontiguousarray(np.stack([np.tile(p["ln_g"][l][None].astype(f32), (128, 1)) for l in range(L)]))
    m["o_lnb"] = np.ascontiguousarray(np.stack([np.tile(p["ln_b"][l][None].astype(f32), (128, 1)) for l in range(L)]))
    return m


def kernel(**inputs):
    p = {k: np.asarray(v, dtype=np.float32) for k, v in inputs.items()}
    if "nc" not in _NC:
        _NC["nc"] = build_fused()
    in_maps = [_host_inputs(p, b) for b in range(BATCH)]
    res = run_bass_kernel_spmd(_NC["nc"], in_maps, core_ids=list(range(BATCH)))
    _NC["res"] = res
    return np.stack([res.results[b]["out"] for b in range(BATCH)]).astype(np.float32)
```
